# Optimizing a Trainium2 kernel written in Bass

```python
import math
import jax, jax.numpy as jnp
from jax import lax
import numpy as np

D_MODEL = 1024
BATCH = 8
SEQ = 2048
DEPTH = 2

D_MIX = D_MODEL
HEAD_DIM = 64
N_ATT_HEADS = 8
D_ATT = N_ATT_HEADS * HEAD_DIM
DILATED_PATTERNS = ((128, 1), (512, 4), (2048, 16))
ATT_BLOCK = 128
SSM_GROUP = 16
D_SSM = D_MIX // 4
N_SSM_GROUPS = D_SSM // SSM_GROUP
SSM_STATE = 64
POOL_WINDOWS = (2, 4, 8, 16)
D_POOL = D_MIX - D_ATT - D_SSM
POOL_GROUP = D_POOL // len(POOL_WINDOWS)
D_IN = 3 * D_ATT + D_SSM + D_POOL
IN_SPLITS = (D_ATT, 2 * D_ATT, 3 * D_ATT, 3 * D_ATT + D_SSM)
D_FF = 256 * int(math.ceil(8 * D_MODEL / 3 / 256))
N_BUCKETS = 32
MAX_DISTANCE = 2048
ALPHA = (2 * DEPTH) ** 0.25
BETA = (8 * DEPTH) ** -0.25
FFN_RES = 0.5
LN_EPS = 1e-5
NEG = -1e30

kernel_name = "hybrid_dilated_attn_s5_pool_macaron_deepnorm"


def _layernorm(x):
    xf = x.astype(jnp.float32)
    mu = xf.mean(-1, keepdims=True)
    var = jnp.square(xf - mu).mean(-1, keepdims=True)
    return ((xf - mu) * lax.rsqrt(var + LN_EPS)).astype(x.dtype)


def _layernorm_affine(x, gain, bias):
    return _layernorm(x) * gain + bias


def _modulate(x, shift, scale):
    return _layernorm(x) * (1.0 + scale) + shift


def _swiglu(h, w_gate, w_up, w_down):
    return (jax.nn.silu(h @ w_gate) * (h @ w_up)) @ w_down


def _t5_bucket(dist):
    max_exact = N_BUCKETS // 2
    d = np.maximum(dist, 1).astype(np.float32)
    large = max_exact + (np.log(d / max_exact) / math.log(MAX_DISTANCE / max_exact)
                         * (N_BUCKETS - max_exact)).astype(np.int32)
    large = np.minimum(large, N_BUCKETS - 1)
    return np.where(dist < max_exact, dist, large).astype(np.int32)


def _dilated_branch(q, k, v, rel_bias, window, dilation):
    B, S, H, E = q.shape
    Q = ATT_BLOCK
    n_keys = window // dilation
    L = S // dilation
    nb = -(-L // Q)
    Lp = nb * Q

    def by_residue(t):
        t = t.reshape(B, L, dilation, H, E).transpose(0, 2, 1, 3, 4)
        return jnp.pad(t, ((0, 0), (0, 0), (0, Lp - L), (0, 0), (0, 0)))

    def band(t):
        t = jnp.pad(t, ((0, 0), (0, 0), (Q, 0), (0, 0), (0, 0))).reshape(B, dilation, nb + 1, Q, H, E)
        return jnp.concatenate([t[:, :, :-1], t[:, :, 1:]], axis=3)

    qb = by_residue(q).reshape(B, dilation, nb, Q, H, E)
    kb = band(by_residue(k))
    vb = band(by_residue(v))

    i = np.arange(Q)[:, None]
    j = np.arange(2 * Q)[None, :]
    r = i + Q - j
    in_band = (r >= 0) & (r <= n_keys)
    k_abs = np.arange(nb)[:, None, None] * Q + j[None] - Q
    valid = (in_band[None] & (k_abs >= 0))[:, None]
    bucket = _t5_bucket(np.clip(r, 0, None) * dilation)
    bias = jnp.transpose(rel_bias[bucket], (2, 0, 1)).astype(jnp.float32)

    s = jnp.einsum('brnqhe,brnkhe->brnhqk', qb, kb, preferred_element_type=jnp.float32)
    s = jnp.where(valid, s + bias, NEG)
    m = s.max(-1, keepdims=True)
    p = jnp.exp(s - m)
    den = p.sum(-1, keepdims=True)
    o = jnp.einsum('brnhqk,brnkhe->brnqhe', p, vb.astype(jnp.float32))
    o = o / jnp.swapaxes(den, 3, 4)
    lse = jnp.swapaxes((m + jnp.log(den))[..., 0], 3, 4)

    o = o.reshape(B, dilation, Lp, H, E)[:, :, :L].transpose(0, 2, 1, 3, 4).reshape(B, S, H, E)
    lse = lse.reshape(B, dilation, Lp, H)[:, :, :L].transpose(0, 2, 1, 3).reshape(B, S, H)
    return o, lse


def _dilated_attention(q, k, v, rel_bias):
    outs, lses = [], []
    for window, dilation in DILATED_PATTERNS:
        o, lse = _dilated_branch(q, k, v, rel_bias, window, dilation)
        outs.append(o)
        lses.append(lse)
    w = jax.nn.softmax(jnp.stack(lses, 0), axis=0)
    return jnp.einsum('pbsh,pbshe->bshe', w, jnp.stack(outs, 0))


def _s5(u, a_re, a_im, log_dt, b_re, b_im, c_re, c_im, d_skip, glu_w, glu_b):
    Bsz, S, _ = u.shape
    f32 = jnp.float32
    lam = lax.complex(a_re.astype(f32), a_im.astype(f32))
    dt = jnp.exp(log_dt.astype(f32))[:, None]
    a_bar = jnp.exp(lam * dt)
    b_bar = ((a_bar - 1.0) / lam)[:, :, None] * lax.complex(b_re.astype(f32), b_im.astype(f32))
    uf = u.astype(f32)
    bu = jnp.einsum('bsgc,gpc->bsgp', uf.reshape(Bsz, S, N_SSM_GROUPS, SSM_GROUP), b_bar)
    a_full = jnp.broadcast_to(a_bar, bu.shape)

    def combine(e1, e2):
        a1, b1 = e1
        a2, b2 = e2
        return a2 * a1, a2 * b1 + b2

    _, states = lax.associative_scan(combine, (a_full, bu), axis=1)
    cm = lax.complex(c_re.astype(f32), c_im.astype(f32))
    y = jnp.einsum('gcp,bsgp->bsgc', cm, states).real.reshape(Bsz, S, D_SSM)
    y = y + d_skip.astype(f32) * uf
    return y * jax.nn.sigmoid(jax.nn.gelu(y) @ glu_w.astype(f32) + glu_b.astype(f32))


def _pool_mixer(u, pool_w, pool_scale):
    Bsz, S, _ = u.shape
    ug = u.astype(jnp.float32).reshape(Bsz, S, len(POOL_WINDOWS), POOL_GROUP)
    cs = jnp.cumsum(ug, axis=1)
    count = jnp.arange(1, S + 1, dtype=jnp.float32)
    means = []
    for g, w in enumerate(POOL_WINDOWS):
        c_g = cs[:, :, g]
        lagged = jnp.pad(c_g[:, :-w], ((0, 0), (w, 0), (0, 0)))
        means.append((c_g - lagged) / jnp.minimum(count, float(w))[None, :, None])
    pooled = jnp.stack(means, axis=2) - ug
    y = jnp.einsum('bsgc,gcd->bsgd', pooled, pool_w.astype(jnp.float32)).reshape(Bsz, S, D_POOL)
    return y * pool_scale.astype(jnp.float32)


def _hybrid_mixer(h, rel_bias, w_in, w_out, a_re, a_im, log_dt, b_re, b_im, c_re, c_im,
                  d_skip, glu_w, glu_b, pool_w, pool_scale):
    Bsz, S, _ = h.shape
    z = h @ w_in
    q, k, v, u_ssm, u_pool = jnp.split(z, IN_SPLITS, axis=-1)
    heads = lambda t: t.reshape(Bsz, S, N_ATT_HEADS, HEAD_DIM)
    y_att = _dilated_attention(heads(q) * HEAD_DIM ** -0.5, heads(k), heads(v), rel_bias)
    y_att = y_att.reshape(Bsz, S, D_ATT)
    y_ssm = _s5(u_ssm, a_re, a_im, log_dt, b_re, b_im, c_re, c_im, d_skip, glu_w, glu_b)
    y_pool = _pool_mixer(u_pool, pool_w, pool_scale)
    y = jnp.concatenate([y_att.astype(h.dtype), y_ssm.astype(h.dtype), y_pool.astype(h.dtype)], axis=-1)
    return y @ w_out


def setup_inputs(seed: int = 0) -> dict:
    key = jax.random.key(seed)
    ks = jax.random.split(key, 26)
    f32 = jnp.float32
    nrm = lambda i, shape, std: std * jax.random.normal(ks[i], shape, f32)
    L, G, P = DEPTH, N_SSM_GROUPS, SSM_STATE
    n = jnp.arange(P, dtype=f32)
    return {
        "x": nrm(0, (BATCH, SEQ, D_MODEL), 1.0),
        "c": nrm(1, (BATCH, D_MODEL), 1.0),
        "rel_bias": nrm(2, (N_BUCKETS, N_ATT_HEADS), 0.1),
        "ada_w": nrm(3, (L, D_MODEL, 9 * D_MODEL), D_MODEL ** -0.5),
        "ada_b": nrm(4, (L, 9 * D_MODEL), 0.02),
        "ln_g": 1.0 + nrm(5, (L, 3, D_MODEL), 0.02),
        "ln_b": nrm(6, (L, 3, D_MODEL), 0.02),
        "ffn_w_gate": nrm(7, (L, 2, D_MODEL, D_FF), D_MODEL ** -0.5),
        "ffn_w_up": nrm(8, (L, 2, D_MODEL, D_FF), D_MODEL ** -0.5),
        "ffn_w_down": nrm(9, (L, 2, D_FF, D_MODEL), BETA * D_FF ** -0.5),
        "w_in": nrm(10, (L, D_MODEL, D_IN), D_MODEL ** -0.5),
        "w_out": nrm(11, (L, D_MIX, D_MODEL), BETA * D_MIX ** -0.5),
        "ssm_a_re": -0.5 + nrm(12, (L, G, P), 0.01),
        "ssm_a_im": math.pi * n + nrm(13, (L, G, P), 0.01),
        "ssm_log_dt": jax.random.uniform(ks[14], (L, G), f32, math.log(1e-3), math.log(1e-1)),
        "ssm_b_re": nrm(15, (L, G, P, SSM_GROUP), (2 * SSM_GROUP) ** -0.5),
        "ssm_b_im": nrm(16, (L, G, P, SSM_GROUP), (2 * SSM_GROUP) ** -0.5),
        "ssm_c_re": nrm(17, (L, G, SSM_GROUP, P), (2 * P) ** -0.5),
        "ssm_c_im": nrm(18, (L, G, SSM_GROUP, P), (2 * P) ** -0.5),
        "ssm_d": nrm(19, (L, D_SSM), 1.0),
        "glu_w": nrm(20, (L, D_SSM, D_SSM), D_SSM ** -0.5),
        "glu_b": nrm(21, (L, D_SSM), 0.02),
        "pool_w": nrm(22, (L, len(POOL_WINDOWS), POOL_GROUP, POOL_GROUP), POOL_GROUP ** -0.5),
        "pool_scale": 1.0 + nrm(23, (L, D_POOL), 0.02),
    }


def reference(x, c, rel_bias, ada_w, ada_b, ln_g, ln_b, ffn_w_gate, ffn_w_up, ffn_w_down,
              w_in, w_out, ssm_a_re, ssm_a_im, ssm_log_dt, ssm_b_re, ssm_b_im, ssm_c_re,
              ssm_c_im, ssm_d, glu_w, glu_b, pool_w, pool_scale):
    Bsz = x.shape[0]
    cond = jax.nn.silu(c)
    for l in range(DEPTH):
        mod = (cond @ ada_w[l] + ada_b[l]).reshape(Bsz, 3, 3, 1, D_MODEL)
        h = _modulate(x, mod[:, 0, 0], mod[:, 0, 1])
        f = _swiglu(h, ffn_w_gate[l, 0], ffn_w_up[l, 0], ffn_w_down[l, 0])
        x = _layernorm_affine(ALPHA * x + FFN_RES * mod[:, 0, 2] * f, ln_g[l, 0], ln_b[l, 0])
        h = _modulate(x, mod[:, 1, 0], mod[:, 1, 1])
        y = _hybrid_mixer(h, rel_bias, w_in[l], w_out[l], ssm_a_re[l], ssm_a_im[l], ssm_log_dt[l],
                          ssm_b_re[l], ssm_b_im[l], ssm_c_re[l], ssm_c_im[l], ssm_d[l],
                          glu_w[l], glu_b[l], pool_w[l], pool_scale[l])
        x = _layernorm_affine(ALPHA * x + mod[:, 1, 2] * y, ln_g[l, 1], ln_b[l, 1])
        h = _modulate(x, mod[:, 2, 0], mod[:, 2, 1])
        f = _swiglu(h, ffn_w_gate[l, 1], ffn_w_up[l, 1], ffn_w_down[l, 1])
        x = _layernorm_affine(ALPHA * x + FFN_RES * mod[:, 2, 2] * f, ln_g[l, 2], ln_b[l, 2])
    return x
```

```python
import contextlib
import math
import numpy as np
import concourse.bass as bass
import concourse.mybir as mybir
from concourse.bass_utils import run_bass_kernel_spmd

F32 = mybir.dt.float32
BF16 = mybir.dt.bfloat16
I32 = mybir.dt.int32
AF = mybir.ActivationFunctionType
ALU = mybir.AluOpType

SEQ = 2048
D = 1024
DFF = 2816
DEPTH = 2
NT = SEQ // 128
ALPHA = (2 * DEPTH) ** 0.25
LN_EPS = 1e-5
NEG = -1e30
PATTERNS = ((128, 1), (512, 4), (2048, 16))

ENGS = ("pe", "act", "dve", "pool", "sp")
NDMASEM = 12


class Op:
    __slots__ = ("eng", "fn", "deps", "marked", "val", "is_dma", "dsem", "dval", "idx")

    def __init__(self, eng, fn, is_dma):
        self.eng = eng
        self.fn = fn
        self.deps = []
        self.marked = False
        self.val = None
        self.is_dma = is_dma
        self.dsem = None
        self.dval = None
        self.idx = None


class Sched:
    def __init__(self, nc):
        self.nc = nc
        self.ops = {e: [] for e in ENGS}
        self.writers = {}
        self.readers = {}
        self.ndma = {e: 0 for e in ENGS}
        self.final_waits = []
        self.scope = None

    def op(self, eng, fn, r=(), w=(), dma=False, final=False):
        o = Op(eng, fn, dma)
        deps = []
        r = list(r)
        w = list(w)
        if "ALL" not in w:
            r.append("ALL")
        if self.scope is not None and self.scope not in w:
            r.append(self.scope)
        for x in r:
            deps.extend(self.writers.get(x, ()))
        for x in w:
            deps.extend(self.writers.get(x, ()))
            deps.extend(self.readers.get(x, ()))
        seen = set()
        for d in deps:
            if d is o or id(d) in seen:
                continue
            seen.add(id(d))
            if d.eng == "pe" and eng == "pe" and not d.is_dma and not dma:
                continue
            o.deps.append(d)
            if not d.is_dma:
                d.marked = True
        for x in w:
            if self.readers.get(x):
                self.writers[x] = [o]
                self.readers[x] = []
            else:
                ws = self.writers.setdefault(x, [])
                ws[:] = [p for p in ws if not (p.eng == eng and p.is_dma == dma and not dma)]
                ws.append(o)
        for x in r:
            if x not in w:
                rs = self.readers.setdefault(x, [])
                rs[:] = [p for p in rs if not (p.eng == eng and not p.is_dma and not dma)]
                rs.append(o)
        if dma:
            j = self.ndma[eng]
            self.ndma[eng] = j + 1
            o.dsem = j % NDMASEM
            o.dval = 16 * (j // NDMASEM + 1)
            o.idx = j
        self.ops[eng].append(o)
        if final:
            self.final_waits.append(o)
        return o

    def emit(self):
        nc = self.nc
        with contextlib.ExitStack() as st:
            csem = {e: st.enter_context(nc.semaphore("c_" + e)) for e in ENGS}
            dsem = {e: [st.enter_context(nc.semaphore("d_%s_%d" % (e, i))) for i in range(NDMASEM)]
                    for e in ENGS if self.ndma[e] > 0}
            for e in ENGS:
                c = 0
                for o in self.ops[e]:
                    if o.marked and not o.is_dma:
                        c += 1
                        o.val = c
            block = st.enter_context(nc.Block())

            def run(e, eng):
                waited = {}
                for o in self.ops[e]:
                    waits = []
                    for d in o.deps:
                        if d.is_dma:
                            waits.append((("d", d.eng, d.dsem), dsem[d.eng][d.dsem], d.dval))
                        else:
                            waits.append((("c", d.eng), csem[d.eng], d.val))
                    if o.is_dma and o.idx >= NDMASEM:
                        waits.append((("d", e, o.dsem), dsem[e][o.dsem], o.dval - 16))
                    for key, s, v in waits:
                        if waited.get(key, 0) >= v:
                            continue
                        eng.wait_ge(s, v)
                        waited[key] = v
                    ins = o.fn(eng)
                    if o.is_dma:
                        ins.then_inc(dsem[e][o.dsem], 16)
                    elif o.marked:
                        ins.then_inc(csem[e], 1)
                for o in self.final_waits:
                    if o.eng == e:
                        eng.wait_ge(dsem[e][o.dsem], o.dval)

            if self.ops["sp"]:
                @block.sync
                def _(eng):
                    run("sp", eng)
            if self.ops["pe"]:
                @block.tensor
                def _(eng):
                    run("pe", eng)
            if self.ops["act"]:
                @block.scalar
                def _(eng):
                    run("act", eng)
            if self.ops["dve"]:
                @block.vector
                def _(eng):
                    run("dve", eng)
            if self.ops["pool"]:
                @block.gpsimd
                def _(eng):
                    run("pool", eng)


def t5_bucket(dist):
    n_buckets, max_distance = 32, 2048
    max_exact = n_buckets // 2
    d = np.maximum(dist, 1).astype(np.float32)
    large = max_exact + (np.log(d / max_exact) / math.log(max_distance / max_exact)
                         * (n_buckets - max_exact)).astype(np.int32)
    large = np.minimum(large, n_buckets - 1)
    return np.where(dist < max_exact, dist, large).astype(np.int32)


def static_consts():
    c = {}
    c["identf"] = np.eye(128, dtype=np.float32)
    c["jex"] = np.eye(128, dtype=np.float32)[::-1].copy()
    jsw = np.zeros((128, 128), np.float32)
    for m in range(64):
        jsw[m + 64, m] = -1.0
        jsw[m, m + 64] = 1.0
    c["jsw"] = jsw
    oh = np.zeros((32, 3 * 384), np.float32)
    negm = np.zeros((8, 3 * 384), np.float32)
    for bi, (win, dil) in enumerate(PATTERNS):
        for u in range(384):
            dist = u - 127
            if 0 <= dist <= win // dil:
                oh[t5_bucket(np.array([dist * dil]))[0], bi * 384 + u] = 1.0
            else:
                negm[:, bi * 384 + u] = NEG
    c["oh"] = oh
    c["negm"] = negm
    gm = np.zeros((128, 8), np.float32)
    for p in range(128):
        gm[p, p // 16] = 1.0
    c["gmask"] = gm
    wins = np.array([2, 4, 8, 16], np.float32)
    wp = np.zeros((128, 2), np.float32)
    for ch in range(2):
        for p in range(128):
            wp[p, ch] = wins[ch * 2 + p // 64]
    c["invw"] = (1.0 / wp).astype(np.float32)
    t = np.arange(16, dtype=np.float32)[None, None, :]
    c["rc16"] = (1.0 / np.minimum(t + 1.0, wp[:, :, None])).astype(np.float32)
    c["sgn"] = np.concatenate([-np.ones((64, 1), np.float32), np.ones((64, 1), np.float32)], 0)
    return c


def build(stop=None, parts=("pool", "ssm", "attn"), dbg=False):
    nc = bass.Bass("TRN2", target_bir_lowering=False)

    def din(name, shape, dt=F32):
        return nc.dram_tensor(name, list(shape), dt, kind="ExternalInput").ap()

    x_d = din("x", [SEQ, D])
    cT_d = din("cT", [128, 8])
    adaw_d = din("ada_w", [DEPTH, D, 9 * D])
    adab_d = din("ada_b", [DEPTH, 9 * D])
    lng_d = din("ln_g", [DEPTH, 3, D])
    lnb_d = din("ln_b", [DEPTH, 3, D])
    wg_d = din("ffn_w_gate", [DEPTH, 2, D, DFF])
    wu_d = din("ffn_w_up", [DEPTH, 2, D, DFF])
    wd_d = din("ffn_w_down", [DEPTH, 2, DFF, D])
    win_d = din("w_in", [DEPTH, D, 2048])
    wout_d = din("w_out", [DEPTH, D, D])
    relb_d = din("rel_bias", [32, 8])
    sare_d = din("sa_re", [DEPTH, 128, 16])
    saim_d = din("sa_im", [DEPTH, 128, 16])
    sldt_d = din("sldt", [DEPTH, 128, 16])
    sp1_d = din("sP1", [DEPTH, 128, 256])
    sp2_d = din("sP2", [DEPTH, 128, 256])
    sct_d = din("sCT", [DEPTH, 128, 256])
    sd_d = din("sd", [DEPTH, 128, 2])
    gluw_d = din("glu_w", [DEPTH, 128, 2, 256])
    glub_d = din("glu_b", [DEPTH, 128, 2])
    poolw_d = din("pool_w", [DEPTH, 128, 2, 128])
    pools_d = din("pool_s", [DEPTH, 128, 2])
    identf_d = din("identf", [128, 128])
    jex_d = din("jex", [128, 128])
    jsw_d = din("jsw", [128, 128])
    oh_d = din("oh", [32, 1152])
    negm_d = din("negm", [8, 1152])
    gmask_d = din("gmask", [128, 8])
    invw_d = din("invw", [128, 2])
    rc16_d = din("rc16", [128, 2, 16])
    sgn_d = din("sgn", [128, 1])
    out_d = nc.dram_tensor("out", [SEQ, D], F32, kind="ExternalOutput").ap()
    fv_d = nc.dram_tensor("fv_scratch", [8, 1152], F32, kind="Internal").ap()

    st = contextlib.ExitStack()
    with st:
        def T(name, shape, dt):
            return st.enter_context(nc.sbuf_tensor(name, list(shape), dt))

        PSB = [st.enter_context(nc.psum_tensor("ps%d" % i, [128, 512], F32)) for i in range(8)]

        def ps(k):
            return PSB[k], ("ps", k)

        X = T("X", [128, NT, D], F32)
        HT = T("HT", [128, 8, SEQ], BF16)
        YT = T("YT", [128, 8, SEQ], BF16)
        ARB = 49152
        AR = T("AR", [128, ARB // 2], BF16)
        ident = T("ident", [128, 128], BF16)
        identf = T("identf_s", [128, 128], F32)
        jex = T("jex_s", [128, 128], F32)
        jsw = T("jsw_s", [128, 128], F32)
        onesf = T("onesf", [128, 128], F32)
        jswb = T("jswb", [128, 128], BF16)
        modcol = T("modcol", [128, DEPTH, 72], F32)
        XH = [T("XH%d" % i, [128, D], BF16) for i in range(4)]
        mv = T("mv", [128, 4, 2], F32)
        sc = T("sc", [128, 4, 4], F32)
        condT = T("condT", [128, 8], F32)
        gmask = T("gmask_s", [128, 8], F32)
        invw = T("invw_s", [128, 2], F32)
        rc16 = T("rc16_s", [128, 2, 16], F32)
        sgn = T("sgn_s", [128, 1], F32)

        S = Sched(nc)

        class Arena:
            def __init__(self, base, size):
                self.off = base
                self.end = base + size

            def alloc(self, shape, dt):
                n = int(np.prod(shape[1:]))
                nb = n * (4 if dt in (F32, I32) else 2)
                nb = (nb + 31) // 32 * 32
                assert self.off + nb <= self.end, ("arena overflow", shape, self.off, nb, self.end)
                v = AR[0:shape[0], self.off // 2:(self.off + nb) // 2]
                self.off += nb
                if dt != BF16:
                    v = v.bitcast(dt)
                v = v[:, 0:n]
                if len(shape) == 3:
                    v = v.rearrange("p (a b) -> p a b", a=shape[1])
                elif len(shape) == 4:
                    v = v.rearrange("p (a b c) -> p a b c", a=shape[1], b=shape[2])
                return v

        class ArenaYT(Arena):
            def alloc(self, shape, dt):
                n = int(np.prod(shape[1:]))
                nb = n * (4 if dt in (F32, I32) else 2)
                nb = (nb + 31) // 32 * 32
                assert self.off + nb <= self.end
                flat = YT[:, :, :].rearrange("p a b -> p (a b)")
                v = flat[0:shape[0], self.off // 2:(self.off + nb) // 2]
                self.off += nb
                if dt != BF16:
                    v = v.bitcast(dt)
                v = v[:, 0:n]
                if len(shape) == 3:
                    v = v.rearrange("p (a b) -> p a b", a=shape[1])
                return v

        def dma(eng, out, in_, r=(), w=(), final=False, slow=False):
            if slow:
                return S.op(eng, lambda e: e.dma_start(out=out, in_=in_, allow_slow_non_contiguous=True), r=r, w=w, dma=True, final=final)
            return S.op(eng, lambda e: e.dma_start(out=out, in_=in_), r=r, w=w, dma=True, final=final)

        def mm(out, lhsT, rhs, start, stop, r, w):
            return S.op("pe", lambda e: e.matmul(out, lhsT=lhsT, rhs=rhs, start=start, stop=stop), r=r, w=w)

        def tr(out, in_, idn, r, w):
            return S.op("pe", lambda e: e.transpose(out=out, in_=in_, identity=idn), r=r, w=w)

        def act(out, in_, func, r, w, bias=0.0, scale=1.0):
            return S.op("act", lambda e: e.activation(out=out, in_=in_, func=func, bias=bias, scale=scale), r=r, w=w)

        def ts(eng, out, in0, s1, s2, op0, op1, r, w):
            if op1 is None:
                return S.op(eng, lambda e: e.tensor_scalar(out=out, in0=in0, scalar1=s1, scalar2=None, op0=op0), r=r, w=w)
            return S.op(eng, lambda e: e.tensor_scalar(out=out, in0=in0, scalar1=s1, scalar2=s2, op0=op0, op1=op1), r=r, w=w)

        def tt(eng, out, in0, in1, op, r, w):
            return S.op(eng, lambda e: e.tensor_tensor(out=out, in0=in0, in1=in1, op=op), r=r, w=w)

        def stt(out, in0, scalar, in1, op0, op1, r, w):
            return S.op("dve", lambda e: e.scalar_tensor_tensor(out=out, in0=in0, scalar=scalar, in1=in1, op0=op0, op1=op1), r=r, w=w)

        def cp(eng, out, in_, r, w):
            if eng == "act":
                return S.op("act", lambda e: e.copy(out=out, in_=in_), r=r, w=w)
            return S.op(eng, lambda e: e.tensor_copy(out=out, in_=in_), r=r, w=w)

        def memset(eng, ap, val, w):
            return S.op(eng, lambda e: e.memset(ap, val), w=w)

        fsrc_d = nc.dram_tensor("fence_src", [1, 16], F32, kind="Internal").ap()
        fdst_d = nc.dram_tensor("fence_dst", [1, 16], F32, kind="Internal").ap()

        def fence():
            sv = S.scope
            S.scope = None
            S.op("sp", lambda e: e.dma_start(out=fdst_d, in_=fsrc_d), w=["ARENA"], dma=True)
            S.scope = sv

        def fence_all():
            S.op("dve", lambda e: e.memset(sc[:, 0, 3:4], 0.0), w=["ALL", "ARENA", ("sc3", 0)])

        class arena_scope:
            def __enter__(self):
                self.sv = S.scope
                S.scope = "ARENA"

            def __exit__(self, *a):
                S.scope = self.sv

        dma("sp", identf[:], identf_d, w=["identf"])
        dma("sp", jex[:], jex_d, w=["jex"])
        dma("sp", jsw[:], jsw_d, w=["jsw"])
        dma("sp", gmask[:], gmask_d, w=["gmask"])
        dma("sp", invw[:], invw_d, w=["invw"])
        dma("sp", rc16[:], rc16_d, w=["rc16"])
        dma("sp", sgn[:], sgn_d, w=["sgn"])
        cp("dve", ident[:], identf[:], r=["identf"], w=["ident"])
        cp("dve", jswb[:], jsw[:], r=["jsw"], w=["jswb"])
        memset("dve", onesf[:], 1.0, w=["onesf"])
        xv = x_d.rearrange("(t p) d -> p t d", p=128)
        for q in range(4):
            dma("sp", X[:, q * 4:(q + 1) * 4, :], xv[:, q * 4:(q + 1) * 4, :], w=[("X", t) for t in range(q * 4, q * 4 + 4)])

        def ada_phase():
            ar = Arena(0, ARB)
            AW = [ar.alloc([128, 8, 512], F32) for _ in range(2)]
            AB = [ar.alloc([1, 512], F32) for _ in range(2)]
            ROW = [ar.alloc([1, 512], F32) for _ in range(2)]
            cTs = ar.alloc([128, 8], F32)
            dma("sp", cTs, cT_d, w=["cTs"])
            act(condT[:], cTs, AF.Silu, r=["cTs"], w=["condT"])
            it = 0
            import itertools
            gen = itertools.chain(ssm_setup_gen(0), ssm_setup_gen(1))
            prep_q = list(range(NT))
            gen_done = [False]
            for l in range(DEPTH):
                for nb in range(18):
                    for _ in range(6):
                        if next(gen, "done") == "done":
                            gen_done[0] = True
                    b = it % 2
                    it += 1
                    src = adaw_d[l, :, nb * 512:(nb + 1) * 512].rearrange("(kc p) n -> p kc n", p=128)
                    dma("sp", AW[b], src, w=[("AW", b)])
                    dma("sp", AB[b], adab_d[l:l + 1, nb * 512:(nb + 1) * 512], w=[("AB", b)])
                    pr, prr = ps(b)
                    for kc in range(8):
                        mm(pr[0:1, :], condT[:, kc:kc + 1], AW[b][:, kc, :], kc == 0, False,
                           r=["condT", ("AW", b)], w=[prr])
                    mm(pr[0:1, :], onesf[0:1, 0:1], AB[b], False, True, r=["onesf", ("AB", b)], w=[prr])
                    cp("act", ROW[b], pr[0:1, :], r=[prr], w=[("ROW", b)])
                    pc, pcr = ps(2 + b)
                    for j in range(4):
                        mm(pc[:, j:j + 1], ROW[b][0:1, j * 128:(j + 1) * 128], onesf[0:1, 0:1], True, True,
                           r=[("ROW", b), "onesf"], w=[pcr])
                    v = nb // 2
                    addc = 1.0 if v % 3 == 1 else 0.0
                    act(modcol[:, l, nb * 4:(nb + 1) * 4], pc[:, 0:4], AF.Identity, r=[pcr], w=[("modcol", l, v)], bias=float(addc))
                    if gen_done[0] and prep_q:
                        tq_ = prep_q.pop(0)
                        sv_ = S.scope
                        S.scope = None
                        prep_tile_a(0, 0, tq_)
                        prep_tile_b(0, 0, tq_, extra=[("ssc", 0), ("ssc", 1)])
                        S.scope = sv_
            for _ in gen:
                pass
            sv_ = S.scope
            S.scope = None
            while prep_q:
                tq_ = prep_q.pop(0)
                prep_tile_a(0, 0, tq_)
                prep_tile_b(0, 0, tq_, extra=[("ssc", 0), ("ssc", 1)])
            S.scope = sv_


        NSLOT = 4
        sums = T("sums", [128, NSLOT, 4], F32)

        def finish_stats(slot, eps, c0, c1):
            ts("dve", mv[:, slot, 0:1], sums[:, slot, c0:c0 + 1], 1.0 / D, None, ALU.mult, None, r=[("sums", slot, c0)], w=[("mv", slot)])
            tt("dve", mv[:, slot, 1:2], mv[:, slot, 0:1], mv[:, slot, 0:1], ALU.mult, r=[("mv", slot)], w=[("mv1", slot)])
            stt(sc[:, slot, 3:4], sums[:, slot, c1:c1 + 1], 1.0 / D, mv[:, slot, 1:2], ALU.mult, ALU.subtract,
                r=[("sums", slot, c1), ("mv1", slot)], w=[("sc3", slot)])
            act(sc[:, slot, 0:1], sc[:, slot, 3:4], AF.Sqrt, r=[("sc3", slot)], w=[("sc0", slot)], bias=float(eps))
            S.op("dve", lambda e: e.reciprocal(out=sc[:, slot, 1:2], in_=sc[:, slot, 0:1]), r=[("sc0", slot)], w=[("sc1", slot)])

        def act_accum(tt_, slot, func, col):
            S.op("act", lambda e: e.activation(out=XH[slot][:], in_=X[:, tt_, :], func=func, accum_out=sums[:, slot, col:col + 1]),
                 r=[("X", tt_)], w=[("XH", slot), ("sums", slot, col)])

        def prep_tile_a(l, i, tt_, have_sum=False):
            slot = tt_ % NSLOT
            if not have_sum:
                act_accum(tt_, slot, AF.Identity, 2)
            act_accum(tt_, slot, AF.Square, 3)
            finish_stats(slot, LN_EPS, 2, 3)
            ts("dve", sc[:, slot, 2:3], mv[:, slot, 0:1], -1.0, sc[:, slot, 1:2], ALU.mult, ALU.mult,
               r=[("mv", slot), ("sc1", slot)], w=[("sc2", slot)])
            xh = XH[slot]
            act(xh[:], X[:, tt_, :], AF.Identity, r=[("X", tt_), ("sc1", slot), ("sc2", slot)], w=[("XH", slot)],
                bias=sc[:, slot, 2:3], scale=sc[:, slot, 1:2])

        def prep_tile_b(l, i, tt_, extra=()):
            slot = tt_ % NSLOT
            xh = XH[slot]
            pt, ptr = ps(6 + tt_ % 2)
            ptb = pt[:, :].bitcast(BF16)
            for kc in range(8):
                tr(ptb[:, kc * 128:(kc + 1) * 128], xh[:, kc * 128:(kc + 1) * 128], ident[:], r=[("XH", slot), "ident"], w=[ptr])
            for kc in range(8):
                scl = modcol[:, l, (3 * i + 1) * 8 + kc:(3 * i + 1) * 8 + kc + 1]
                shf = modcol[:, l, (3 * i) * 8 + kc:(3 * i) * 8 + kc + 1]
                if kc % 2 == 0:
                    ts("dve", HT[:, kc, tt_ * 128:(tt_ + 1) * 128], ptb[:, kc * 128:(kc + 1) * 128], scl, shf,
                       ALU.mult, ALU.add, r=[ptr, ("modcol", l, 3 * i), ("modcol", l, 3 * i + 1)] + list(extra), w=[("HT", tt_ // 4)])
                else:
                    act(HT[:, kc, tt_ * 128:(tt_ + 1) * 128], ptb[:, kc * 128:(kc + 1) * 128], AF.Identity,
                        r=[ptr, ("modcol", l, 3 * i), ("modcol", l, 3 * i + 1)] + list(extra), w=[("HT", tt_ // 4)], bias=shf, scale=scl)

        def prep(l, i):
            for tt_ in range(NT):
                prep_tile_a(l, i, tt_)
                prep_tile_b(l, i, tt_)

        LNGB = [T("LNG", [128, D], F32), T("LNB", [128, D], F32)]

        def post_setup(l, i):
            LNG, LNB = LNGB[0][:], LNGB[1][:]
            sv = S.scope
            S.scope = None
            dma("sp", LNG, bass.AP(lng_d.tensor, (l * 3 + i) * D, [[0, 128], [1, D]]), w=["LNG"])
            dma("sp", LNB, bass.AP(lnb_d.tensor, (l * 3 + i) * D, [[0, 128], [1, D]]), w=["LNB"])
            S.scope = sv
            return LNG, LNB

        def post_tile(l, i, tt_, LNG, LNB, want_sum):
            slot = tt_ % NSLOT
            act_accum(tt_, slot, AF.Identity, 0)
            act_accum(tt_, slot, AF.Square, 1)
            finish_stats(slot, LN_EPS / (ALPHA * ALPHA), 0, 1)
            stt(X[:, tt_, :], X[:, tt_, :], mv[:, slot, 0:1], LNG, ALU.subtract, ALU.mult,
                r=[("X", tt_), ("mv", slot), "LNG"], w=[("X", tt_)])
            if want_sum:
                S.op("dve", lambda e: e.scalar_tensor_tensor(out=X[:, tt_, :], in0=X[:, tt_, :], scalar=sc[:, slot, 1:2], in1=LNB,
                                                             op0=ALU.mult, op1=ALU.add, accum_out=sums[:, slot, 2:3]),
                     r=[("X", tt_), ("sc1", slot), "LNB"], w=[("X", tt_), ("sums", slot, 2)])
            else:
                stt(X[:, tt_, :], X[:, tt_, :], sc[:, slot, 1:2], LNB, ALU.mult, ALU.add,
                    r=[("X", tt_), ("sc1", slot), "LNB"], w=[("X", tt_)])

        ov = out_d.rearrange("(t p) d -> p t d", p=128)

        class Tail:
            def __init__(self, postli, prepli, final):
                self.postli, self.prepli, self.final = postli, prepli, final
                self.pend = []
                self.LNG, self.LNB = post_setup(*postli)

            def tile_done(self, tt_):
                sv = S.scope
                S.scope = None
                post_tile(self.postli[0], self.postli[1], tt_, self.LNG, self.LNB, self.prepli is not None)
                if self.prepli is not None:
                    prep_tile_a(self.prepli[0], self.prepli[1], tt_, have_sum=True)
                    self.pend.append(tt_)
                if self.final:
                    dma("sp", ov[:, tt_, :], X[:, tt_, :], r=[("X", tt_)], final=True)
                S.scope = sv

            def lagged(self, keep):
                sv = S.scope
                S.scope = None
                while len(self.pend) > keep:
                    prep_tile_b(self.prepli[0], self.prepli[1], self.pend.pop(0))
                S.scope = sv

        def gate_bc(l, i, scale, GBC, D8):
            for kc in range(8):
                col = (3 * i + 2) * 8 + kc
                ts("dve", D8[:, kc, :], identf[:], modcol[:, l, col:col + 1], float(scale), ALU.mult, ALU.mult,
                   r=["identf", ("modcol", l, 3 * i + 2)], w=["D8"])
            for hf in range(2):
                pg, pgr = ps(hf)
                mm(pg[:, :], onesf[:], D8[:, hf * 4:(hf + 1) * 4, :].rearrange("p a b -> p (a b)"), True, True, r=["onesf", "D8"], w=[pgr])
                cp("act", GBC[:, hf * 512:(hf + 1) * 512], pg[:, :], r=[pgr], w=["GBC"])

        def ffn(l, i, si, tail):
            ar = Arena(0, ARB)
            ay = ArenaYT(0, 32768)
            WG = [ay.alloc([128, 8, 512], BF16) for _ in range(2)]
            WU = [ay.alloc([128, 8, 512], BF16) for _ in range(2)]
            WD = [ar.alloc([128, 4, D], BF16) for _ in range(2)]
            ACTB = [ar.alloc([128, 4, 512], BF16) for _ in range(2)]
            SG = [ar.alloc([128, 512], F32) for _ in range(2)]
            GBC = ar.alloc([128, D], F32)
            D8 = ar.alloc([128, 8, 128], F32)
            gate_bc(l, si, 0.5 / ALPHA, GBC, D8)
            groups = [(g * 4, 4) for g in range(5)] + [(20, 2)]
            pend = None
            nsg = 0
            for gi, (c0, nf) in enumerate(groups):
                b = gi % 2
                f0 = c0 * 128
                fw = nf * 128
                dma("pool", WG[b][:, :, 0:fw], wg_d[l, i, :, f0:f0 + fw].rearrange("(kc p) f -> p kc f", p=128), w=[("WG", b)])
                dma("pool", WU[b][:, :, 0:fw], wu_d[l, i, :, f0:f0 + fw].rearrange("(kc p) f -> p kc f", p=128), w=[("WU", b)])
                dma("pool", WD[b][:, 0:nf, :], wd_d[l, i, f0:f0 + fw, :].rearrange("(c p) d -> p c d", p=128), w=[("WD", b)])
                for c in range(nf):
                    tt("pool", WD[b][:, c, :], WD[b][:, c, :], GBC, ALU.mult, r=["GBC", ("WD", b)], w=[("WD", b)])
                for tsi in range(4):
                    ab = (gi * 4 + tsi) % 2
                    for c in range(nf):
                        pgk = (gi * 16 + tsi * 4 + c) % 2
                        pg, pgr = ps(pgk)
                        pu, pur = ps(2 + pgk)
                        for kc in range(8):
                            mm(pg[:, :], WG[b][:, kc, c * 128:(c + 1) * 128], HT[:, kc, tsi * 512:(tsi + 1) * 512], kc == 0, kc == 7,
                               r=[("WG", b), ("HT", tsi)], w=[pgr])
                        for kc in range(8):
                            mm(pu[:, :], WU[b][:, kc, c * 128:(c + 1) * 128], HT[:, kc, tsi * 512:(tsi + 1) * 512], kc == 0, kc == 7,
                               r=[("WU", b), ("HT", tsi)], w=[pur])
                        sgb = nsg % 2
                        nsg += 1
                        act(SG[sgb], pg[:, :], AF.Silu, r=[pgr], w=[("SG", sgb)])
                        tt("dve", ACTB[ab][:, c, :], SG[sgb], pu[:, :], ALU.mult, r=[("SG", sgb), pur], w=[("ACTB", ab, c)])
                    cur = (b, ab, nf, tsi, gi == len(groups) - 1)
                    if pend is not None:
                        down(pend, WD, ACTB, tail)
                    pend = cur
            down(pend, WD, ACTB, tail)
            tail.lagged(0)

        dcount = [0]

        def down(p, WD, ACTB, tail):
            b, ab, nf, tsi, last = p
            for t4 in range(4):
                tt_ = tsi * 4 + t4
                for hf in range(2):
                    k = 4 + dcount[0] % 2
                    dcount[0] += 1
                    pd, pdr = ps(k)
                    for c in range(nf):
                        mm(pd[:, :], ACTB[ab][:, c, t4 * 128:(t4 + 1) * 128], WD[b][:, c, hf * 512:(hf + 1) * 512], c == 0, c == nf - 1,
                           r=[("ACTB", ab, c), ("WD", b)], w=[pdr])
                    tt("dve", X[:, tt_, hf * 512:(hf + 1) * 512], X[:, tt_, hf * 512:(hf + 1) * 512], pd[:, :], ALU.add,
                       r=[("X", tt_), pdr], w=[("X", tt_)])
                if last:
                    tail.tile_done(tt_)
                    tail.lagged(2)

        stage = [0]

        def done_stage():
            stage[0] += 1
            return stop is not None and stage[0] >= stop

        def run_layers():
            for l in range(DEPTH):
                fence()
                with arena_scope():
                    ffn(l, 0, 0, Tail((l, 0), (l, 1), False))
                if done_stage():
                    return
                fence()
                with arena_scope():
                    mixer(l, Tail((l, 1), (l, 2), False) if not dbg else None)
                if dbg:
                    return
                if done_stage():
                    return
                fence()
                lastl = l == DEPTH - 1
                with arena_scope():
                    ffn(l, 1, 2, Tail((l, 2), None if lastl else (l + 1, 0), lastl))
                if done_stage():
                    return

        TWO_PI = 2.0 * math.pi

        def act_b(out, in_, func, r, w, bias, scale):
            return act(out, in_, func, r, w, bias=bias, scale=scale)

        def pool_phase(l):
            ar = Arena(0, ARB)
            WUP = ar.alloc([128, 8, 256], BF16)
            PW = ar.alloc([128, 2, 128], BF16)
            pscol = ar.alloc([128, 2], F32)
            UP = [ar.alloc([128, 2, 528], F32) for _ in range(2)]
            Pb = ar.alloc([128, 2, 528], F32)
            Qb = ar.alloc([128, 2, 528], F32)
            PL = ar.alloc([128, 2, 512], BF16)
            TM = ar.alloc([128, 2, 16], F32)
            dma("pool", WUP, win_d[l, :, 1792:2048].rearrange("(kc p) f -> p kc f", p=128), w=["WUP"])
            dma("pool", PW, poolw_d[l], w=["PW"])
            dma("sp", pscol, pools_d[l], w=["pscol"])
            memset("pool", UP[0][:, :, 0:16], 0.0, w=[("UP", 0)])
            for tsi in range(4):
                ub = UP[tsi % 2]
                ur = ("UP", tsi % 2)
                for ch in range(2):
                    pu, pur = ps(ch)
                    for kc in range(8):
                        mm(pu[:, :], WUP[:, kc, ch * 128:(ch + 1) * 128], HT[:, kc, tsi * 512:(tsi + 1) * 512], kc == 0, kc == 7,
                           r=["WUP", ("HT", tsi)], w=[pur])
                    cp("act", ub[:, ch, 16:528], pu[:, :], r=[pur], w=[ur])
                tt("pool", Pb[:, :, 1:528], ub[:, :, 1:528], ub[:, :, 0:527], ALU.add, r=[ur], w=["Pb"])
                tt("pool", Qb[64:128, 0, 3:528], Pb[64:128, 0, 3:528], Pb[64:128, 0, 1:526], ALU.add, r=["Pb"], w=["Qb"])
                tt("pool", Qb[:, 1, 3:528], Pb[:, 1, 3:528], Pb[:, 1, 1:526], ALU.add, r=["Pb"], w=["Qb"])
                tt("pool", Pb[:, 1, 7:528], Qb[:, 1, 7:528], Qb[:, 1, 3:524], ALU.add, r=["Qb"], w=["Pb"])
                tt("pool", Qb[64:128, 1, 15:528], Pb[64:128, 1, 15:528], Pb[64:128, 1, 7:520], ALU.add, r=["Pb"], w=["Qb"])
                srcs = [(Pb, 0, 64, 0), (Qb, 64, 128, 0), (Pb, 0, 64, 1), (Qb, 64, 128, 1)]
                for (sb, p0, p1, ch) in srcs:
                    stt(PL[p0:p1, ch, :], sb[p0:p1, ch, 16:528], invw[p0:p1, ch:ch + 1], ub[p0:p1, ch, 16:528], ALU.mult, ALU.subtract,
                        r=["Pb", "Qb", ur, "invw"], w=["PL"])
                    if tsi == 0:
                        tt("dve", TM[p0:p1, ch, :], sb[p0:p1, ch, 16:32], rc16[p0:p1, ch, :], ALU.mult, r=["Pb", "Qb", "rc16"], w=["TM"])
                        tt("dve", PL[p0:p1, ch, 0:16], TM[p0:p1, ch, :], ub[p0:p1, ch, 16:32], ALU.subtract, r=["TM", ur], w=["PL"])
                if tsi < 3:
                    cp("pool", UP[(tsi + 1) % 2][:, :, 0:16], ub[:, :, 512:528], r=[ur], w=[("UP", (tsi + 1) % 2)])
                for ch in range(2):
                    py, pyr = ps(2 + ch)
                    mm(py[:, :], PW[:, ch, :], PL[:, ch, :], True, True, r=["PW", "PL"], w=[pyr])
                    act_b(YT[:, 6 + ch, tsi * 512:(tsi + 1) * 512], py[:, :], AF.Identity, r=[pyr, "pscol"], w=[("YT", 6 + ch, tsi)],
                          bias=0.0, scale=pscol[:, ch:ch + 1])

        TL = 128
        NTB = 16 * (TL + 1)
        ssc_f = nc.dram_tensor("ssm_scr_f", [DEPTH, 128, 2 * NTB + 224], F32, kind="Internal").ap()
        ssc_b = nc.dram_tensor("ssm_scr_b", [DEPTH, 128, NTB + 3 * 2048], BF16, kind="Internal").ap()

        class ArenaOn(Arena):
            def __init__(self, flat, size):
                self.flat = flat
                self.off = 0
                self.end = size

            def alloc(self, shape, dt):
                n = int(np.prod(shape[1:]))
                nb = n * (4 if dt in (F32, I32) else 2)
                nb = (nb + 31) // 32 * 32
                assert self.off + nb <= self.end, ("arenaOn overflow", shape, self.off, nb, self.end)
                v = self.flat[0:shape[0], self.off // 2:(self.off + nb) // 2]
                self.off += nb
                if dt != BF16:
                    v = v.bitcast(dt)
                v = v[:, 0:n]
                if len(shape) == 3:
                    v = v.rearrange("p (a b) -> p a b", a=shape[1])
                return v

        def ssm_setup_gen(l):
            ah = ArenaOn(HT[:, :, :].rearrange("p a b -> p (a b)"), 32768)
            ay = ArenaOn(YT[:, :, :].rearrange("p a b -> p (a b)"), 32768)
            ar = Arena(40992, ARB - 40992)
            TC = ah.alloc([128, 16, TL + 1], F32)
            TS = ah.alloc([128, 16, TL + 1], F32)
            ANG = ah.alloc([128, 16, TL + 1], F32)
            BL = ay.alloc([128, 16, 128], BF16)
            IBL = ay.alloc([128, 16, 128], BF16)
            CL = ay.alloc([128, 16, 128], BF16)
            SV = ay.alloc([128, 14, 16], F32)
            P1 = ay.alloc([128, 16, 16], F32)
            P2 = ay.alloc([128, 16, 16], F32)
            CT = ay.alloc([128, 256], F32)
            TCb = ay.alloc([128, 16, TL + 1], BF16)
            AI = ay.alloc([128, 16, TL + 1], I32)
            JF = ar.alloc([128, TL + 1], F32)
            JI = ar.alloc([128, TL + 1], I32)
            Bc = ar.alloc([128, 16, 16], F32)
            IBc = ar.alloc([128, 16, 16], F32)
            Tm = ar.alloc([128, 16, 16], F32)
            are, aim, ldt, dtv, lr, thn, rho, sn, cs, er, ei, gr, gi, tq = [SV[:, k, :] for k in range(14)]
            V = "ssmv"
            dma("pool", are, sare_d[l], w=["are", V])
            dma("pool", aim, saim_d[l], w=["aim", V])
            dma("pool", ldt, sldt_d[l], w=["ldt", V])
            dma("pool", P1, sp1_d[l].rearrange("p (g c) -> p g c", g=16), w=["P1"])
            dma("pool", P2, sp2_d[l].rearrange("p (g c) -> p g c", g=16), w=["P2"])
            dma("pool", CT, sct_d[l], w=["CT"])
            yield
            act(dtv, ldt, AF.Exp, r=["ldt"], w=[V])
            tt("dve", lr, are, dtv, ALU.mult, r=["are", V], w=[V])
            tt("dve", thn, aim, dtv, ALU.mult, r=["aim", V], w=[V])
            ts("dve", thn, thn, 1.0 / TWO_PI, None, ALU.mult, None, r=[V], w=[V])
            yield
            ts("dve", rho, lr, 1.0 / 720.0, 1.0 / 120.0, ALU.mult, ALU.add, r=[V], w=[V])
            for cst in (1.0 / 24.0, 1.0 / 6.0, 0.5, 1.0, 1.0):
                tt("dve", rho, rho, lr, ALU.mult, r=[V], w=[V])
                ts("dve", rho, rho, float(cst), None, ALU.add, None, r=[V], w=[V])
                yield

            def sincos(dst, src, shift, ai):
                ts("dve", dst, src, float(shift), None, ALU.add, None, r=[V], w=[V])
                cp("dve", ai, dst, r=[V], w=[V])
                tt("dve", dst, dst, ai, ALU.subtract, r=[V], w=[V])
                act(dst, dst, AF.Sin, r=[V], w=[V], scale=TWO_PI)

            sincos(sn, thn, 0.0, AI[:, 0, 0:16])
            yield
            sincos(cs, thn, 0.25, AI[:, 0, 0:16])
            yield
            tt("dve", er, rho, cs, ALU.mult, r=[V], w=[V])
            ts("dve", er, er, -1.0, None, ALU.add, None, r=[V], w=[V])
            tt("dve", ei, rho, sn, ALU.mult, r=[V], w=[V])
            tt("dve", tq, are, are, ALU.mult, r=[V, "are"], w=[V])
            yield
            tt("dve", gr, aim, aim, ALU.mult, r=[V, "aim"], w=[V])
            tt("dve", tq, tq, gr, ALU.add, r=[V], w=[V])
            S.op("dve", lambda e: e.reciprocal(out=tq, in_=tq), r=[V], w=[V])
            tt("dve", gr, er, are, ALU.mult, r=[V], w=[V])
            yield
            tt("dve", gi, ei, aim, ALU.mult, r=[V], w=[V])
            tt("dve", gr, gr, gi, ALU.add, r=[V], w=[V])
            tt("dve", gr, gr, tq, ALU.mult, r=[V], w=[V])
            tt("dve", gi, ei, are, ALU.mult, r=[V], w=[V])
            yield
            tt("dve", er, er, aim, ALU.mult, r=[V], w=[V])
            tt("dve", gi, gi, er, ALU.subtract, r=[V], w=[V])
            tt("dve", gi, gi, tq, ALU.mult, r=[V], w=[V])
            S2, S3, S4 = ei, er, tq
            ts("dve", S2, gi, sgn[:, 0:1], None, ALU.mult, None, r=[V, "sgn"], w=[V])
            yield
            ts("dve", S3, gr, sgn[:, 0:1], None, ALU.mult, None, r=[V, "sgn"], w=[V])
            ts("dve", S4, gi, -1.0, None, ALU.mult, None, r=[V], w=[V])
            bc = lambda v: v.unsqueeze(2).to_broadcast([128, 16, 16])
            tt("dve", Bc, P1, bc(gr), ALU.mult, r=[V, "P1"], w=["Bc"])
            tt("dve", Tm, P2, bc(S2), ALU.mult, r=[V, "P2"], w=["Tm"])
            yield
            tt("dve", Bc, Bc, Tm, ALU.add, r=["Tm"], w=["Bc"])
            tt("dve", IBc, P2, bc(S3), ALU.mult, r=[V, "P2"], w=["IBc"])
            tt("dve", Tm, P1, bc(S4), ALU.mult, r=[V, "P1", "Bc"], w=["Tm"])
            tt("dve", IBc, IBc, Tm, ALU.add, r=["Tm"], w=["IBc"])
            yield
            for (src, dst, nm) in ((Bc, BL, "Bc"), (IBc, IBL, "IBc")):
                flat = src.rearrange("p g c -> p (g c)")
                for ch in range(2):
                    pt_, ptr_ = ps(7)
                    tr(pt_[:, 0:128], flat[:, ch * 128:(ch + 1) * 128], identf[:], r=[nm, "identf"], w=[ptr_])
                    for g8 in range(8):
                        ts("dve", dst[:, ch * 8 + g8, :], pt_[:, 0:128], gmask[:, g8:g8 + 1], None, ALU.mult, None,
                           r=[ptr_, "gmask"], w=["BL"])
                        if g8 % 4 == 3:
                            yield
            ts("dve", CT, CT, sgn[:, 0:1], -1.0, ALU.mult, ALU.mult, r=["CT", "sgn"], w=["CT"])
            memset("pool", CL, 0.0, w=["CL"])
            for g in range(16):
                g8 = g % 8
                cp("dve", CL[:, g, 16 * g8:16 * g8 + 16], CT[:, g * 16:(g + 1) * 16], r=["CT"], w=["CL"])
                if g % 4 == 3:
                    yield
            S.op("pool", lambda e: e.iota(out=JI, pattern=[[1, TL + 1]], base=0, channel_multiplier=0), w=["JI"])
            cp("dve", JF, JI, r=["JI"], w=["JF"])
            tt("dve", ANG, JF.unsqueeze(1).to_broadcast([128, 16, TL + 1]), thn.unsqueeze(2).to_broadcast([128, 16, TL + 1]),
               ALU.mult, r=["JF", V], w=[V])
            yield
            sincos(TS, ANG, 0.0, AI)
            yield
            sincos(TC, ANG, 0.25, AI)
            yield
            cp("dve", TCb, TC, r=[V], w=[V])
            dma("pool", ssc_f[l, :, 0:NTB], TC.rearrange("p a b -> p (a b)"), r=[V], w=[("ssc", l)])
            dma("pool", ssc_f[l, :, NTB:2 * NTB], TS.rearrange("p a b -> p (a b)"), r=[V], w=[("ssc", l)])
            dma("pool", ssc_f[l, :, 2 * NTB:2 * NTB + 224], SV.rearrange("p a b -> p (a b)"), r=[V], w=[("ssc", l)])
            dma("pool", ssc_b[l, :, 0:NTB], TCb.rearrange("p a b -> p (a b)"), r=[V], w=[("ssc", l)])
            dma("pool", ssc_b[l, :, NTB:NTB + 2048], BL.rearrange("p a b -> p (a b)"), r=["BL"], w=[("ssc", l)])
            dma("pool", ssc_b[l, :, NTB + 2048:NTB + 4096], IBL.rearrange("p a b -> p (a b)"), r=["BL"], w=[("ssc", l)])
            dma("pool", ssc_b[l, :, NTB + 4096:NTB + 6144], CL.rearrange("p a b -> p (a b)"), r=["CL"], w=[("ssc", l)])
            yield

        def ssm_phase(l):
            ay = ArenaOn(YT[:, 0:4, :].rearrange("p a b -> p (a b)"), 16384)
            ar = Arena(0, ARB)
            BL = ay.alloc([128, 16, 128], BF16)
            IBL = ay.alloc([128, 16, 128], BF16)
            CL = ay.alloc([128, 16, 128], BF16)
            SV = ay.alloc([128, 14, 16], F32)
            WUS = ar.alloc([128, 8, 256], BF16)
            TC = ar.alloc([128, 16, TL + 1], F32)
            TS = ar.alloc([128, 16, TL + 1], F32)
            GW = ar.alloc([128, 2, 256], BF16)
            sdcol = ar.alloc([128, 2], F32)
            glub = ar.alloc([128, 2], F32)
            TCb = ar.alloc([128, 16, TL + 1], BF16)
            rho = SV[:, 6, :]
            V = "ssmv2"
            dma("sp", TC.rearrange("p a b -> p (a b)"), ssc_f[l, :, 0:NTB], r=[("ssc", l)], w=[V])
            dma("sp", TS.rearrange("p a b -> p (a b)"), ssc_f[l, :, NTB:2 * NTB], r=[("ssc", l)], w=[V])
            dma("sp", SV.rearrange("p a b -> p (a b)"), ssc_f[l, :, 2 * NTB:2 * NTB + 224], r=[("ssc", l)], w=[V])
            dma("sp", TCb.rearrange("p a b -> p (a b)"), ssc_b[l, :, 0:NTB], r=[("ssc", l)], w=[V])
            dma("sp", BL.rearrange("p a b -> p (a b)"), ssc_b[l, :, NTB:NTB + 2048], r=[("ssc", l)], w=["BL"])
            dma("sp", IBL.rearrange("p a b -> p (a b)"), ssc_b[l, :, NTB + 2048:NTB + 4096], r=[("ssc", l)], w=["BL"])
            dma("sp", CL.rearrange("p a b -> p (a b)"), ssc_b[l, :, NTB + 4096:NTB + 6144], r=[("ssc", l)], w=["CL"])
            dma("sp", sdcol, sd_d[l], w=["sdcol"])
            dma("sp", glub, glub_d[l], w=["glub"])
            dma("pool", GW, gluw_d[l], w=["GW"])
            dma("pool", WUS, win_d[l, :, 1536:1792].rearrange("(kc p) f -> p kc f", p=128), w=["WUS"])
            USS2 = [ar.alloc([128, 2, 512], BF16) for _ in range(2)]
            Wb = [ar.alloc([128, 4, TL], BF16) for _ in range(2)]
            T2 = ar.alloc([128, 4, TL], BF16)
            STb = [ar.alloc([128, 4, TL], F32) for _ in range(2)]
            SBh = [ar.alloc([128, 4, TL], BF16) for _ in range(2)]
            T2b = ar.alloc([128, 4, TL], BF16)
            S1 = ar.alloc([128, 4, TL], BF16)
            SB = [ar.alloc([128, 4, TL], BF16) for _ in range(2)]
            GE = ar.alloc([128, 2, TL], BF16)
            CI = ar.alloc([128, 16], F32)
            CA = ar.alloc([128, 4], F32)
            CB = ar.alloc([128, 4], F32)
            YV = ay.alloc([128, 2, TL], F32)
            G1 = ay.alloc([128, 2, TL], F32)
            G2 = ay.alloc([128, 2, TL], F32)
            NB = 64
            p0_, p0r = ps(0)
            pL, pLr = ps(6)
            pAs = [ps(1), ps(2)]
            pBs = [ps(3), ps(7)]
            pC, pCr = ps(4)
            pC3 = pC[:, :].rearrange("p (a b) -> p a b", a=4)

            def stage_F_pe(n):
                k, bq = n // 4, n % 4
                tsi, kk, ch = k // 4, k % 4, bq // 2
                USS = USS2[tsi % 2]
                ur = ("USS", tsi % 2)
                if kk == 0 and bq == 0:
                    for c2 in range(2):
                        for kc in range(8):
                            mm(p0_[:, :], WUS[:, kc, c2 * 128:(c2 + 1) * 128], HT[:, kc, tsi * 512:(tsi + 1) * 512], kc == 0, kc == 7,
                               r=["WUS", ("HT", tsi)], w=[p0r])
                        cp("act", USS[:, c2, :], p0_[:, :], r=[p0r], w=[ur])
                pA, pAr = pAs[n % 2]
                pB, pBr = pBs[n % 2]
                for gi_ in range(4):
                    mm(pA[:, gi_ * TL:(gi_ + 1) * TL], BL[:, 4 * bq + gi_, :], USS[:, ch, kk * TL:(kk + 1) * TL], True, True, r=["BL", ur], w=[pAr])
                for gi_ in range(4):
                    mm(pB[:, gi_ * TL:(gi_ + 1) * TL], IBL[:, 4 * bq + gi_, :], USS[:, ch, kk * TL:(kk + 1) * TL], True, True, r=["BL", ur], w=[pBr])

            def stage_F_dve(n):
                k, bq = n // 4, n % 4
                gs = slice(4 * bq, 4 * bq + 4)
                wb = Wb[n % 2]
                wbr = ("Wb", n % 2)
                pA, pAr = pAs[n % 2]
                pB, pBr = pBs[n % 2]
                pA3 = pA[:, :].rearrange("p (a b) -> p a b", a=4)
                pB3 = pB[:, :].rearrange("p (a b) -> p a b", a=4)
                tt("dve", wb, pA3, TC[:, gs, 0:TL], ALU.mult, r=[pAr, V], w=[wbr])
                tt("dve", T2, pB3, TS[:, gs, 0:TL], ALU.mult, r=[pBr, V], w=["T2"])
                tt("dve", wb, wb, T2, ALU.subtract, r=["T2"], w=[wbr])

            def stage_S(n):
                k, bq = n // 4, n % 4
                wb = Wb[n % 2]
                wbr = ("Wb", n % 2)
                stb = STb[n % 2]
                stbr = ("STb", n % 2)
                sbh = SBh[n % 2]
                sbhr = ("SBh", n % 2)
                for gi_ in range(4):
                    g = 4 * bq + gi_
                    ini = CI[:, g:g + 1] if k > 0 else 0.0
                    S.op("dve", lambda e, gi_=gi_, g=g, ini=ini: e.tensor_tensor_scan(
                        out=stb[:, gi_, :], data0=rho[:, g:g + 1].to_broadcast([128, TL]), data1=wb[:, gi_, :],
                        initial=ini, op0=ALU.mult, op1=ALU.add), r=[wbr, V, ("CI", bq)], w=[stbr])
                cp("act", sbh, stb, r=[stbr], w=[sbhr])
                mm(pC[:, :], jswb[:], sbh.rearrange("p a b -> p (a b)"), True, True, r=["jswb", sbhr], w=[pCr])
                if k < 15:
                    mm(pL[:, 0:4], jsw[:], stb[:, :, TL - 1], True, True, r=["jsw", stbr], w=[pLr])

            def stage_B(n):
                k, bq = n // 4, n % 4
                kk, ch = k % 4, bq // 2
                tsi = k // 4
                gs = slice(4 * bq, 4 * bq + 4)
                stb = STb[n % 2]
                stbr = ("STb", n % 2)
                sbh = SBh[n % 2]
                sbhr = ("SBh", n % 2)
                sb = SB[n % 2]
                sbr = ("SB", n % 2)
                USS = USS2[tsi % 2]
                ur = ("USS", tsi % 2)
                tt("dve", T2b, pC3, TS[:, gs, 0:TL], ALU.mult, r=[pCr, V], w=["T2b"])
                tt("dve", S1, sbh, TCb[:, gs, 0:TL], ALU.mult, r=[sbhr, V], w=["S1"])
                tt("dve", sb, S1, T2b, ALU.add, r=["S1", "T2b"], w=[sbr])
                if k < 15:
                    tt("dve", CA, pL[:, 0:4], TS[:, gs, TL], ALU.mult, r=[pLr, V], w=["CA"])
                    tt("dve", CB, stb[:, :, TL - 1], TC[:, gs, TL], ALU.mult, r=[stbr, V], w=["CB"])
                    tt("dve", CI[:, gs], CA, CB, ALU.add, r=["CA", "CB"], w=[("CI", bq)])
                pY, pYr = ps(5)
                for gi_ in range(4):
                    g = 4 * bq + gi_
                    mm(pY[:, ch * TL:(ch + 1) * TL], CL[:, g, :], sb[:, gi_, :], g % 8 == 0, g % 8 == 7, r=["CL", sbr], w=[pYr])
                if bq == 3:
                    glu_a(k)
                if bq == 0 and k > 0:
                    glu_b(k - 1)

            def glu_a(k):
                kk, tsi = k % 4, k // 4
                USS = USS2[tsi % 2]
                ur = ("USS", tsi % 2)
                cols = slice(kk * TL, (kk + 1) * TL)
                for c2 in range(2):
                    pY2, pY2r = ps(5)
                    stt(YV[:, c2, :], USS[:, c2, cols], sdcol[:, c2:c2 + 1], pY2[:, c2 * TL:(c2 + 1) * TL], ALU.mult, ALU.add, r=[ur, "sdcol", pY2r], w=["YV"])
                act(G1, YV, AF.Square, r=["YV"], w=["G1"])
                ts("pool", G1, G1, 0.044715, 1.0, ALU.mult, ALU.add, r=["G1"], w=["G1"])
                tt("pool", G1, G1, YV, ALU.mult, r=["G1", "YV"], w=["G1"])
                act(G2, G1, AF.Sigmoid, r=["G1"], w=["G2"], scale=2.0 * math.sqrt(2.0 / math.pi))
                tt("pool", GE, YV, G2, ALU.mult, r=["G2", "YV"], w=["GE"])

            def glu_b(k):
                tsi = k // 4
                tok = slice(k * TL, (k + 1) * TL)
                for dch in range(2):
                    pG, pGr = ps(6)
                    for c2 in range(2):
                        mm(pG[:, 128:128 + TL], GW[:, c2, dch * 128:(dch + 1) * 128], GE[:, c2, :], c2 == 0, c2 == 1, r=["GW", "GE"], w=[pGr])
                    act_b(G1[:, dch, :], pG[:, 128:128 + TL], AF.Sigmoid, r=[pGr, "glub", "G1"], w=["G1"], bias=glub[:, dch:dch + 1], scale=1.0)
                    tt("pool", YT[:, 4 + dch, tok], YV[:, dch, :], G1[:, dch, :], ALU.mult, r=["G1", "YV"], w=[("YT", 4 + dch, tsi)])

            stage_F_pe(0)
            for it in range(NB + 2):
                if it + 1 < NB:
                    stage_F_pe(it + 1)
                if it < NB:
                    stage_F_dve(it)
                if 0 <= it - 2 < NB:
                    stage_B(it - 2)
                if 0 <= it - 1 < NB:
                    stage_S(it - 1)
            glu_b(15)

        def bias_setup():
            ar = Arena(0, ARB)
            relb = ar.alloc([32, 8], F32)
            OH = ar.alloc([32, 1152], F32)
            NG = ar.alloc([8, 1152], F32)
            FV = ar.alloc([8, 1152], F32)
            dma("sp", relb, relb_d, w=["relb"])
            dma("sp", OH, oh_d, w=["OH"])
            dma("sp", NG, negm_d, w=["NG"])
            for br in range(3):
                p_, pr_ = ps(br)
                mm(p_[0:8, 0:384], relb, OH[:, br * 384:(br + 1) * 384], True, True, r=["relb", "OH"], w=[pr_])
                tt("dve", FV[:, br * 384:(br + 1) * 384], p_[0:8, 0:384], NG[:, br * 384:(br + 1) * 384], ALU.add, r=[pr_, "NG"], w=["FV"])
            dma("sp", fv_d, FV, r=["FV"], w=["fv_d"])

        def attn_phase(l):
            ar = Arena(0, ARB)
            WQKV = ar.alloc([128, 8, 384], BF16)
            QZ = [ar.alloc([128, SEQ], BF16) for _ in range(2)]
            KT = ar.alloc([128, SEQ], BF16)
            VP = ar.alloc([128, 3, 16, 192], BF16)
            BTp = ar.alloc([128, 2, 768], BF16)
            Hb = ar.alloc([128, 256], F32)
            PT = [ar.alloc([128, 128], BF16) for _ in range(4)]
            RD = [ar.alloc([128, 512], F32) for _ in range(1)]
            VT = ar.alloc([128, SEQ], BF16)
            memset("pool", QZ[0][64:128, :], 0.0, w=[("QZ", 0)])
            memset("pool", QZ[1][0:64, :], 0.0, w=[("QZ", 1)])
            memset("pool", VP[:, :, :, 64:128], 1.0, w=["VP"])
            npt = 0
            nsc = 0
            nrd = 0
            for hp in range(4):
                for j, base in enumerate((0, 512, 1024)):
                    dma("pool", WQKV[:, :, j * 128:(j + 1) * 128],
                        win_d[l, :, base + hp * 128:base + (hp + 1) * 128].rearrange("(kc p) f -> p kc f", p=128), w=["WQKV"])
                for tsi in range(4):
                    pq, pqr = ps(6)
                    for kc in range(8):
                        mm(pq[:, :], WQKV[:, kc, 0:128], HT[:, kc, tsi * 512:(tsi + 1) * 512], kc == 0, kc == 7, r=["WQKV", ("HT", tsi)], w=[pqr])
                    act(QZ[0][0:64, tsi * 512:(tsi + 1) * 512], pq[0:64, :], AF.Copy, r=[pqr], w=[("QZ", 0)], scale=0.125)
                    act(QZ[1][64:128, tsi * 512:(tsi + 1) * 512], pq[64:128, :], AF.Copy, r=[pqr], w=[("QZ", 1)], scale=0.125)
                    pk, pkr = ps(7)
                    for kc in range(8):
                        mm(pk[:, :], WQKV[:, kc, 128:256], HT[:, kc, tsi * 512:(tsi + 1) * 512], kc == 0, kc == 7, r=["WQKV", ("HT", tsi)], w=[pkr])
                    cp("dve", KT[:, tsi * 512:(tsi + 1) * 512], pk[:, :], r=[pkr], w=["KT"])
                for tsi in range(4):
                    pvt, pvtr = ps(6 + tsi % 2)
                    for kc in range(8):
                        mm(pvt[:, :], WQKV[:, kc, 256:384], HT[:, kc, tsi * 512:(tsi + 1) * 512], kc == 0, kc == 7, r=["WQKV", ("HT", tsi)], w=[pvtr])
                    cp("act", VT[:, tsi * 512:(tsi + 1) * 512], pvt[:, :], r=[pvtr], w=["VT"])
                nv = 0
                for br, (win, dil) in enumerate(PATTERNS):
                    nbk = 16 // dil
                    for q4 in range(4):
                        pv, pvr = ps(6 + nv % 2)
                        nv += 1
                        pvb = pv[:, :].bitcast(BF16)
                        for t4 in range(4):
                            tix = q4 * 4 + t4
                            rr, m = tix // nbk, tix % nbk
                            t0 = rr + dil * 128 * m
                            tr(pvb[:, t4 * 128:(t4 + 1) * 128], VT[:, t0:t0 + dil * 127 + 1:dil], ident[:], r=["VT", "ident"], w=[pvr])
                        outv = VP[:, br, q4 * 4:(q4 + 1) * 4, :].rearrange("p t (a c) -> p t a c", a=3)[:, :, 0:3:2, :]
                        inv_ = pvb[:, 0:512].rearrange("p (t a c) -> p t a c", t=4, a=2)
                        cp("dve", outv, inv_, r=[pvr], w=["VP"])
                for hh in range(2):
                    h = 2 * hp + hh
                    for br in range(3):
                        dma("sp", Hb, bass.AP(fv_d.tensor, h * 1152 + br * 384, [[1, 128], [1, 256]]), r=["fv_d"], w=["Hb"])
                        pb_, pbr_ = ps(6 + br % 2)
                        mm(pb_[:, 0:256], jex[:], Hb, True, True, r=["jex", "Hb"], w=[pbr_])
                        cp("act", BTp[:, hh, br * 256:(br + 1) * 256], pb_[:, 0:256], r=[pbr_], w=["BTp"])
                for hh in range(2):
                    accs = [ps(b) for b in range(4)]
                    started = [False] * 4
                    tasks = []
                    for br, (win, dil) in enumerate(PATTERNS):
                        nbk = 16 // dil
                        for rr in range(dil):
                            for m in range(nbk):
                                for qb in (m, m + 1):
                                    if qb >= nbk:
                                        continue
                                    tasks.append((br, dil, nbk, rr, m, qb))
                    pendq = []

                    def pieces_of(task):
                        br, dil, nbk, rr, m, qb = task
                        if dil == 16:
                            return [(b, rr, 32 * b, 32) for b in range(4)]
                        elif dil == 4:
                            return [(qb, rr, 0, 128)]
                        t0 = 128 * qb
                        return [(t0 // 512, t0 % 512, 0, 128)]

                    remaining = [0] * 4
                    for task in tasks:
                        for (b, c0, i0, n) in pieces_of(task):
                            remaining[b] += 1

                    def do_pv(task, slot):
                        br, dil, nbk, rr, m, qb = task
                        tix = rr * nbk + m
                        lhs = VP[:, br, tix, hh * 64:hh * 64 + 128]
                        for (b, c0, i0, n) in pieces_of(task):
                            acc, accr = accs[b]
                            remaining[b] -= 1
                            mm(acc[:, c0:c0 + dil * (n - 1) + 1:dil], lhs, PT[slot][:, i0:i0 + n], not started[b], remaining[b] == 0,
                               r=["VP", ("PT", slot)], w=[accr])
                            started[b] = True

                    for task in tasks:
                        br, dil, nbk, rr, m, qb = task
                        k0 = rr + dil * 128 * m
                        q0 = rr + dil * 128 * qb
                        psc, pscr = ps(4 + nsc % 2)
                        nsc += 1
                        mm(psc[:, 0:128], KT[:, k0:k0 + dil * 127 + 1:dil], QZ[hh][:, q0:q0 + dil * 127 + 1:dil], True, False,
                           r=["KT", ("QZ", hh)], w=[pscr])
                        off = (qb - m) * 128
                        mm(psc[:, 0:128], ident[:], BTp[:, hh, br * 256 + off:br * 256 + off + 128], False, True, r=["ident", "BTp"], w=[pscr])
                        slot = npt % 4
                        npt += 1
                        act(PT[slot], psc[:, 0:128], AF.Exp, r=[pscr], w=[("PT", slot)])
                        pendq.append((task, slot))
                        if len(pendq) > 2:
                            do_pv(*pendq.pop(0))
                    while pendq:
                        do_pv(*pendq.pop(0))
                    for b in range(4):
                        acc, accr = accs[b]
                        rd = RD[0]
                        rdr = ("RD", 0)
                        nrd += 1
                        if hh == 0:
                            S.op("dve", lambda e, rd=rd, acc=acc: e.reciprocal(out=rd[0:64, :], in_=acc[64:128, :]), r=[accr], w=[rdr])
                            tt("dve", YT[0:64, hp, b * 512:(b + 1) * 512], acc[0:64, :], rd[0:64, :], ALU.mult, r=[accr, rdr], w=[("YT", hp, b)])
                        else:
                            S.op("dve", lambda e, rd=rd, acc=acc: e.reciprocal(out=rd[64:128, :], in_=acc[0:64, :]), r=[accr], w=[rdr])
                            tt("dve", YT[64:128, hp, b * 512:(b + 1) * 512], acc[64:128, :], rd[64:128, :], ALU.mult, r=[accr, rdr], w=[("YT", hp, b)])

        def wout_phase(l, tail):
            ar = Arena(0, ARB)
            WO = ar.alloc([128, 8, D], BF16)
            GBC = ar.alloc([128, D], F32)
            D8 = ar.alloc([128, 8, 128], F32)
            dma("pool", WO, wout_d[l].rearrange("(kc p) d -> p kc d", p=128), w=["WO"])
            gate_bc(l, 1, 1.0 / ALPHA, GBC, D8)
            for kc in range(8):
                tt("pool", WO[:, kc, :], WO[:, kc, :], GBC, ALU.mult, r=["GBC", "WO"], w=["WO"])
            n = 0
            for tt_ in range(NT):
                for hf in range(2):
                    po, por = ps(n % 2)
                    n += 1
                    for kc in range(8):
                        mm(po[:, :], YT[:, kc, tt_ * 128:(tt_ + 1) * 128], WO[:, kc, hf * 512:(hf + 1) * 512], kc == 0, kc == 7,
                           r=["WO"] + [("YT", kc, b) for b in range(4)], w=[por])
                    tt("dve", X[:, tt_, hf * 512:(hf + 1) * 512], X[:, tt_, hf * 512:(hf + 1) * 512], po[:, :], ALU.add,
                       r=[("X", tt_), por], w=[("X", tt_)])
                tail.tile_done(tt_)
                tail.lagged(2)
            tail.lagged(0)

        def mixer(l, tail):
            if "pool" in parts:
                pool_phase(l)
                fence()
            if "ssm" in parts:
                ssm_phase(l)
                fence()
            if "attn" in parts:
                attn_phase(l)
                fence()
            if dbg:
                return
            wout_phase(l, tail)

        with arena_scope():
            ada_phase()
        fence_all()
        with arena_scope():
            bias_setup()
        run_layers()
        if dbg:
            fence_all()
            dbg_d = nc.dram_tensor("dbg", [128, 8, SEQ], F32, kind="ExternalOutput").ap()
            dma("pool", dbg_d, YT[:, :, :], r=["ALL"], final=True)
        if dbg or stop is not None:
            fence_all()
            for q in range(4):
                dma("sp", ov[:, q * 4:(q + 1) * 4, :], X[:, q * 4:(q + 1) * 4, :], r=[("X", t) for t in range(q * 4, q * 4 + 4)], final=True)
        S.emit()
    return nc


def host_inputs(inputs):
    f = lambda a: np.ascontiguousarray(np.asarray(a, dtype=np.float32))
    consts = static_consts()
    shared = {}
    for k in ("ada_w", "ada_b", "ln_g", "ln_b", "ffn_w_gate", "ffn_w_up", "ffn_w_down", "w_in", "w_out", "rel_bias"):
        shared[k] = f(inputs[k])
    a_re, a_im = f(inputs["ssm_a_re"]), f(inputs["ssm_a_im"])
    dup = lambda a: np.concatenate([a, a], axis=1)
    shared["sa_re"] = f(dup(a_re.transpose(0, 2, 1)))
    shared["sa_im"] = f(dup(a_im.transpose(0, 2, 1)))
    shared["sldt"] = f(np.broadcast_to(f(inputs["ssm_log_dt"])[:, None, :], (DEPTH, 128, 16)))
    b_re = f(inputs["ssm_b_re"]).transpose(0, 2, 1, 3).reshape(DEPTH, 64, 256)
    b_im = f(inputs["ssm_b_im"]).transpose(0, 2, 1, 3).reshape(DEPTH, 64, 256)
    shared["sP1"] = f(np.concatenate([b_re, b_im], axis=1))
    shared["sP2"] = f(np.concatenate([b_im, b_re], axis=1))
    c_re = f(inputs["ssm_c_re"]).transpose(0, 3, 1, 2).reshape(DEPTH, 64, 256)
    c_im = f(inputs["ssm_c_im"]).transpose(0, 3, 1, 2).reshape(DEPTH, 64, 256)
    shared["sCT"] = f(np.concatenate([c_re, c_im], axis=1))
    col2 = lambda a: f(f(a).reshape(DEPTH, 2, 128).transpose(0, 2, 1))
    shared["sd"] = col2(inputs["ssm_d"])
    shared["glu_b"] = col2(inputs["glu_b"])
    shared["pool_s"] = col2(inputs["pool_scale"])
    shared["glu_w"] = f(f(inputs["glu_w"]).reshape(DEPTH, 2, 128, 256).transpose(0, 2, 1, 3))
    pw = f(inputs["pool_w"])
    pbd = np.zeros((DEPTH, 128, 2, 128), np.float32)
    for ch in range(2):
        for h in range(2):
            pbd[:, h * 64:(h + 1) * 64, ch, h * 64:(h + 1) * 64] = pw[:, ch * 2 + h]
    shared["pool_w"] = pbd
    shared.update(consts)
    x = f(inputs["x"])
    c = f(inputs["c"])
    maps = []
    for b in range(8):
        m = dict(shared)
        m["x"] = x[b]
        m["cT"] = f(c[b].reshape(8, 128).T)
        maps.append(m)
    return maps


_NC_CACHE = {}


def kernel(**inputs):
    if "nc" not in _NC_CACHE:
        _NC_CACHE["nc"] = build()
    nc = _NC_CACHE["nc"]
    maps = host_inputs(inputs)
    res = run_bass_kernel_spmd(nc, maps, core_ids=list(range(8)))
    return np.stack([np.asarray(r["out"], dtype=np.float32) for r in res.results], axis=0)
```

```python
import contextlib
import math
import numpy as np
import concourse.bass as bass
import concourse.mybir as mybir
from concourse.bass_utils import run_bass_kernel_spmd

F32 = mybir.dt.float32
BF16 = mybir.dt.bfloat16
I32 = mybir.dt.int32
AF = mybir.ActivationFunctionType
ALU = mybir.AluOpType

SEQ = 2048
D = 1024
DFF = 2816
DEPTH = 2
NT = SEQ // 128
ALPHA = (2 * DEPTH) ** 0.25
LN_EPS = 1e-5
NEG = -1e30
PATTERNS = ((128, 1), (512, 4), (2048, 16))

ENGS = ("pe", "act", "dve", "pool", "sp")
NDMASEM = 12


class Op:
    __slots__ = ("eng", "fn", "deps", "marked", "val", "is_dma", "dsem", "dval", "idx")

    def __init__(self, eng, fn, is_dma):
        self.eng = eng
        self.fn = fn
        self.deps = []
        self.marked = False
        self.val = None
        self.is_dma = is_dma
        self.dsem = None
        self.dval = None
        self.idx = None


class Sched:
    def __init__(self, nc):
        self.nc = nc
        self.ops = {e: [] for e in ENGS}
        self.writers = {}
        self.readers = {}
        self.ndma = {e: 0 for e in ENGS}
        self.final_waits = []
        self.scope = None

    def op(self, eng, fn, r=(), w=(), dma=False, final=False):
        o = Op(eng, fn, dma)
        deps = []
        r = list(r)
        w = list(w)
        if "ALL" not in w:
            r.append("ALL")
        if self.scope is not None and self.scope not in w:
            r.append(self.scope)
        for x in r:
            deps.extend(self.writers.get(x, ()))
        for x in w:
            deps.extend(self.writers.get(x, ()))
            deps.extend(self.readers.get(x, ()))
        seen = set()
        for d in deps:
            if d is o or id(d) in seen:
                continue
            seen.add(id(d))
            if d.eng == "pe" and eng == "pe" and not d.is_dma and not dma:
                continue
            o.deps.append(d)
            if not d.is_dma:
                d.marked = True
        for x in w:
            if self.readers.get(x):
                self.writers[x] = [o]
                self.readers[x] = []
            else:
                ws = self.writers.setdefault(x, [])
                ws[:] = [p for p in ws if not (p.eng == eng and p.is_dma == dma and not dma)]
                ws.append(o)
        for x in r:
            if x not in w:
                rs = self.readers.setdefault(x, [])
                rs[:] = [p for p in rs if not (p.eng == eng and not p.is_dma and not dma)]
                rs.append(o)
        if dma:
            j = self.ndma[eng]
            self.ndma[eng] = j + 1
            o.dsem = j % NDMASEM
            o.dval = 16 * (j // NDMASEM + 1)
            o.idx = j
        self.ops[eng].append(o)
        if final:
            self.final_waits.append(o)
        return o

    def emit(self):
        nc = self.nc
        with contextlib.ExitStack() as st:
            csem = {e: st.enter_context(nc.semaphore("c_" + e)) for e in ENGS}
            dsem = {e: [st.enter_context(nc.semaphore("d_%s_%d" % (e, i))) for i in range(NDMASEM)]
                    for e in ENGS if self.ndma[e] > 0}
            for e in ENGS:
                c = 0
                for o in self.ops[e]:
                    if o.marked and not o.is_dma:
                        c += 1
                        o.val = c
            block = st.enter_context(nc.Block())

            def run(e, eng):
                waited = {}
                for o in self.ops[e]:
                    waits = []
                    for d in o.deps:
                        if d.is_dma:
                            waits.append((("d", d.eng, d.dsem), dsem[d.eng][d.dsem], d.dval))
                        else:
                            waits.append((("c", d.eng), csem[d.eng], d.val))
                    if o.is_dma and o.idx >= NDMASEM:
                        waits.append((("d", e, o.dsem), dsem[e][o.dsem], o.dval - 16))
                    for key, s, v in waits:
                        if waited.get(key, 0) >= v:
                            continue
                        eng.wait_ge(s, v)
                        waited[key] = v
                    ins = o.fn(eng)
                    if o.is_dma:
                        ins.then_inc(dsem[e][o.dsem], 16)
                    elif o.marked:
                        ins.then_inc(csem[e], 1)
                for o in self.final_waits:
                    if o.eng == e:
                        eng.wait_ge(dsem[e][o.dsem], o.dval)

            if self.ops["sp"]:
                @block.sync
                def _(eng):
                    run("sp", eng)
            if self.ops["pe"]:
                @block.tensor
                def _(eng):
                    run("pe", eng)
            if self.ops["act"]:
                @block.scalar
                def _(eng):
                    run("act", eng)
            if self.ops["dve"]:
                @block.vector
                def _(eng):
                    run("dve", eng)
            if self.ops["pool"]:
                @block.gpsimd
                def _(eng):
                    run("pool", eng)


def t5_bucket(dist):
    n_buckets, max_distance = 32, 2048
    max_exact = n_buckets // 2
    d = np.maximum(dist, 1).astype(np.float32)
    large = max_exact + (np.log(d / max_exact) / math.log(max_distance / max_exact)
                         * (n_buckets - max_exact)).astype(np.int32)
    large = np.minimum(large, n_buckets - 1)
    return np.where(dist < max_exact, dist, large).astype(np.int32)


def static_consts():
    c = {}
    c["identf"] = np.eye(128, dtype=np.float32)
    c["jex"] = np.eye(128, dtype=np.float32)[::-1].copy()
    jsw = np.zeros((128, 128), np.float32)
    for m in range(64):
        jsw[m + 64, m] = -1.0
        jsw[m, m + 64] = 1.0
    c["jsw"] = jsw
    oh = np.zeros((32, 3 * 384), np.float32)
    negm = np.zeros((8, 3 * 384), np.float32)
    for bi, (win, dil) in enumerate(PATTERNS):
        for u in range(384):
            dist = u - 127
            if 0 <= dist <= win // dil:
                oh[t5_bucket(np.array([dist * dil]))[0], bi * 384 + u] = 1.0
            else:
                negm[:, bi * 384 + u] = NEG
    c["oh"] = oh
    c["negm"] = negm
    gm = np.zeros((128, 8), np.float32)
    for p in range(128):
        gm[p, p // 16] = 1.0
    c["gmask"] = gm
    wins = np.array([2, 4, 8, 16], np.float32)
    wp = np.zeros((128, 2), np.float32)
    for ch in range(2):
        for p in range(128):
            wp[p, ch] = wins[ch * 2 + p // 64]
    c["invw"] = (1.0 / wp).astype(np.float32)
    t = np.arange(16, dtype=np.float32)[None, None, :]
    c["rc16"] = (1.0 / np.minimum(t + 1.0, wp[:, :, None])).astype(np.float32)
    c["sgn"] = np.concatenate([-np.ones((64, 1), np.float32), np.ones((64, 1), np.float32)], 0)
    return c


def build(stop=None, parts=("pool", "ssm", "attn"), dbg=False):
    nc = bass.Bass("TRN2", target_bir_lowering=False)

    def din(name, shape, dt=F32):
        return nc.dram_tensor(name, list(shape), dt, kind="ExternalInput").ap()

    x_d = din("x", [SEQ, D])
    cT_d = din("cT", [128, 8])
    adaw_d = din("ada_w", [DEPTH, D, 9 * D])
    adab_d = din("ada_b", [DEPTH, 9 * D])
    lng_d = din("ln_g", [DEPTH, 3, D])
    lnb_d = din("ln_b", [DEPTH, 3, D])
    wg_d = din("ffn_w_gate", [DEPTH, 2, D, DFF])
    wu_d = din("ffn_w_up", [DEPTH, 2, D, DFF])
    wd_d = din("ffn_w_down", [DEPTH, 2, DFF, D])
    win_d = din("w_in", [DEPTH, D, 2048])
    wout_d = din("w_out", [DEPTH, D, D])
    relb_d = din("rel_bias", [32, 8])
    sare_d = din("sa_re", [DEPTH, 128, 16])
    saim_d = din("sa_im", [DEPTH, 128, 16])
    sldt_d = din("sldt", [DEPTH, 128, 16])
    sp1_d = din("sP1", [DEPTH, 128, 256])
    sp2_d = din("sP2", [DEPTH, 128, 256])
    sct_d = din("sCT", [DEPTH, 128, 256])
    sd_d = din("sd", [DEPTH, 128, 2])
    gluw_d = din("glu_w", [DEPTH, 128, 2, 256])
    glub_d = din("glu_b", [DEPTH, 128, 2])
    poolw_d = din("pool_w", [DEPTH, 128, 2, 128])
    pools_d = din("pool_s", [DEPTH, 128, 2])
    identf_d = din("identf", [128, 128])
    jex_d = din("jex", [128, 128])
    jsw_d = din("jsw", [128, 128])
    oh_d = din("oh", [32, 1152])
    negm_d = din("negm", [8, 1152])
    gmask_d = din("gmask", [128, 8])
    invw_d = din("invw", [128, 2])
    rc16_d = din("rc16", [128, 2, 16])
    sgn_d = din("sgn", [128, 1])
    out_d = nc.dram_tensor("out", [SEQ, D], F32, kind="ExternalOutput").ap()
    fv_d = nc.dram_tensor("fv_scratch", [8, 1152], F32, kind="Internal").ap()

    st = contextlib.ExitStack()
    with st:
        def T(name, shape, dt):
            return st.enter_context(nc.sbuf_tensor(name, list(shape), dt))

        PSB = [st.enter_context(nc.psum_tensor("ps%d" % i, [128, 512], F32)) for i in range(8)]

        def ps(k):
            return PSB[k], ("ps", k)

        X = T("X", [128, NT, D], F32)
        HT = T("HT", [128, 8, SEQ], BF16)
        YT = T("YT", [128, 8, SEQ], BF16)
        ARB = 49152
        AR = T("AR", [128, ARB // 2], BF16)
        ident = T("ident", [128, 128], BF16)
        identf = T("identf_s", [128, 128], F32)
        jex = T("jex_s", [128, 128], F32)
        jsw = T("jsw_s", [128, 128], F32)
        onesf = T("onesf", [128, 128], F32)
        jswb = T("jswb", [128, 128], BF16)
        modcol = T("modcol", [128, DEPTH, 72], F32)
        XH = [T("XH%d" % i, [128, D], BF16) for i in range(4)]
        mv = T("mv", [128, 4, 2], F32)
        sc = T("sc", [128, 4, 4], F32)
        condT = T("condT", [128, 8], F32)
        gmask = T("gmask_s", [128, 8], F32)
        invw = T("invw_s", [128, 2], F32)
        rc16 = T("rc16_s", [128, 2, 16], F32)
        sgn = T("sgn_s", [128, 1], F32)

        S = Sched(nc)

        class Arena:
            def __init__(self, base, size):
                self.off = base
                self.end = base + size

            def alloc(self, shape, dt):
                n = int(np.prod(shape[1:]))
                nb = n * (4 if dt in (F32, I32) else 2)
                nb = (nb + 31) // 32 * 32
                assert self.off + nb <= self.end, ("arena overflow", shape, self.off, nb, self.end)
                v = AR[0:shape[0], self.off // 2:(self.off + nb) // 2]
                self.off += nb
                if dt != BF16:
                    v = v.bitcast(dt)
                v = v[:, 0:n]
                if len(shape) == 3:
                    v = v.rearrange("p (a b) -> p a b", a=shape[1])
                elif len(shape) == 4:
                    v = v.rearrange("p (a b c) -> p a b c", a=shape[1], b=shape[2])
                return v

        class ArenaYT(Arena):
            def alloc(self, shape, dt):
                n = int(np.prod(shape[1:]))
                nb = n * (4 if dt in (F32, I32) else 2)
                nb = (nb + 31) // 32 * 32
                assert self.off + nb <= self.end
                flat = YT[:, :, :].rearrange("p a b -> p (a b)")
                v = flat[0:shape[0], self.off // 2:(self.off + nb) // 2]
                self.off += nb
                if dt != BF16:
                    v = v.bitcast(dt)
                v = v[:, 0:n]
                if len(shape) == 3:
                    v = v.rearrange("p (a b) -> p a b", a=shape[1])
                return v

        def dma(eng, out, in_, r=(), w=(), final=False, slow=False):
            if slow:
                return S.op(eng, lambda e: e.dma_start(out=out, in_=in_, allow_slow_non_contiguous=True), r=r, w=w, dma=True, final=final)
            return S.op(eng, lambda e: e.dma_start(out=out, in_=in_), r=r, w=w, dma=True, final=final)

        def mm(out, lhsT, rhs, start, stop, r, w):
            return S.op("pe", lambda e: e.matmul(out, lhsT=lhsT, rhs=rhs, start=start, stop=stop), r=r, w=w)

        def tr(out, in_, idn, r, w):
            return S.op("pe", lambda e: e.transpose(out=out, in_=in_, identity=idn), r=r, w=w)

        def act(out, in_, func, r, w, bias=0.0, scale=1.0):
            return S.op("act", lambda e: e.activation(out=out, in_=in_, func=func, bias=bias, scale=scale), r=r, w=w)

        def ts(eng, out, in0, s1, s2, op0, op1, r, w):
            if op1 is None:
                return S.op(eng, lambda e: e.tensor_scalar(out=out, in0=in0, scalar1=s1, scalar2=None, op0=op0), r=r, w=w)
            return S.op(eng, lambda e: e.tensor_scalar(out=out, in0=in0, scalar1=s1, scalar2=s2, op0=op0, op1=op1), r=r, w=w)

        def tt(eng, out, in0, in1, op, r, w):
            return S.op(eng, lambda e: e.tensor_tensor(out=out, in0=in0, in1=in1, op=op), r=r, w=w)

        def stt(out, in0, scalar, in1, op0, op1, r, w):
            return S.op("dve", lambda e: e.scalar_tensor_tensor(out=out, in0=in0, scalar=scalar, in1=in1, op0=op0, op1=op1), r=r, w=w)

        def cp(eng, out, in_, r, w):
            if eng == "act":
                return S.op("act", lambda e: e.copy(out=out, in_=in_), r=r, w=w)
            return S.op(eng, lambda e: e.tensor_copy(out=out, in_=in_), r=r, w=w)

        def memset(eng, ap, val, w):
            return S.op(eng, lambda e: e.memset(ap, val), w=w)

        fsrc_d = nc.dram_tensor("fence_src", [1, 16], F32, kind="Internal").ap()
        fdst_d = nc.dram_tensor("fence_dst", [1, 16], F32, kind="Internal").ap()

        def fence():
            sv = S.scope
            S.scope = None
            S.op("sp", lambda e: e.dma_start(out=fdst_d, in_=fsrc_d), w=["ARENA"], dma=True)
            S.scope = sv

        def fence_all():
            S.op("dve", lambda e: e.memset(sc[:, 0, 3:4], 0.0), w=["ALL", "ARENA", ("sc3", 0)])

        class arena_scope:
            def __enter__(self):
                self.sv = S.scope
                S.scope = "ARENA"

            def __exit__(self, *a):
                S.scope = self.sv

        dma("sp", identf[:], identf_d, w=["identf"])
        dma("sp", jex[:], jex_d, w=["jex"])
        dma("sp", jsw[:], jsw_d, w=["jsw"])
        dma("sp", gmask[:], gmask_d, w=["gmask"])
        dma("sp", invw[:], invw_d, w=["invw"])
        dma("sp", rc16[:], rc16_d, w=["rc16"])
        dma("sp", sgn[:], sgn_d, w=["sgn"])
        cp("dve", ident[:], identf[:], r=["identf"], w=["ident"])
        cp("dve", jswb[:], jsw[:], r=["jsw"], w=["jswb"])
        memset("dve", onesf[:], 1.0, w=["onesf"])
        xv = x_d.rearrange("(t p) d -> p t d", p=128)
        for q in range(4):
            dma("sp", X[:, q * 4:(q + 1) * 4, :], xv[:, q * 4:(q + 1) * 4, :], w=[("X", t) for t in range(q * 4, q * 4 + 4)])

        def ada_phase():
            ar = Arena(0, ARB)
            AW = [ar.alloc([128, 8, 512], F32) for _ in range(2)]
            AB = [ar.alloc([1, 512], F32) for _ in range(2)]
            ROW = [ar.alloc([1, 512], F32) for _ in range(2)]
            cTs = ar.alloc([128, 8], F32)
            dma("sp", cTs, cT_d, w=["cTs"])
            act(condT[:], cTs, AF.Silu, r=["cTs"], w=["condT"])
            it = 0
            import itertools
            gen = itertools.chain(ssm_setup_gen(0), ssm_setup_gen(1))
            prep_q = list(range(NT))
            gen_done = [False]
            for l in range(DEPTH):
                for nb in range(18):
                    for _ in range(6):
                        if next(gen, "done") == "done":
                            gen_done[0] = True
                    b = it % 2
                    it += 1
                    src = adaw_d[l, :, nb * 512:(nb + 1) * 512].rearrange("(kc p) n -> p kc n", p=128)
                    dma("sp", AW[b], src, w=[("AW", b)])
                    dma("sp", AB[b], adab_d[l:l + 1, nb * 512:(nb + 1) * 512], w=[("AB", b)])
                    pr, prr = ps(b)
                    for kc in range(8):
                        mm(pr[0:1, :], condT[:, kc:kc + 1], AW[b][:, kc, :], kc == 0, False,
                           r=["condT", ("AW", b)], w=[prr])
                    mm(pr[0:1, :], onesf[0:1, 0:1], AB[b], False, True, r=["onesf", ("AB", b)], w=[prr])
                    cp("act", ROW[b], pr[0:1, :], r=[prr], w=[("ROW", b)])
                    pc, pcr = ps(2 + b)
                    for j in range(4):
                        mm(pc[:, j:j + 1], ROW[b][0:1, j * 128:(j + 1) * 128], onesf[0:1, 0:1], True, True,
                           r=[("ROW", b), "onesf"], w=[pcr])
                    v = nb // 2
                    addc = 1.0 if v % 3 == 1 else 0.0
                    act(modcol[:, l, nb * 4:(nb + 1) * 4], pc[:, 0:4], AF.Identity, r=[pcr], w=[("modcol", l, v)], bias=float(addc))
                    if gen_done[0] and prep_q:
                        tq_ = prep_q.pop(0)
                        sv_ = S.scope
                        S.scope = None
                        prep_tile_a(0, 0, tq_)
                        prep_tile_b(0, 0, tq_, extra=[("ssc", 0), ("ssc", 1)])
                        S.scope = sv_
            for _ in gen:
                pass
            sv_ = S.scope
            S.scope = None
            while prep_q:
                tq_ = prep_q.pop(0)
                prep_tile_a(0, 0, tq_)
                prep_tile_b(0, 0, tq_, extra=[("ssc", 0), ("ssc", 1)])
            S.scope = sv_


        NSLOT = 4
        sums = T("sums", [128, NSLOT, 4], F32)

        def finish_stats(slot, eps, c0, c1):
            ts("dve", mv[:, slot, 0:1], sums[:, slot, c0:c0 + 1], 1.0 / D, None, ALU.mult, None, r=[("sums", slot, c0)], w=[("mv", slot)])
            tt("dve", mv[:, slot, 1:2], mv[:, slot, 0:1], mv[:, slot, 0:1], ALU.mult, r=[("mv", slot)], w=[("mv1", slot)])
            stt(sc[:, slot, 3:4], sums[:, slot, c1:c1 + 1], 1.0 / D, mv[:, slot, 1:2], ALU.mult, ALU.subtract,
                r=[("sums", slot, c1), ("mv1", slot)], w=[("sc3", slot)])
            act(sc[:, slot, 0:1], sc[:, slot, 3:4], AF.Sqrt, r=[("sc3", slot)], w=[("sc0", slot)], bias=float(eps))
            S.op("dve", lambda e: e.reciprocal(out=sc[:, slot, 1:2], in_=sc[:, slot, 0:1]), r=[("sc0", slot)], w=[("sc1", slot)])

        def act_accum(tt_, slot, func, col):
            S.op("act", lambda e: e.activation(out=XH[slot][:], in_=X[:, tt_, :], func=func, accum_out=sums[:, slot, col:col + 1]),
                 r=[("X", tt_)], w=[("XH", slot), ("sums", slot, col)])

        def prep_tile_a(l, i, tt_, have_sum=False):
            slot = tt_ % NSLOT
            if not have_sum:
                act_accum(tt_, slot, AF.Identity, 2)
            act_accum(tt_, slot, AF.Square, 3)
            finish_stats(slot, LN_EPS, 2, 3)
            ts("dve", sc[:, slot, 2:3], mv[:, slot, 0:1], -1.0, sc[:, slot, 1:2], ALU.mult, ALU.mult,
               r=[("mv", slot), ("sc1", slot)], w=[("sc2", slot)])
            xh = XH[slot]
            act(xh[:], X[:, tt_, :], AF.Identity, r=[("X", tt_), ("sc1", slot), ("sc2", slot)], w=[("XH", slot)],
                bias=sc[:, slot, 2:3], scale=sc[:, slot, 1:2])

        def prep_tile_b(l, i, tt_, extra=()):
            slot = tt_ % NSLOT
            xh = XH[slot]
            pt, ptr = ps(6 + tt_ % 2)
            ptb = pt[:, :].bitcast(BF16)
            for kc in range(8):
                tr(ptb[:, kc * 128:(kc + 1) * 128], xh[:, kc * 128:(kc + 1) * 128], ident[:], r=[("XH", slot), "ident"], w=[ptr])
            for kc in range(8):
                scl = modcol[:, l, (3 * i + 1) * 8 + kc:(3 * i + 1) * 8 + kc + 1]
                shf = modcol[:, l, (3 * i) * 8 + kc:(3 * i) * 8 + kc + 1]
                if kc % 2 == 0:
                    ts("dve", HT[:, kc, tt_ * 128:(tt_ + 1) * 128], ptb[:, kc * 128:(kc + 1) * 128], scl, shf,
                       ALU.mult, ALU.add, r=[ptr, ("modcol", l, 3 * i), ("modcol", l, 3 * i + 1)] + list(extra), w=[("HT", tt_ // 4)])
                else:
                    act(HT[:, kc, tt_ * 128:(tt_ + 1) * 128], ptb[:, kc * 128:(kc + 1) * 128], AF.Identity,
                        r=[ptr, ("modcol", l, 3 * i), ("modcol", l, 3 * i + 1)] + list(extra), w=[("HT", tt_ // 4)], bias=shf, scale=scl)

        def prep(l, i):
            for tt_ in range(NT):
                prep_tile_a(l, i, tt_)
                prep_tile_b(l, i, tt_)

        LNGB = [T("LNG", [128, D], F32), T("LNB", [128, D], F32)]

        def post_setup(l, i):
            LNG, LNB = LNGB[0][:], LNGB[1][:]
            sv = S.scope
            S.scope = None
            dma("sp", LNG, bass.AP(lng_d.tensor, (l * 3 + i) * D, [[0, 128], [1, D]]), w=["LNG"])
            dma("sp", LNB, bass.AP(lnb_d.tensor, (l * 3 + i) * D, [[0, 128], [1, D]]), w=["LNB"])
            S.scope = sv
            return LNG, LNB

        def post_tile(l, i, tt_, LNG, LNB, want_sum):
            slot = tt_ % NSLOT
            act_accum(tt_, slot, AF.Identity, 0)
            act_accum(tt_, slot, AF.Square, 1)
            finish_stats(slot, LN_EPS / (ALPHA * ALPHA), 0, 1)
            stt(X[:, tt_, :], X[:, tt_, :], mv[:, slot, 0:1], LNG, ALU.subtract, ALU.mult,
                r=[("X", tt_), ("mv", slot), "LNG"], w=[("X", tt_)])
            if want_sum:
                S.op("dve", lambda e: e.scalar_tensor_tensor(out=X[:, tt_, :], in0=X[:, tt_, :], scalar=sc[:, slot, 1:2], in1=LNB,
                                                             op0=ALU.mult, op1=ALU.add, accum_out=sums[:, slot, 2:3]),
                     r=[("X", tt_), ("sc1", slot), "LNB"], w=[("X", tt_), ("sums", slot, 2)])
            else:
                stt(X[:, tt_, :], X[:, tt_, :], sc[:, slot, 1:2], LNB, ALU.mult, ALU.add,
                    r=[("X", tt_), ("sc1", slot), "LNB"], w=[("X", tt_)])

        ov = out_d.rearrange("(t p) d -> p t d", p=128)

        class Tail:
            def __init__(self, postli, prepli, final):
                self.postli, self.prepli, self.final = postli, prepli, final
                self.pend = []
                self.LNG, self.LNB = post_setup(*postli)

            def tile_done(self, tt_):
                sv = S.scope
                S.scope = None
                post_tile(self.postli[0], self.postli[1], tt_, self.LNG, self.LNB, self.prepli is not None)
                if self.prepli is not None:
                    prep_tile_a(self.prepli[0], self.prepli[1], tt_, have_sum=True)
                    self.pend.append(tt_)
                if self.final:
                    dma("sp", ov[:, tt_, :], X[:, tt_, :], r=[("X", tt_)], final=True)
                S.scope = sv

            def lagged(self, keep):
                sv = S.scope
                S.scope = None
                while len(self.pend) > keep:
                    prep_tile_b(self.prepli[0], self.prepli[1], self.pend.pop(0))
                S.scope = sv

        def gate_bc(l, i, scale, GBC, D8):
            for kc in range(8):
                col = (3 * i + 2) * 8 + kc
                ts("dve", D8[:, kc, :], identf[:], modcol[:, l, col:col + 1], float(scale), ALU.mult, ALU.mult,
                   r=["identf", ("modcol", l, 3 * i + 2)], w=["D8"])
            for hf in range(2):
                pg, pgr = ps(hf)
                mm(pg[:, :], onesf[:], D8[:, hf * 4:(hf + 1) * 4, :].rearrange("p a b -> p (a b)"), True, True, r=["onesf", "D8"], w=[pgr])
                cp("act", GBC[:, hf * 512:(hf + 1) * 512], pg[:, :], r=[pgr], w=["GBC"])

        def ffn(l, i, si, tail):
            ar = Arena(0, ARB)
            ay = ArenaYT(0, 32768)
            WG = [ay.alloc([128, 8, 512], BF16) for _ in range(2)]
            WU = [ay.alloc([128, 8, 512], BF16) for _ in range(2)]
            WD = [ar.alloc([128, 4, D], BF16) for _ in range(2)]
            ACTB = [ar.alloc([128, 4, 512], BF16) for _ in range(2)]
            SG = [ar.alloc([128, 512], F32) for _ in range(2)]
            GBC = ar.alloc([128, D], F32)
            D8 = ar.alloc([128, 8, 128], F32)
            gate_bc(l, si, 0.5 / ALPHA, GBC, D8)
            groups = [(g * 4, 4) for g in range(5)] + [(20, 2)]
            pend = None
            nsg = 0
            for gi, (c0, nf) in enumerate(groups):
                b = gi % 2
                f0 = c0 * 128
                fw = nf * 128
                dma("pool", WG[b][:, :, 0:fw], wg_d[l, i, :, f0:f0 + fw].rearrange("(kc p) f -> p kc f", p=128), w=[("WG", b)])
                dma("pool", WU[b][:, :, 0:fw], wu_d[l, i, :, f0:f0 + fw].rearrange("(kc p) f -> p kc f", p=128), w=[("WU", b)])
                dma("pool", WD[b][:, 0:nf, :], wd_d[l, i, f0:f0 + fw, :].rearrange("(c p) d -> p c d", p=128), w=[("WD", b)])
                for c in range(nf):
                    tt("pool", WD[b][:, c, :], WD[b][:, c, :], GBC, ALU.mult, r=["GBC", ("WD", b)], w=[("WD", b)])
                for tsi in range(4):
                    ab = (gi * 4 + tsi) % 2
                    for c in range(nf):
                        pgk = (gi * 16 + tsi * 4 + c) % 2
                        pg, pgr = ps(pgk)
                        pu, pur = ps(2 + pgk)
                        for kc in range(8):
                            mm(pg[:, :], WG[b][:, kc, c * 128:(c + 1) * 128], HT[:, kc, tsi * 512:(tsi + 1) * 512], kc == 0, kc == 7,
                               r=[("WG", b), ("HT", tsi)], w=[pgr])
                        for kc in range(8):
                            mm(pu[:, :], WU[b][:, kc, c * 128:(c + 1) * 128], HT[:, kc, tsi * 512:(tsi + 1) * 512], kc == 0, kc == 7,
                               r=[("WU", b), ("HT", tsi)], w=[pur])
                        sgb = nsg % 2
                        nsg += 1
                        act(SG[sgb], pg[:, :], AF.Silu, r=[pgr], w=[("SG", sgb)])
                        tt("dve", ACTB[ab][:, c, :], SG[sgb], pu[:, :], ALU.mult, r=[("SG", sgb), pur], w=[("ACTB", ab, c)])
                    cur = (b, ab, nf, tsi, gi == len(groups) - 1)
                    if pend is not None:
                        down(pend, WD, ACTB, tail)
                    pend = cur
            down(pend, WD, ACTB, tail)
            tail.lagged(0)

        dcount = [0]

        def down(p, WD, ACTB, tail):
            b, ab, nf, tsi, last = p
            for t4 in range(4):
                tt_ = tsi * 4 + t4
                for hf in range(2):
                    k = 4 + dcount[0] % 2
                    dcount[0] += 1
                    pd, pdr = ps(k)
                    for c in range(nf):
                        mm(pd[:, :], ACTB[ab][:, c, t4 * 128:(t4 + 1) * 128], WD[b][:, c, hf * 512:(hf + 1) * 512], c == 0, c == nf - 1,
                           r=[("ACTB", ab, c), ("WD", b)], w=[pdr])
                    tt("dve", X[:, tt_, hf * 512:(hf + 1) * 512], X[:, tt_, hf * 512:(hf + 1) * 512], pd[:, :], ALU.add,
                       r=[("X", tt_), pdr], w=[("X", tt_)])
                if last:
                    tail.tile_done(tt_)
                    tail.lagged(2)

        stage = [0]

        def done_stage():
            stage[0] += 1
            return stop is not None and stage[0] >= stop

        def run_layers():
            for l in range(DEPTH):
                fence()
                with arena_scope():
                    ffn(l, 0, 0, Tail((l, 0), (l, 1), False))
                if done_stage():
                    return
                fence()
                with arena_scope():
                    mixer(l, Tail((l, 1), (l, 2), False) if not dbg else None)
                if dbg:
                    return
                if done_stage():
                    return
                fence()
                lastl = l == DEPTH - 1
                with arena_scope():
                    ffn(l, 1, 2, Tail((l, 2), None if lastl else (l + 1, 0), lastl))
                if done_stage():
                    return

        TWO_PI = 2.0 * math.pi

        def act_b(out, in_, func, r, w, bias, scale):
            return act(out, in_, func, r, w, bias=bias, scale=scale)

        def pool_phase(l):
            ar = Arena(0, ARB)
            WUP = ar.alloc([128, 8, 256], BF16)
            PW = ar.alloc([128, 2, 128], BF16)
            pscol = ar.alloc([128, 2], F32)
            UP = [ar.alloc([128, 2, 528], F32) for _ in range(2)]
            Pb = ar.alloc([128, 2, 528], F32)
            Qb = ar.alloc([128, 2, 528], F32)
            PL = ar.alloc([128, 2, 512], BF16)
            TM = ar.alloc([128, 2, 16], F32)
            dma("pool", WUP, win_d[l, :, 1792:2048].rearrange("(kc p) f -> p kc f", p=128), w=["WUP"])
            dma("pool", PW, poolw_d[l], w=["PW"])
            dma("sp", pscol, pools_d[l], w=["pscol"])
            memset("pool", UP[0][:, :, 0:16], 0.0, w=[("UP", 0)])
            for tsi in range(4):
                ub = UP[tsi % 2]
                ur = ("UP", tsi % 2)
                for ch in range(2):
                    pu, pur = ps(ch)
                    for kc in range(8):
                        mm(pu[:, :], WUP[:, kc, ch * 128:(ch + 1) * 128], HT[:, kc, tsi * 512:(tsi + 1) * 512], kc == 0, kc == 7,
                           r=["WUP", ("HT", tsi)], w=[pur])
                    cp("act", ub[:, ch, 16:528], pu[:, :], r=[pur], w=[ur])
                tt("pool", Pb[:, :, 1:528], ub[:, :, 1:528], ub[:, :, 0:527], ALU.add, r=[ur], w=["Pb"])
                tt("pool", Qb[64:128, 0, 3:528], Pb[64:128, 0, 3:528], Pb[64:128, 0, 1:526], ALU.add, r=["Pb"], w=["Qb"])
                tt("pool", Qb[:, 1, 3:528], Pb[:, 1, 3:528], Pb[:, 1, 1:526], ALU.add, r=["Pb"], w=["Qb"])
                tt("pool", Pb[:, 1, 7:528], Qb[:, 1, 7:528], Qb[:, 1, 3:524], ALU.add, r=["Qb"], w=["Pb"])
                tt("pool", Qb[64:128, 1, 15:528], Pb[64:128, 1, 15:528], Pb[64:128, 1, 7:520], ALU.add, r=["Pb"], w=["Qb"])
                srcs = [(Pb, 0, 64, 0), (Qb, 64, 128, 0), (Pb, 0, 64, 1), (Qb, 64, 128, 1)]
                for (sb, p0, p1, ch) in srcs:
                    stt(PL[p0:p1, ch, :], sb[p0:p1, ch, 16:528], invw[p0:p1, ch:ch + 1], ub[p0:p1, ch, 16:528], ALU.mult, ALU.subtract,
                        r=["Pb", "Qb", ur, "invw"], w=["PL"])
                    if tsi == 0:
                        tt("dve", TM[p0:p1, ch, :], sb[p0:p1, ch, 16:32], rc16[p0:p1, ch, :], ALU.mult, r=["Pb", "Qb", "rc16"], w=["TM"])
                        tt("dve", PL[p0:p1, ch, 0:16], TM[p0:p1, ch, :], ub[p0:p1, ch, 16:32], ALU.subtract, r=["TM", ur], w=["PL"])
                if tsi < 3:
                    cp("pool", UP[(tsi + 1) % 2][:, :, 0:16], ub[:, :, 512:528], r=[ur], w=[("UP", (tsi + 1) % 2)])
                for ch in range(2):
                    py, pyr = ps(2 + ch)
                    mm(py[:, :], PW[:, ch, :], PL[:, ch, :], True, True, r=["PW", "PL"], w=[pyr])
                    act_b(YT[:, 6 + ch, tsi * 512:(tsi + 1) * 512], py[:, :], AF.Identity, r=[pyr, "pscol"], w=[("YT", 6 + ch, tsi)],
                          bias=0.0, scale=pscol[:, ch:ch + 1])

        TL = 128
        NTB = 16 * (TL + 1)
        ssc_f = nc.dram_tensor("ssm_scr_f", [DEPTH, 128, 2 * NTB + 224], F32, kind="Internal").ap()
        ssc_b = nc.dram_tensor("ssm_scr_b", [DEPTH, 128, NTB + 3 * 2048], BF16, kind="Internal").ap()

        class ArenaOn(Arena):
            def __init__(self, flat, size):
                self.flat = flat
                self.off = 0
                self.end = size

            def alloc(self, shape, dt):
                n = int(np.prod(shape[1:]))
                nb = n * (4 if dt in (F32, I32) else 2)
                nb = (nb + 31) // 32 * 32
                assert self.off + nb <= self.end, ("arenaOn overflow", shape, self.off, nb, self.end)
                v = self.flat[0:shape[0], self.off // 2:(self.off + nb) // 2]
                self.off += nb
                if dt != BF16:
                    v = v.bitcast(dt)
                v = v[:, 0:n]
                if len(shape) == 3:
                    v = v.rearrange("p (a b) -> p a b", a=shape[1])
                return v

        def ssm_setup_gen(l):
            ah = ArenaOn(HT[:, :, :].rearrange("p a b -> p (a b)"), 32768)
            ay = ArenaOn(YT[:, :, :].rearrange("p a b -> p (a b)"), 32768)
            ar = Arena(40992, ARB - 40992)
            TC = ah.alloc([128, 16, TL + 1], F32)
            TS = ah.alloc([128, 16, TL + 1], F32)
            ANG = ah.alloc([128, 16, TL + 1], F32)
            BL = ay.alloc([128, 16, 128], BF16)
            IBL = ay.alloc([128, 16, 128], BF16)
            CL = ay.alloc([128, 16, 128], BF16)
            SV = ay.alloc([128, 14, 16], F32)
            P1 = ay.alloc([128, 16, 16], F32)
            P2 = ay.alloc([128, 16, 16], F32)
            CT = ay.alloc([128, 256], F32)
            TCb = ay.alloc([128, 16, TL + 1], BF16)
            AI = ay.alloc([128, 16, TL + 1], I32)
            JF = ar.alloc([128, TL + 1], F32)
            JI = ar.alloc([128, TL + 1], I32)
            Bc = ar.alloc([128, 16, 16], F32)
            IBc = ar.alloc([128, 16, 16], F32)
            Tm = ar.alloc([128, 16, 16], F32)
            are, aim, ldt, dtv, lr, thn, rho, sn, cs, er, ei, gr, gi, tq = [SV[:, k, :] for k in range(14)]
            V = "ssmv"
            dma("pool", are, sare_d[l], w=["are", V])
            dma("pool", aim, saim_d[l], w=["aim", V])
            dma("pool", ldt, sldt_d[l], w=["ldt", V])
            dma("pool", P1, sp1_d[l].rearrange("p (g c) -> p g c", g=16), w=["P1"])
            dma("pool", P2, sp2_d[l].rearrange("p (g c) -> p g c", g=16), w=["P2"])
            dma("pool", CT, sct_d[l], w=["CT"])
            yield
            act(dtv, ldt, AF.Exp, r=["ldt"], w=[V])
            tt("dve", lr, are, dtv, ALU.mult, r=["are", V], w=[V])
            tt("dve", thn, aim, dtv, ALU.mult, r=["aim", V], w=[V])
            ts("dve", thn, thn, 1.0 / TWO_PI, None, ALU.mult, None, r=[V], w=[V])
            yield
            ts("dve", rho, lr, 1.0 / 720.0, 1.0 / 120.0, ALU.mult, ALU.add, r=[V], w=[V])
            for cst in (1.0 / 24.0, 1.0 / 6.0, 0.5, 1.0, 1.0):
                tt("dve", rho, rho, lr, ALU.mult, r=[V], w=[V])
                ts("dve", rho, rho, float(cst), None, ALU.add, None, r=[V], w=[V])
                yield

            def sincos(dst, src, shift, ai):
                ts("dve", dst, src, float(shift), None, ALU.add, None, r=[V], w=[V])
                cp("dve", ai, dst, r=[V], w=[V])
                tt("dve", dst, dst, ai, ALU.subtract, r=[V], w=[V])
                act(dst, dst, AF.Sin, r=[V], w=[V], scale=TWO_PI)

            sincos(sn, thn, 0.0, AI[:, 0, 0:16])
            yield
            sincos(cs, thn, 0.25, AI[:, 0, 0:16])
            yield
            tt("dve", er, rho, cs, ALU.mult, r=[V], w=[V])
            ts("dve", er, er, -1.0, None, ALU.add, None, r=[V], w=[V])
            tt("dve", ei, rho, sn, ALU.mult, r=[V], w=[V])
            tt("dve", tq, are, are, ALU.mult, r=[V, "are"], w=[V])
            yield
            tt("dve", gr, aim, aim, ALU.mult, r=[V, "aim"], w=[V])
            tt("dve", tq, tq, gr, ALU.add, r=[V], w=[V])
            S.op("dve", lambda e: e.reciprocal(out=tq, in_=tq), r=[V], w=[V])
            tt("dve", gr, er, are, ALU.mult, r=[V], w=[V])
            yield
            tt("dve", gi, ei, aim, ALU.mult, r=[V], w=[V])
            tt("dve", gr, gr, gi, ALU.add, r=[V], w=[V])
            tt("dve", gr, gr, tq, ALU.mult, r=[V], w=[V])
            tt("dve", gi, ei, are, ALU.mult, r=[V], w=[V])
            yield
            tt("dve", er, er, aim, ALU.mult, r=[V], w=[V])
            tt("dve", gi, gi, er, ALU.subtract, r=[V], w=[V])
            tt("dve", gi, gi, tq, ALU.mult, r=[V], w=[V])
            S2, S3, S4 = ei, er, tq
            ts("dve", S2, gi, sgn[:, 0:1], None, ALU.mult, None, r=[V, "sgn"], w=[V])
            yield
            ts("dve", S3, gr, sgn[:, 0:1], None, ALU.mult, None, r=[V, "sgn"], w=[V])
            ts("dve", S4, gi, -1.0, None, ALU.mult, None, r=[V], w=[V])
            bc = lambda v: v.unsqueeze(2).to_broadcast([128, 16, 16])
            tt("dve", Bc, P1, bc(gr), ALU.mult, r=[V, "P1"], w=["Bc"])
            tt("dve", Tm, P2, bc(S2), ALU.mult, r=[V, "P2"], w=["Tm"])
            yield
            tt("dve", Bc, Bc, Tm, ALU.add, r=["Tm"], w=["Bc"])
            tt("dve", IBc, P2, bc(S3), ALU.mult, r=[V, "P2"], w=["IBc"])
            tt("dve", Tm, P1, bc(S4), ALU.mult, r=[V, "P1", "Bc"], w=["Tm"])
            tt("dve", IBc, IBc, Tm, ALU.add, r=["Tm"], w=["IBc"])
            yield
            for (src, dst, nm) in ((Bc, BL, "Bc"), (IBc, IBL, "IBc")):
                flat = src.rearrange("p g c -> p (g c)")
                for ch in range(2):
                    pt_, ptr_ = ps(7)
                    tr(pt_[:, 0:128], flat[:, ch * 128:(ch + 1) * 128], identf[:], r=[nm, "identf"], w=[ptr_])
                    for g8 in range(8):
                        ts("dve", dst[:, ch * 8 + g8, :], pt_[:, 0:128], gmask[:, g8:g8 + 1], None, ALU.mult, None,
                           r=[ptr_, "gmask"], w=["BL"])
                        if g8 % 4 == 3:
                            yield
            ts("dve", CT, CT, sgn[:, 0:1], -1.0, ALU.mult, ALU.mult, r=["CT", "sgn"], w=["CT"])
            memset("pool", CL, 0.0, w=["CL"])
            for g in range(16):
                g8 = g % 8
                cp("dve", CL[:, g, 16 * g8:16 * g8 + 16], CT[:, g * 16:(g + 1) * 16], r=["CT"], w=["CL"])
                if g % 4 == 3:
                    yield
            S.op("pool", lambda e: e.iota(out=JI, pattern=[[1, TL + 1]], base=0, channel_multiplier=0), w=["JI"])
            cp("dve", JF, JI, r=["JI"], w=["JF"])
            tt("dve", ANG, JF.unsqueeze(1).to_broadcast([128, 16, TL + 1]), thn.unsqueeze(2).to_broadcast([128, 16, TL + 1]),
               ALU.mult, r=["JF", V], w=[V])
            yield
            sincos(TS, ANG, 0.0, AI)
            yield
            sincos(TC, ANG, 0.25, AI)
            yield
            cp("dve", TCb, TC, r=[V], w=[V])
            dma("pool", ssc_f[l, :, 0:NTB], TC.rearrange("p a b -> p (a b)"), r=[V], w=[("ssc", l)])
            dma("pool", ssc_f[l, :, NTB:2 * NTB], TS.rearrange("p a b -> p (a b)"), r=[V], w=[("ssc", l)])
            dma("pool", ssc_f[l, :, 2 * NTB:2 * NTB + 224], SV.rearrange("p a b -> p (a b)"), r=[V], w=[("ssc", l)])
            dma("pool", ssc_b[l, :, 0:NTB], TCb.rearrange("p a b -> p (a b)"), r=[V], w=[("ssc", l)])
            dma("pool", ssc_b[l, :, NTB:NTB + 2048], BL.rearrange("p a b -> p (a b)"), r=["BL"], w=[("ssc", l)])
            dma("pool", ssc_b[l, :, NTB + 2048:NTB + 4096], IBL.rearrange("p a b -> p (a b)"), r=["BL"], w=[("ssc", l)])
            dma("pool", ssc_b[l, :, NTB + 4096:NTB + 6144], CL.rearrange("p a b -> p (a b)"), r=["CL"], w=[("ssc", l)])
            yield

        def ssm_phase(l):
            ay = ArenaOn(YT[:, 0:4, :].rearrange("p a b -> p (a b)"), 16384)
            ar = Arena(0, ARB)
            BL = ay.alloc([128, 16, 128], BF16)
            IBL = ay.alloc([128, 16, 128], BF16)
            CL = ay.alloc([128, 16, 128], BF16)
            SV = ay.alloc([128, 14, 16], F32)
            WUS = ar.alloc([128, 8, 256], BF16)
            TC = ar.alloc([128, 16, TL + 1], F32)
            TS = ar.alloc([128, 16, TL + 1], F32)
            GW = ar.alloc([128, 2, 256], BF16)
            sdcol = ar.alloc([128, 2], F32)
            glub = ar.alloc([128, 2], F32)
            TCb = ar.alloc([128, 16, TL + 1], BF16)
            rho = SV[:, 6, :]
            V = "ssmv2"
            dma("sp", TC.rearrange("p a b -> p (a b)"), ssc_f[l, :, 0:NTB], r=[("ssc", l)], w=[V])
            dma("sp", TS.rearrange("p a b -> p (a b)"), ssc_f[l, :, NTB:2 * NTB], r=[("ssc", l)], w=[V])
            dma("sp", SV.rearrange("p a b -> p (a b)"), ssc_f[l, :, 2 * NTB:2 * NTB + 224], r=[("ssc", l)], w=[V])
            dma("sp", TCb.rearrange("p a b -> p (a b)"), ssc_b[l, :, 0:NTB], r=[("ssc", l)], w=[V])
            dma("sp", BL.rearrange("p a b -> p (a b)"), ssc_b[l, :, NTB:NTB + 2048], r=[("ssc", l)], w=["BL"])
            dma("sp", IBL.rearrange("p a b -> p (a b)"), ssc_b[l, :, NTB + 2048:NTB + 4096], r=[("ssc", l)], w=["BL"])
            dma("sp", CL.rearrange("p a b -> p (a b)"), ssc_b[l, :, NTB + 4096:NTB + 6144], r=[("ssc", l)], w=["CL"])
            dma("sp", sdcol, sd_d[l], w=["sdcol"])
            dma("sp", glub, glub_d[l], w=["glub"])
            dma("pool", GW, gluw_d[l], w=["GW"])
            dma("pool", WUS, win_d[l, :, 1536:1792].rearrange("(kc p) f -> p kc f", p=128), w=["WUS"])
            USS2 = [ar.alloc([128, 2, 512], BF16) for _ in range(2)]
            Wb = [ar.alloc([128, 4, TL], BF16) for _ in range(2)]
            T2 = ar.alloc([128, 4, TL], BF16)
            STb = [ar.alloc([128, 4, TL], F32) for _ in range(2)]
            SBh = [ar.alloc([128, 4, TL], BF16) for _ in range(2)]
            T2b = ar.alloc([128, 4, TL], BF16)
            S1 = ar.alloc([128, 4, TL], BF16)
            SB = [ar.alloc([128, 4, TL], BF16) for _ in range(2)]
            GE = ar.alloc([128, 2, TL], BF16)
            CI = ar.alloc([128, 16], F32)
            CA = ar.alloc([128, 4], F32)
            CB = ar.alloc([128, 4], F32)
            YV = ay.alloc([128, 2, TL], F32)
            G1 = ay.alloc([128, 2, TL], F32)
            G2 = ay.alloc([128, 2, TL], F32)
            NB = 64
            p0_, p0r = ps(0)
            pL, pLr = ps(6)
            pAs = [ps(1), ps(2)]
            pBs = [ps(3), ps(7)]
            pC, pCr = ps(4)
            pC3 = pC[:, :].rearrange("p (a b) -> p a b", a=4)

            def stage_F_pe(n):
                k, bq = n // 4, n % 4
                tsi, kk, ch = k // 4, k % 4, bq // 2
                USS = USS2[tsi % 2]
                ur = ("USS", tsi % 2)
                if kk == 0 and bq == 0:
                    for c2 in range(2):
                        for kc in range(8):
                            mm(p0_[:, :], WUS[:, kc, c2 * 128:(c2 + 1) * 128], HT[:, kc, tsi * 512:(tsi + 1) * 512], kc == 0, kc == 7,
                               r=["WUS", ("HT", tsi)], w=[p0r])
                        cp("act", USS[:, c2, :], p0_[:, :], r=[p0r], w=[ur])
                pA, pAr = pAs[n % 2]
                pB, pBr = pBs[n % 2]
                for gi_ in range(4):
                    mm(pA[:, gi_ * TL:(gi_ + 1) * TL], BL[:, 4 * bq + gi_, :], USS[:, ch, kk * TL:(kk + 1) * TL], True, True, r=["BL", ur], w=[pAr])
                for gi_ in range(4):
                    mm(pB[:, gi_ * TL:(gi_ + 1) * TL], IBL[:, 4 * bq + gi_, :], USS[:, ch, kk * TL:(kk + 1) * TL], True, True, r=["BL", ur], w=[pBr])

            def stage_F_dve(n):
                k, bq = n // 4, n % 4
                gs = slice(4 * bq, 4 * bq + 4)
                wb = Wb[n % 2]
                wbr = ("Wb", n % 2)
                pA, pAr = pAs[n % 2]
                pB, pBr = pBs[n % 2]
                pA3 = pA[:, :].rearrange("p (a b) -> p a b", a=4)
                pB3 = pB[:, :].rearrange("p (a b) -> p a b", a=4)
                tt("dve", wb, pA3, TC[:, gs, 0:TL], ALU.mult, r=[pAr, V], w=[wbr])
                tt("dve", T2, pB3, TS[:, gs, 0:TL], ALU.mult, r=[pBr, V], w=["T2"])
                tt("dve", wb, wb, T2, ALU.subtract, r=["T2"], w=[wbr])

            def stage_S(n):
                k, bq = n // 4, n % 4
                wb = Wb[n % 2]
                wbr = ("Wb", n % 2)
                stb = STb[n % 2]
                stbr = ("STb", n % 2)
                sbh = SBh[n % 2]
                sbhr = ("SBh", n % 2)
                for gi_ in range(4):
                    g = 4 * bq + gi_
                    ini = CI[:, g:g + 1] if k > 0 else 0.0
                    S.op("dve", lambda e, gi_=gi_, g=g, ini=ini: e.tensor_tensor_scan(
                        out=stb[:, gi_, :], data0=rho[:, g:g + 1].to_broadcast([128, TL]), data1=wb[:, gi_, :],
                        initial=ini, op0=ALU.mult, op1=ALU.add), r=[wbr, V, ("CI", bq)], w=[stbr])
                cp("act", sbh, stb, r=[stbr], w=[sbhr])
                mm(pC[:, :], jswb[:], sbh.rearrange("p a b -> p (a b)"), True, True, r=["jswb", sbhr], w=[pCr])
                if k < 15:
                    mm(pL[:, 0:4], jsw[:], stb[:, :, TL - 1], True, True, r=["jsw", stbr], w=[pLr])

            def stage_B(n):
                k, bq = n // 4, n % 4
                kk, ch = k % 4, bq // 2
                tsi = k // 4
                gs = slice(4 * bq, 4 * bq + 4)
                stb = STb[n % 2]
                stbr = ("STb", n % 2)
                sbh = SBh[n % 2]
                sbhr = ("SBh", n % 2)
                sb = SB[n % 2]
                sbr = ("SB", n % 2)
                USS = USS2[tsi % 2]
                ur = ("USS", tsi % 2)
                tt("dve", T2b, pC3, TS[:, gs, 0:TL], ALU.mult, r=[pCr, V], w=["T2b"])
                tt("dve", S1, sbh, TCb[:, gs, 0:TL], ALU.mult, r=[sbhr, V], w=["S1"])
                tt("dve", sb, S1, T2b, ALU.add, r=["S1", "T2b"], w=[sbr])
                if k < 15:
                    tt("dve", CA, pL[:, 0:4], TS[:, gs, TL], ALU.mult, r=[pLr, V], w=["CA"])
                    tt("dve", CB, stb[:, :, TL - 1], TC[:, gs, TL], ALU.mult, r=[stbr, V], w=["CB"])
                    tt("dve", CI[:, gs], CA, CB, ALU.add, r=["CA", "CB"], w=[("CI", bq)])
                pY, pYr = ps(5)
                for gi_ in range(4):
                    g = 4 * bq + gi_
                    mm(pY[:, ch * TL:(ch + 1) * TL], CL[:, g, :], sb[:, gi_, :], g % 8 == 0, g % 8 == 7, r=["CL", sbr], w=[pYr])
                if bq == 3:
                    glu_a(k)
                if bq == 0 and k > 0:
                    glu_b(k - 1)

            def glu_a(k):
                kk, tsi = k % 4, k // 4
                USS = USS2[tsi % 2]
                ur = ("USS", tsi % 2)
                cols = slice(kk * TL, (kk + 1) * TL)
                for c2 in range(2):
                    pY2, pY2r = ps(5)
                    stt(YV[:, c2, :], USS[:, c2, cols], sdcol[:, c2:c2 + 1], pY2[:, c2 * TL:(c2 + 1) * TL], ALU.mult, ALU.add, r=[ur, "sdcol", pY2r], w=["YV"])
                act(G1, YV, AF.Square, r=["YV"], w=["G1"])
                ts("pool", G1, G1, 0.044715, 1.0, ALU.mult, ALU.add, r=["G1"], w=["G1"])
                tt("pool", G1, G1, YV, ALU.mult, r=["G1", "YV"], w=["G1"])
                act(G2, G1, AF.Sigmoid, r=["G1"], w=["G2"], scale=2.0 * math.sqrt(2.0 / math.pi))
                tt("pool", GE, YV, G2, ALU.mult, r=["G2", "YV"], w=["GE"])

            def glu_b(k):
                tsi = k // 4
                tok = slice(k * TL, (k + 1) * TL)
                for dch in range(2):
                    pG, pGr = ps(6)
                    for c2 in range(2):
                        mm(pG[:, 128:128 + TL], GW[:, c2, dch * 128:(dch + 1) * 128], GE[:, c2, :], c2 == 0, c2 == 1, r=["GW", "GE"], w=[pGr])
                    act_b(G1[:, dch, :], pG[:, 128:128 + TL], AF.Sigmoid, r=[pGr, "glub", "G1"], w=["G1"], bias=glub[:, dch:dch + 1], scale=1.0)
                    tt("pool", YT[:, 4 + dch, tok], YV[:, dch, :], G1[:, dch, :], ALU.mult, r=["G1", "YV"], w=[("YT", 4 + dch, tsi)])

            stage_F_pe(0)
            for it in range(NB + 2):
                if it + 1 < NB:
                    stage_F_pe(it + 1)
                if it < NB:
                    stage_F_dve(it)
                if 0 <= it - 2 < NB:
                    stage_B(it - 2)
                if 0 <= it - 1 < NB:
                    stage_S(it - 1)
            glu_b(15)

        def bias_setup():
            ar = Arena(0, ARB)
            relb = ar.alloc([32, 8], F32)
            OH = ar.alloc([32, 1152], F32)
            NG = ar.alloc([8, 1152], F32)
            FV = ar.alloc([8, 1152], F32)
            dma("sp", relb, relb_d, w=["relb"])
            dma("sp", OH, oh_d, w=["OH"])
            dma("sp", NG, negm_d, w=["NG"])
            for br in range(3):
                p_, pr_ = ps(br)
                mm(p_[0:8, 0:384], relb, OH[:, br * 384:(br + 1) * 384], True, True, r=["relb", "OH"], w=[pr_])
                tt("dve", FV[:, br * 384:(br + 1) * 384], p_[0:8, 0:384], NG[:, br * 384:(br + 1) * 384], ALU.add, r=[pr_, "NG"], w=["FV"])
            dma("sp", fv_d, FV, r=["FV"], w=["fv_d"])

        def attn_phase(l):
            ar = Arena(0, ARB)
            WQKV = ar.alloc([128, 8, 384], BF16)
            QZ = [ar.alloc([128, SEQ], BF16) for _ in range(2)]
            KT = ar.alloc([128, SEQ], BF16)
            VP = ar.alloc([128, 3, 16, 192], BF16)
            BTp = ar.alloc([128, 2, 768], BF16)
            Hb = ar.alloc([128, 256], F32)
            PT = [ar.alloc([128, 128], BF16) for _ in range(4)]
            RD = [ar.alloc([128, 512], F32) for _ in range(1)]
            VT = ar.alloc([128, SEQ], BF16)
            memset("pool", QZ[0][64:128, :], 0.0, w=[("QZ", 0)])
            memset("pool", QZ[1][0:64, :], 0.0, w=[("QZ", 1)])
            memset("pool", VP[:, :, :, 64:128], 1.0, w=["VP"])
            npt = 0
            nsc = 0
            nrd = 0
            for hp in range(4):
                for j, base in enumerate((0, 512, 1024)):
                    dma("pool", WQKV[:, :, j * 128:(j + 1) * 128],
                        win_d[l, :, base + hp * 128:base + (hp + 1) * 128].rearrange("(kc p) f -> p kc f", p=128), w=["WQKV"])
                for tsi in range(4):
                    pq, pqr = ps(6)
                    for kc in range(8):
                        mm(pq[:, :], WQKV[:, kc, 0:128], HT[:, kc, tsi * 512:(tsi + 1) * 512], kc == 0, kc == 7, r=["WQKV", ("HT", tsi)], w=[pqr])
                    act(QZ[0][0:64, tsi * 512:(tsi + 1) * 512], pq[0:64, :], AF.Copy, r=[pqr], w=[("QZ", 0)], scale=0.125)
                    act(QZ[1][64:128, tsi * 512:(tsi + 1) * 512], pq[64:128, :], AF.Copy, r=[pqr], w=[("QZ", 1)], scale=0.125)
                    pk, pkr = ps(7)
                    for kc in range(8):
                        mm(pk[:, :], WQKV[:, kc, 128:256], HT[:, kc, tsi * 512:(tsi + 1) * 512], kc == 0, kc == 7, r=["WQKV", ("HT", tsi)], w=[pkr])
                    cp("dve", KT[:, tsi * 512:(tsi + 1) * 512], pk[:, :], r=[pkr], w=["KT"])
                for tsi in range(4):
                    pvt, pvtr = ps(6 + tsi % 2)
                    for kc in range(8):
                        mm(pvt[:, :], WQKV[:, kc, 256:384], HT[:, kc, tsi * 512:(tsi + 1) * 512], kc == 0, kc == 7, r=["WQKV", ("HT", tsi)], w=[pvtr])
                    cp("act", VT[:, tsi * 512:(tsi + 1) * 512], pvt[:, :], r=[pvtr], w=["VT"])
                nv = 0
                for br, (win, dil) in enumerate(PATTERNS):
                    nbk = 16 // dil
                    for q4 in range(4):
                        pv, pvr = ps(6 + nv % 2)
                        nv += 1
                        pvb = pv[:, :].bitcast(BF16)
                        for t4 in range(4):
                            tix = q4 * 4 + t4
                            rr, m = tix // nbk, tix % nbk
                            t0 = rr + dil * 128 * m
                            tr(pvb[:, t4 * 128:(t4 + 1) * 128], VT[:, t0:t0 + dil * 127 + 1:dil], ident[:], r=["VT", "ident"], w=[pvr])
                        outv = VP[:, br, q4 * 4:(q4 + 1) * 4, :].rearrange("p t (a c) -> p t a c", a=3)[:, :, 0:3:2, :]
                        inv_ = pvb[:, 0:512].rearrange("p (t a c) -> p t a c", t=4, a=2)
                        cp("dve", outv, inv_, r=[pvr], w=["VP"])
                for hh in range(2):
                    h = 2 * hp + hh
                    for br in range(3):
                        dma("sp", Hb, bass.AP(fv_d.tensor, h * 1152 + br * 384, [[1, 128], [1, 256]]), r=["fv_d"], w=["Hb"])
                        pb_, pbr_ = ps(6 + br % 2)
                        mm(pb_[:, 0:256], jex[:], Hb, True, True, r=["jex", "Hb"], w=[pbr_])
                        cp("act", BTp[:, hh, br * 256:(br + 1) * 256], pb_[:, 0:256], r=[pbr_], w=["BTp"])
                for hh in range(2):
                    accs = [ps(b) for b in range(4)]
                    started = [False] * 4
                    tasks = []
                    for br, (win, dil) in enumerate(PATTERNS):
                        nbk = 16 // dil
                        for rr in range(dil):
                            for m in range(nbk):
                                for qb in (m, m + 1):
                                    if qb >= nbk:
                                        continue
                                    tasks.append((br, dil, nbk, rr, m, qb))
                    pendq = []

                    def pieces_of(task):
                        br, dil, nbk, rr, m, qb = task
                        if dil == 16:
                            return [(b, rr, 32 * b, 32) for b in range(4)]
                        elif dil == 4:
                            return [(qb, rr, 0, 128)]
                        t0 = 128 * qb
                        return [(t0 // 512, t0 % 512, 0, 128)]

                    remaining = [0] * 4
                    for task in tasks:
                        for (b, c0, i0, n) in pieces_of(task):
                            remaining[b] += 1

                    def do_pv(task, slot):
                        br, dil, nbk, rr, m, qb = task
                        tix = rr * nbk + m
                        lhs = VP[:, br, tix, hh * 64:hh * 64 + 128]
                        for (b, c0, i0, n) in pieces_of(task):
                            acc, accr = accs[b]
                            remaining[b] -= 1
                            mm(acc[:, c0:c0 + dil * (n - 1) + 1:dil], lhs, PT[slot][:, i0:i0 + n], not started[b], remaining[b] == 0,
                               r=["VP", ("PT", slot)], w=[accr])
                            started[b] = True

                    for task in tasks:
                        br, dil, nbk, rr, m, qb = task
                        k0 = rr + dil * 128 * m
                        q0 = rr + dil * 128 * qb
                        psc, pscr = ps(4 + nsc % 2)
                        nsc += 1
                        mm(psc[:, 0:128], KT[:, k0:k0 + dil * 127 + 1:dil], QZ[hh][:, q0:q0 + dil * 127 + 1:dil], True, False,
                           r=["KT", ("QZ", hh)], w=[pscr])
                        off = (qb - m) * 128
                        mm(psc[:, 0:128], ident[:], BTp[:, hh, br * 256 + off:br * 256 + off + 128], False, True, r=["ident", "BTp"], w=[pscr])
                        slot = npt % 4
                        npt += 1
                        act(PT[slot], psc[:, 0:128], AF.Exp, r=[pscr], w=[("PT", slot)])
                        pendq.append((task, slot))
                        if len(pendq) > 2:
                            do_pv(*pendq.pop(0))
                    while pendq:
                        do_pv(*pendq.pop(0))
                    for b in range(4):
                        acc, accr = accs[b]
                        rd = RD[0]
                        rdr = ("RD", 0)
                        nrd += 1
                        if hh == 0:
                            act(rd[0:64, :], acc[64:128, :], AF.Ln, r=[accr], w=[rdr])
                            act(rd[0:64, :], rd[0:64, :], AF.Exp, r=[rdr], w=[rdr], scale=-1.0)
                            tt("dve", YT[0:64, hp, b * 512:(b + 1) * 512], acc[0:64, :], rd[0:64, :], ALU.mult, r=[accr, rdr], w=[("YT", hp, b)])
                        else:
                            act(rd[64:128, :], acc[0:64, :], AF.Ln, r=[accr], w=[rdr])
                            act(rd[64:128, :], rd[64:128, :], AF.Exp, r=[rdr], w=[rdr], scale=-1.0)
                            tt("dve", YT[64:128, hp, b * 512:(b + 1) * 512], acc[64:128, :], rd[64:128, :], ALU.mult, r=[accr, rdr], w=[("YT", hp, b)])

        def wout_phase(l, tail):
            ar = Arena(0, ARB)
            WO = ar.alloc([128, 8, D], BF16)
            GBC = ar.alloc([128, D], F32)
            D8 = ar.alloc([128, 8, 128], F32)
            dma("pool", WO, wout_d[l].rearrange("(kc p) d -> p kc d", p=128), w=["WO"])
            gate_bc(l, 1, 1.0 / ALPHA, GBC, D8)
            for kc in range(8):
                tt("pool", WO[:, kc, :], WO[:, kc, :], GBC, ALU.mult, r=["GBC", "WO"], w=["WO"])
            n = 0
            for tt_ in range(NT):
                for hf in range(2):
                    po, por = ps(n % 2)
                    n += 1
                    for kc in range(8):
                        mm(po[:, :], YT[:, kc, tt_ * 128:(tt_ + 1) * 128], WO[:, kc, hf * 512:(hf + 1) * 512], kc == 0, kc == 7,
                           r=["WO"] + [("YT", kc, b) for b in range(4)], w=[por])
                    tt("dve", X[:, tt_, hf * 512:(hf + 1) * 512], X[:, tt_, hf * 512:(hf + 1) * 512], po[:, :], ALU.add,
                       r=[("X", tt_), por], w=[("X", tt_)])
                tail.tile_done(tt_)
                tail.lagged(2)
            tail.lagged(0)

        def mixer(l, tail):
            if "pool" in parts:
                pool_phase(l)
                fence()
            if "ssm" in parts:
                ssm_phase(l)
                fence()
            if "attn" in parts:
                attn_phase(l)
                fence()
            if dbg:
                return
            wout_phase(l, tail)

        with arena_scope():
            ada_phase()
        fence_all()
        with arena_scope():
            bias_setup()
        run_layers()
        if dbg:
            fence_all()
            dbg_d = nc.dram_tensor("dbg", [128, 8, SEQ], F32, kind="ExternalOutput").ap()
            dma("pool", dbg_d, YT[:, :, :], r=["ALL"], final=True)
        if dbg or stop is not None:
            fence_all()
            for q in range(4):
                dma("sp", ov[:, q * 4:(q + 1) * 4, :], X[:, q * 4:(q + 1) * 4, :], r=[("X", t) for t in range(q * 4, q * 4 + 4)], final=True)
        S.emit()
    return nc


def host_inputs(inputs):
    f = lambda a: np.ascontiguousarray(np.asarray(a, dtype=np.float32))
    consts = static_consts()
    shared = {}
    for k in ("ada_w", "ada_b", "ln_g", "ln_b", "ffn_w_gate", "ffn_w_up", "ffn_w_down", "w_in", "w_out", "rel_bias"):
        shared[k] = f(inputs[k])
    a_re, a_im = f(inputs["ssm_a_re"]), f(inputs["ssm_a_im"])
    dup = lambda a: np.concatenate([a, a], axis=1)
    shared["sa_re"] = f(dup(a_re.transpose(0, 2, 1)))
    shared["sa_im"] = f(dup(a_im.transpose(0, 2, 1)))
    shared["sldt"] = f(np.broadcast_to(f(inputs["ssm_log_dt"])[:, None, :], (DEPTH, 128, 16)))
    b_re = f(inputs["ssm_b_re"]).transpose(0, 2, 1, 3).reshape(DEPTH, 64, 256)
    b_im = f(inputs["ssm_b_im"]).transpose(0, 2, 1, 3).reshape(DEPTH, 64, 256)
    shared["sP1"] = f(np.concatenate([b_re, b_im], axis=1))
    shared["sP2"] = f(np.concatenate([b_im, b_re], axis=1))
    c_re = f(inputs["ssm_c_re"]).transpose(0, 3, 1, 2).reshape(DEPTH, 64, 256)
    c_im = f(inputs["ssm_c_im"]).transpose(0, 3, 1, 2).reshape(DEPTH, 64, 256)
    shared["sCT"] = f(np.concatenate([c_re, c_im], axis=1))
    col2 = lambda a: f(f(a).reshape(DEPTH, 2, 128).transpose(0, 2, 1))
    shared["sd"] = col2(inputs["ssm_d"])
    shared["glu_b"] = col2(inputs["glu_b"])
    shared["pool_s"] = col2(inputs["pool_scale"])
    shared["glu_w"] = f(f(inputs["glu_w"]).reshape(DEPTH, 2, 128, 256).transpose(0, 2, 1, 3))
    pw = f(inputs["pool_w"])
    pbd = np.zeros((DEPTH, 128, 2, 128), np.float32)
    for ch in range(2):
        for h in range(2):
            pbd[:, h * 64:(h + 1) * 64, ch, h * 64:(h + 1) * 64] = pw[:, ch * 2 + h]
    shared["pool_w"] = pbd
    shared.update(consts)
    x = f(inputs["x"])
    c = f(inputs["c"])
    maps = []
    for b in range(8):
        m = dict(shared)
        m["x"] = x[b]
        m["cT"] = f(c[b].reshape(8, 128).T)
        maps.append(m)
    return maps


_NC_CACHE = {}


def kernel(**inputs):
    if "nc" not in _NC_CACHE:
        _NC_CACHE["nc"] = build()
    nc = _NC_CACHE["nc"]
    maps = host_inputs(inputs)
    res = run_bass_kernel_spmd(nc, maps, core_ids=list(range(8)))
    return np.stack([np.asarray(r["out"], dtype=np.float32) for r in res.results], axis=0)
```

```python
import contextlib
import math
import numpy as np
import concourse.bass as bass
import concourse.mybir as mybir
from concourse.bass_utils import run_bass_kernel_spmd

F32 = mybir.dt.float32
BF16 = mybir.dt.bfloat16
I32 = mybir.dt.int32
AF = mybir.ActivationFunctionType
ALU = mybir.AluOpType

SEQ = 2048
D = 1024
DFF = 2816
DEPTH = 2
NT = SEQ // 128
ALPHA = (2 * DEPTH) ** 0.25
LN_EPS = 1e-5
NEG = -1e30
PATTERNS = ((128, 1), (512, 4), (2048, 16))

ENGS = ("pe", "act", "dve", "pool", "sp")
NDMASEM = 12


class Op:
    __slots__ = ("eng", "fn", "deps", "marked", "val", "is_dma", "dsem", "dval", "idx")

    def __init__(self, eng, fn, is_dma):
        self.eng = eng
        self.fn = fn
        self.deps = []
        self.marked = False
        self.val = None
        self.is_dma = is_dma
        self.dsem = None
        self.dval = None
        self.idx = None


class Sched:
    def __init__(self, nc):
        self.nc = nc
        self.ops = {e: [] for e in ENGS}
        self.writers = {}
        self.readers = {}
        self.ndma = {e: 0 for e in ENGS}
        self.final_waits = []
        self.scope = None

    def op(self, eng, fn, r=(), w=(), dma=False, final=False):
        o = Op(eng, fn, dma)
        deps = []
        r = list(r)
        w = list(w)
        if "ALL" not in w:
            r.append("ALL")
        if self.scope is not None and self.scope not in w:
            r.append(self.scope)
        for x in r:
            deps.extend(self.writers.get(x, ()))
        for x in w:
            deps.extend(self.writers.get(x, ()))
            deps.extend(self.readers.get(x, ()))
        seen = set()
        for d in deps:
            if d is o or id(d) in seen:
                continue
            seen.add(id(d))
            if d.eng == "pe" and eng == "pe" and not d.is_dma and not dma:
                continue
            o.deps.append(d)
            if not d.is_dma:
                d.marked = True
        for x in w:
            if self.readers.get(x):
                self.writers[x] = [o]
                self.readers[x] = []
            else:
                ws = self.writers.setdefault(x, [])
                ws[:] = [p for p in ws if not (p.eng == eng and p.is_dma == dma and not dma)]
                ws.append(o)
        for x in r:
            if x not in w:
                rs = self.readers.setdefault(x, [])
                rs[:] = [p for p in rs if not (p.eng == eng and not p.is_dma and not dma)]
                rs.append(o)
        if dma:
            j = self.ndma[eng]
            self.ndma[eng] = j + 1
            o.dsem = j % NDMASEM
            o.dval = 16 * (j // NDMASEM + 1)
            o.idx = j
        self.ops[eng].append(o)
        if final:
            self.final_waits.append(o)
        return o

    def emit(self):
        nc = self.nc
        with contextlib.ExitStack() as st:
            csem = {e: st.enter_context(nc.semaphore("c_" + e)) for e in ENGS}
            dsem = {e: [st.enter_context(nc.semaphore("d_%s_%d" % (e, i))) for i in range(NDMASEM)]
                    for e in ENGS if self.ndma[e] > 0}
            for e in ENGS:
                c = 0
                for o in self.ops[e]:
                    if o.marked and not o.is_dma:
                        c += 1
                        o.val = c
            block = st.enter_context(nc.Block())

            def run(e, eng):
                waited = {}
                for o in self.ops[e]:
                    waits = []
                    for d in o.deps:
                        if d.is_dma:
                            waits.append((("d", d.eng, d.dsem), dsem[d.eng][d.dsem], d.dval))
                        else:
                            waits.append((("c", d.eng), csem[d.eng], d.val))
                    if o.is_dma and o.idx >= NDMASEM:
                        waits.append((("d", e, o.dsem), dsem[e][o.dsem], o.dval - 16))
                    for key, s, v in waits:
                        if waited.get(key, 0) >= v:
                            continue
                        eng.wait_ge(s, v)
                        waited[key] = v
                    ins = o.fn(eng)
                    if o.is_dma:
                        ins.then_inc(dsem[e][o.dsem], 16)
                    elif o.marked:
                        ins.then_inc(csem[e], 1)
                for o in self.final_waits:
                    if o.eng == e:
                        eng.wait_ge(dsem[e][o.dsem], o.dval)

            if self.ops["sp"]:
                @block.sync
                def _(eng):
                    run("sp", eng)
            if self.ops["pe"]:
                @block.tensor
                def _(eng):
                    run("pe", eng)
            if self.ops["act"]:
                @block.scalar
                def _(eng):
                    run("act", eng)
            if self.ops["dve"]:
                @block.vector
                def _(eng):
                    run("dve", eng)
            if self.ops["pool"]:
                @block.gpsimd
                def _(eng):
                    run("pool", eng)


def t5_bucket(dist):
    n_buckets, max_distance = 32, 2048
    max_exact = n_buckets // 2
    d = np.maximum(dist, 1).astype(np.float32)
    large = max_exact + (np.log(d / max_exact) / math.log(max_distance / max_exact)
                         * (n_buckets - max_exact)).astype(np.int32)
    large = np.minimum(large, n_buckets - 1)
    return np.where(dist < max_exact, dist, large).astype(np.int32)


def static_consts():
    c = {}
    c["identf"] = np.eye(128, dtype=np.float32)
    c["jex"] = np.eye(128, dtype=np.float32)[::-1].copy()
    jsw = np.zeros((128, 128), np.float32)
    for m in range(64):
        jsw[m + 64, m] = -1.0
        jsw[m, m + 64] = 1.0
    c["jsw"] = jsw
    oh = np.zeros((32, 3 * 384), np.float32)
    negm = np.zeros((8, 3 * 384), np.float32)
    for bi, (win, dil) in enumerate(PATTERNS):
        for u in range(384):
            dist = u - 127
            if 0 <= dist <= win // dil:
                oh[t5_bucket(np.array([dist * dil]))[0], bi * 384 + u] = 1.0
            else:
                negm[:, bi * 384 + u] = NEG
    c["oh"] = oh
    c["negm"] = negm
    gm = np.zeros((128, 8), np.float32)
    for p in range(128):
        gm[p, p // 16] = 1.0
    c["gmask"] = gm
    wins = np.array([2, 4, 8, 16], np.float32)
    wp = np.zeros((128, 2), np.float32)
    for ch in range(2):
        for p in range(128):
            wp[p, ch] = wins[ch * 2 + p // 64]
    c["invw"] = (1.0 / wp).astype(np.float32)
    t = np.arange(16, dtype=np.float32)[None, None, :]
    c["rc16"] = (1.0 / np.minimum(t + 1.0, wp[:, :, None])).astype(np.float32)
    c["sgn"] = np.concatenate([-np.ones((64, 1), np.float32), np.ones((64, 1), np.float32)], 0)
    return c


def build(stop=None, parts=("pool", "ssm", "attn"), dbg=False):
    nc = bass.Bass("TRN2", target_bir_lowering=False)

    def din(name, shape, dt=F32):
        return nc.dram_tensor(name, list(shape), dt, kind="ExternalInput").ap()

    x_d = din("x", [SEQ, D])
    cT_d = din("cT", [128, 8])
    adaw_d = din("ada_w", [DEPTH, D, 9 * D])
    adab_d = din("ada_b", [DEPTH, 9 * D])
    lng_d = din("ln_g", [DEPTH, 3, D])
    lnb_d = din("ln_b", [DEPTH, 3, D])
    wg_d = din("ffn_w_gate", [DEPTH, 2, D, DFF])
    wu_d = din("ffn_w_up", [DEPTH, 2, D, DFF])
    wd_d = din("ffn_w_down", [DEPTH, 2, DFF, D])
    win_d = din("w_in", [DEPTH, D, 2048])
    wout_d = din("w_out", [DEPTH, D, D])
    relb_d = din("rel_bias", [32, 8])
    sare_d = din("sa_re", [DEPTH, 128, 16])
    saim_d = din("sa_im", [DEPTH, 128, 16])
    sldt_d = din("sldt", [DEPTH, 128, 16])
    sp1_d = din("sP1", [DEPTH, 128, 256])
    sp2_d = din("sP2", [DEPTH, 128, 256])
    sct_d = din("sCT", [DEPTH, 128, 256])
    sd_d = din("sd", [DEPTH, 128, 2])
    gluw_d = din("glu_w", [DEPTH, 128, 2, 256])
    glub_d = din("glu_b", [DEPTH, 128, 2])
    poolw_d = din("pool_w", [DEPTH, 128, 2, 128])
    pools_d = din("pool_s", [DEPTH, 128, 2])
    identf_d = din("identf", [128, 128])
    jex_d = din("jex", [128, 128])
    jsw_d = din("jsw", [128, 128])
    oh_d = din("oh", [32, 1152])
    negm_d = din("negm", [8, 1152])
    gmask_d = din("gmask", [128, 8])
    invw_d = din("invw", [128, 2])
    rc16_d = din("rc16", [128, 2, 16])
    sgn_d = din("sgn", [128, 1])
    out_d = nc.dram_tensor("out", [SEQ, D], F32, kind="ExternalOutput").ap()
    fv_d = nc.dram_tensor("fv_scratch", [8, 1152], F32, kind="Internal").ap()

    st = contextlib.ExitStack()
    with st:
        def T(name, shape, dt):
            return st.enter_context(nc.sbuf_tensor(name, list(shape), dt))

        PSB = [st.enter_context(nc.psum_tensor("ps%d" % i, [128, 512], F32)) for i in range(8)]

        def ps(k):
            return PSB[k], ("ps", k)

        X = T("X", [128, NT, D], F32)
        HT = T("HT", [128, 8, SEQ], BF16)
        YT = T("YT", [128, 8, SEQ], BF16)
        ARB = 49152
        AR = T("AR", [128, ARB // 2], BF16)
        ident = T("ident", [128, 128], BF16)
        identf = T("identf_s", [128, 128], F32)
        jex = T("jex_s", [128, 128], F32)
        jsw = T("jsw_s", [128, 128], F32)
        onesf = T("onesf", [128, 128], F32)
        jswb = T("jswb", [128, 128], BF16)
        modcol = T("modcol", [128, DEPTH, 72], F32)
        XH = [T("XH%d" % i, [128, D], BF16) for i in range(4)]
        mv = T("mv", [128, 4, 2], F32)
        sc = T("sc", [128, 4, 4], F32)
        condT = T("condT", [128, 8], F32)
        gmask = T("gmask_s", [128, 8], F32)
        invw = T("invw_s", [128, 2], F32)
        rc16 = T("rc16_s", [128, 2, 16], F32)
        sgn = T("sgn_s", [128, 1], F32)

        S = Sched(nc)

        class Arena:
            def __init__(self, base, size):
                self.off = base
                self.end = base + size

            def alloc(self, shape, dt):
                n = int(np.prod(shape[1:]))
                nb = n * (4 if dt in (F32, I32) else 2)
                nb = (nb + 31) // 32 * 32
                assert self.off + nb <= self.end, ("arena overflow", shape, self.off, nb, self.end)
                v = AR[0:shape[0], self.off // 2:(self.off + nb) // 2]
                self.off += nb
                if dt != BF16:
                    v = v.bitcast(dt)
                v = v[:, 0:n]
                if len(shape) == 3:
                    v = v.rearrange("p (a b) -> p a b", a=shape[1])
                elif len(shape) == 4:
                    v = v.rearrange("p (a b c) -> p a b c", a=shape[1], b=shape[2])
                return v

        class ArenaYT(Arena):
            def alloc(self, shape, dt):
                n = int(np.prod(shape[1:]))
                nb = n * (4 if dt in (F32, I32) else 2)
                nb = (nb + 31) // 32 * 32
                assert self.off + nb <= self.end
                flat = YT[:, :, :].rearrange("p a b -> p (a b)")
                v = flat[0:shape[0], self.off // 2:(self.off + nb) // 2]
                self.off += nb
                if dt != BF16:
                    v = v.bitcast(dt)
                v = v[:, 0:n]
                if len(shape) == 3:
                    v = v.rearrange("p (a b) -> p a b", a=shape[1])
                return v

        def dma(eng, out, in_, r=(), w=(), final=False, slow=False):
            if slow:
                return S.op(eng, lambda e: e.dma_start(out=out, in_=in_, allow_slow_non_contiguous=True), r=r, w=w, dma=True, final=final)
            return S.op(eng, lambda e: e.dma_start(out=out, in_=in_), r=r, w=w, dma=True, final=final)

        def mm(out, lhsT, rhs, start, stop, r, w):
            return S.op("pe", lambda e: e.matmul(out, lhsT=lhsT, rhs=rhs, start=start, stop=stop), r=r, w=w)

        def tr(out, in_, idn, r, w):
            return S.op("pe", lambda e: e.transpose(out=out, in_=in_, identity=idn), r=r, w=w)

        def act(out, in_, func, r, w, bias=0.0, scale=1.0):
            return S.op("act", lambda e: e.activation(out=out, in_=in_, func=func, bias=bias, scale=scale), r=r, w=w)

        def ts(eng, out, in0, s1, s2, op0, op1, r, w):
            if op1 is None:
                return S.op(eng, lambda e: e.tensor_scalar(out=out, in0=in0, scalar1=s1, scalar2=None, op0=op0), r=r, w=w)
            return S.op(eng, lambda e: e.tensor_scalar(out=out, in0=in0, scalar1=s1, scalar2=s2, op0=op0, op1=op1), r=r, w=w)

        def tt(eng, out, in0, in1, op, r, w):
            return S.op(eng, lambda e: e.tensor_tensor(out=out, in0=in0, in1=in1, op=op), r=r, w=w)

        def stt(out, in0, scalar, in1, op0, op1, r, w):
            return S.op("dve", lambda e: e.scalar_tensor_tensor(out=out, in0=in0, scalar=scalar, in1=in1, op0=op0, op1=op1), r=r, w=w)

        def cp(eng, out, in_, r, w):
            if eng == "act":
                return S.op("act", lambda e: e.copy(out=out, in_=in_), r=r, w=w)
            return S.op(eng, lambda e: e.tensor_copy(out=out, in_=in_), r=r, w=w)

        def memset(eng, ap, val, w):
            return S.op(eng, lambda e: e.memset(ap, val), w=w)

        fsrc_d = nc.dram_tensor("fence_src", [1, 16], F32, kind="Internal").ap()
        fdst_d = nc.dram_tensor("fence_dst", [1, 16], F32, kind="Internal").ap()

        def fence():
            sv = S.scope
            S.scope = None
            S.op("sp", lambda e: e.dma_start(out=fdst_d, in_=fsrc_d), w=["ARENA"], dma=True)
            S.scope = sv

        def fence_all():
            S.op("dve", lambda e: e.memset(sc[:, 0, 3:4], 0.0), w=["ALL", "ARENA", ("sc3", 0)])

        class arena_scope:
            def __enter__(self):
                self.sv = S.scope
                S.scope = "ARENA"

            def __exit__(self, *a):
                S.scope = self.sv

        dma("sp", identf[:], identf_d, w=["identf"])
        dma("sp", jex[:], jex_d, w=["jex"])
        dma("sp", jsw[:], jsw_d, w=["jsw"])
        dma("sp", gmask[:], gmask_d, w=["gmask"])
        dma("sp", invw[:], invw_d, w=["invw"])
        dma("sp", rc16[:], rc16_d, w=["rc16"])
        dma("sp", sgn[:], sgn_d, w=["sgn"])
        cp("dve", ident[:], identf[:], r=["identf"], w=["ident"])
        cp("dve", jswb[:], jsw[:], r=["jsw"], w=["jswb"])
        memset("dve", onesf[:], 1.0, w=["onesf"])
        xv = x_d.rearrange("(t p) d -> p t d", p=128)
        for q in range(4):
            dma("sp", X[:, q * 4:(q + 1) * 4, :], xv[:, q * 4:(q + 1) * 4, :], w=[("X", t) for t in range(q * 4, q * 4 + 4)])

        def ada_phase():
            ar = Arena(0, ARB)
            AW = [ar.alloc([128, 8, 512], F32) for _ in range(2)]
            AB = [ar.alloc([1, 512], F32) for _ in range(2)]
            ROW = [ar.alloc([1, 512], F32) for _ in range(2)]
            cTs = ar.alloc([128, 8], F32)
            dma("sp", cTs, cT_d, w=["cTs"])
            act(condT[:], cTs, AF.Silu, r=["cTs"], w=["condT"])
            it = 0
            import itertools
            gen = itertools.chain(ssm_setup_gen(0), ssm_setup_gen(1))
            prep_q = list(range(NT))
            gen_done = [False]
            for l in range(DEPTH):
                for nb in range(18):
                    for _ in range(6):
                        if next(gen, "done") == "done":
                            gen_done[0] = True
                    b = it % 2
                    it += 1
                    src = adaw_d[l, :, nb * 512:(nb + 1) * 512].rearrange("(kc p) n -> p kc n", p=128)
                    dma("sp", AW[b], src, w=[("AW", b)])
                    dma("sp", AB[b], adab_d[l:l + 1, nb * 512:(nb + 1) * 512], w=[("AB", b)])
                    pr, prr = ps(b)
                    for kc in range(8):
                        mm(pr[0:1, :], condT[:, kc:kc + 1], AW[b][:, kc, :], kc == 0, False,
                           r=["condT", ("AW", b)], w=[prr])
                    mm(pr[0:1, :], onesf[0:1, 0:1], AB[b], False, True, r=["onesf", ("AB", b)], w=[prr])
                    cp("act", ROW[b], pr[0:1, :], r=[prr], w=[("ROW", b)])
                    pc, pcr = ps(2 + b)
                    for j in range(4):
                        mm(pc[:, j:j + 1], ROW[b][0:1, j * 128:(j + 1) * 128], onesf[0:1, 0:1], True, True,
                           r=[("ROW", b), "onesf"], w=[pcr])
                    v = nb // 2
                    addc = 1.0 if v % 3 == 1 else 0.0
                    act(modcol[:, l, nb * 4:(nb + 1) * 4], pc[:, 0:4], AF.Identity, r=[pcr], w=[("modcol", l, v)], bias=float(addc))
                    if gen_done[0] and prep_q:
                        tq_ = prep_q.pop(0)
                        sv_ = S.scope
                        S.scope = None
                        prep_tile_a(0, 0, tq_)
                        prep_tile_b(0, 0, tq_, extra=[("ssc", 0), ("ssc", 1)])
                        S.scope = sv_
            for _ in gen:
                pass
            sv_ = S.scope
            S.scope = None
            while prep_q:
                tq_ = prep_q.pop(0)
                prep_tile_a(0, 0, tq_)
                prep_tile_b(0, 0, tq_, extra=[("ssc", 0), ("ssc", 1)])
            S.scope = sv_


        NSLOT = 4
        sums = T("sums", [128, NSLOT, 4], F32)

        def finish_stats(slot, eps, c0, c1):
            ts("dve", mv[:, slot, 0:1], sums[:, slot, c0:c0 + 1], 1.0 / D, None, ALU.mult, None, r=[("sums", slot, c0)], w=[("mv", slot)])
            tt("dve", mv[:, slot, 1:2], mv[:, slot, 0:1], mv[:, slot, 0:1], ALU.mult, r=[("mv", slot)], w=[("mv1", slot)])
            stt(sc[:, slot, 3:4], sums[:, slot, c1:c1 + 1], 1.0 / D, mv[:, slot, 1:2], ALU.mult, ALU.subtract,
                r=[("sums", slot, c1), ("mv1", slot)], w=[("sc3", slot)])
            act(sc[:, slot, 0:1], sc[:, slot, 3:4], AF.Sqrt, r=[("sc3", slot)], w=[("sc0", slot)], bias=float(eps))
            S.op("dve", lambda e: e.reciprocal(out=sc[:, slot, 1:2], in_=sc[:, slot, 0:1]), r=[("sc0", slot)], w=[("sc1", slot)])

        def act_accum(tt_, slot, func, col):
            S.op("act", lambda e: e.activation(out=XH[slot][:], in_=X[:, tt_, :], func=func, accum_out=sums[:, slot, col:col + 1]),
                 r=[("X", tt_)], w=[("XH", slot), ("sums", slot, col)])

        def prep_tile_a(l, i, tt_, have_sum=False):
            slot = tt_ % NSLOT
            if not have_sum:
                act_accum(tt_, slot, AF.Identity, 2)
            act_accum(tt_, slot, AF.Square, 3)
            finish_stats(slot, LN_EPS, 2, 3)
            ts("dve", sc[:, slot, 2:3], mv[:, slot, 0:1], -1.0, sc[:, slot, 1:2], ALU.mult, ALU.mult,
               r=[("mv", slot), ("sc1", slot)], w=[("sc2", slot)])
            xh = XH[slot]
            act(xh[:], X[:, tt_, :], AF.Identity, r=[("X", tt_), ("sc1", slot), ("sc2", slot)], w=[("XH", slot)],
                bias=sc[:, slot, 2:3], scale=sc[:, slot, 1:2])

        def prep_tile_b(l, i, tt_, extra=()):
            slot = tt_ % NSLOT
            xh = XH[slot]
            pt, ptr = ps(6 + tt_ % 2)
            ptb = pt[:, :].bitcast(BF16)
            for kc in range(8):
                tr(ptb[:, kc * 128:(kc + 1) * 128], xh[:, kc * 128:(kc + 1) * 128], ident[:], r=[("XH", slot), "ident"], w=[ptr])
            for kc in range(8):
                scl = modcol[:, l, (3 * i + 1) * 8 + kc:(3 * i + 1) * 8 + kc + 1]
                shf = modcol[:, l, (3 * i) * 8 + kc:(3 * i) * 8 + kc + 1]
                if kc % 2 == 0:
                    ts("dve", HT[:, kc, tt_ * 128:(tt_ + 1) * 128], ptb[:, kc * 128:(kc + 1) * 128], scl, shf,
                       ALU.mult, ALU.add, r=[ptr, ("modcol", l, 3 * i), ("modcol", l, 3 * i + 1)] + list(extra), w=[("HT", tt_ // 4)])
                else:
                    act(HT[:, kc, tt_ * 128:(tt_ + 1) * 128], ptb[:, kc * 128:(kc + 1) * 128], AF.Identity,
                        r=[ptr, ("modcol", l, 3 * i), ("modcol", l, 3 * i + 1)] + list(extra), w=[("HT", tt_ // 4)], bias=shf, scale=scl)

        def prep(l, i):
            for tt_ in range(NT):
                prep_tile_a(l, i, tt_)
                prep_tile_b(l, i, tt_)

        LNGB = [T("LNG", [128, D], F32), T("LNB", [128, D], F32)]

        def post_setup(l, i):
            LNG, LNB = LNGB[0][:], LNGB[1][:]
            sv = S.scope
            S.scope = None
            dma("sp", LNG, bass.AP(lng_d.tensor, (l * 3 + i) * D, [[0, 128], [1, D]]), w=["LNG"])
            dma("sp", LNB, bass.AP(lnb_d.tensor, (l * 3 + i) * D, [[0, 128], [1, D]]), w=["LNB"])
            S.scope = sv
            return LNG, LNB

        def post_tile(l, i, tt_, LNG, LNB, want_sum):
            slot = tt_ % NSLOT
            act_accum(tt_, slot, AF.Identity, 0)
            act_accum(tt_, slot, AF.Square, 1)
            finish_stats(slot, LN_EPS / (ALPHA * ALPHA), 0, 1)
            stt(X[:, tt_, :], X[:, tt_, :], mv[:, slot, 0:1], LNG, ALU.subtract, ALU.mult,
                r=[("X", tt_), ("mv", slot), "LNG"], w=[("X", tt_)])
            if want_sum:
                S.op("dve", lambda e: e.scalar_tensor_tensor(out=X[:, tt_, :], in0=X[:, tt_, :], scalar=sc[:, slot, 1:2], in1=LNB,
                                                             op0=ALU.mult, op1=ALU.add, accum_out=sums[:, slot, 2:3]),
                     r=[("X", tt_), ("sc1", slot), "LNB"], w=[("X", tt_), ("sums", slot, 2)])
            else:
                stt(X[:, tt_, :], X[:, tt_, :], sc[:, slot, 1:2], LNB, ALU.mult, ALU.add,
                    r=[("X", tt_), ("sc1", slot), "LNB"], w=[("X", tt_)])

        ov = out_d.rearrange("(t p) d -> p t d", p=128)

        class Tail:
            def __init__(self, postli, prepli, final):
                self.postli, self.prepli, self.final = postli, prepli, final
                self.pend = []
                self.LNG, self.LNB = post_setup(*postli)

            def tile_done(self, tt_):
                sv = S.scope
                S.scope = None
                post_tile(self.postli[0], self.postli[1], tt_, self.LNG, self.LNB, self.prepli is not None)
                if self.prepli is not None:
                    prep_tile_a(self.prepli[0], self.prepli[1], tt_, have_sum=True)
                    self.pend.append(tt_)
                if self.final:
                    dma("sp", ov[:, tt_, :], X[:, tt_, :], r=[("X", tt_)], final=True)
                S.scope = sv

            def lagged(self, keep):
                sv = S.scope
                S.scope = None
                while len(self.pend) > keep:
                    prep_tile_b(self.prepli[0], self.prepli[1], self.pend.pop(0))
                S.scope = sv

        def gate_bc(l, i, scale, GBC, D8):
            for kc in range(8):
                col = (3 * i + 2) * 8 + kc
                ts("dve", D8[:, kc, :], identf[:], modcol[:, l, col:col + 1], float(scale), ALU.mult, ALU.mult,
                   r=["identf", ("modcol", l, 3 * i + 2)], w=["D8"])
            for hf in range(2):
                pg, pgr = ps(hf)
                mm(pg[:, :], onesf[:], D8[:, hf * 4:(hf + 1) * 4, :].rearrange("p a b -> p (a b)"), True, True, r=["onesf", "D8"], w=[pgr])
                cp("act", GBC[:, hf * 512:(hf + 1) * 512], pg[:, :], r=[pgr], w=["GBC"])

        def ffn(l, i, si, tail):
            ar = Arena(0, ARB)
            ay = ArenaYT(0, 32768)
            WG = [ay.alloc([128, 8, 512], BF16) for _ in range(2)]
            WU = [ay.alloc([128, 8, 512], BF16) for _ in range(2)]
            WD = [ar.alloc([128, 4, D], BF16) for _ in range(2)]
            ACTB = [ar.alloc([128, 4, 512], BF16) for _ in range(2)]
            SG = [ar.alloc([128, 512], F32) for _ in range(2)]
            GBC = ar.alloc([128, D], F32)
            D8 = ar.alloc([128, 8, 128], F32)
            gate_bc(l, si, 0.5 / ALPHA, GBC, D8)
            groups = [(g * 4, 4) for g in range(5)] + [(20, 2)]
            pend = None
            nsg = 0
            for gi, (c0, nf) in enumerate(groups):
                b = gi % 2
                f0 = c0 * 128
                fw = nf * 128
                dma("pool", WG[b][:, :, 0:fw], wg_d[l, i, :, f0:f0 + fw].rearrange("(kc p) f -> p kc f", p=128), w=[("WG", b)])
                dma("pool", WU[b][:, :, 0:fw], wu_d[l, i, :, f0:f0 + fw].rearrange("(kc p) f -> p kc f", p=128), w=[("WU", b)])
                dma("pool", WD[b][:, 0:nf, :], wd_d[l, i, f0:f0 + fw, :].rearrange("(c p) d -> p c d", p=128), w=[("WD", b)])
                for c in range(nf):
                    tt("pool", WD[b][:, c, :], WD[b][:, c, :], GBC, ALU.mult, r=["GBC", ("WD", b)], w=[("WD", b)])
                for tsi in range(4):
                    ab = (gi * 4 + tsi) % 2
                    for c in range(nf):
                        pgk = (gi * 16 + tsi * 4 + c) % 2
                        pg, pgr = ps(pgk)
                        pu, pur = ps(2 + pgk)
                        for kc in range(8):
                            mm(pg[:, :], WG[b][:, kc, c * 128:(c + 1) * 128], HT[:, kc, tsi * 512:(tsi + 1) * 512], kc == 0, kc == 7,
                               r=[("WG", b), ("HT", tsi)], w=[pgr])
                        for kc in range(8):
                            mm(pu[:, :], WU[b][:, kc, c * 128:(c + 1) * 128], HT[:, kc, tsi * 512:(tsi + 1) * 512], kc == 0, kc == 7,
                               r=[("WU", b), ("HT", tsi)], w=[pur])
                        sgb = nsg % 2
                        nsg += 1
                        act(SG[sgb], pg[:, :], AF.Silu, r=[pgr], w=[("SG", sgb)])
                        tt("dve", ACTB[ab][:, c, :], SG[sgb], pu[:, :], ALU.mult, r=[("SG", sgb), pur], w=[("ACTB", ab, c)])
                    cur = (b, ab, nf, tsi, gi == len(groups) - 1)
                    if pend is not None:
                        down(pend, WD, ACTB, tail)
                    pend = cur
            down(pend, WD, ACTB, tail)
            tail.lagged(0)

        dcount = [0]

        def down(p, WD, ACTB, tail):
            b, ab, nf, tsi, last = p
            for t4 in range(4):
                tt_ = tsi * 4 + t4
                for hf in range(2):
                    k = 4 + dcount[0] % 2
                    dcount[0] += 1
                    pd, pdr = ps(k)
                    for c in range(nf):
                        mm(pd[:, :], ACTB[ab][:, c, t4 * 128:(t4 + 1) * 128], WD[b][:, c, hf * 512:(hf + 1) * 512], c == 0, c == nf - 1,
                           r=[("ACTB", ab, c), ("WD", b)], w=[pdr])
                    tt("dve", X[:, tt_, hf * 512:(hf + 1) * 512], X[:, tt_, hf * 512:(hf + 1) * 512], pd[:, :], ALU.add,
                       r=[("X", tt_), pdr], w=[("X", tt_)])
                if last:
                    tail.tile_done(tt_)
                    tail.lagged(2)

        stage = [0]

        def done_stage():
            stage[0] += 1
            return stop is not None and stage[0] >= stop

        def run_layers():
            for l in range(DEPTH):
                fence()
                with arena_scope():
                    ffn(l, 0, 0, Tail((l, 0), (l, 1), False))
                if done_stage():
                    return
                fence()
                with arena_scope():
                    mixer(l, Tail((l, 1), (l, 2), False) if not dbg else None)
                if dbg:
                    return
                if done_stage():
                    return
                fence()
                lastl = l == DEPTH - 1
                with arena_scope():
                    ffn(l, 1, 2, Tail((l, 2), None if lastl else (l + 1, 0), lastl))
                if done_stage():
                    return

        TWO_PI = 2.0 * math.pi

        def act_b(out, in_, func, r, w, bias, scale):
            return act(out, in_, func, r, w, bias=bias, scale=scale)

        def pool_phase(l):
            ar = Arena(0, ARB)
            WUP = ar.alloc([128, 8, 256], BF16)
            PW = ar.alloc([128, 2, 128], BF16)
            pscol = ar.alloc([128, 2], F32)
            UP = [ar.alloc([128, 2, 528], F32) for _ in range(2)]
            Pb = ar.alloc([128, 2, 528], F32)
            Qb = ar.alloc([128, 2, 528], F32)
            PL = ar.alloc([128, 2, 512], BF16)
            TM = ar.alloc([128, 2, 16], F32)
            dma("pool", WUP, win_d[l, :, 1792:2048].rearrange("(kc p) f -> p kc f", p=128), w=["WUP"])
            dma("pool", PW, poolw_d[l], w=["PW"])
            dma("sp", pscol, pools_d[l], w=["pscol"])
            memset("pool", UP[0][:, :, 0:16], 0.0, w=[("UP", 0)])
            for tsi in range(4):
                ub = UP[tsi % 2]
                ur = ("UP", tsi % 2)
                for ch in range(2):
                    pu, pur = ps(ch)
                    for kc in range(8):
                        mm(pu[:, :], WUP[:, kc, ch * 128:(ch + 1) * 128], HT[:, kc, tsi * 512:(tsi + 1) * 512], kc == 0, kc == 7,
                           r=["WUP", ("HT", tsi)], w=[pur])
                    cp("act", ub[:, ch, 16:528], pu[:, :], r=[pur], w=[ur])
                tt("pool", Pb[:, :, 1:528], ub[:, :, 1:528], ub[:, :, 0:527], ALU.add, r=[ur], w=["Pb"])
                tt("pool", Qb[64:128, 0, 3:528], Pb[64:128, 0, 3:528], Pb[64:128, 0, 1:526], ALU.add, r=["Pb"], w=["Qb"])
                tt("pool", Qb[:, 1, 3:528], Pb[:, 1, 3:528], Pb[:, 1, 1:526], ALU.add, r=["Pb"], w=["Qb"])
                tt("pool", Pb[:, 1, 7:528], Qb[:, 1, 7:528], Qb[:, 1, 3:524], ALU.add, r=["Qb"], w=["Pb"])
                tt("pool", Qb[64:128, 1, 15:528], Pb[64:128, 1, 15:528], Pb[64:128, 1, 7:520], ALU.add, r=["Pb"], w=["Qb"])
                srcs = [(Pb, 0, 64, 0), (Qb, 64, 128, 0), (Pb, 0, 64, 1), (Qb, 64, 128, 1)]
                for (sb, p0, p1, ch) in srcs:
                    stt(PL[p0:p1, ch, :], sb[p0:p1, ch, 16:528], invw[p0:p1, ch:ch + 1], ub[p0:p1, ch, 16:528], ALU.mult, ALU.subtract,
                        r=["Pb", "Qb", ur, "invw"], w=["PL"])
                    if tsi == 0:
                        tt("dve", TM[p0:p1, ch, :], sb[p0:p1, ch, 16:32], rc16[p0:p1, ch, :], ALU.mult, r=["Pb", "Qb", "rc16"], w=["TM"])
                        tt("dve", PL[p0:p1, ch, 0:16], TM[p0:p1, ch, :], ub[p0:p1, ch, 16:32], ALU.subtract, r=["TM", ur], w=["PL"])
                if tsi < 3:
                    cp("pool", UP[(tsi + 1) % 2][:, :, 0:16], ub[:, :, 512:528], r=[ur], w=[("UP", (tsi + 1) % 2)])
                for ch in range(2):
                    py, pyr = ps(2 + ch)
                    mm(py[:, :], PW[:, ch, :], PL[:, ch, :], True, True, r=["PW", "PL"], w=[pyr])
                    act_b(YT[:, 6 + ch, tsi * 512:(tsi + 1) * 512], py[:, :], AF.Identity, r=[pyr, "pscol"], w=[("YT", 6 + ch, tsi)],
                          bias=0.0, scale=pscol[:, ch:ch + 1])

        TL = 128
        NTB = 16 * (TL + 1)
        ssc_f = nc.dram_tensor("ssm_scr_f", [DEPTH, 128, 2 * NTB + 224], F32, kind="Internal").ap()
        ssc_b = nc.dram_tensor("ssm_scr_b", [DEPTH, 128, NTB + 3 * 2048], BF16, kind="Internal").ap()

        class ArenaOn(Arena):
            def __init__(self, flat, size):
                self.flat = flat
                self.off = 0
                self.end = size

            def alloc(self, shape, dt):
                n = int(np.prod(shape[1:]))
                nb = n * (4 if dt in (F32, I32) else 2)
                nb = (nb + 31) // 32 * 32
                assert self.off + nb <= self.end, ("arenaOn overflow", shape, self.off, nb, self.end)
                v = self.flat[0:shape[0], self.off // 2:(self.off + nb) // 2]
                self.off += nb
                if dt != BF16:
                    v = v.bitcast(dt)
                v = v[:, 0:n]
                if len(shape) == 3:
                    v = v.rearrange("p (a b) -> p a b", a=shape[1])
                return v

        def ssm_setup_gen(l):
            ah = ArenaOn(HT[:, :, :].rearrange("p a b -> p (a b)"), 32768)
            ay = ArenaOn(YT[:, :, :].rearrange("p a b -> p (a b)"), 32768)
            ar = Arena(40992, ARB - 40992)
            TC = ah.alloc([128, 16, TL + 1], F32)
            TS = ah.alloc([128, 16, TL + 1], F32)
            ANG = ah.alloc([128, 16, TL + 1], F32)
            BL = ay.alloc([128, 16, 128], BF16)
            IBL = ay.alloc([128, 16, 128], BF16)
            CL = ay.alloc([128, 16, 128], BF16)
            SV = ay.alloc([128, 14, 16], F32)
            P1 = ay.alloc([128, 16, 16], F32)
            P2 = ay.alloc([128, 16, 16], F32)
            CT = ay.alloc([128, 256], F32)
            TCb = ay.alloc([128, 16, TL + 1], BF16)
            AI = ay.alloc([128, 16, TL + 1], I32)
            JF = ar.alloc([128, TL + 1], F32)
            JI = ar.alloc([128, TL + 1], I32)
            Bc = ar.alloc([128, 16, 16], F32)
            IBc = ar.alloc([128, 16, 16], F32)
            Tm = ar.alloc([128, 16, 16], F32)
            are, aim, ldt, dtv, lr, thn, rho, sn, cs, er, ei, gr, gi, tq = [SV[:, k, :] for k in range(14)]
            V = "ssmv"
            dma("pool", are, sare_d[l], w=["are", V])
            dma("pool", aim, saim_d[l], w=["aim", V])
            dma("pool", ldt, sldt_d[l], w=["ldt", V])
            dma("pool", P1, sp1_d[l].rearrange("p (g c) -> p g c", g=16), w=["P1"])
            dma("pool", P2, sp2_d[l].rearrange("p (g c) -> p g c", g=16), w=["P2"])
            dma("pool", CT, sct_d[l], w=["CT"])
            yield
            act(dtv, ldt, AF.Exp, r=["ldt"], w=[V])
            tt("dve", lr, are, dtv, ALU.mult, r=["are", V], w=[V])
            tt("dve", thn, aim, dtv, ALU.mult, r=["aim", V], w=[V])
            ts("dve", thn, thn, 1.0 / TWO_PI, None, ALU.mult, None, r=[V], w=[V])
            yield
            ts("dve", rho, lr, 1.0 / 720.0, 1.0 / 120.0, ALU.mult, ALU.add, r=[V], w=[V])
            for cst in (1.0 / 24.0, 1.0 / 6.0, 0.5, 1.0, 1.0):
                tt("dve", rho, rho, lr, ALU.mult, r=[V], w=[V])
                ts("dve", rho, rho, float(cst), None, ALU.add, None, r=[V], w=[V])
                yield

            def sincos(dst, src, shift, ai):
                ts("dve", dst, src, float(shift), None, ALU.add, None, r=[V], w=[V])
                cp("dve", ai, dst, r=[V], w=[V])
                tt("dve", dst, dst, ai, ALU.subtract, r=[V], w=[V])
                act(dst, dst, AF.Sin, r=[V], w=[V], scale=TWO_PI)

            sincos(sn, thn, 0.0, AI[:, 0, 0:16])
            yield
            sincos(cs, thn, 0.25, AI[:, 0, 0:16])
            yield
            tt("dve", er, rho, cs, ALU.mult, r=[V], w=[V])
            ts("dve", er, er, -1.0, None, ALU.add, None, r=[V], w=[V])
            tt("dve", ei, rho, sn, ALU.mult, r=[V], w=[V])
            tt("dve", tq, are, are, ALU.mult, r=[V, "are"], w=[V])
            yield
            tt("dve", gr, aim, aim, ALU.mult, r=[V, "aim"], w=[V])
            tt("dve", tq, tq, gr, ALU.add, r=[V], w=[V])
            S.op("dve", lambda e: e.reciprocal(out=tq, in_=tq), r=[V], w=[V])
            tt("dve", gr, er, are, ALU.mult, r=[V], w=[V])
            yield
            tt("dve", gi, ei, aim, ALU.mult, r=[V], w=[V])
            tt("dve", gr, gr, gi, ALU.add, r=[V], w=[V])
            tt("dve", gr, gr, tq, ALU.mult, r=[V], w=[V])
            tt("dve", gi, ei, are, ALU.mult, r=[V], w=[V])
            yield
            tt("dve", er, er, aim, ALU.mult, r=[V], w=[V])
            tt("dve", gi, gi, er, ALU.subtract, r=[V], w=[V])
            tt("dve", gi, gi, tq, ALU.mult, r=[V], w=[V])
            S2, S3, S4 = ei, er, tq
            ts("dve", S2, gi, sgn[:, 0:1], None, ALU.mult, None, r=[V, "sgn"], w=[V])
            yield
            ts("dve", S3, gr, sgn[:, 0:1], None, ALU.mult, None, r=[V, "sgn"], w=[V])
            ts("dve", S4, gi, -1.0, None, ALU.mult, None, r=[V], w=[V])
            bc = lambda v: v.unsqueeze(2).to_broadcast([128, 16, 16])
            tt("dve", Bc, P1, bc(gr), ALU.mult, r=[V, "P1"], w=["Bc"])
            tt("dve", Tm, P2, bc(S2), ALU.mult, r=[V, "P2"], w=["Tm"])
            yield
            tt("dve", Bc, Bc, Tm, ALU.add, r=["Tm"], w=["Bc"])
            tt("dve", IBc, P2, bc(S3), ALU.mult, r=[V, "P2"], w=["IBc"])
            tt("dve", Tm, P1, bc(S4), ALU.mult, r=[V, "P1", "Bc"], w=["Tm"])
            tt("dve", IBc, IBc, Tm, ALU.add, r=["Tm"], w=["IBc"])
            yield
            for (src, dst, nm) in ((Bc, BL, "Bc"), (IBc, IBL, "IBc")):
                flat = src.rearrange("p g c -> p (g c)")
                for ch in range(2):
                    pt_, ptr_ = ps(7)
                    tr(pt_[:, 0:128], flat[:, ch * 128:(ch + 1) * 128], identf[:], r=[nm, "identf"], w=[ptr_])
                    for g8 in range(8):
                        ts("dve", dst[:, ch * 8 + g8, :], pt_[:, 0:128], gmask[:, g8:g8 + 1], None, ALU.mult, None,
                           r=[ptr_, "gmask"], w=["BL"])
                        if g8 % 4 == 3:
                            yield
            ts("dve", CT, CT, sgn[:, 0:1], -1.0, ALU.mult, ALU.mult, r=["CT", "sgn"], w=["CT"])
            memset("pool", CL, 0.0, w=["CL"])
            for g in range(16):
                g8 = g % 8
                cp("dve", CL[:, g, 16 * g8:16 * g8 + 16], CT[:, g * 16:(g + 1) * 16], r=["CT"], w=["CL"])
                if g % 4 == 3:
                    yield
            S.op("pool", lambda e: e.iota(out=JI, pattern=[[1, TL + 1]], base=0, channel_multiplier=0), w=["JI"])
            cp("dve", JF, JI, r=["JI"], w=["JF"])
            tt("dve", ANG, JF.unsqueeze(1).to_broadcast([128, 16, TL + 1]), thn.unsqueeze(2).to_broadcast([128, 16, TL + 1]),
               ALU.mult, r=["JF", V], w=[V])
            yield
            sincos(TS, ANG, 0.0, AI)
            yield
            sincos(TC, ANG, 0.25, AI)
            yield
            cp("dve", TCb, TC, r=[V], w=[V])
            dma("pool", ssc_f[l, :, 0:NTB], TC.rearrange("p a b -> p (a b)"), r=[V], w=[("ssc", l)])
            dma("pool", ssc_f[l, :, NTB:2 * NTB], TS.rearrange("p a b -> p (a b)"), r=[V], w=[("ssc", l)])
            dma("pool", ssc_f[l, :, 2 * NTB:2 * NTB + 224], SV.rearrange("p a b -> p (a b)"), r=[V], w=[("ssc", l)])
            dma("pool", ssc_b[l, :, 0:NTB], TCb.rearrange("p a b -> p (a b)"), r=[V], w=[("ssc", l)])
            dma("pool", ssc_b[l, :, NTB:NTB + 2048], BL.rearrange("p a b -> p (a b)"), r=["BL"], w=[("ssc", l)])
            dma("pool", ssc_b[l, :, NTB + 2048:NTB + 4096], IBL.rearrange("p a b -> p (a b)"), r=["BL"], w=[("ssc", l)])
            dma("pool", ssc_b[l, :, NTB + 4096:NTB + 6144], CL.rearrange("p a b -> p (a b)"), r=["CL"], w=[("ssc", l)])
            yield

        def ssm_phase(l):
            ay = ArenaOn(YT[:, 0:4, :].rearrange("p a b -> p (a b)"), 16384)
            ar = Arena(0, ARB)
            BL = ay.alloc([128, 16, 128], BF16)
            IBL = ay.alloc([128, 16, 128], BF16)
            CL = ay.alloc([128, 16, 128], BF16)
            SV = ay.alloc([128, 14, 16], F32)
            WUS = ar.alloc([128, 8, 256], BF16)
            TC = ar.alloc([128, 16, TL + 1], F32)
            TS = ar.alloc([128, 16, TL + 1], F32)
            GW = ar.alloc([128, 2, 256], BF16)
            sdcol = ar.alloc([128, 2], F32)
            glub = ar.alloc([128, 2], F32)
            TCb = ar.alloc([128, 16, TL + 1], BF16)
            rho = SV[:, 6, :]
            V = "ssmv2"
            dma("sp", TC.rearrange("p a b -> p (a b)"), ssc_f[l, :, 0:NTB], r=[("ssc", l)], w=[V])
            dma("sp", TS.rearrange("p a b -> p (a b)"), ssc_f[l, :, NTB:2 * NTB], r=[("ssc", l)], w=[V])
            dma("sp", SV.rearrange("p a b -> p (a b)"), ssc_f[l, :, 2 * NTB:2 * NTB + 224], r=[("ssc", l)], w=[V])
            dma("sp", TCb.rearrange("p a b -> p (a b)"), ssc_b[l, :, 0:NTB], r=[("ssc", l)], w=[V])
            dma("sp", BL.rearrange("p a b -> p (a b)"), ssc_b[l, :, NTB:NTB + 2048], r=[("ssc", l)], w=["BL"])
            dma("sp", IBL.rearrange("p a b -> p (a b)"), ssc_b[l, :, NTB + 2048:NTB + 4096], r=[("ssc", l)], w=["BL"])
            dma("sp", CL.rearrange("p a b -> p (a b)"), ssc_b[l, :, NTB + 4096:NTB + 6144], r=[("ssc", l)], w=["CL"])
            dma("sp", sdcol, sd_d[l], w=["sdcol"])
            dma("sp", glub, glub_d[l], w=["glub"])
            dma("pool", GW, gluw_d[l], w=["GW"])
            dma("pool", WUS, win_d[l, :, 1536:1792].rearrange("(kc p) f -> p kc f", p=128), w=["WUS"])
            USS2 = [ar.alloc([128, 2, 512], BF16) for _ in range(2)]
            Wb = [ar.alloc([128, 4, TL], BF16) for _ in range(2)]
            T2 = ar.alloc([128, 4, TL], BF16)
            STb = [ar.alloc([128, 4, TL], F32) for _ in range(2)]
            SBh = [ar.alloc([128, 4, TL], BF16) for _ in range(2)]
            T2b = ar.alloc([128, 4, TL], BF16)
            S1 = ar.alloc([128, 4, TL], BF16)
            SB = [ar.alloc([128, 4, TL], BF16) for _ in range(2)]
            GE = ar.alloc([128, 2, TL], BF16)
            CI = ar.alloc([128, 16], F32)
            CA = ar.alloc([128, 4], F32)
            CB = ar.alloc([128, 4], F32)
            YV = ay.alloc([128, 2, TL], F32)
            G1 = ay.alloc([128, 2, TL], F32)
            G2 = ay.alloc([128, 2, TL], F32)
            NB = 64
            p0_, p0r = ps(0)
            pL, pLr = ps(6)
            pAs = [ps(1), ps(2)]
            pBs = [ps(3), ps(7)]
            pC, pCr = ps(4)
            pC3 = pC[:, :].rearrange("p (a b) -> p a b", a=4)

            def stage_F_pe(n):
                k, bq = n // 4, n % 4
                tsi, kk, ch = k // 4, k % 4, bq // 2
                USS = USS2[tsi % 2]
                ur = ("USS", tsi % 2)
                if kk == 0 and bq == 0:
                    for c2 in range(2):
                        for kc in range(8):
                            mm(p0_[:, :], WUS[:, kc, c2 * 128:(c2 + 1) * 128], HT[:, kc, tsi * 512:(tsi + 1) * 512], kc == 0, kc == 7,
                               r=["WUS", ("HT", tsi)], w=[p0r])
                        cp("act", USS[:, c2, :], p0_[:, :], r=[p0r], w=[ur])
                pA, pAr = pAs[n % 2]
                pB, pBr = pBs[n % 2]
                for gi_ in range(4):
                    mm(pA[:, gi_ * TL:(gi_ + 1) * TL], BL[:, 4 * bq + gi_, :], USS[:, ch, kk * TL:(kk + 1) * TL], True, True, r=["BL", ur], w=[pAr])
                for gi_ in range(4):
                    mm(pB[:, gi_ * TL:(gi_ + 1) * TL], IBL[:, 4 * bq + gi_, :], USS[:, ch, kk * TL:(kk + 1) * TL], True, True, r=["BL", ur], w=[pBr])

            def stage_F_dve(n):
                k, bq = n // 4, n % 4
                gs = slice(4 * bq, 4 * bq + 4)
                wb = Wb[n % 2]
                wbr = ("Wb", n % 2)
                pA, pAr = pAs[n % 2]
                pB, pBr = pBs[n % 2]
                pA3 = pA[:, :].rearrange("p (a b) -> p a b", a=4)
                pB3 = pB[:, :].rearrange("p (a b) -> p a b", a=4)
                tt("dve", wb, pA3, TC[:, gs, 0:TL], ALU.mult, r=[pAr, V], w=[wbr])
                tt("dve", T2, pB3, TS[:, gs, 0:TL], ALU.mult, r=[pBr, V], w=["T2"])
                tt("dve", wb, wb, T2, ALU.subtract, r=["T2"], w=[wbr])

            def stage_S(n):
                k, bq = n // 4, n % 4
                wb = Wb[n % 2]
                wbr = ("Wb", n % 2)
                stb = STb[n % 2]
                stbr = ("STb", n % 2)
                sbh = SBh[n % 2]
                sbhr = ("SBh", n % 2)
                for gi_ in range(4):
                    g = 4 * bq + gi_
                    ini = CI[:, g:g + 1] if k > 0 else 0.0
                    S.op("dve", lambda e, gi_=gi_, g=g, ini=ini: e.tensor_tensor_scan(
                        out=stb[:, gi_, :], data0=rho[:, g:g + 1].to_broadcast([128, TL]), data1=wb[:, gi_, :],
                        initial=ini, op0=ALU.mult, op1=ALU.add), r=[wbr, V, ("CI", bq)], w=[stbr])
                cp("act", sbh, stb, r=[stbr], w=[sbhr])
                mm(pC[:, :], jswb[:], sbh.rearrange("p a b -> p (a b)"), True, True, r=["jswb", sbhr], w=[pCr])
                if k < 15:
                    mm(pL[:, 0:4], jsw[:], stb[:, :, TL - 1], True, True, r=["jsw", stbr], w=[pLr])

            def stage_B(n):
                k, bq = n // 4, n % 4
                kk, ch = k % 4, bq // 2
                tsi = k // 4
                gs = slice(4 * bq, 4 * bq + 4)
                stb = STb[n % 2]
                stbr = ("STb", n % 2)
                sbh = SBh[n % 2]
                sbhr = ("SBh", n % 2)
                sb = SB[n % 2]
                sbr = ("SB", n % 2)
                USS = USS2[tsi % 2]
                ur = ("USS", tsi % 2)
                tt("dve", T2b, pC3, TS[:, gs, 0:TL], ALU.mult, r=[pCr, V], w=["T2b"])
                tt("dve", S1, sbh, TCb[:, gs, 0:TL], ALU.mult, r=[sbhr, V], w=["S1"])
                tt("dve", sb, S1, T2b, ALU.add, r=["S1", "T2b"], w=[sbr])
                if k < 15:
                    tt("dve", CA, pL[:, 0:4], TS[:, gs, TL], ALU.mult, r=[pLr, V], w=["CA"])
                    tt("dve", CB, stb[:, :, TL - 1], TC[:, gs, TL], ALU.mult, r=[stbr, V], w=["CB"])
                    tt("dve", CI[:, gs], CA, CB, ALU.add, r=["CA", "CB"], w=[("CI", bq)])
                pY, pYr = ps(5)
                for gi_ in range(4):
                    g = 4 * bq + gi_
                    mm(pY[:, ch * TL:(ch + 1) * TL], CL[:, g, :], sb[:, gi_, :], g % 8 == 0, g % 8 == 7, r=["CL", sbr], w=[pYr])
                if bq == 3:
                    glu_a(k)
                if bq == 0 and k > 0:
                    glu_b(k - 1)

            def glu_a(k):
                kk, tsi = k % 4, k // 4
                USS = USS2[tsi % 2]
                ur = ("USS", tsi % 2)
                cols = slice(kk * TL, (kk + 1) * TL)
                for c2 in range(2):
                    pY2, pY2r = ps(5)
                    stt(YV[:, c2, :], USS[:, c2, cols], sdcol[:, c2:c2 + 1], pY2[:, c2 * TL:(c2 + 1) * TL], ALU.mult, ALU.add, r=[ur, "sdcol", pY2r], w=["YV"])
                act(G1, YV, AF.Square, r=["YV"], w=["G1"])
                ts("pool", G1, G1, 0.044715, 1.0, ALU.mult, ALU.add, r=["G1"], w=["G1"])
                tt("pool", G1, G1, YV, ALU.mult, r=["G1", "YV"], w=["G1"])
                act(G2, G1, AF.Sigmoid, r=["G1"], w=["G2"], scale=2.0 * math.sqrt(2.0 / math.pi))
                tt("pool", GE, YV, G2, ALU.mult, r=["G2", "YV"], w=["GE"])

            def glu_b(k):
                tsi = k // 4
                tok = slice(k * TL, (k + 1) * TL)
                for dch in range(2):
                    pG, pGr = ps(6)
                    for c2 in range(2):
                        mm(pG[:, 128:128 + TL], GW[:, c2, dch * 128:(dch + 1) * 128], GE[:, c2, :], c2 == 0, c2 == 1, r=["GW", "GE"], w=[pGr])
                    act_b(G1[:, dch, :], pG[:, 128:128 + TL], AF.Sigmoid, r=[pGr, "glub", "G1"], w=["G1"], bias=glub[:, dch:dch + 1], scale=1.0)
                    tt("pool", YT[:, 4 + dch, tok], YV[:, dch, :], G1[:, dch, :], ALU.mult, r=["G1", "YV"], w=[("YT", 4 + dch, tsi)])

            stage_F_pe(0)
            for it in range(NB + 2):
                if it + 1 < NB:
                    stage_F_pe(it + 1)
                if it < NB:
                    stage_F_dve(it)
                if 0 <= it - 2 < NB:
                    stage_B(it - 2)
                if 0 <= it - 1 < NB:
                    stage_S(it - 1)
            glu_b(15)

        def bias_setup():
            ar = Arena(0, ARB)
            relb = ar.alloc([32, 8], F32)
            OH = ar.alloc([32, 1152], F32)
            NG = ar.alloc([8, 1152], F32)
            FV = ar.alloc([8, 1152], F32)
            dma("sp", relb, relb_d, w=["relb"])
            dma("sp", OH, oh_d, w=["OH"])
            dma("sp", NG, negm_d, w=["NG"])
            for br in range(3):
                p_, pr_ = ps(br)
                mm(p_[0:8, 0:384], relb, OH[:, br * 384:(br + 1) * 384], True, True, r=["relb", "OH"], w=[pr_])
                tt("dve", FV[:, br * 384:(br + 1) * 384], p_[0:8, 0:384], NG[:, br * 384:(br + 1) * 384], ALU.add, r=[pr_, "NG"], w=["FV"])
            dma("sp", fv_d, FV, r=["FV"], w=["fv_d"])

        def attn_phase(l):
            ar = Arena(0, ARB)
            WQKV = ar.alloc([128, 8, 384], BF16)
            QZ = [ar.alloc([128, SEQ], BF16) for _ in range(2)]
            KT = ar.alloc([128, SEQ], BF16)
            VP = ar.alloc([128, 3, 16, 192], BF16)
            BTp = ar.alloc([128, 2, 768], BF16)
            Hb = ar.alloc([128, 256], F32)
            PT = [ar.alloc([128, 128], BF16) for _ in range(4)]
            RD = [ar.alloc([128, 512], F32) for _ in range(1)]
            VT = ar.alloc([128, SEQ], BF16)
            memset("pool", QZ[0][64:128, :], 0.0, w=[("QZ", 0)])
            memset("pool", QZ[1][0:64, :], 0.0, w=[("QZ", 1)])
            memset("pool", VP[:, :, :, 64:128], 1.0, w=["VP"])
            npt = 0
            nsc = 0
            nrd = 0
            for hp in range(4):
                for j, base in enumerate((0, 512, 1024)):
                    dma("pool", WQKV[:, :, j * 128:(j + 1) * 128],
                        win_d[l, :, base + hp * 128:base + (hp + 1) * 128].rearrange("(kc p) f -> p kc f", p=128), w=["WQKV"])
                for tsi in range(4):
                    pq, pqr = ps(6)
                    for kc in range(8):
                        mm(pq[:, :], WQKV[:, kc, 0:128], HT[:, kc, tsi * 512:(tsi + 1) * 512], kc == 0, kc == 7, r=["WQKV", ("HT", tsi)], w=[pqr])
                    act(QZ[0][0:64, tsi * 512:(tsi + 1) * 512], pq[0:64, :], AF.Copy, r=[pqr], w=[("QZ", 0)], scale=0.125)
                    act(QZ[1][64:128, tsi * 512:(tsi + 1) * 512], pq[64:128, :], AF.Copy, r=[pqr], w=[("QZ", 1)], scale=0.125)
                    pk, pkr = ps(7)
                    for kc in range(8):
                        mm(pk[:, :], WQKV[:, kc, 128:256], HT[:, kc, tsi * 512:(tsi + 1) * 512], kc == 0, kc == 7, r=["WQKV", ("HT", tsi)], w=[pkr])
                    cp("dve", KT[:, tsi * 512:(tsi + 1) * 512], pk[:, :], r=[pkr], w=["KT"])
                for tsi in range(4):
                    pvt, pvtr = ps(6 + tsi % 2)
                    for kc in range(8):
                        mm(pvt[:, :], WQKV[:, kc, 256:384], HT[:, kc, tsi * 512:(tsi + 1) * 512], kc == 0, kc == 7, r=["WQKV", ("HT", tsi)], w=[pvtr])
                    cp("act", VT[:, tsi * 512:(tsi + 1) * 512], pvt[:, :], r=[pvtr], w=["VT"])
                nv = 0
                for br, (win, dil) in enumerate(PATTERNS):
                    nbk = 16 // dil
                    for q4 in range(4):
                        pv, pvr = ps(6 + nv % 2)
                        nv += 1
                        pvb = pv[:, :].bitcast(BF16)
                        for t4 in range(4):
                            tix = q4 * 4 + t4
                            rr, m = tix // nbk, tix % nbk
                            t0 = rr + dil * 128 * m
                            tr(pvb[:, t4 * 128:(t4 + 1) * 128], VT[:, t0:t0 + dil * 127 + 1:dil], ident[:], r=["VT", "ident"], w=[pvr])
                        outv = VP[:, br, q4 * 4:(q4 + 1) * 4, :].rearrange("p t (a c) -> p t a c", a=3)[:, :, 0:3:2, :]
                        inv_ = pvb[:, 0:512].rearrange("p (t a c) -> p t a c", t=4, a=2)
                        cp("dve", outv, inv_, r=[pvr], w=["VP"])
                for hh in range(2):
                    h = 2 * hp + hh
                    for br in range(3):
                        dma("sp", Hb, bass.AP(fv_d.tensor, h * 1152 + br * 384, [[1, 128], [1, 256]]), r=["fv_d"], w=["Hb"])
                        pb_, pbr_ = ps(6 + br % 2)
                        mm(pb_[:, 0:256], jex[:], Hb, True, True, r=["jex", "Hb"], w=[pbr_])
                        act(BTp[:, hh, br * 256:(br + 1) * 256], pb_[:, 0:256], AF.Exp, r=[pbr_], w=["BTp"])
                for hh in range(2):
                    accs = [ps(b) for b in range(4)]
                    started = [False] * 4
                    tasks = []
                    for br, (win, dil) in enumerate(PATTERNS):
                        nbk = 16 // dil
                        for rr in range(dil):
                            for m in range(nbk):
                                for qb in (m, m + 1):
                                    if qb >= nbk:
                                        continue
                                    tasks.append((br, dil, nbk, rr, m, qb))
                    pendq = []

                    def pieces_of(task):
                        br, dil, nbk, rr, m, qb = task
                        if dil == 16:
                            return [(b, rr, 32 * b, 32) for b in range(4)]
                        elif dil == 4:
                            return [(qb, rr, 0, 128)]
                        t0 = 128 * qb
                        return [(t0 // 512, t0 % 512, 0, 128)]

                    remaining = [0] * 4
                    for task in tasks:
                        for (b, c0, i0, n) in pieces_of(task):
                            remaining[b] += 1

                    def do_pv(task, slot):
                        br, dil, nbk, rr, m, qb = task
                        tix = rr * nbk + m
                        lhs = VP[:, br, tix, hh * 64:hh * 64 + 128]
                        for (b, c0, i0, n) in pieces_of(task):
                            acc, accr = accs[b]
                            remaining[b] -= 1
                            mm(acc[:, c0:c0 + dil * (n - 1) + 1:dil], lhs, PT[slot][:, i0:i0 + n], not started[b], remaining[b] == 0,
                               r=["VP", ("PT", slot)], w=[accr])
                            started[b] = True

                    for task in tasks:
                        br, dil, nbk, rr, m, qb = task
                        k0 = rr + dil * 128 * m
                        q0 = rr + dil * 128 * qb
                        psc, pscr = ps(4 + nsc % 2)
                        nsc += 1
                        mm(psc[:, 0:128], KT[:, k0:k0 + dil * 127 + 1:dil], QZ[hh][:, q0:q0 + dil * 127 + 1:dil], True, True,
                           r=["KT", ("QZ", hh)], w=[pscr])
                        off = (qb - m) * 128
                        slot = npt % 4
                        npt += 1
                        act(PT[slot], psc[:, 0:128], AF.Exp, r=[pscr], w=[("PT", slot)])
                        tt("dve", PT[slot], PT[slot], BTp[:, hh, br * 256 + off:br * 256 + off + 128], ALU.mult, r=["BTp", ("PT", slot)], w=[("PT", slot)])
                        pendq.append((task, slot))
                        if len(pendq) > 3:
                            do_pv(*pendq.pop(0))
                    while pendq:
                        do_pv(*pendq.pop(0))
                    for b in range(4):
                        acc, accr = accs[b]
                        rd = RD[0]
                        rdr = ("RD", 0)
                        nrd += 1
                        if hh == 0:
                            act(rd[0:64, :], acc[64:128, :], AF.Ln, r=[accr], w=[rdr])
                            act(rd[0:64, :], rd[0:64, :], AF.Exp, r=[rdr], w=[rdr], scale=-1.0)
                            tt("dve", YT[0:64, hp, b * 512:(b + 1) * 512], acc[0:64, :], rd[0:64, :], ALU.mult, r=[accr, rdr], w=[("YT", hp, b)])
                        else:
                            act(rd[64:128, :], acc[0:64, :], AF.Ln, r=[accr], w=[rdr])
                            act(rd[64:128, :], rd[64:128, :], AF.Exp, r=[rdr], w=[rdr], scale=-1.0)
                            tt("dve", YT[64:128, hp, b * 512:(b + 1) * 512], acc[64:128, :], rd[64:128, :], ALU.mult, r=[accr, rdr], w=[("YT", hp, b)])

        def wout_phase(l, tail):
            ar = Arena(0, ARB)
            WO = ar.alloc([128, 8, D], BF16)
            GBC = ar.alloc([128, D], F32)
            D8 = ar.alloc([128, 8, 128], F32)
            dma("pool", WO, wout_d[l].rearrange("(kc p) d -> p kc d", p=128), w=["WO"])
            gate_bc(l, 1, 1.0 / ALPHA, GBC, D8)
            for kc in range(8):
                tt("pool", WO[:, kc, :], WO[:, kc, :], GBC, ALU.mult, r=["GBC", "WO"], w=["WO"])
            n = 0
            for tt_ in range(NT):
                for hf in range(2):
                    po, por = ps(n % 2)
                    n += 1
                    for kc in range(8):
                        mm(po[:, :], YT[:, kc, tt_ * 128:(tt_ + 1) * 128], WO[:, kc, hf * 512:(hf + 1) * 512], kc == 0, kc == 7,
                           r=["WO"] + [("YT", kc, b) for b in range(4)], w=[por])
                    tt("dve", X[:, tt_, hf * 512:(hf + 1) * 512], X[:, tt_, hf * 512:(hf + 1) * 512], po[:, :], ALU.add,
                       r=[("X", tt_), por], w=[("X", tt_)])
                tail.tile_done(tt_)
                tail.lagged(2)
            tail.lagged(0)

        def mixer(l, tail):
            if "pool" in parts:
                pool_phase(l)
                fence()
            if "ssm" in parts:
                ssm_phase(l)
                fence()
            if "attn" in parts:
                attn_phase(l)
                fence()
            if dbg:
                return
            wout_phase(l, tail)

        with arena_scope():
            ada_phase()
        fence_all()
        with arena_scope():
            bias_setup()
        run_layers()
        if dbg:
            fence_all()
            dbg_d = nc.dram_tensor("dbg", [128, 8, SEQ], F32, kind="ExternalOutput").ap()
            dma("pool", dbg_d, YT[:, :, :], r=["ALL"], final=True)
        if dbg or stop is not None:
            fence_all()
            for q in range(4):
                dma("sp", ov[:, q * 4:(q + 1) * 4, :], X[:, q * 4:(q + 1) * 4, :], r=[("X", t) for t in range(q * 4, q * 4 + 4)], final=True)
        S.emit()
    return nc


def host_inputs(inputs):
    f = lambda a: np.ascontiguousarray(np.asarray(a, dtype=np.float32))
    consts = static_consts()
    shared = {}
    for k in ("ada_w", "ada_b", "ln_g", "ln_b", "ffn_w_gate", "ffn_w_up", "ffn_w_down", "w_in", "w_out", "rel_bias"):
        shared[k] = f(inputs[k])
    a_re, a_im = f(inputs["ssm_a_re"]), f(inputs["ssm_a_im"])
    dup = lambda a: np.concatenate([a, a], axis=1)
    shared["sa_re"] = f(dup(a_re.transpose(0, 2, 1)))
    shared["sa_im"] = f(dup(a_im.transpose(0, 2, 1)))
    shared["sldt"] = f(np.broadcast_to(f(inputs["ssm_log_dt"])[:, None, :], (DEPTH, 128, 16)))
    b_re = f(inputs["ssm_b_re"]).transpose(0, 2, 1, 3).reshape(DEPTH, 64, 256)
    b_im = f(inputs["ssm_b_im"]).transpose(0, 2, 1, 3).reshape(DEPTH, 64, 256)
    shared["sP1"] = f(np.concatenate([b_re, b_im], axis=1))
    shared["sP2"] = f(np.concatenate([b_im, b_re], axis=1))
    c_re = f(inputs["ssm_c_re"]).transpose(0, 3, 1, 2).reshape(DEPTH, 64, 256)
    c_im = f(inputs["ssm_c_im"]).transpose(0, 3, 1, 2).reshape(DEPTH, 64, 256)
    shared["sCT"] = f(np.concatenate([c_re, c_im], axis=1))
    col2 = lambda a: f(f(a).reshape(DEPTH, 2, 128).transpose(0, 2, 1))
    shared["sd"] = col2(inputs["ssm_d"])
    shared["glu_b"] = col2(inputs["glu_b"])
    shared["pool_s"] = col2(inputs["pool_scale"])
    shared["glu_w"] = f(f(inputs["glu_w"]).reshape(DEPTH, 2, 128, 256).transpose(0, 2, 1, 3))
    pw = f(inputs["pool_w"])
    pbd = np.zeros((DEPTH, 128, 2, 128), np.float32)
    for ch in range(2):
        for h in range(2):
            pbd[:, h * 64:(h + 1) * 64, ch, h * 64:(h + 1) * 64] = pw[:, ch * 2 + h]
    shared["pool_w"] = pbd
    shared.update(consts)
    x = f(inputs["x"])
    c = f(inputs["c"])
    maps = []
    for b in range(8):
        m = dict(shared)
        m["x"] = x[b]
        m["cT"] = f(c[b].reshape(8, 128).T)
        maps.append(m)
    return maps


_NC_CACHE = {}


def kernel(**inputs):
    if "nc" not in _NC_CACHE:
        _NC_CACHE["nc"] = build()
    nc = _NC_CACHE["nc"]
    maps = host_inputs(inputs)
    res = run_bass_kernel_spmd(nc, maps, core_ids=list(range(8)))
    return np.stack([np.asarray(r["out"], dtype=np.float32) for r in res.results], axis=0)
```

```python
import contextlib
import math
import numpy as np
import concourse.bass as bass
import concourse.mybir as mybir
from concourse.bass_utils import run_bass_kernel_spmd

F32 = mybir.dt.float32
BF16 = mybir.dt.bfloat16
I32 = mybir.dt.int32
AF = mybir.ActivationFunctionType
ALU = mybir.AluOpType

SEQ = 2048
D = 1024
DFF = 2816
DEPTH = 2
NT = SEQ // 128
ALPHA = (2 * DEPTH) ** 0.25
LN_EPS = 1e-5
NEG = -1e30
PATTERNS = ((128, 1), (512, 4), (2048, 16))

ENGS = ("pe", "act", "dve", "pool", "sp")
NDMASEM = 12


class Op:
    __slots__ = ("eng", "fn", "deps", "marked", "val", "is_dma", "dsem", "dval", "idx")

    def __init__(self, eng, fn, is_dma):
        self.eng = eng
        self.fn = fn
        self.deps = []
        self.marked = False
        self.val = None
        self.is_dma = is_dma
        self.dsem = None
        self.dval = None
        self.idx = None


class Sched:
    def __init__(self, nc):
        self.nc = nc
        self.ops = {e: [] for e in ENGS}
        self.writers = {}
        self.readers = {}
        self.ndma = {e: 0 for e in ENGS}
        self.final_waits = []
        self.scope = None

    def op(self, eng, fn, r=(), w=(), dma=False, final=False):
        o = Op(eng, fn, dma)
        deps = []
        r = list(r)
        w = list(w)
        if "ALL" not in w:
            r.append("ALL")
        if self.scope is not None and self.scope not in w:
            r.append(self.scope)
        for x in r:
            deps.extend(self.writers.get(x, ()))
        for x in w:
            deps.extend(self.writers.get(x, ()))
            deps.extend(self.readers.get(x, ()))
        seen = set()
        for d in deps:
            if d is o or id(d) in seen:
                continue
            seen.add(id(d))
            if d.eng == "pe" and eng == "pe" and not d.is_dma and not dma:
                continue
            o.deps.append(d)
            if not d.is_dma:
                d.marked = True
        for x in w:
            if self.readers.get(x):
                self.writers[x] = [o]
                self.readers[x] = []
            else:
                ws = self.writers.setdefault(x, [])
                ws[:] = [p for p in ws if not (p.eng == eng and p.is_dma == dma and not dma)]
                ws.append(o)
        for x in r:
            if x not in w:
                rs = self.readers.setdefault(x, [])
                rs[:] = [p for p in rs if not (p.eng == eng and not p.is_dma and not dma)]
                rs.append(o)
        if dma:
            j = self.ndma[eng]
            self.ndma[eng] = j + 1
            o.dsem = j % NDMASEM
            o.dval = 16 * (j // NDMASEM + 1)
            o.idx = j
        self.ops[eng].append(o)
        if final:
            self.final_waits.append(o)
        return o

    def emit(self):
        nc = self.nc
        with contextlib.ExitStack() as st:
            csem = {e: st.enter_context(nc.semaphore("c_" + e)) for e in ENGS}
            dsem = {e: [st.enter_context(nc.semaphore("d_%s_%d" % (e, i))) for i in range(NDMASEM)]
                    for e in ENGS if self.ndma[e] > 0}
            for e in ENGS:
                c = 0
                for o in self.ops[e]:
                    if o.marked and not o.is_dma:
                        c += 1
                        o.val = c
            block = st.enter_context(nc.Block())

            def run(e, eng):
                waited = {}
                for o in self.ops[e]:
                    waits = []
                    for d in o.deps:
                        if d.is_dma:
                            waits.append((("d", d.eng, d.dsem), dsem[d.eng][d.dsem], d.dval))
                        else:
                            waits.append((("c", d.eng), csem[d.eng], d.val))
                    if o.is_dma and o.idx >= NDMASEM:
                        waits.append((("d", e, o.dsem), dsem[e][o.dsem], o.dval - 16))
                    for key, s, v in waits:
                        if waited.get(key, 0) >= v:
                            continue
                        eng.wait_ge(s, v)
                        waited[key] = v
                    ins = o.fn(eng)
                    if o.is_dma:
                        ins.then_inc(dsem[e][o.dsem], 16)
                    elif o.marked:
                        ins.then_inc(csem[e], 1)
                for o in self.final_waits:
                    if o.eng == e:
                        eng.wait_ge(dsem[e][o.dsem], o.dval)

            if self.ops["sp"]:
                @block.sync
                def _(eng):
                    run("sp", eng)
            if self.ops["pe"]:
                @block.tensor
                def _(eng):
                    run("pe", eng)
            if self.ops["act"]:
                @block.scalar
                def _(eng):
                    run("act", eng)
            if self.ops["dve"]:
                @block.vector
                def _(eng):
                    run("dve", eng)
            if self.ops["pool"]:
                @block.gpsimd
                def _(eng):
                    run("pool", eng)


def t5_bucket(dist):
    n_buckets, max_distance = 32, 2048
    max_exact = n_buckets // 2
    d = np.maximum(dist, 1).astype(np.float32)
    large = max_exact + (np.log(d / max_exact) / math.log(max_distance / max_exact)
                         * (n_buckets - max_exact)).astype(np.int32)
    large = np.minimum(large, n_buckets - 1)
    return np.where(dist < max_exact, dist, large).astype(np.int32)


def static_consts():
    c = {}
    c["identf"] = np.eye(128, dtype=np.float32)
    c["jex"] = np.eye(128, dtype=np.float32)[::-1].copy()
    jsw = np.zeros((128, 128), np.float32)
    for m in range(64):
        jsw[m + 64, m] = -1.0
        jsw[m, m + 64] = 1.0
    c["jsw"] = jsw
    oh = np.zeros((32, 3 * 384), np.float32)
    negm = np.zeros((8, 3 * 384), np.float32)
    for bi, (win, dil) in enumerate(PATTERNS):
        for u in range(384):
            dist = u - 127
            if 0 <= dist <= win // dil:
                oh[t5_bucket(np.array([dist * dil]))[0], bi * 384 + u] = 1.0
            else:
                negm[:, bi * 384 + u] = NEG
    c["oh"] = oh
    c["negm"] = negm
    gm = np.zeros((128, 8), np.float32)
    for p in range(128):
        gm[p, p // 16] = 1.0
    c["gmask"] = gm
    wins = np.array([2, 4, 8, 16], np.float32)
    wp = np.zeros((128, 2), np.float32)
    for ch in range(2):
        for p in range(128):
            wp[p, ch] = wins[ch * 2 + p // 64]
    c["invw"] = (1.0 / wp).astype(np.float32)
    t = np.arange(16, dtype=np.float32)[None, None, :]
    c["rc16"] = (1.0 / np.minimum(t + 1.0, wp[:, :, None])).astype(np.float32)
    c["sgn"] = np.concatenate([-np.ones((64, 1), np.float32), np.ones((64, 1), np.float32)], 0)
    return c


def build(stop=None, parts=("pool", "ssm", "attn"), dbg=False):
    nc = bass.Bass("TRN2", target_bir_lowering=False)

    def din(name, shape, dt=F32):
        return nc.dram_tensor(name, list(shape), dt, kind="ExternalInput").ap()

    x_d = din("x", [SEQ, D])
    cT_d = din("cT", [128, 8])
    adaw_d = din("ada_w", [DEPTH, D, 9 * D])
    adab_d = din("ada_b", [DEPTH, 9 * D])
    lng_d = din("ln_g", [DEPTH, 3, D])
    lnb_d = din("ln_b", [DEPTH, 3, D])
    wg_d = din("ffn_w_gate", [DEPTH, 2, D, DFF])
    wu_d = din("ffn_w_up", [DEPTH, 2, D, DFF])
    wd_d = din("ffn_w_down", [DEPTH, 2, DFF, D])
    win_d = din("w_in", [DEPTH, D, 2048])
    wout_d = din("w_out", [DEPTH, D, D])
    relb_d = din("rel_bias", [32, 8])
    sare_d = din("sa_re", [DEPTH, 128, 16])
    saim_d = din("sa_im", [DEPTH, 128, 16])
    sldt_d = din("sldt", [DEPTH, 128, 16])
    sp1_d = din("sP1", [DEPTH, 128, 256])
    sp2_d = din("sP2", [DEPTH, 128, 256])
    sct_d = din("sCT", [DEPTH, 128, 256])
    sd_d = din("sd", [DEPTH, 128, 2])
    gluw_d = din("glu_w", [DEPTH, 128, 2, 256])
    glub_d = din("glu_b", [DEPTH, 128, 2])
    poolw_d = din("pool_w", [DEPTH, 128, 2, 128])
    pools_d = din("pool_s", [DEPTH, 128, 2])
    identf_d = din("identf", [128, 128])
    jex_d = din("jex", [128, 128])
    jsw_d = din("jsw", [128, 128])
    oh_d = din("oh", [32, 1152])
    negm_d = din("negm", [8, 1152])
    gmask_d = din("gmask", [128, 8])
    invw_d = din("invw", [128, 2])
    rc16_d = din("rc16", [128, 2, 16])
    sgn_d = din("sgn", [128, 1])
    out_d = nc.dram_tensor("out", [SEQ, D], F32, kind="ExternalOutput").ap()
    fv_d = nc.dram_tensor("fv_scratch", [8, 1152], F32, kind="Internal").ap()

    st = contextlib.ExitStack()
    with st:
        def T(name, shape, dt):
            return st.enter_context(nc.sbuf_tensor(name, list(shape), dt))

        PSB = [st.enter_context(nc.psum_tensor("ps%d" % i, [128, 512], F32)) for i in range(8)]

        def ps(k):
            return PSB[k], ("ps", k)

        X = T("X", [128, NT, D], F32)
        HT = T("HT", [128, 8, SEQ], BF16)
        YT = T("YT", [128, 8, SEQ], BF16)
        ARB = 49152
        AR = T("AR", [128, ARB // 2], BF16)
        ident = T("ident", [128, 128], BF16)
        identf = T("identf_s", [128, 128], F32)
        jex = T("jex_s", [128, 128], F32)
        jsw = T("jsw_s", [128, 128], F32)
        onesf = T("onesf", [128, 128], F32)
        jswb = T("jswb", [128, 128], BF16)
        modcol = T("modcol", [128, DEPTH, 72], F32)
        XH = [T("XH%d" % i, [128, D], BF16) for i in range(4)]
        mv = T("mv", [128, 4, 2], F32)
        sc = T("sc", [128, 4, 4], F32)
        condT = T("condT", [128, 8], F32)
        gmask = T("gmask_s", [128, 8], F32)
        invw = T("invw_s", [128, 2], F32)
        rc16 = T("rc16_s", [128, 2, 16], F32)
        sgn = T("sgn_s", [128, 1], F32)

        S = Sched(nc)

        class Arena:
            def __init__(self, base, size):
                self.off = base
                self.end = base + size

            def alloc(self, shape, dt):
                n = int(np.prod(shape[1:]))
                nb = n * (4 if dt in (F32, I32) else 2)
                nb = (nb + 31) // 32 * 32
                assert self.off + nb <= self.end, ("arena overflow", shape, self.off, nb, self.end)
                v = AR[0:shape[0], self.off // 2:(self.off + nb) // 2]
                self.off += nb
                if dt != BF16:
                    v = v.bitcast(dt)
                v = v[:, 0:n]
                if len(shape) == 3:
                    v = v.rearrange("p (a b) -> p a b", a=shape[1])
                elif len(shape) == 4:
                    v = v.rearrange("p (a b c) -> p a b c", a=shape[1], b=shape[2])
                return v

        class ArenaYT(Arena):
            def alloc(self, shape, dt):
                n = int(np.prod(shape[1:]))
                nb = n * (4 if dt in (F32, I32) else 2)
                nb = (nb + 31) // 32 * 32
                assert self.off + nb <= self.end
                flat = YT[:, :, :].rearrange("p a b -> p (a b)")
                v = flat[0:shape[0], self.off // 2:(self.off + nb) // 2]
                self.off += nb
                if dt != BF16:
                    v = v.bitcast(dt)
                v = v[:, 0:n]
                if len(shape) == 3:
                    v = v.rearrange("p (a b) -> p a b", a=shape[1])
                return v

        def dma(eng, out, in_, r=(), w=(), final=False, slow=False):
            if slow:
                return S.op(eng, lambda e: e.dma_start(out=out, in_=in_, allow_slow_non_contiguous=True), r=r, w=w, dma=True, final=final)
            return S.op(eng, lambda e: e.dma_start(out=out, in_=in_), r=r, w=w, dma=True, final=final)

        def mm(out, lhsT, rhs, start, stop, r, w):
            return S.op("pe", lambda e: e.matmul(out, lhsT=lhsT, rhs=rhs, start=start, stop=stop), r=r, w=w)

        def tr(out, in_, idn, r, w):
            return S.op("pe", lambda e: e.transpose(out=out, in_=in_, identity=idn), r=r, w=w)

        def act(out, in_, func, r, w, bias=0.0, scale=1.0):
            return S.op("act", lambda e: e.activation(out=out, in_=in_, func=func, bias=bias, scale=scale), r=r, w=w)

        def ts(eng, out, in0, s1, s2, op0, op1, r, w):
            if op1 is None:
                return S.op(eng, lambda e: e.tensor_scalar(out=out, in0=in0, scalar1=s1, scalar2=None, op0=op0), r=r, w=w)
            return S.op(eng, lambda e: e.tensor_scalar(out=out, in0=in0, scalar1=s1, scalar2=s2, op0=op0, op1=op1), r=r, w=w)

        def tt(eng, out, in0, in1, op, r, w):
            return S.op(eng, lambda e: e.tensor_tensor(out=out, in0=in0, in1=in1, op=op), r=r, w=w)

        def stt(out, in0, scalar, in1, op0, op1, r, w):
            return S.op("dve", lambda e: e.scalar_tensor_tensor(out=out, in0=in0, scalar=scalar, in1=in1, op0=op0, op1=op1), r=r, w=w)

        def cp(eng, out, in_, r, w):
            if eng == "act":
                return S.op("act", lambda e: e.copy(out=out, in_=in_), r=r, w=w)
            return S.op(eng, lambda e: e.tensor_copy(out=out, in_=in_), r=r, w=w)

        def memset(eng, ap, val, w):
            return S.op(eng, lambda e: e.memset(ap, val), w=w)

        fsrc_d = nc.dram_tensor("fence_src", [1, 16], F32, kind="Internal").ap()
        fdst_d = nc.dram_tensor("fence_dst", [1, 16], F32, kind="Internal").ap()

        def fence():
            sv = S.scope
            S.scope = None
            S.op("sp", lambda e: e.dma_start(out=fdst_d, in_=fsrc_d), w=["ARENA"], dma=True)
            S.scope = sv

        def fence_all():
            S.op("dve", lambda e: e.memset(sc[:, 0, 3:4], 0.0), w=["ALL", "ARENA", ("sc3", 0)])

        class arena_scope:
            def __enter__(self):
                self.sv = S.scope
                S.scope = "ARENA"

            def __exit__(self, *a):
                S.scope = self.sv

        dma("sp", identf[:], identf_d, w=["identf"])
        dma("sp", jex[:], jex_d, w=["jex"])
        dma("sp", jsw[:], jsw_d, w=["jsw"])
        dma("sp", gmask[:], gmask_d, w=["gmask"])
        dma("sp", invw[:], invw_d, w=["invw"])
        dma("sp", rc16[:], rc16_d, w=["rc16"])
        dma("sp", sgn[:], sgn_d, w=["sgn"])
        cp("dve", ident[:], identf[:], r=["identf"], w=["ident"])
        cp("dve", jswb[:], jsw[:], r=["jsw"], w=["jswb"])
        memset("dve", onesf[:], 1.0, w=["onesf"])
        xv = x_d.rearrange("(t p) d -> p t d", p=128)
        for q in range(4):
            dma("sp", X[:, q * 4:(q + 1) * 4, :], xv[:, q * 4:(q + 1) * 4, :], w=[("X", t) for t in range(q * 4, q * 4 + 4)])

        def ada_phase():
            ar = Arena(0, ARB)
            AW = [ar.alloc([128, 8, 512], F32) for _ in range(2)]
            AB = [ar.alloc([1, 512], F32) for _ in range(2)]
            ROW = [ar.alloc([1, 512], F32) for _ in range(2)]
            cTs = ar.alloc([128, 8], F32)
            dma("sp", cTs, cT_d, w=["cTs"])
            act(condT[:], cTs, AF.Silu, r=["cTs"], w=["condT"])
            it = 0
            import itertools
            gen = itertools.chain(ssm_setup_gen(0), ssm_setup_gen(1))
            prep_q = list(range(NT))
            gen_done = [False]
            for l in range(DEPTH):
                for nb in range(18):
                    for _ in range(6):
                        if next(gen, "done") == "done":
                            gen_done[0] = True
                    b = it % 2
                    it += 1
                    src = adaw_d[l, :, nb * 512:(nb + 1) * 512].rearrange("(kc p) n -> p kc n", p=128)
                    dma("sp", AW[b], src, w=[("AW", b)])
                    dma("sp", AB[b], adab_d[l:l + 1, nb * 512:(nb + 1) * 512], w=[("AB", b)])
                    pr, prr = ps(b)
                    for kc in range(8):
                        mm(pr[0:1, :], condT[:, kc:kc + 1], AW[b][:, kc, :], kc == 0, False,
                           r=["condT", ("AW", b)], w=[prr])
                    mm(pr[0:1, :], onesf[0:1, 0:1], AB[b], False, True, r=["onesf", ("AB", b)], w=[prr])
                    ev = "dve" if gen_done[0] else "act"
                    cp(ev, ROW[b], pr[0:1, :], r=[prr], w=[("ROW", b)])
                    pc, pcr = ps(2 + b)
                    for j in range(4):
                        mm(pc[:, j:j + 1], ROW[b][0:1, j * 128:(j + 1) * 128], onesf[0:1, 0:1], True, True,
                           r=[("ROW", b), "onesf"], w=[pcr])
                    v = nb // 2
                    addc = 1.0 if v % 3 == 1 else 0.0
                    if ev == "act":
                        act(modcol[:, l, nb * 4:(nb + 1) * 4], pc[:, 0:4], AF.Identity, r=[pcr], w=[("modcol", l, v)], bias=float(addc))
                    else:
                        ts("dve", modcol[:, l, nb * 4:(nb + 1) * 4], pc[:, 0:4], float(addc), None, ALU.add, None, r=[pcr], w=[("modcol", l, v)])
                    if gen_done[0] and prep_q:
                        tq_ = prep_q.pop(0)
                        sv_ = S.scope
                        S.scope = None
                        prep_tile_a(0, 0, tq_)
                        prep_tile_b(0, 0, tq_, extra=[("ssc", 0), ("ssc", 1)])
                        S.scope = sv_
            for _ in gen:
                pass
            sv_ = S.scope
            S.scope = None
            while prep_q:
                tq_ = prep_q.pop(0)
                prep_tile_a(0, 0, tq_)
                prep_tile_b(0, 0, tq_, extra=[("ssc", 0), ("ssc", 1)])
            S.scope = sv_


        NSLOT = 4
        sums = T("sums", [128, NSLOT, 4], F32)

        def finish_stats(slot, eps, c0, c1):
            ts("dve", mv[:, slot, 0:1], sums[:, slot, c0:c0 + 1], 1.0 / D, None, ALU.mult, None, r=[("sums", slot, c0)], w=[("mv", slot)])
            tt("dve", mv[:, slot, 1:2], mv[:, slot, 0:1], mv[:, slot, 0:1], ALU.mult, r=[("mv", slot)], w=[("mv1", slot)])
            stt(sc[:, slot, 3:4], sums[:, slot, c1:c1 + 1], 1.0 / D, mv[:, slot, 1:2], ALU.mult, ALU.subtract,
                r=[("sums", slot, c1), ("mv1", slot)], w=[("sc3", slot)])
            act(sc[:, slot, 0:1], sc[:, slot, 3:4], AF.Sqrt, r=[("sc3", slot)], w=[("sc0", slot)], bias=float(eps))
            S.op("dve", lambda e: e.reciprocal(out=sc[:, slot, 1:2], in_=sc[:, slot, 0:1]), r=[("sc0", slot)], w=[("sc1", slot)])

        def act_accum(tt_, slot, func, col):
            S.op("act", lambda e: e.activation(out=XH[slot][:], in_=X[:, tt_, :], func=func, accum_out=sums[:, slot, col:col + 1]),
                 r=[("X", tt_)], w=[("XH", slot), ("sums", slot, col)])

        def prep_tile_a(l, i, tt_, have_sum=False):
            slot = tt_ % NSLOT
            if not have_sum:
                act_accum(tt_, slot, AF.Identity, 2)
            act_accum(tt_, slot, AF.Square, 3)
            finish_stats(slot, LN_EPS, 2, 3)
            ts("dve", sc[:, slot, 2:3], mv[:, slot, 0:1], -1.0, sc[:, slot, 1:2], ALU.mult, ALU.mult,
               r=[("mv", slot), ("sc1", slot)], w=[("sc2", slot)])
            xh = XH[slot]
            act(xh[:], X[:, tt_, :], AF.Identity, r=[("X", tt_), ("sc1", slot), ("sc2", slot)], w=[("XH", slot)],
                bias=sc[:, slot, 2:3], scale=sc[:, slot, 1:2])

        def prep_tile_b(l, i, tt_, extra=()):
            slot = tt_ % NSLOT
            xh = XH[slot]
            pt, ptr = ps(6 + tt_ % 2)
            ptb = pt[:, :].bitcast(BF16)
            for kc in range(8):
                tr(ptb[:, kc * 128:(kc + 1) * 128], xh[:, kc * 128:(kc + 1) * 128], ident[:], r=[("XH", slot), "ident"], w=[ptr])
            for kc in range(8):
                scl = modcol[:, l, (3 * i + 1) * 8 + kc:(3 * i + 1) * 8 + kc + 1]
                shf = modcol[:, l, (3 * i) * 8 + kc:(3 * i) * 8 + kc + 1]
                if kc % 2 == 0:
                    ts("dve", HT[:, kc, tt_ * 128:(tt_ + 1) * 128], ptb[:, kc * 128:(kc + 1) * 128], scl, shf,
                       ALU.mult, ALU.add, r=[ptr, ("modcol", l, 3 * i), ("modcol", l, 3 * i + 1)] + list(extra), w=[("HT", tt_ // 4)])
                else:
                    act(HT[:, kc, tt_ * 128:(tt_ + 1) * 128], ptb[:, kc * 128:(kc + 1) * 128], AF.Identity,
                        r=[ptr, ("modcol", l, 3 * i), ("modcol", l, 3 * i + 1)] + list(extra), w=[("HT", tt_ // 4)], bias=shf, scale=scl)

        def prep(l, i):
            for tt_ in range(NT):
                prep_tile_a(l, i, tt_)
                prep_tile_b(l, i, tt_)

        LNGB = [T("LNG", [128, D], F32), T("LNB", [128, D], F32)]

        def post_setup(l, i):
            LNG, LNB = LNGB[0][:], LNGB[1][:]
            sv = S.scope
            S.scope = None
            dma("sp", LNG, bass.AP(lng_d.tensor, (l * 3 + i) * D, [[0, 128], [1, D]]), w=["LNG"])
            dma("sp", LNB, bass.AP(lnb_d.tensor, (l * 3 + i) * D, [[0, 128], [1, D]]), w=["LNB"])
            S.scope = sv
            return LNG, LNB

        def post_tile(l, i, tt_, LNG, LNB, want_sum):
            slot = tt_ % NSLOT
            act_accum(tt_, slot, AF.Identity, 0)
            act_accum(tt_, slot, AF.Square, 1)
            finish_stats(slot, LN_EPS / (ALPHA * ALPHA), 0, 1)
            stt(X[:, tt_, :], X[:, tt_, :], mv[:, slot, 0:1], LNG, ALU.subtract, ALU.mult,
                r=[("X", tt_), ("mv", slot), "LNG"], w=[("X", tt_)])
            if want_sum:
                S.op("dve", lambda e: e.scalar_tensor_tensor(out=X[:, tt_, :], in0=X[:, tt_, :], scalar=sc[:, slot, 1:2], in1=LNB,
                                                             op0=ALU.mult, op1=ALU.add, accum_out=sums[:, slot, 2:3]),
                     r=[("X", tt_), ("sc1", slot), "LNB"], w=[("X", tt_), ("sums", slot, 2)])
            else:
                stt(X[:, tt_, :], X[:, tt_, :], sc[:, slot, 1:2], LNB, ALU.mult, ALU.add,
                    r=[("X", tt_), ("sc1", slot), "LNB"], w=[("X", tt_)])

        ov = out_d.rearrange("(t p) d -> p t d", p=128)

        class Tail:
            def __init__(self, postli, prepli, final):
                self.postli, self.prepli, self.final = postli, prepli, final
                self.pend = []
                self.LNG, self.LNB = post_setup(*postli)

            def tile_done(self, tt_):
                sv = S.scope
                S.scope = None
                post_tile(self.postli[0], self.postli[1], tt_, self.LNG, self.LNB, self.prepli is not None)
                if self.prepli is not None:
                    prep_tile_a(self.prepli[0], self.prepli[1], tt_, have_sum=True)
                    self.pend.append(tt_)
                if self.final:
                    dma("sp", ov[:, tt_, :], X[:, tt_, :], r=[("X", tt_)], final=True)
                S.scope = sv

            def lagged(self, keep):
                sv = S.scope
                S.scope = None
                while len(self.pend) > keep:
                    prep_tile_b(self.prepli[0], self.prepli[1], self.pend.pop(0))
                S.scope = sv

        def gate_bc(l, i, scale, GBC, D8):
            for kc in range(8):
                col = (3 * i + 2) * 8 + kc
                ts("dve", D8[:, kc, :], identf[:], modcol[:, l, col:col + 1], float(scale), ALU.mult, ALU.mult,
                   r=["identf", ("modcol", l, 3 * i + 2)], w=["D8"])
            for hf in range(2):
                pg, pgr = ps(hf)
                mm(pg[:, :], onesf[:], D8[:, hf * 4:(hf + 1) * 4, :].rearrange("p a b -> p (a b)"), True, True, r=["onesf", "D8"], w=[pgr])
                cp("act", GBC[:, hf * 512:(hf + 1) * 512], pg[:, :], r=[pgr], w=["GBC"])

        def ffn(l, i, si, tail):
            ar = Arena(0, ARB)
            ay = ArenaYT(0, 32768)
            WG = [ay.alloc([128, 8, 512], BF16) for _ in range(2)]
            WU = [ay.alloc([128, 8, 512], BF16) for _ in range(2)]
            WD = [ar.alloc([128, 4, D], BF16) for _ in range(2)]
            ACTB = [ar.alloc([128, 4, 512], BF16) for _ in range(2)]
            SG = [ar.alloc([128, 512], F32) for _ in range(2)]
            GBC = ar.alloc([128, D], F32)
            D8 = ar.alloc([128, 8, 128], F32)
            gate_bc(l, si, 0.5 / ALPHA, GBC, D8)
            groups = [(g * 4, 4) for g in range(5)] + [(20, 2)]
            pend = None
            nsg = 0
            for gi, (c0, nf) in enumerate(groups):
                b = gi % 2
                f0 = c0 * 128
                fw = nf * 128
                dma("pool", WG[b][:, :, 0:fw], wg_d[l, i, :, f0:f0 + fw].rearrange("(kc p) f -> p kc f", p=128), w=[("WG", b)])
                dma("pool", WU[b][:, :, 0:fw], wu_d[l, i, :, f0:f0 + fw].rearrange("(kc p) f -> p kc f", p=128), w=[("WU", b)])
                dma("pool", WD[b][:, 0:nf, :], wd_d[l, i, f0:f0 + fw, :].rearrange("(c p) d -> p c d", p=128), w=[("WD", b)])
                for c in range(nf):
                    tt("pool", WD[b][:, c, :], WD[b][:, c, :], GBC, ALU.mult, r=["GBC", ("WD", b)], w=[("WD", b)])
                for tsi in range(4):
                    ab = (gi * 4 + tsi) % 2
                    for c in range(nf):
                        pgk = (gi * 16 + tsi * 4 + c) % 2
                        pg, pgr = ps(pgk)
                        pu, pur = ps(2 + pgk)
                        for kc in range(8):
                            mm(pg[:, :], WG[b][:, kc, c * 128:(c + 1) * 128], HT[:, kc, tsi * 512:(tsi + 1) * 512], kc == 0, kc == 7,
                               r=[("WG", b), ("HT", tsi)], w=[pgr])
                        for kc in range(8):
                            mm(pu[:, :], WU[b][:, kc, c * 128:(c + 1) * 128], HT[:, kc, tsi * 512:(tsi + 1) * 512], kc == 0, kc == 7,
                               r=[("WU", b), ("HT", tsi)], w=[pur])
                        sgb = nsg % 2
                        nsg += 1
                        act(SG[sgb], pg[:, :], AF.Silu, r=[pgr], w=[("SG", sgb)])
                        tt("dve", ACTB[ab][:, c, :], SG[sgb], pu[:, :], ALU.mult, r=[("SG", sgb), pur], w=[("ACTB", ab, c)])
                    cur = (b, ab, nf, tsi, gi == len(groups) - 1)
                    if pend is not None:
                        down(pend, WD, ACTB, tail)
                    pend = cur
            down(pend, WD, ACTB, tail)
            tail.lagged(0)

        dcount = [0]

        def down(p, WD, ACTB, tail):
            b, ab, nf, tsi, last = p
            for t4 in range(4):
                tt_ = tsi * 4 + t4
                for hf in range(2):
                    k = 4 + dcount[0] % 2
                    dcount[0] += 1
                    pd, pdr = ps(k)
                    for c in range(nf):
                        mm(pd[:, :], ACTB[ab][:, c, t4 * 128:(t4 + 1) * 128], WD[b][:, c, hf * 512:(hf + 1) * 512], c == 0, c == nf - 1,
                           r=[("ACTB", ab, c), ("WD", b)], w=[pdr])
                    tt("dve", X[:, tt_, hf * 512:(hf + 1) * 512], X[:, tt_, hf * 512:(hf + 1) * 512], pd[:, :], ALU.add,
                       r=[("X", tt_), pdr], w=[("X", tt_)])
                if last:
                    tail.tile_done(tt_)
                    tail.lagged(2)

        stage = [0]

        def done_stage():
            stage[0] += 1
            return stop is not None and stage[0] >= stop

        def run_layers():
            for l in range(DEPTH):
                fence()
                with arena_scope():
                    ffn(l, 0, 0, Tail((l, 0), (l, 1), False))
                if done_stage():
                    return
                fence()
                with arena_scope():
                    mixer(l, Tail((l, 1), (l, 2), False) if not dbg else None)
                if dbg:
                    return
                if done_stage():
                    return
                fence()
                lastl = l == DEPTH - 1
                with arena_scope():
                    ffn(l, 1, 2, Tail((l, 2), None if lastl else (l + 1, 0), lastl))
                if done_stage():
                    return

        TWO_PI = 2.0 * math.pi

        def act_b(out, in_, func, r, w, bias, scale):
            return act(out, in_, func, r, w, bias=bias, scale=scale)

        def pool_phase(l):
            ar = Arena(0, ARB)
            WUP = ar.alloc([128, 8, 256], BF16)
            PW = ar.alloc([128, 2, 128], BF16)
            pscol = ar.alloc([128, 2], F32)
            UP = [ar.alloc([128, 2, 528], F32) for _ in range(2)]
            Pb = ar.alloc([128, 2, 528], F32)
            Qb = ar.alloc([128, 2, 528], F32)
            PL = ar.alloc([128, 2, 512], BF16)
            TM = ar.alloc([128, 2, 16], F32)
            dma("pool", WUP, win_d[l, :, 1792:2048].rearrange("(kc p) f -> p kc f", p=128), w=["WUP"])
            dma("pool", PW, poolw_d[l], w=["PW"])
            dma("sp", pscol, pools_d[l], w=["pscol"])
            memset("pool", UP[0][:, :, 0:16], 0.0, w=[("UP", 0)])
            for tsi in range(4):
                ub = UP[tsi % 2]
                ur = ("UP", tsi % 2)
                for ch in range(2):
                    pu, pur = ps(ch)
                    for kc in range(8):
                        mm(pu[:, :], WUP[:, kc, ch * 128:(ch + 1) * 128], HT[:, kc, tsi * 512:(tsi + 1) * 512], kc == 0, kc == 7,
                           r=["WUP", ("HT", tsi)], w=[pur])
                    cp("act", ub[:, ch, 16:528], pu[:, :], r=[pur], w=[ur])
                tt("pool", Pb[:, :, 1:528], ub[:, :, 1:528], ub[:, :, 0:527], ALU.add, r=[ur], w=["Pb"])
                tt("pool", Qb[64:128, 0, 3:528], Pb[64:128, 0, 3:528], Pb[64:128, 0, 1:526], ALU.add, r=["Pb"], w=["Qb"])
                tt("pool", Qb[:, 1, 3:528], Pb[:, 1, 3:528], Pb[:, 1, 1:526], ALU.add, r=["Pb"], w=["Qb"])
                tt("pool", Pb[:, 1, 7:528], Qb[:, 1, 7:528], Qb[:, 1, 3:524], ALU.add, r=["Qb"], w=["Pb"])
                tt("pool", Qb[64:128, 1, 15:528], Pb[64:128, 1, 15:528], Pb[64:128, 1, 7:520], ALU.add, r=["Pb"], w=["Qb"])
                srcs = [(Pb, 0, 64, 0), (Qb, 64, 128, 0), (Pb, 0, 64, 1), (Qb, 64, 128, 1)]
                for (sb, p0, p1, ch) in srcs:
                    stt(PL[p0:p1, ch, :], sb[p0:p1, ch, 16:528], invw[p0:p1, ch:ch + 1], ub[p0:p1, ch, 16:528], ALU.mult, ALU.subtract,
                        r=["Pb", "Qb", ur, "invw"], w=["PL"])
                    if tsi == 0:
                        tt("dve", TM[p0:p1, ch, :], sb[p0:p1, ch, 16:32], rc16[p0:p1, ch, :], ALU.mult, r=["Pb", "Qb", "rc16"], w=["TM"])
                        tt("dve", PL[p0:p1, ch, 0:16], TM[p0:p1, ch, :], ub[p0:p1, ch, 16:32], ALU.subtract, r=["TM", ur], w=["PL"])
                if tsi < 3:
                    cp("pool", UP[(tsi + 1) % 2][:, :, 0:16], ub[:, :, 512:528], r=[ur], w=[("UP", (tsi + 1) % 2)])
                for ch in range(2):
                    py, pyr = ps(2 + ch)
                    mm(py[:, :], PW[:, ch, :], PL[:, ch, :], True, True, r=["PW", "PL"], w=[pyr])
                    act_b(YT[:, 6 + ch, tsi * 512:(tsi + 1) * 512], py[:, :], AF.Identity, r=[pyr, "pscol"], w=[("YT", 6 + ch, tsi)],
                          bias=0.0, scale=pscol[:, ch:ch + 1])

        TL = 128
        NTB = 16 * (TL + 1)
        ssc_f = nc.dram_tensor("ssm_scr_f", [DEPTH, 128, 2 * NTB + 224], F32, kind="Internal").ap()
        ssc_b = nc.dram_tensor("ssm_scr_b", [DEPTH, 128, NTB + 3 * 2048], BF16, kind="Internal").ap()

        class ArenaOn(Arena):
            def __init__(self, flat, size):
                self.flat = flat
                self.off = 0
                self.end = size

            def alloc(self, shape, dt):
                n = int(np.prod(shape[1:]))
                nb = n * (4 if dt in (F32, I32) else 2)
                nb = (nb + 31) // 32 * 32
                assert self.off + nb <= self.end, ("arenaOn overflow", shape, self.off, nb, self.end)
                v = self.flat[0:shape[0], self.off // 2:(self.off + nb) // 2]
                self.off += nb
                if dt != BF16:
                    v = v.bitcast(dt)
                v = v[:, 0:n]
                if len(shape) == 3:
                    v = v.rearrange("p (a b) -> p a b", a=shape[1])
                return v

        def ssm_setup_gen(l):
            ah = ArenaOn(HT[:, :, :].rearrange("p a b -> p (a b)"), 32768)
            ay = ArenaOn(YT[:, :, :].rearrange("p a b -> p (a b)"), 32768)
            ar = Arena(40992, ARB - 40992)
            TC = ah.alloc([128, 16, TL + 1], F32)
            TS = ah.alloc([128, 16, TL + 1], F32)
            ANG = ah.alloc([128, 16, TL + 1], F32)
            BL = ay.alloc([128, 16, 128], BF16)
            IBL = ay.alloc([128, 16, 128], BF16)
            CL = ay.alloc([128, 16, 128], BF16)
            SV = ay.alloc([128, 14, 16], F32)
            P1 = ay.alloc([128, 16, 16], F32)
            P2 = ay.alloc([128, 16, 16], F32)
            CT = ay.alloc([128, 256], F32)
            TCb = ay.alloc([128, 16, TL + 1], BF16)
            AI = ay.alloc([128, 16, TL + 1], I32)
            JF = ar.alloc([128, TL + 1], F32)
            JI = ar.alloc([128, TL + 1], I32)
            Bc = ar.alloc([128, 16, 16], F32)
            IBc = ar.alloc([128, 16, 16], F32)
            Tm = ar.alloc([128, 16, 16], F32)
            are, aim, ldt, dtv, lr, thn, rho, sn, cs, er, ei, gr, gi, tq = [SV[:, k, :] for k in range(14)]
            V = "ssmv"
            dma("pool", are, sare_d[l], w=["are", V])
            dma("pool", aim, saim_d[l], w=["aim", V])
            dma("pool", ldt, sldt_d[l], w=["ldt", V])
            dma("pool", P1, sp1_d[l].rearrange("p (g c) -> p g c", g=16), w=["P1"])
            dma("pool", P2, sp2_d[l].rearrange("p (g c) -> p g c", g=16), w=["P2"])
            dma("pool", CT, sct_d[l], w=["CT"])
            yield
            act(dtv, ldt, AF.Exp, r=["ldt"], w=[V])
            tt("dve", lr, are, dtv, ALU.mult, r=["are", V], w=[V])
            tt("dve", thn, aim, dtv, ALU.mult, r=["aim", V], w=[V])
            ts("dve", thn, thn, 1.0 / TWO_PI, None, ALU.mult, None, r=[V], w=[V])
            yield
            ts("dve", rho, lr, 1.0 / 720.0, 1.0 / 120.0, ALU.mult, ALU.add, r=[V], w=[V])
            for cst in (1.0 / 24.0, 1.0 / 6.0, 0.5, 1.0, 1.0):
                tt("dve", rho, rho, lr, ALU.mult, r=[V], w=[V])
                ts("dve", rho, rho, float(cst), None, ALU.add, None, r=[V], w=[V])
                yield

            def sincos(dst, src, shift, ai):
                ts("dve", dst, src, float(shift), None, ALU.add, None, r=[V], w=[V])
                cp("dve", ai, dst, r=[V], w=[V])
                tt("dve", dst, dst, ai, ALU.subtract, r=[V], w=[V])
                act(dst, dst, AF.Sin, r=[V], w=[V], scale=TWO_PI)

            sincos(sn, thn, 0.0, AI[:, 0, 0:16])
            yield
            sincos(cs, thn, 0.25, AI[:, 0, 0:16])
            yield
            tt("dve", er, rho, cs, ALU.mult, r=[V], w=[V])
            ts("dve", er, er, -1.0, None, ALU.add, None, r=[V], w=[V])
            tt("dve", ei, rho, sn, ALU.mult, r=[V], w=[V])
            tt("dve", tq, are, are, ALU.mult, r=[V, "are"], w=[V])
            yield
            tt("dve", gr, aim, aim, ALU.mult, r=[V, "aim"], w=[V])
            tt("dve", tq, tq, gr, ALU.add, r=[V], w=[V])
            S.op("dve", lambda e: e.reciprocal(out=tq, in_=tq), r=[V], w=[V])
            tt("dve", gr, er, are, ALU.mult, r=[V], w=[V])
            yield
            tt("dve", gi, ei, aim, ALU.mult, r=[V], w=[V])
            tt("dve", gr, gr, gi, ALU.add, r=[V], w=[V])
            tt("dve", gr, gr, tq, ALU.mult, r=[V], w=[V])
            tt("dve", gi, ei, are, ALU.mult, r=[V], w=[V])
            yield
            tt("dve", er, er, aim, ALU.mult, r=[V], w=[V])
            tt("dve", gi, gi, er, ALU.subtract, r=[V], w=[V])
            tt("dve", gi, gi, tq, ALU.mult, r=[V], w=[V])
            S2, S3, S4 = ei, er, tq
            ts("dve", S2, gi, sgn[:, 0:1], None, ALU.mult, None, r=[V, "sgn"], w=[V])
            yield
            ts("dve", S3, gr, sgn[:, 0:1], None, ALU.mult, None, r=[V, "sgn"], w=[V])
            ts("dve", S4, gi, -1.0, None, ALU.mult, None, r=[V], w=[V])
            bc = lambda v: v.unsqueeze(2).to_broadcast([128, 16, 16])
            tt("dve", Bc, P1, bc(gr), ALU.mult, r=[V, "P1"], w=["Bc"])
            tt("dve", Tm, P2, bc(S2), ALU.mult, r=[V, "P2"], w=["Tm"])
            yield
            tt("dve", Bc, Bc, Tm, ALU.add, r=["Tm"], w=["Bc"])
            tt("dve", IBc, P2, bc(S3), ALU.mult, r=[V, "P2"], w=["IBc"])
            tt("dve", Tm, P1, bc(S4), ALU.mult, r=[V, "P1", "Bc"], w=["Tm"])
            tt("dve", IBc, IBc, Tm, ALU.add, r=["Tm"], w=["IBc"])
            yield
            for (src, dst, nm) in ((Bc, BL, "Bc"), (IBc, IBL, "IBc")):
                flat = src.rearrange("p g c -> p (g c)")
                for ch in range(2):
                    pt_, ptr_ = ps(7)
                    tr(pt_[:, 0:128], flat[:, ch * 128:(ch + 1) * 128], identf[:], r=[nm, "identf"], w=[ptr_])
                    for g8 in range(8):
                        ts("dve", dst[:, ch * 8 + g8, :], pt_[:, 0:128], gmask[:, g8:g8 + 1], None, ALU.mult, None,
                           r=[ptr_, "gmask"], w=["BL"])
                        if g8 % 4 == 3:
                            yield
            ts("dve", CT, CT, sgn[:, 0:1], -1.0, ALU.mult, ALU.mult, r=["CT", "sgn"], w=["CT"])
            memset("pool", CL, 0.0, w=["CL"])
            for g in range(16):
                g8 = g % 8
                cp("dve", CL[:, g, 16 * g8:16 * g8 + 16], CT[:, g * 16:(g + 1) * 16], r=["CT"], w=["CL"])
                if g % 4 == 3:
                    yield
            S.op("pool", lambda e: e.iota(out=JI, pattern=[[1, TL + 1]], base=0, channel_multiplier=0), w=["JI"])
            cp("dve", JF, JI, r=["JI"], w=["JF"])
            tt("dve", ANG, JF.unsqueeze(1).to_broadcast([128, 16, TL + 1]), thn.unsqueeze(2).to_broadcast([128, 16, TL + 1]),
               ALU.mult, r=["JF", V], w=[V])
            yield
            sincos(TS, ANG, 0.0, AI)
            yield
            sincos(TC, ANG, 0.25, AI)
            yield
            cp("dve", TCb, TC, r=[V], w=[V])
            dma("pool", ssc_f[l, :, 0:NTB], TC.rearrange("p a b -> p (a b)"), r=[V], w=[("ssc", l)])
            dma("pool", ssc_f[l, :, NTB:2 * NTB], TS.rearrange("p a b -> p (a b)"), r=[V], w=[("ssc", l)])
            dma("pool", ssc_f[l, :, 2 * NTB:2 * NTB + 224], SV.rearrange("p a b -> p (a b)"), r=[V], w=[("ssc", l)])
            dma("pool", ssc_b[l, :, 0:NTB], TCb.rearrange("p a b -> p (a b)"), r=[V], w=[("ssc", l)])
            dma("pool", ssc_b[l, :, NTB:NTB + 2048], BL.rearrange("p a b -> p (a b)"), r=["BL"], w=[("ssc", l)])
            dma("pool", ssc_b[l, :, NTB + 2048:NTB + 4096], IBL.rearrange("p a b -> p (a b)"), r=["BL"], w=[("ssc", l)])
            dma("pool", ssc_b[l, :, NTB + 4096:NTB + 6144], CL.rearrange("p a b -> p (a b)"), r=["CL"], w=[("ssc", l)])
            yield

        def ssm_phase(l):
            ay = ArenaOn(YT[:, 0:4, :].rearrange("p a b -> p (a b)"), 16384)
            ar = Arena(0, ARB)
            BL = ay.alloc([128, 16, 128], BF16)
            IBL = ay.alloc([128, 16, 128], BF16)
            CL = ay.alloc([128, 16, 128], BF16)
            SV = ay.alloc([128, 14, 16], F32)
            WUS = ar.alloc([128, 8, 256], BF16)
            TC = ar.alloc([128, 16, TL + 1], F32)
            TS = ar.alloc([128, 16, TL + 1], F32)
            GW = ar.alloc([128, 2, 256], BF16)
            sdcol = ar.alloc([128, 2], F32)
            glub = ar.alloc([128, 2], F32)
            TCb = ar.alloc([128, 16, TL + 1], BF16)
            rho = SV[:, 6, :]
            V = "ssmv2"
            dma("sp", TC.rearrange("p a b -> p (a b)"), ssc_f[l, :, 0:NTB], r=[("ssc", l)], w=[V])
            dma("sp", TS.rearrange("p a b -> p (a b)"), ssc_f[l, :, NTB:2 * NTB], r=[("ssc", l)], w=[V])
            dma("sp", SV.rearrange("p a b -> p (a b)"), ssc_f[l, :, 2 * NTB:2 * NTB + 224], r=[("ssc", l)], w=[V])
            dma("sp", TCb.rearrange("p a b -> p (a b)"), ssc_b[l, :, 0:NTB], r=[("ssc", l)], w=[V])
            dma("sp", BL.rearrange("p a b -> p (a b)"), ssc_b[l, :, NTB:NTB + 2048], r=[("ssc", l)], w=["BL"])
            dma("sp", IBL.rearrange("p a b -> p (a b)"), ssc_b[l, :, NTB + 2048:NTB + 4096], r=[("ssc", l)], w=["BL"])
            dma("sp", CL.rearrange("p a b -> p (a b)"), ssc_b[l, :, NTB + 4096:NTB + 6144], r=[("ssc", l)], w=["CL"])
            dma("sp", sdcol, sd_d[l], w=["sdcol"])
            dma("sp", glub, glub_d[l], w=["glub"])
            dma("pool", GW, gluw_d[l], w=["GW"])
            dma("pool", WUS, win_d[l, :, 1536:1792].rearrange("(kc p) f -> p kc f", p=128), w=["WUS"])
            USS2 = [ar.alloc([128, 2, 512], BF16) for _ in range(2)]
            Wb = [ar.alloc([128, 4, TL], BF16) for _ in range(2)]
            T2 = ar.alloc([128, 4, TL], BF16)
            STb = [ar.alloc([128, 4, TL], F32) for _ in range(2)]
            SBh = [ar.alloc([128, 4, TL], BF16) for _ in range(2)]
            T2b = ar.alloc([128, 4, TL], BF16)
            S1 = ar.alloc([128, 4, TL], BF16)
            SB = [ar.alloc([128, 4, TL], BF16) for _ in range(2)]
            GE = ar.alloc([128, 2, TL], BF16)
            CI = ar.alloc([128, 16], F32)
            CA = ar.alloc([128, 4], F32)
            CB = ar.alloc([128, 4], F32)
            YV = ay.alloc([128, 2, TL], F32)
            G1 = ay.alloc([128, 2, TL], F32)
            G2 = ay.alloc([128, 2, TL], F32)
            NB = 64
            p0_, p0r = ps(0)
            pL, pLr = ps(6)
            pAs = [ps(1), ps(2)]
            pBs = [ps(3), ps(7)]
            pC, pCr = ps(4)
            pC3 = pC[:, :].rearrange("p (a b) -> p a b", a=4)

            def stage_F_pe(n):
                k, bq = n // 4, n % 4
                tsi, kk, ch = k // 4, k % 4, bq // 2
                USS = USS2[tsi % 2]
                ur = ("USS", tsi % 2)
                if kk == 0 and bq == 0:
                    for c2 in range(2):
                        for kc in range(8):
                            mm(p0_[:, :], WUS[:, kc, c2 * 128:(c2 + 1) * 128], HT[:, kc, tsi * 512:(tsi + 1) * 512], kc == 0, kc == 7,
                               r=["WUS", ("HT", tsi)], w=[p0r])
                        cp("act", USS[:, c2, :], p0_[:, :], r=[p0r], w=[ur])
                pA, pAr = pAs[n % 2]
                pB, pBr = pBs[n % 2]
                for gi_ in range(4):
                    mm(pA[:, gi_ * TL:(gi_ + 1) * TL], BL[:, 4 * bq + gi_, :], USS[:, ch, kk * TL:(kk + 1) * TL], True, True, r=["BL", ur], w=[pAr])
                for gi_ in range(4):
                    mm(pB[:, gi_ * TL:(gi_ + 1) * TL], IBL[:, 4 * bq + gi_, :], USS[:, ch, kk * TL:(kk + 1) * TL], True, True, r=["BL", ur], w=[pBr])

            def stage_F_dve(n):
                k, bq = n // 4, n % 4
                gs = slice(4 * bq, 4 * bq + 4)
                wb = Wb[n % 2]
                wbr = ("Wb", n % 2)
                pA, pAr = pAs[n % 2]
                pB, pBr = pBs[n % 2]
                pA3 = pA[:, :].rearrange("p (a b) -> p a b", a=4)
                pB3 = pB[:, :].rearrange("p (a b) -> p a b", a=4)
                tt("dve", wb, pA3, TC[:, gs, 0:TL], ALU.mult, r=[pAr, V], w=[wbr])
                tt("dve", T2, pB3, TS[:, gs, 0:TL], ALU.mult, r=[pBr, V], w=["T2"])
                tt("dve", wb, wb, T2, ALU.subtract, r=["T2"], w=[wbr])

            def stage_S(n):
                k, bq = n // 4, n % 4
                wb = Wb[n % 2]
                wbr = ("Wb", n % 2)
                stb = STb[n % 2]
                stbr = ("STb", n % 2)
                sbh = SBh[n % 2]
                sbhr = ("SBh", n % 2)
                for gi_ in range(4):
                    g = 4 * bq + gi_
                    ini = CI[:, g:g + 1] if k > 0 else 0.0
                    S.op("dve", lambda e, gi_=gi_, g=g, ini=ini: e.tensor_tensor_scan(
                        out=stb[:, gi_, :], data0=rho[:, g:g + 1].to_broadcast([128, TL]), data1=wb[:, gi_, :],
                        initial=ini, op0=ALU.mult, op1=ALU.add), r=[wbr, V, ("CI", bq)], w=[stbr])
                cp("act", sbh, stb, r=[stbr], w=[sbhr])
                mm(pC[:, :], jswb[:], sbh.rearrange("p a b -> p (a b)"), True, True, r=["jswb", sbhr], w=[pCr])
                if k < 15:
                    mm(pL[:, 0:4], jsw[:], stb[:, :, TL - 1], True, True, r=["jsw", stbr], w=[pLr])

            def stage_B(n):
                k, bq = n // 4, n % 4
                kk, ch = k % 4, bq // 2
                tsi = k // 4
                gs = slice(4 * bq, 4 * bq + 4)
                stb = STb[n % 2]
                stbr = ("STb", n % 2)
                sbh = SBh[n % 2]
                sbhr = ("SBh", n % 2)
                sb = SB[n % 2]
                sbr = ("SB", n % 2)
                USS = USS2[tsi % 2]
                ur = ("USS", tsi % 2)
                tt("dve", T2b, pC3, TS[:, gs, 0:TL], ALU.mult, r=[pCr, V], w=["T2b"])
                tt("dve", S1, sbh, TCb[:, gs, 0:TL], ALU.mult, r=[sbhr, V], w=["S1"])
                tt("dve", sb, S1, T2b, ALU.add, r=["S1", "T2b"], w=[sbr])
                if k < 15:
                    tt("dve", CA, pL[:, 0:4], TS[:, gs, TL], ALU.mult, r=[pLr, V], w=["CA"])
                    tt("dve", CB, stb[:, :, TL - 1], TC[:, gs, TL], ALU.mult, r=[stbr, V], w=["CB"])
                    tt("dve", CI[:, gs], CA, CB, ALU.add, r=["CA", "CB"], w=[("CI", bq)])
                pY, pYr = ps(5)
                for gi_ in range(4):
                    g = 4 * bq + gi_
                    mm(pY[:, ch * TL:(ch + 1) * TL], CL[:, g, :], sb[:, gi_, :], g % 8 == 0, g % 8 == 7, r=["CL", sbr], w=[pYr])
                if bq == 3:
                    glu_a(k)
                if bq == 0 and k > 0:
                    glu_b(k - 1)

            def glu_a(k):
                kk, tsi = k % 4, k // 4
                USS = USS2[tsi % 2]
                ur = ("USS", tsi % 2)
                cols = slice(kk * TL, (kk + 1) * TL)
                for c2 in range(2):
                    pY2, pY2r = ps(5)
                    stt(YV[:, c2, :], USS[:, c2, cols], sdcol[:, c2:c2 + 1], pY2[:, c2 * TL:(c2 + 1) * TL], ALU.mult, ALU.add, r=[ur, "sdcol", pY2r], w=["YV"])
                act(G1, YV, AF.Square, r=["YV"], w=["G1"])
                ts("pool", G1, G1, 0.044715, 1.0, ALU.mult, ALU.add, r=["G1"], w=["G1"])
                tt("pool", G1, G1, YV, ALU.mult, r=["G1", "YV"], w=["G1"])
                act(G2, G1, AF.Sigmoid, r=["G1"], w=["G2"], scale=2.0 * math.sqrt(2.0 / math.pi))
                tt("pool", GE, YV, G2, ALU.mult, r=["G2", "YV"], w=["GE"])

            def glu_b(k):
                tsi = k // 4
                tok = slice(k * TL, (k + 1) * TL)
                for dch in range(2):
                    pG, pGr = ps(6)
                    for c2 in range(2):
                        mm(pG[:, 128:128 + TL], GW[:, c2, dch * 128:(dch + 1) * 128], GE[:, c2, :], c2 == 0, c2 == 1, r=["GW", "GE"], w=[pGr])
                    act_b(G1[:, dch, :], pG[:, 128:128 + TL], AF.Sigmoid, r=[pGr, "glub", "G1"], w=["G1"], bias=glub[:, dch:dch + 1], scale=1.0)
                    tt("pool", YT[:, 4 + dch, tok], YV[:, dch, :], G1[:, dch, :], ALU.mult, r=["G1", "YV"], w=[("YT", 4 + dch, tsi)])

            stage_F_pe(0)
            for it in range(NB + 2):
                if it + 1 < NB:
                    stage_F_pe(it + 1)
                if it < NB:
                    stage_F_dve(it)
                if 0 <= it - 2 < NB:
                    stage_B(it - 2)
                if 0 <= it - 1 < NB:
                    stage_S(it - 1)
            glu_b(15)

        def bias_setup():
            ar = Arena(0, ARB)
            relb = ar.alloc([32, 8], F32)
            OH = ar.alloc([32, 1152], F32)
            NG = ar.alloc([8, 1152], F32)
            FV = ar.alloc([8, 1152], F32)
            dma("sp", relb, relb_d, w=["relb"])
            dma("sp", OH, oh_d, w=["OH"])
            dma("sp", NG, negm_d, w=["NG"])
            for br in range(3):
                p_, pr_ = ps(br)
                mm(p_[0:8, 0:384], relb, OH[:, br * 384:(br + 1) * 384], True, True, r=["relb", "OH"], w=[pr_])
                tt("dve", FV[:, br * 384:(br + 1) * 384], p_[0:8, 0:384], NG[:, br * 384:(br + 1) * 384], ALU.add, r=[pr_, "NG"], w=["FV"])
            dma("sp", fv_d, FV, r=["FV"], w=["fv_d"])

        def attn_phase(l):
            ar = Arena(0, ARB)
            WQKV = ar.alloc([128, 8, 384], BF16)
            QZ = [ar.alloc([128, SEQ], BF16) for _ in range(2)]
            KT = ar.alloc([128, SEQ], BF16)
            VP = ar.alloc([128, 3, 16, 192], BF16)
            BTp = ar.alloc([128, 2, 768], BF16)
            Hb = ar.alloc([128, 256], F32)
            PT = [ar.alloc([128, 128], BF16) for _ in range(4)]
            RD = [ar.alloc([128, 512], F32) for _ in range(1)]
            VT = ar.alloc([128, SEQ], BF16)
            memset("pool", QZ[0][64:128, :], 0.0, w=[("QZ", 0)])
            memset("pool", QZ[1][0:64, :], 0.0, w=[("QZ", 1)])
            memset("pool", VP[:, :, :, 64:128], 1.0, w=["VP"])
            npt = 0
            nsc = 0
            nrd = 0
            for hp in range(4):
                for j, base in enumerate((0, 512, 1024)):
                    dma("pool", WQKV[:, :, j * 128:(j + 1) * 128],
                        win_d[l, :, base + hp * 128:base + (hp + 1) * 128].rearrange("(kc p) f -> p kc f", p=128), w=["WQKV"])
                for tsi in range(4):
                    pq, pqr = ps(6)
                    for kc in range(8):
                        mm(pq[:, :], WQKV[:, kc, 0:128], HT[:, kc, tsi * 512:(tsi + 1) * 512], kc == 0, kc == 7, r=["WQKV", ("HT", tsi)], w=[pqr])
                    act(QZ[0][0:64, tsi * 512:(tsi + 1) * 512], pq[0:64, :], AF.Copy, r=[pqr], w=[("QZ", 0)], scale=0.125)
                    act(QZ[1][64:128, tsi * 512:(tsi + 1) * 512], pq[64:128, :], AF.Copy, r=[pqr], w=[("QZ", 1)], scale=0.125)
                    pk, pkr = ps(7)
                    for kc in range(8):
                        mm(pk[:, :], WQKV[:, kc, 128:256], HT[:, kc, tsi * 512:(tsi + 1) * 512], kc == 0, kc == 7, r=["WQKV", ("HT", tsi)], w=[pkr])
                    cp("dve", KT[:, tsi * 512:(tsi + 1) * 512], pk[:, :], r=[pkr], w=["KT"])
                for tsi in range(4):
                    pvt, pvtr = ps(6 + tsi % 2)
                    for kc in range(8):
                        mm(pvt[:, :], WQKV[:, kc, 256:384], HT[:, kc, tsi * 512:(tsi + 1) * 512], kc == 0, kc == 7, r=["WQKV", ("HT", tsi)], w=[pvtr])
                    cp("act", VT[:, tsi * 512:(tsi + 1) * 512], pvt[:, :], r=[pvtr], w=["VT"])
                nv = 0
                for br, (win, dil) in enumerate(PATTERNS):
                    nbk = 16 // dil
                    for q4 in range(4):
                        pv, pvr = ps(6 + nv % 2)
                        nv += 1
                        pvb = pv[:, :].bitcast(BF16)
                        for t4 in range(4):
                            tix = q4 * 4 + t4
                            rr, m = tix // nbk, tix % nbk
                            t0 = rr + dil * 128 * m
                            tr(pvb[:, t4 * 128:(t4 + 1) * 128], VT[:, t0:t0 + dil * 127 + 1:dil], ident[:], r=["VT", "ident"], w=[pvr])
                        outv = VP[:, br, q4 * 4:(q4 + 1) * 4, :].rearrange("p t (a c) -> p t a c", a=3)[:, :, 0:3:2, :]
                        inv_ = pvb[:, 0:512].rearrange("p (t a c) -> p t a c", t=4, a=2)
                        cp("dve", outv, inv_, r=[pvr], w=["VP"])
                for hh in range(2):
                    h = 2 * hp + hh
                    for br in range(3):
                        dma("sp", Hb, bass.AP(fv_d.tensor, h * 1152 + br * 384, [[1, 128], [1, 256]]), r=["fv_d"], w=["Hb"])
                        pb_, pbr_ = ps(6 + br % 2)
                        mm(pb_[:, 0:256], jex[:], Hb, True, True, r=["jex", "Hb"], w=[pbr_])
                        act(BTp[:, hh, br * 256:(br + 1) * 256], pb_[:, 0:256], AF.Exp, r=[pbr_], w=["BTp"])
                for hh in range(2):
                    accs = [ps(b) for b in range(4)]
                    started = [False] * 4
                    tasks = []
                    for br, (win, dil) in enumerate(PATTERNS):
                        nbk = 16 // dil
                        for rr in range(dil):
                            for m in range(nbk):
                                for qb in (m, m + 1):
                                    if qb >= nbk:
                                        continue
                                    tasks.append((br, dil, nbk, rr, m, qb))
                    pendq = []

                    def pieces_of(task):
                        br, dil, nbk, rr, m, qb = task
                        if dil == 16:
                            return [(b, rr, 32 * b, 32) for b in range(4)]
                        elif dil == 4:
                            return [(qb, rr, 0, 128)]
                        t0 = 128 * qb
                        return [(t0 // 512, t0 % 512, 0, 128)]

                    remaining = [0] * 4
                    for task in tasks:
                        for (b, c0, i0, n) in pieces_of(task):
                            remaining[b] += 1

                    def do_pv(task, slot):
                        br, dil, nbk, rr, m, qb = task
                        tix = rr * nbk + m
                        lhs = VP[:, br, tix, hh * 64:hh * 64 + 128]
                        for (b, c0, i0, n) in pieces_of(task):
                            acc, accr = accs[b]
                            remaining[b] -= 1
                            mm(acc[:, c0:c0 + dil * (n - 1) + 1:dil], lhs, PT[slot][:, i0:i0 + n], not started[b], remaining[b] == 0,
                               r=["VP", ("PT", slot)], w=[accr])
                            started[b] = True

                    for task in tasks:
                        br, dil, nbk, rr, m, qb = task
                        k0 = rr + dil * 128 * m
                        q0 = rr + dil * 128 * qb
                        psc, pscr = ps(4 + nsc % 2)
                        nsc += 1
                        mm(psc[:, 0:128], KT[:, k0:k0 + dil * 127 + 1:dil], QZ[hh][:, q0:q0 + dil * 127 + 1:dil], True, True,
                           r=["KT", ("QZ", hh)], w=[pscr])
                        off = (qb - m) * 128
                        slot = npt % 4
                        npt += 1
                        act(PT[slot], psc[:, 0:128], AF.Exp, r=[pscr], w=[("PT", slot)])
                        tt("dve", PT[slot], PT[slot], BTp[:, hh, br * 256 + off:br * 256 + off + 128], ALU.mult, r=["BTp", ("PT", slot)], w=[("PT", slot)])
                        pendq.append((task, slot))
                        if len(pendq) > 3:
                            do_pv(*pendq.pop(0))
                    while pendq:
                        do_pv(*pendq.pop(0))
                    for b in range(4):
                        acc, accr = accs[b]
                        rd = RD[0]
                        rdr = ("RD", 0)
                        nrd += 1
                        if hh == 0:
                            act(rd[0:64, :], acc[64:128, :], AF.Ln, r=[accr], w=[rdr])
                            act(rd[0:64, :], rd[0:64, :], AF.Exp, r=[rdr], w=[rdr], scale=-1.0)
                            tt("dve", YT[0:64, hp, b * 512:(b + 1) * 512], acc[0:64, :], rd[0:64, :], ALU.mult, r=[accr, rdr], w=[("YT", hp, b)])
                        else:
                            act(rd[64:128, :], acc[0:64, :], AF.Ln, r=[accr], w=[rdr])
                            act(rd[64:128, :], rd[64:128, :], AF.Exp, r=[rdr], w=[rdr], scale=-1.0)
                            tt("dve", YT[64:128, hp, b * 512:(b + 1) * 512], acc[64:128, :], rd[64:128, :], ALU.mult, r=[accr, rdr], w=[("YT", hp, b)])

        def wout_phase(l, tail):
            ar = Arena(0, ARB)
            WO = ar.alloc([128, 8, D], BF16)
            GBC = ar.alloc([128, D], F32)
            D8 = ar.alloc([128, 8, 128], F32)
            dma("pool", WO, wout_d[l].rearrange("(kc p) d -> p kc d", p=128), w=["WO"])
            gate_bc(l, 1, 1.0 / ALPHA, GBC, D8)
            for kc in range(8):
                tt("pool", WO[:, kc, :], WO[:, kc, :], GBC, ALU.mult, r=["GBC", "WO"], w=["WO"])
            n = 0
            for tt_ in range(NT):
                for hf in range(2):
                    po, por = ps(n % 2)
                    n += 1
                    for kc in range(8):
                        mm(po[:, :], YT[:, kc, tt_ * 128:(tt_ + 1) * 128], WO[:, kc, hf * 512:(hf + 1) * 512], kc == 0, kc == 7,
                           r=["WO"] + [("YT", kc, b) for b in range(4)], w=[por])
                    tt("dve", X[:, tt_, hf * 512:(hf + 1) * 512], X[:, tt_, hf * 512:(hf + 1) * 512], po[:, :], ALU.add,
                       r=[("X", tt_), por], w=[("X", tt_)])
                tail.tile_done(tt_)
                tail.lagged(2)
            tail.lagged(0)

        def mixer(l, tail):
            if "pool" in parts:
                pool_phase(l)
                fence()
            if "ssm" in parts:
                ssm_phase(l)
                fence()
            if "attn" in parts:
                attn_phase(l)
                fence()
            if dbg:
                return
            wout_phase(l, tail)

        with arena_scope():
            ada_phase()
        fence_all()
        with arena_scope():
            bias_setup()
        run_layers()
        if dbg:
            fence_all()
            dbg_d = nc.dram_tensor("dbg", [128, 8, SEQ], F32, kind="ExternalOutput").ap()
            dma("pool", dbg_d, YT[:, :, :], r=["ALL"], final=True)
        if dbg or stop is not None:
            fence_all()
            for q in range(4):
                dma("sp", ov[:, q * 4:(q + 1) * 4, :], X[:, q * 4:(q + 1) * 4, :], r=[("X", t) for t in range(q * 4, q * 4 + 4)], final=True)
        S.emit()
    return nc


def host_inputs(inputs):
    f = lambda a: np.ascontiguousarray(np.asarray(a, dtype=np.float32))
    consts = static_consts()
    shared = {}
    for k in ("ada_w", "ada_b", "ln_g", "ln_b", "ffn_w_gate", "ffn_w_up", "ffn_w_down", "w_in", "w_out", "rel_bias"):
        shared[k] = f(inputs[k])
    a_re, a_im = f(inputs["ssm_a_re"]), f(inputs["ssm_a_im"])
    dup = lambda a: np.concatenate([a, a], axis=1)
    shared["sa_re"] = f(dup(a_re.transpose(0, 2, 1)))
    shared["sa_im"] = f(dup(a_im.transpose(0, 2, 1)))
    shared["sldt"] = f(np.broadcast_to(f(inputs["ssm_log_dt"])[:, None, :], (DEPTH, 128, 16)))
    b_re = f(inputs["ssm_b_re"]).transpose(0, 2, 1, 3).reshape(DEPTH, 64, 256)
    b_im = f(inputs["ssm_b_im"]).transpose(0, 2, 1, 3).reshape(DEPTH, 64, 256)
    shared["sP1"] = f(np.concatenate([b_re, b_im], axis=1))
    shared["sP2"] = f(np.concatenate([b_im, b_re], axis=1))
    c_re = f(inputs["ssm_c_re"]).transpose(0, 3, 1, 2).reshape(DEPTH, 64, 256)
    c_im = f(inputs["ssm_c_im"]).transpose(0, 3, 1, 2).reshape(DEPTH, 64, 256)
    shared["sCT"] = f(np.concatenate([c_re, c_im], axis=1))
    col2 = lambda a: f(f(a).reshape(DEPTH, 2, 128).transpose(0, 2, 1))
    shared["sd"] = col2(inputs["ssm_d"])
    shared["glu_b"] = col2(inputs["glu_b"])
    shared["pool_s"] = col2(inputs["pool_scale"])
    shared["glu_w"] = f(f(inputs["glu_w"]).reshape(DEPTH, 2, 128, 256).transpose(0, 2, 1, 3))
    pw = f(inputs["pool_w"])
    pbd = np.zeros((DEPTH, 128, 2, 128), np.float32)
    for ch in range(2):
        for h in range(2):
            pbd[:, h * 64:(h + 1) * 64, ch, h * 64:(h + 1) * 64] = pw[:, ch * 2 + h]
    shared["pool_w"] = pbd
    shared.update(consts)
    x = f(inputs["x"])
    c = f(inputs["c"])
    maps = []
    for b in range(8):
        m = dict(shared)
        m["x"] = x[b]
        m["cT"] = f(c[b].reshape(8, 128).T)
        maps.append(m)
    return maps


_NC_CACHE = {}


def kernel(**inputs):
    if "nc" not in _NC_CACHE:
        _NC_CACHE["nc"] = build()
    nc = _NC_CACHE["nc"]
    maps = host_inputs(inputs)
    res = run_bass_kernel_spmd(nc, maps, core_ids=list(range(8)))
    return np.stack([np.asarray(r["out"], dtype=np.float32) for r in res.results], axis=0)
```

```python
import contextlib
import math
import numpy as np
import concourse.bass as bass
import concourse.mybir as mybir
from concourse.bass_utils import run_bass_kernel_spmd

F32 = mybir.dt.float32
BF16 = mybir.dt.bfloat16
I32 = mybir.dt.int32
AF = mybir.ActivationFunctionType
ALU = mybir.AluOpType

SEQ = 2048
D = 1024
DFF = 2816
DEPTH = 2
NT = SEQ // 128
ALPHA = (2 * DEPTH) ** 0.25
LN_EPS = 1e-5
NEG = -1e30
PATTERNS = ((128, 1), (512, 4), (2048, 16))

ENGS = ("pe", "act", "dve", "pool", "sp")
NDMASEM = 12


class Op:
    __slots__ = ("eng", "fn", "deps", "marked", "val", "is_dma", "dsem", "dval", "idx")

    def __init__(self, eng, fn, is_dma):
        self.eng = eng
        self.fn = fn
        self.deps = []
        self.marked = False
        self.val = None
        self.is_dma = is_dma
        self.dsem = None
        self.dval = None
        self.idx = None


class Sched:
    def __init__(self, nc):
        self.nc = nc
        self.ops = {e: [] for e in ENGS}
        self.writers = {}
        self.readers = {}
        self.ndma = {e: 0 for e in ENGS}
        self.final_waits = []
        self.scope = None

    def op(self, eng, fn, r=(), w=(), dma=False, final=False):
        o = Op(eng, fn, dma)
        deps = []
        r = list(r)
        w = list(w)
        if "ALL" not in w:
            r.append("ALL")
        if self.scope is not None and self.scope not in w:
            r.append(self.scope)
        for x in r:
            deps.extend(self.writers.get(x, ()))
        for x in w:
            deps.extend(self.writers.get(x, ()))
            deps.extend(self.readers.get(x, ()))
        seen = set()
        for d in deps:
            if d is o or id(d) in seen:
                continue
            seen.add(id(d))
            if d.eng == "pe" and eng == "pe" and not d.is_dma and not dma:
                continue
            o.deps.append(d)
            if not d.is_dma:
                d.marked = True
        for x in w:
            if self.readers.get(x):
                self.writers[x] = [o]
                self.readers[x] = []
            else:
                ws = self.writers.setdefault(x, [])
                ws[:] = [p for p in ws if not (p.eng == eng and p.is_dma == dma and not dma)]
                ws.append(o)
        for x in r:
            if x not in w:
                rs = self.readers.setdefault(x, [])
                rs[:] = [p for p in rs if not (p.eng == eng and not p.is_dma and not dma)]
                rs.append(o)
        if dma:
            j = self.ndma[eng]
            self.ndma[eng] = j + 1
            o.dsem = j % NDMASEM
            o.dval = 16 * (j // NDMASEM + 1)
            o.idx = j
        self.ops[eng].append(o)
        if final:
            self.final_waits.append(o)
        return o

    def emit(self):
        nc = self.nc
        with contextlib.ExitStack() as st:
            csem = {e: st.enter_context(nc.semaphore("c_" + e)) for e in ENGS}
            dsem = {e: [st.enter_context(nc.semaphore("d_%s_%d" % (e, i))) for i in range(NDMASEM)]
                    for e in ENGS if self.ndma[e] > 0}
            for e in ENGS:
                c = 0
                for o in self.ops[e]:
                    if o.marked and not o.is_dma:
                        c += 1
                        o.val = c
            block = st.enter_context(nc.Block())

            def run(e, eng):
                waited = {}
                for o in self.ops[e]:
                    waits = []
                    for d in o.deps:
                        if d.is_dma:
                            waits.append((("d", d.eng, d.dsem), dsem[d.eng][d.dsem], d.dval))
                        else:
                            waits.append((("c", d.eng), csem[d.eng], d.val))
                    if o.is_dma and o.idx >= NDMASEM:
                        waits.append((("d", e, o.dsem), dsem[e][o.dsem], o.dval - 16))
                    for key, s, v in waits:
                        if waited.get(key, 0) >= v:
                            continue
                        eng.wait_ge(s, v)
                        waited[key] = v
                    ins = o.fn(eng)
                    if o.is_dma:
                        ins.then_inc(dsem[e][o.dsem], 16)
                    elif o.marked:
                        ins.then_inc(csem[e], 1)
                for o in self.final_waits:
                    if o.eng == e:
                        eng.wait_ge(dsem[e][o.dsem], o.dval)

            if self.ops["sp"]:
                @block.sync
                def _(eng):
                    run("sp", eng)
            if self.ops["pe"]:
                @block.tensor
                def _(eng):
                    run("pe", eng)
            if self.ops["act"]:
                @block.scalar
                def _(eng):
                    run("act", eng)
            if self.ops["dve"]:
                @block.vector
                def _(eng):
                    run("dve", eng)
            if self.ops["pool"]:
                @block.gpsimd
                def _(eng):
                    run("pool", eng)


def t5_bucket(dist):
    n_buckets, max_distance = 32, 2048
    max_exact = n_buckets // 2
    d = np.maximum(dist, 1).astype(np.float32)
    large = max_exact + (np.log(d / max_exact) / math.log(max_distance / max_exact)
                         * (n_buckets - max_exact)).astype(np.int32)
    large = np.minimum(large, n_buckets - 1)
    return np.where(dist < max_exact, dist, large).astype(np.int32)


def static_consts():
    c = {}
    c["identf"] = np.eye(128, dtype=np.float32)
    c["jex"] = np.eye(128, dtype=np.float32)[::-1].copy()
    jsw = np.zeros((128, 128), np.float32)
    for m in range(64):
        jsw[m + 64, m] = -1.0
        jsw[m, m + 64] = 1.0
    c["jsw"] = jsw
    oh = np.zeros((32, 3 * 384), np.float32)
    negm = np.zeros((8, 3 * 384), np.float32)
    for bi, (win, dil) in enumerate(PATTERNS):
        for u in range(384):
            dist = u - 127
            if 0 <= dist <= win // dil:
                oh[t5_bucket(np.array([dist * dil]))[0], bi * 384 + u] = 1.0
            else:
                negm[:, bi * 384 + u] = NEG
    c["oh"] = oh
    c["negm"] = negm
    gm = np.zeros((128, 8), np.float32)
    for p in range(128):
        gm[p, p // 16] = 1.0
    c["gmask"] = gm
    wins = np.array([2, 4, 8, 16], np.float32)
    wp = np.zeros((128, 2), np.float32)
    for ch in range(2):
        for p in range(128):
            wp[p, ch] = wins[ch * 2 + p // 64]
    c["invw"] = (1.0 / wp).astype(np.float32)
    t = np.arange(16, dtype=np.float32)[None, None, :]
    c["rc16"] = (1.0 / np.minimum(t + 1.0, wp[:, :, None])).astype(np.float32)
    c["sgn"] = np.concatenate([-np.ones((64, 1), np.float32), np.ones((64, 1), np.float32)], 0)
    return c


def build(stop=None, parts=("pool", "ssm", "attn"), dbg=False):
    nc = bass.Bass("TRN2", target_bir_lowering=False)

    def din(name, shape, dt=F32):
        return nc.dram_tensor(name, list(shape), dt, kind="ExternalInput").ap()

    x_d = din("x", [SEQ, D])
    cT_d = din("cT", [128, 8])
    adaw_d = din("ada_w", [DEPTH, D, 9 * D])
    adab_d = din("ada_b", [DEPTH, 9 * D])
    lng_d = din("ln_g", [DEPTH, 3, D])
    lnb_d = din("ln_b", [DEPTH, 3, D])
    wg_d = din("ffn_w_gate", [DEPTH, 2, D, DFF])
    wu_d = din("ffn_w_up", [DEPTH, 2, D, DFF])
    wd_d = din("ffn_w_down", [DEPTH, 2, DFF, D])
    win_d = din("w_in", [DEPTH, D, 2048])
    wout_d = din("w_out", [DEPTH, D, D])
    relb_d = din("rel_bias", [32, 8])
    sare_d = din("sa_re", [DEPTH, 128, 16])
    saim_d = din("sa_im", [DEPTH, 128, 16])
    sldt_d = din("sldt", [DEPTH, 128, 16])
    sp1_d = din("sP1", [DEPTH, 128, 256])
    sp2_d = din("sP2", [DEPTH, 128, 256])
    sct_d = din("sCT", [DEPTH, 128, 256])
    sd_d = din("sd", [DEPTH, 128, 2])
    gluw_d = din("glu_w", [DEPTH, 128, 2, 256])
    glub_d = din("glu_b", [DEPTH, 128, 2])
    poolw_d = din("pool_w", [DEPTH, 128, 2, 128])
    pools_d = din("pool_s", [DEPTH, 128, 2])
    identf_d = din("identf", [128, 128])
    jex_d = din("jex", [128, 128])
    jsw_d = din("jsw", [128, 128])
    oh_d = din("oh", [32, 1152])
    negm_d = din("negm", [8, 1152])
    gmask_d = din("gmask", [128, 8])
    invw_d = din("invw", [128, 2])
    rc16_d = din("rc16", [128, 2, 16])
    sgn_d = din("sgn", [128, 1])
    out_d = nc.dram_tensor("out", [SEQ, D], F32, kind="ExternalOutput").ap()
    fv_d = nc.dram_tensor("fv_scratch", [8, 1152], F32, kind="Internal").ap()

    st = contextlib.ExitStack()
    with st:
        def T(name, shape, dt):
            return st.enter_context(nc.sbuf_tensor(name, list(shape), dt))

        PSB = [st.enter_context(nc.psum_tensor("ps%d" % i, [128, 512], F32)) for i in range(8)]

        def ps(k):
            return PSB[k], ("ps", k)

        X = T("X", [128, NT, D], F32)
        HT = T("HT", [128, 8, SEQ], BF16)
        YT = T("YT", [128, 8, SEQ], BF16)
        ARB = 49152
        AR = T("AR", [128, ARB // 2], BF16)
        ident = T("ident", [128, 128], BF16)
        identf = T("identf_s", [128, 128], F32)
        jex = T("jex_s", [128, 128], F32)
        jsw = T("jsw_s", [128, 128], F32)
        onesf = T("onesf", [128, 128], F32)
        jswb = T("jswb", [128, 128], BF16)
        modcol = T("modcol", [128, DEPTH, 72], F32)
        XH = [T("XH%d" % i, [128, D], BF16) for i in range(4)]
        mv = T("mv", [128, 4, 2], F32)
        sc = T("sc", [128, 4, 4], F32)
        condT = T("condT", [128, 8], F32)
        gmask = T("gmask_s", [128, 8], F32)
        invw = T("invw_s", [128, 2], F32)
        rc16 = T("rc16_s", [128, 2, 16], F32)
        sgn = T("sgn_s", [128, 1], F32)

        S = Sched(nc)

        class Arena:
            def __init__(self, base, size):
                self.off = base
                self.end = base + size

            def alloc(self, shape, dt):
                n = int(np.prod(shape[1:]))
                nb = n * (4 if dt in (F32, I32) else 2)
                nb = (nb + 31) // 32 * 32
                assert self.off + nb <= self.end, ("arena overflow", shape, self.off, nb, self.end)
                v = AR[0:shape[0], self.off // 2:(self.off + nb) // 2]
                self.off += nb
                if dt != BF16:
                    v = v.bitcast(dt)
                v = v[:, 0:n]
                if len(shape) == 3:
                    v = v.rearrange("p (a b) -> p a b", a=shape[1])
                elif len(shape) == 4:
                    v = v.rearrange("p (a b c) -> p a b c", a=shape[1], b=shape[2])
                return v

        class ArenaYT(Arena):
            def alloc(self, shape, dt):
                n = int(np.prod(shape[1:]))
                nb = n * (4 if dt in (F32, I32) else 2)
                nb = (nb + 31) // 32 * 32
                assert self.off + nb <= self.end
                flat = YT[:, :, :].rearrange("p a b -> p (a b)")
                v = flat[0:shape[0], self.off // 2:(self.off + nb) // 2]
                self.off += nb
                if dt != BF16:
                    v = v.bitcast(dt)
                v = v[:, 0:n]
                if len(shape) == 3:
                    v = v.rearrange("p (a b) -> p a b", a=shape[1])
                return v

        def dma(eng, out, in_, r=(), w=(), final=False, slow=False):
            if slow:
                return S.op(eng, lambda e: e.dma_start(out=out, in_=in_, allow_slow_non_contiguous=True), r=r, w=w, dma=True, final=final)
            return S.op(eng, lambda e: e.dma_start(out=out, in_=in_), r=r, w=w, dma=True, final=final)

        def mm(out, lhsT, rhs, start, stop, r, w):
            return S.op("pe", lambda e: e.matmul(out, lhsT=lhsT, rhs=rhs, start=start, stop=stop), r=r, w=w)

        def tr(out, in_, idn, r, w):
            return S.op("pe", lambda e: e.transpose(out=out, in_=in_, identity=idn), r=r, w=w)

        def act(out, in_, func, r, w, bias=0.0, scale=1.0):
            return S.op("act", lambda e: e.activation(out=out, in_=in_, func=func, bias=bias, scale=scale), r=r, w=w)

        def ts(eng, out, in0, s1, s2, op0, op1, r, w):
            if op1 is None:
                return S.op(eng, lambda e: e.tensor_scalar(out=out, in0=in0, scalar1=s1, scalar2=None, op0=op0), r=r, w=w)
            return S.op(eng, lambda e: e.tensor_scalar(out=out, in0=in0, scalar1=s1, scalar2=s2, op0=op0, op1=op1), r=r, w=w)

        def tt(eng, out, in0, in1, op, r, w):
            return S.op(eng, lambda e: e.tensor_tensor(out=out, in0=in0, in1=in1, op=op), r=r, w=w)

        def stt(out, in0, scalar, in1, op0, op1, r, w):
            return S.op("dve", lambda e: e.scalar_tensor_tensor(out=out, in0=in0, scalar=scalar, in1=in1, op0=op0, op1=op1), r=r, w=w)

        def cp(eng, out, in_, r, w):
            if eng == "act":
                return S.op("act", lambda e: e.copy(out=out, in_=in_), r=r, w=w)
            return S.op(eng, lambda e: e.tensor_copy(out=out, in_=in_), r=r, w=w)

        def memset(eng, ap, val, w):
            return S.op(eng, lambda e: e.memset(ap, val), w=w)

        fsrc_d = identf_d[0:1, 0:16]
        fdst_d = nc.dram_tensor("fence_dst", [1, 16], F32, kind="Internal").ap()

        def fence():
            sv = S.scope
            S.scope = None
            S.op("sp", lambda e: e.dma_start(out=fdst_d, in_=fsrc_d), w=["ARENA"], dma=True)
            S.scope = sv

        def fence_all():
            S.op("dve", lambda e: e.memset(sc[:, 0, 3:4], 0.0), w=["ALL", "ARENA", ("sc3", 0)])

        class arena_scope:
            def __enter__(self):
                self.sv = S.scope
                S.scope = "ARENA"

            def __exit__(self, *a):
                S.scope = self.sv

        dma("sp", identf[:], identf_d, w=["identf"])
        dma("sp", jex[:], jex_d, w=["jex"])
        dma("sp", jsw[:], jsw_d, w=["jsw"])
        dma("sp", gmask[:], gmask_d, w=["gmask"])
        dma("sp", invw[:], invw_d, w=["invw"])
        dma("sp", rc16[:], rc16_d, w=["rc16"])
        dma("sp", sgn[:], sgn_d, w=["sgn"])
        cp("dve", ident[:], identf[:], r=["identf"], w=["ident"])
        cp("dve", jswb[:], jsw[:], r=["jsw"], w=["jswb"])
        memset("dve", onesf[:], 1.0, w=["onesf"])
        xv = x_d.rearrange("(t p) d -> p t d", p=128)
        for q in range(4):
            dma("sp", X[:, q * 4:(q + 1) * 4, :], xv[:, q * 4:(q + 1) * 4, :], w=[("X", t) for t in range(q * 4, q * 4 + 4)])

        def ada_phase():
            ar = Arena(0, ARB)
            AW = [ar.alloc([128, 8, 512], F32) for _ in range(2)]
            AB = [ar.alloc([1, 512], F32) for _ in range(2)]
            ROW = [ar.alloc([1, 512], F32) for _ in range(2)]
            cTs = ar.alloc([128, 8], F32)
            dma("sp", cTs, cT_d, w=["cTs"])
            act(condT[:], cTs, AF.Silu, r=["cTs"], w=["condT"])
            it = 0
            import itertools
            gen = itertools.chain(ssm_setup_gen(0), ssm_setup_gen(1))
            prep_q = list(range(NT))
            gen_done = [False]
            for l in range(DEPTH):
                for nb in range(18):
                    for _ in range(6):
                        if next(gen, "done") == "done":
                            gen_done[0] = True
                    b = it % 2
                    it += 1
                    src = adaw_d[l, :, nb * 512:(nb + 1) * 512].rearrange("(kc p) n -> p kc n", p=128)
                    dma("sp", AW[b], src, w=[("AW", b)])
                    dma("sp", AB[b], adab_d[l:l + 1, nb * 512:(nb + 1) * 512], w=[("AB", b)])
                    pr, prr = ps(b)
                    for kc in range(8):
                        mm(pr[0:1, :], condT[:, kc:kc + 1], AW[b][:, kc, :], kc == 0, False,
                           r=["condT", ("AW", b)], w=[prr])
                    mm(pr[0:1, :], onesf[0:1, 0:1], AB[b], False, True, r=["onesf", ("AB", b)], w=[prr])
                    ev = "dve" if gen_done[0] else "act"
                    cp(ev, ROW[b], pr[0:1, :], r=[prr], w=[("ROW", b)])
                    pc, pcr = ps(2 + b)
                    for j in range(4):
                        mm(pc[:, j:j + 1], ROW[b][0:1, j * 128:(j + 1) * 128], onesf[0:1, 0:1], True, True,
                           r=[("ROW", b), "onesf"], w=[pcr])
                    v = nb // 2
                    addc = 1.0 if v % 3 == 1 else 0.0
                    if ev == "act":
                        act(modcol[:, l, nb * 4:(nb + 1) * 4], pc[:, 0:4], AF.Identity, r=[pcr], w=[("modcol", l, v)], bias=float(addc))
                    else:
                        ts("dve", modcol[:, l, nb * 4:(nb + 1) * 4], pc[:, 0:4], float(addc), None, ALU.add, None, r=[pcr], w=[("modcol", l, v)])
                    if gen_done[0] and prep_q:
                        tq_ = prep_q.pop(0)
                        sv_ = S.scope
                        S.scope = None
                        prep_tile_a(0, 0, tq_)
                        prep_tile_b(0, 0, tq_, extra=[("ssc", 0), ("ssc", 1)])
                        S.scope = sv_
            for _ in gen:
                pass
            sv_ = S.scope
            S.scope = None
            while prep_q:
                tq_ = prep_q.pop(0)
                prep_tile_a(0, 0, tq_)
                prep_tile_b(0, 0, tq_, extra=[("ssc", 0), ("ssc", 1)])
            S.scope = sv_


        NSLOT = 4
        sums = T("sums", [128, NSLOT, 4], F32)

        def finish_stats(slot, eps, c0, c1):
            ts("dve", mv[:, slot, 0:1], sums[:, slot, c0:c0 + 1], 1.0 / D, None, ALU.mult, None, r=[("sums", slot, c0)], w=[("mv", slot)])
            tt("dve", mv[:, slot, 1:2], mv[:, slot, 0:1], mv[:, slot, 0:1], ALU.mult, r=[("mv", slot)], w=[("mv1", slot)])
            stt(sc[:, slot, 3:4], sums[:, slot, c1:c1 + 1], 1.0 / D, mv[:, slot, 1:2], ALU.mult, ALU.subtract,
                r=[("sums", slot, c1), ("mv1", slot)], w=[("sc3", slot)])
            act(sc[:, slot, 0:1], sc[:, slot, 3:4], AF.Sqrt, r=[("sc3", slot)], w=[("sc0", slot)], bias=float(eps))
            S.op("dve", lambda e: e.reciprocal(out=sc[:, slot, 1:2], in_=sc[:, slot, 0:1]), r=[("sc0", slot)], w=[("sc1", slot)])

        def act_accum(tt_, slot, func, col):
            S.op("act", lambda e: e.activation(out=XH[slot][:], in_=X[:, tt_, :], func=func, accum_out=sums[:, slot, col:col + 1]),
                 r=[("X", tt_)], w=[("XH", slot), ("sums", slot, col)])

        def prep_tile_a(l, i, tt_, have_sum=False):
            slot = tt_ % NSLOT
            if not have_sum:
                act_accum(tt_, slot, AF.Identity, 2)
            act_accum(tt_, slot, AF.Square, 3)
            finish_stats(slot, LN_EPS, 2, 3)
            ts("dve", sc[:, slot, 2:3], mv[:, slot, 0:1], -1.0, sc[:, slot, 1:2], ALU.mult, ALU.mult,
               r=[("mv", slot), ("sc1", slot)], w=[("sc2", slot)])
            xh = XH[slot]
            act(xh[:], X[:, tt_, :], AF.Identity, r=[("X", tt_), ("sc1", slot), ("sc2", slot)], w=[("XH", slot)],
                bias=sc[:, slot, 2:3], scale=sc[:, slot, 1:2])

        def prep_tile_b(l, i, tt_, extra=()):
            slot = tt_ % NSLOT
            xh = XH[slot]
            pt, ptr = ps(6 + tt_ % 2)
            ptb = pt[:, :].bitcast(BF16)
            for kc in range(8):
                tr(ptb[:, kc * 128:(kc + 1) * 128], xh[:, kc * 128:(kc + 1) * 128], ident[:], r=[("XH", slot), "ident"], w=[ptr])
            for kc in range(8):
                scl = modcol[:, l, (3 * i + 1) * 8 + kc:(3 * i + 1) * 8 + kc + 1]
                shf = modcol[:, l, (3 * i) * 8 + kc:(3 * i) * 8 + kc + 1]
                if kc % 2 == 0:
                    ts("dve", HT[:, kc, tt_ * 128:(tt_ + 1) * 128], ptb[:, kc * 128:(kc + 1) * 128], scl, shf,
                       ALU.mult, ALU.add, r=[ptr, ("modcol", l, 3 * i), ("modcol", l, 3 * i + 1)] + list(extra), w=[("HT", tt_ // 4)])
                else:
                    act(HT[:, kc, tt_ * 128:(tt_ + 1) * 128], ptb[:, kc * 128:(kc + 1) * 128], AF.Identity,
                        r=[ptr, ("modcol", l, 3 * i), ("modcol", l, 3 * i + 1)] + list(extra), w=[("HT", tt_ // 4)], bias=shf, scale=scl)

        def prep(l, i):
            for tt_ in range(NT):
                prep_tile_a(l, i, tt_)
                prep_tile_b(l, i, tt_)

        LNGB = [T("LNG", [128, D], F32), T("LNB", [128, D], F32)]

        def post_setup(l, i):
            LNG, LNB = LNGB[0][:], LNGB[1][:]
            sv = S.scope
            S.scope = None
            dma("sp", LNG, bass.AP(lng_d.tensor, (l * 3 + i) * D, [[0, 128], [1, D]]), w=["LNG"])
            dma("sp", LNB, bass.AP(lnb_d.tensor, (l * 3 + i) * D, [[0, 128], [1, D]]), w=["LNB"])
            S.scope = sv
            return LNG, LNB

        def post_tile(l, i, tt_, LNG, LNB, want_sum):
            slot = tt_ % NSLOT
            act_accum(tt_, slot, AF.Identity, 0)
            act_accum(tt_, slot, AF.Square, 1)
            finish_stats(slot, LN_EPS / (ALPHA * ALPHA), 0, 1)
            stt(X[:, tt_, :], X[:, tt_, :], mv[:, slot, 0:1], LNG, ALU.subtract, ALU.mult,
                r=[("X", tt_), ("mv", slot), "LNG"], w=[("X", tt_)])
            if want_sum:
                S.op("dve", lambda e: e.scalar_tensor_tensor(out=X[:, tt_, :], in0=X[:, tt_, :], scalar=sc[:, slot, 1:2], in1=LNB,
                                                             op0=ALU.mult, op1=ALU.add, accum_out=sums[:, slot, 2:3]),
                     r=[("X", tt_), ("sc1", slot), "LNB"], w=[("X", tt_), ("sums", slot, 2)])
            else:
                stt(X[:, tt_, :], X[:, tt_, :], sc[:, slot, 1:2], LNB, ALU.mult, ALU.add,
                    r=[("X", tt_), ("sc1", slot), "LNB"], w=[("X", tt_)])

        ov = out_d.rearrange("(t p) d -> p t d", p=128)

        class Tail:
            def __init__(self, postli, prepli, final):
                self.postli, self.prepli, self.final = postli, prepli, final
                self.pend = []
                self.LNG, self.LNB = post_setup(*postli)

            def tile_done(self, tt_):
                sv = S.scope
                S.scope = None
                post_tile(self.postli[0], self.postli[1], tt_, self.LNG, self.LNB, self.prepli is not None)
                if self.prepli is not None:
                    prep_tile_a(self.prepli[0], self.prepli[1], tt_, have_sum=True)
                    self.pend.append(tt_)
                if self.final:
                    dma("sp", ov[:, tt_, :], X[:, tt_, :], r=[("X", tt_)], final=True)
                S.scope = sv

            def lagged(self, keep):
                sv = S.scope
                S.scope = None
                while len(self.pend) > keep:
                    prep_tile_b(self.prepli[0], self.prepli[1], self.pend.pop(0))
                S.scope = sv

        def gate_bc(l, i, scale, GBC, D8):
            for kc in range(8):
                col = (3 * i + 2) * 8 + kc
                ts("dve", D8[:, kc, :], identf[:], modcol[:, l, col:col + 1], float(scale), ALU.mult, ALU.mult,
                   r=["identf", ("modcol", l, 3 * i + 2)], w=["D8"])
            for hf in range(2):
                pg, pgr = ps(hf)
                mm(pg[:, :], onesf[:], D8[:, hf * 4:(hf + 1) * 4, :].rearrange("p a b -> p (a b)"), True, True, r=["onesf", "D8"], w=[pgr])
                cp("act", GBC[:, hf * 512:(hf + 1) * 512], pg[:, :], r=[pgr], w=["GBC"])

        def ffn(l, i, si, tail):
            ar = Arena(0, ARB)
            ay = ArenaYT(0, 32768)
            WG = [ay.alloc([128, 8, 512], BF16) for _ in range(2)]
            WU = [ay.alloc([128, 8, 512], BF16) for _ in range(2)]
            WD = [ar.alloc([128, 4, D], BF16) for _ in range(2)]
            ACTB = [ar.alloc([128, 4, 512], BF16) for _ in range(2)]
            SG = [ar.alloc([128, 512], F32) for _ in range(2)]
            GBC = ar.alloc([128, D], F32)
            D8 = ar.alloc([128, 8, 128], F32)
            gate_bc(l, si, 0.5 / ALPHA, GBC, D8)
            groups = [(0, 2)] + [(2 + g * 4, 4) for g in range(5)]
            pend = None
            nsg = 0
            for gi, (c0, nf) in enumerate(groups):
                b = gi % 2
                f0 = c0 * 128
                fw = nf * 128
                dma("pool", WG[b][:, :, 0:fw], wg_d[l, i, :, f0:f0 + fw].rearrange("(kc p) f -> p kc f", p=128), w=[("WG", b)])
                dma("pool", WU[b][:, :, 0:fw], wu_d[l, i, :, f0:f0 + fw].rearrange("(kc p) f -> p kc f", p=128), w=[("WU", b)])
                dma("pool", WD[b][:, 0:nf, :], wd_d[l, i, f0:f0 + fw, :].rearrange("(c p) d -> p c d", p=128), w=[("WD", b)])
                for c in range(nf):
                    tt("pool", WD[b][:, c, :], WD[b][:, c, :], GBC, ALU.mult, r=["GBC", ("WD", b)], w=[("WD", b)])
                for tsi in range(4):
                    ab = (gi * 4 + tsi) % 2
                    for c in range(nf):
                        pgk = (gi * 16 + tsi * 4 + c) % 2
                        pg, pgr = ps(pgk)
                        pu, pur = ps(2 + pgk)
                        for kc in range(8):
                            mm(pg[:, :], WG[b][:, kc, c * 128:(c + 1) * 128], HT[:, kc, tsi * 512:(tsi + 1) * 512], kc == 0, kc == 7,
                               r=[("WG", b), ("HT", tsi)], w=[pgr])
                        for kc in range(8):
                            mm(pu[:, :], WU[b][:, kc, c * 128:(c + 1) * 128], HT[:, kc, tsi * 512:(tsi + 1) * 512], kc == 0, kc == 7,
                               r=[("WU", b), ("HT", tsi)], w=[pur])
                        sgb = nsg % 2
                        nsg += 1
                        act(SG[sgb], pg[:, :], AF.Silu, r=[pgr], w=[("SG", sgb)])
                        tt("dve", ACTB[ab][:, c, :], SG[sgb], pu[:, :], ALU.mult, r=[("SG", sgb), pur], w=[("ACTB", ab, c)])
                    cur = (b, ab, nf, tsi, gi == len(groups) - 1)
                    if pend is not None:
                        down(pend, WD, ACTB, tail)
                    pend = cur
            down(pend, WD, ACTB, tail)
            tail.lagged(0)

        dcount = [0]

        def down(p, WD, ACTB, tail):
            b, ab, nf, tsi, last = p
            for t4 in range(4):
                tt_ = tsi * 4 + t4
                for hf in range(2):
                    k = 4 + dcount[0] % 2
                    dcount[0] += 1
                    pd, pdr = ps(k)
                    for c in range(nf):
                        mm(pd[:, :], ACTB[ab][:, c, t4 * 128:(t4 + 1) * 128], WD[b][:, c, hf * 512:(hf + 1) * 512], c == 0, c == nf - 1,
                           r=[("ACTB", ab, c), ("WD", b)], w=[pdr])
                    tt("dve", X[:, tt_, hf * 512:(hf + 1) * 512], X[:, tt_, hf * 512:(hf + 1) * 512], pd[:, :], ALU.add,
                       r=[("X", tt_), pdr], w=[("X", tt_)])
                if last:
                    tail.tile_done(tt_)
                    tail.lagged(2)

        stage = [0]

        def done_stage():
            stage[0] += 1
            return stop is not None and stage[0] >= stop

        def run_layers():
            for l in range(DEPTH):
                fence()
                with arena_scope():
                    ffn(l, 0, 0, Tail((l, 0), (l, 1), False))
                if done_stage():
                    return
                fence()
                with arena_scope():
                    mixer(l, Tail((l, 1), (l, 2), False) if not dbg else None)
                if dbg:
                    return
                if done_stage():
                    return
                fence()
                lastl = l == DEPTH - 1
                with arena_scope():
                    ffn(l, 1, 2, Tail((l, 2), None if lastl else (l + 1, 0), lastl))
                if done_stage():
                    return

        TWO_PI = 2.0 * math.pi

        def act_b(out, in_, func, r, w, bias, scale):
            return act(out, in_, func, r, w, bias=bias, scale=scale)

        def pool_phase(l):
            ar = Arena(0, ARB)
            WUP = ar.alloc([128, 8, 256], BF16)
            PW = ar.alloc([128, 2, 128], BF16)
            pscol = ar.alloc([128, 2], F32)
            UP = [ar.alloc([128, 2, 528], F32) for _ in range(2)]
            Pb = ar.alloc([128, 2, 528], F32)
            Qb = ar.alloc([128, 2, 528], F32)
            PL = ar.alloc([128, 2, 512], BF16)
            TM = ar.alloc([128, 2, 16], F32)
            dma("pool", WUP, win_d[l, :, 1792:2048].rearrange("(kc p) f -> p kc f", p=128), w=["WUP"])
            dma("pool", PW, poolw_d[l], w=["PW"])
            dma("sp", pscol, pools_d[l], w=["pscol"])
            memset("pool", UP[0][:, :, 0:16], 0.0, w=[("UP", 0)])
            for tsi in range(4):
                ub = UP[tsi % 2]
                ur = ("UP", tsi % 2)
                for ch in range(2):
                    pu, pur = ps(ch)
                    for kc in range(8):
                        mm(pu[:, :], WUP[:, kc, ch * 128:(ch + 1) * 128], HT[:, kc, tsi * 512:(tsi + 1) * 512], kc == 0, kc == 7,
                           r=["WUP", ("HT", tsi)], w=[pur])
                    cp("act", ub[:, ch, 16:528], pu[:, :], r=[pur], w=[ur])
                tt("pool", Pb[:, :, 1:528], ub[:, :, 1:528], ub[:, :, 0:527], ALU.add, r=[ur], w=["Pb"])
                tt("pool", Qb[64:128, 0, 3:528], Pb[64:128, 0, 3:528], Pb[64:128, 0, 1:526], ALU.add, r=["Pb"], w=["Qb"])
                tt("pool", Qb[:, 1, 3:528], Pb[:, 1, 3:528], Pb[:, 1, 1:526], ALU.add, r=["Pb"], w=["Qb"])
                tt("pool", Pb[:, 1, 7:528], Qb[:, 1, 7:528], Qb[:, 1, 3:524], ALU.add, r=["Qb"], w=["Pb"])
                tt("pool", Qb[64:128, 1, 15:528], Pb[64:128, 1, 15:528], Pb[64:128, 1, 7:520], ALU.add, r=["Pb"], w=["Qb"])
                srcs = [(Pb, 0, 64, 0), (Qb, 64, 128, 0), (Pb, 0, 64, 1), (Qb, 64, 128, 1)]
                for (sb, p0, p1, ch) in srcs:
                    stt(PL[p0:p1, ch, :], sb[p0:p1, ch, 16:528], invw[p0:p1, ch:ch + 1], ub[p0:p1, ch, 16:528], ALU.mult, ALU.subtract,
                        r=["Pb", "Qb", ur, "invw"], w=["PL"])
                    if tsi == 0:
                        tt("dve", TM[p0:p1, ch, :], sb[p0:p1, ch, 16:32], rc16[p0:p1, ch, :], ALU.mult, r=["Pb", "Qb", "rc16"], w=["TM"])
                        tt("dve", PL[p0:p1, ch, 0:16], TM[p0:p1, ch, :], ub[p0:p1, ch, 16:32], ALU.subtract, r=["TM", ur], w=["PL"])
                if tsi < 3:
                    cp("pool", UP[(tsi + 1) % 2][:, :, 0:16], ub[:, :, 512:528], r=[ur], w=[("UP", (tsi + 1) % 2)])
                for ch in range(2):
                    py, pyr = ps(2 + ch)
                    mm(py[:, :], PW[:, ch, :], PL[:, ch, :], True, True, r=["PW", "PL"], w=[pyr])
                    act_b(YT[:, 6 + ch, tsi * 512:(tsi + 1) * 512], py[:, :], AF.Identity, r=[pyr, "pscol"], w=[("YT", 6 + ch, tsi)],
                          bias=0.0, scale=pscol[:, ch:ch + 1])

        TL = 128
        NTB = 16 * (TL + 1)
        ssc_f = nc.dram_tensor("ssm_scr_f", [DEPTH, 128, 2 * NTB + 224], F32, kind="Internal").ap()
        ssc_b = nc.dram_tensor("ssm_scr_b", [DEPTH, 128, NTB + 3 * 2048], BF16, kind="Internal").ap()

        class ArenaOn(Arena):
            def __init__(self, flat, size):
                self.flat = flat
                self.off = 0
                self.end = size

            def alloc(self, shape, dt):
                n = int(np.prod(shape[1:]))
                nb = n * (4 if dt in (F32, I32) else 2)
                nb = (nb + 31) // 32 * 32
                assert self.off + nb <= self.end, ("arenaOn overflow", shape, self.off, nb, self.end)
                v = self.flat[0:shape[0], self.off // 2:(self.off + nb) // 2]
                self.off += nb
                if dt != BF16:
                    v = v.bitcast(dt)
                v = v[:, 0:n]
                if len(shape) == 3:
                    v = v.rearrange("p (a b) -> p a b", a=shape[1])
                return v

        def ssm_setup_gen(l):
            ah = ArenaOn(HT[:, :, :].rearrange("p a b -> p (a b)"), 32768)
            ay = ArenaOn(YT[:, :, :].rearrange("p a b -> p (a b)"), 32768)
            ar = Arena(40992, ARB - 40992)
            TC = ah.alloc([128, 16, TL + 1], F32)
            TS = ah.alloc([128, 16, TL + 1], F32)
            ANG = ah.alloc([128, 16, TL + 1], F32)
            BL = ay.alloc([128, 16, 128], BF16)
            IBL = ay.alloc([128, 16, 128], BF16)
            CL = ay.alloc([128, 16, 128], BF16)
            SV = ay.alloc([128, 14, 16], F32)
            P1 = ay.alloc([128, 16, 16], F32)
            P2 = ay.alloc([128, 16, 16], F32)
            CT = ay.alloc([128, 256], F32)
            TCb = ay.alloc([128, 16, TL + 1], BF16)
            AI = ay.alloc([128, 16, TL + 1], I32)
            JF = ar.alloc([128, TL + 1], F32)
            JI = ar.alloc([128, TL + 1], I32)
            Bc = ar.alloc([128, 16, 16], F32)
            IBc = ar.alloc([128, 16, 16], F32)
            Tm = ar.alloc([128, 16, 16], F32)
            are, aim, ldt, dtv, lr, thn, rho, sn, cs, er, ei, gr, gi, tq = [SV[:, k, :] for k in range(14)]
            V = "ssmv"
            dma("pool", are, sare_d[l], w=["are", V])
            dma("pool", aim, saim_d[l], w=["aim", V])
            dma("pool", ldt, sldt_d[l], w=["ldt", V])
            dma("pool", P1, sp1_d[l].rearrange("p (g c) -> p g c", g=16), w=["P1"])
            dma("pool", P2, sp2_d[l].rearrange("p (g c) -> p g c", g=16), w=["P2"])
            dma("pool", CT, sct_d[l], w=["CT"])
            yield
            act(dtv, ldt, AF.Exp, r=["ldt"], w=[V])
            tt("dve", lr, are, dtv, ALU.mult, r=["are", V], w=[V])
            tt("dve", thn, aim, dtv, ALU.mult, r=["aim", V], w=[V])
            ts("dve", thn, thn, 1.0 / TWO_PI, None, ALU.mult, None, r=[V], w=[V])
            yield
            ts("dve", rho, lr, 1.0 / 720.0, 1.0 / 120.0, ALU.mult, ALU.add, r=[V], w=[V])
            for cst in (1.0 / 24.0, 1.0 / 6.0, 0.5, 1.0, 1.0):
                tt("dve", rho, rho, lr, ALU.mult, r=[V], w=[V])
                ts("dve", rho, rho, float(cst), None, ALU.add, None, r=[V], w=[V])
                yield

            def sincos(dst, src, shift, ai):
                ts("dve", dst, src, float(shift), None, ALU.add, None, r=[V], w=[V])
                cp("dve", ai, dst, r=[V], w=[V])
                tt("dve", dst, dst, ai, ALU.subtract, r=[V], w=[V])
                act(dst, dst, AF.Sin, r=[V], w=[V], scale=TWO_PI)

            sincos(sn, thn, 0.0, AI[:, 0, 0:16])
            yield
            sincos(cs, thn, 0.25, AI[:, 0, 0:16])
            yield
            tt("dve", er, rho, cs, ALU.mult, r=[V], w=[V])
            ts("dve", er, er, -1.0, None, ALU.add, None, r=[V], w=[V])
            tt("dve", ei, rho, sn, ALU.mult, r=[V], w=[V])
            tt("dve", tq, are, are, ALU.mult, r=[V, "are"], w=[V])
            yield
            tt("dve", gr, aim, aim, ALU.mult, r=[V, "aim"], w=[V])
            tt("dve", tq, tq, gr, ALU.add, r=[V], w=[V])
            S.op("dve", lambda e: e.reciprocal(out=tq, in_=tq), r=[V], w=[V])
            tt("dve", gr, er, are, ALU.mult, r=[V], w=[V])
            yield
            tt("dve", gi, ei, aim, ALU.mult, r=[V], w=[V])
            tt("dve", gr, gr, gi, ALU.add, r=[V], w=[V])
            tt("dve", gr, gr, tq, ALU.mult, r=[V], w=[V])
            tt("dve", gi, ei, are, ALU.mult, r=[V], w=[V])
            yield
            tt("dve", er, er, aim, ALU.mult, r=[V], w=[V])
            tt("dve", gi, gi, er, ALU.subtract, r=[V], w=[V])
            tt("dve", gi, gi, tq, ALU.mult, r=[V], w=[V])
            S2, S3, S4 = ei, er, tq
            ts("dve", S2, gi, sgn[:, 0:1], None, ALU.mult, None, r=[V, "sgn"], w=[V])
            yield
            ts("dve", S3, gr, sgn[:, 0:1], None, ALU.mult, None, r=[V, "sgn"], w=[V])
            ts("dve", S4, gi, -1.0, None, ALU.mult, None, r=[V], w=[V])
            bc = lambda v: v.unsqueeze(2).to_broadcast([128, 16, 16])
            tt("dve", Bc, P1, bc(gr), ALU.mult, r=[V, "P1"], w=["Bc"])
            tt("dve", Tm, P2, bc(S2), ALU.mult, r=[V, "P2"], w=["Tm"])
            yield
            tt("dve", Bc, Bc, Tm, ALU.add, r=["Tm"], w=["Bc"])
            tt("dve", IBc, P2, bc(S3), ALU.mult, r=[V, "P2"], w=["IBc"])
            tt("dve", Tm, P1, bc(S4), ALU.mult, r=[V, "P1", "Bc"], w=["Tm"])
            tt("dve", IBc, IBc, Tm, ALU.add, r=["Tm"], w=["IBc"])
            yield
            for (src, dst, nm) in ((Bc, BL, "Bc"), (IBc, IBL, "IBc")):
                flat = src.rearrange("p g c -> p (g c)")
                for ch in range(2):
                    pt_, ptr_ = ps(7)
                    tr(pt_[:, 0:128], flat[:, ch * 128:(ch + 1) * 128], identf[:], r=[nm, "identf"], w=[ptr_])
                    for g8 in range(8):
                        ts("dve", dst[:, ch * 8 + g8, :], pt_[:, 0:128], gmask[:, g8:g8 + 1], None, ALU.mult, None,
                           r=[ptr_, "gmask"], w=["BL"])
                        if g8 % 4 == 3:
                            yield
            ts("dve", CT, CT, sgn[:, 0:1], -1.0, ALU.mult, ALU.mult, r=["CT", "sgn"], w=["CT"])
            memset("pool", CL, 0.0, w=["CL"])
            for g in range(16):
                g8 = g % 8
                cp("dve", CL[:, g, 16 * g8:16 * g8 + 16], CT[:, g * 16:(g + 1) * 16], r=["CT"], w=["CL"])
                if g % 4 == 3:
                    yield
            S.op("pool", lambda e: e.iota(out=JI, pattern=[[1, TL + 1]], base=0, channel_multiplier=0), w=["JI"])
            cp("dve", JF, JI, r=["JI"], w=["JF"])
            tt("dve", ANG, JF.unsqueeze(1).to_broadcast([128, 16, TL + 1]), thn.unsqueeze(2).to_broadcast([128, 16, TL + 1]),
               ALU.mult, r=["JF", V], w=[V])
            yield
            sincos(TS, ANG, 0.0, AI)
            yield
            sincos(TC, ANG, 0.25, AI)
            yield
            cp("dve", TCb, TC, r=[V], w=[V])
            dma("pool", ssc_f[l, :, 0:NTB], TC.rearrange("p a b -> p (a b)"), r=[V], w=[("ssc", l)])
            dma("pool", ssc_f[l, :, NTB:2 * NTB], TS.rearrange("p a b -> p (a b)"), r=[V], w=[("ssc", l)])
            dma("pool", ssc_f[l, :, 2 * NTB:2 * NTB + 224], SV.rearrange("p a b -> p (a b)"), r=[V], w=[("ssc", l)])
            dma("pool", ssc_b[l, :, 0:NTB], TCb.rearrange("p a b -> p (a b)"), r=[V], w=[("ssc", l)])
            dma("pool", ssc_b[l, :, NTB:NTB + 2048], BL.rearrange("p a b -> p (a b)"), r=["BL"], w=[("ssc", l)])
            dma("pool", ssc_b[l, :, NTB + 2048:NTB + 4096], IBL.rearrange("p a b -> p (a b)"), r=["BL"], w=[("ssc", l)])
            dma("pool", ssc_b[l, :, NTB + 4096:NTB + 6144], CL.rearrange("p a b -> p (a b)"), r=["CL"], w=[("ssc", l)])
            yield

        def ssm_phase(l):
            ay = ArenaOn(YT[:, 0:4, :].rearrange("p a b -> p (a b)"), 16384)
            ar = Arena(0, ARB)
            BL = ay.alloc([128, 16, 128], BF16)
            IBL = ay.alloc([128, 16, 128], BF16)
            CL = ay.alloc([128, 16, 128], BF16)
            SV = ay.alloc([128, 14, 16], F32)
            WUS = ar.alloc([128, 8, 256], BF16)
            TC = ar.alloc([128, 16, TL + 1], F32)
            TS = ar.alloc([128, 16, TL + 1], F32)
            GW = ar.alloc([128, 2, 256], BF16)
            sdcol = ar.alloc([128, 2], F32)
            glub = ar.alloc([128, 2], F32)
            TCb = ar.alloc([128, 16, TL + 1], BF16)
            rho = SV[:, 6, :]
            V = "ssmv2"
            dma("sp", TC.rearrange("p a b -> p (a b)"), ssc_f[l, :, 0:NTB], r=[("ssc", l)], w=[V])
            dma("sp", TS.rearrange("p a b -> p (a b)"), ssc_f[l, :, NTB:2 * NTB], r=[("ssc", l)], w=[V])
            dma("sp", SV.rearrange("p a b -> p (a b)"), ssc_f[l, :, 2 * NTB:2 * NTB + 224], r=[("ssc", l)], w=[V])
            dma("sp", TCb.rearrange("p a b -> p (a b)"), ssc_b[l, :, 0:NTB], r=[("ssc", l)], w=[V])
            dma("sp", BL.rearrange("p a b -> p (a b)"), ssc_b[l, :, NTB:NTB + 2048], r=[("ssc", l)], w=["BL"])
            dma("sp", IBL.rearrange("p a b -> p (a b)"), ssc_b[l, :, NTB + 2048:NTB + 4096], r=[("ssc", l)], w=["BL"])
            dma("sp", CL.rearrange("p a b -> p (a b)"), ssc_b[l, :, NTB + 4096:NTB + 6144], r=[("ssc", l)], w=["CL"])
            dma("sp", sdcol, sd_d[l], w=["sdcol"])
            dma("sp", glub, glub_d[l], w=["glub"])
            dma("pool", GW, gluw_d[l], w=["GW"])
            dma("pool", WUS, win_d[l, :, 1536:1792].rearrange("(kc p) f -> p kc f", p=128), w=["WUS"])
            USS2 = [ar.alloc([128, 2, 512], BF16) for _ in range(2)]
            Wb = [ar.alloc([128, 4, TL], BF16) for _ in range(2)]
            T2 = ar.alloc([128, 4, TL], BF16)
            STb = [ar.alloc([128, 4, TL], F32) for _ in range(2)]
            SBh = [ar.alloc([128, 4, TL], BF16) for _ in range(2)]
            T2b = ar.alloc([128, 4, TL], BF16)
            S1 = ar.alloc([128, 4, TL], BF16)
            SB = [ar.alloc([128, 4, TL], BF16) for _ in range(2)]
            GE = ar.alloc([128, 2, TL], BF16)
            CI = ar.alloc([128, 16], F32)
            CA = ar.alloc([128, 4], F32)
            CB = ar.alloc([128, 4], F32)
            YV = ay.alloc([128, 2, TL], F32)
            G1 = ay.alloc([128, 2, TL], F32)
            G2 = ay.alloc([128, 2, TL], F32)
            NB = 64
            p0_, p0r = ps(0)
            pL, pLr = ps(6)
            pAs = [ps(1), ps(2)]
            pBs = [ps(3), ps(7)]
            pC, pCr = ps(4)
            pC3 = pC[:, :].rearrange("p (a b) -> p a b", a=4)

            def stage_F_pe(n):
                k, bq = n // 4, n % 4
                tsi, kk, ch = k // 4, k % 4, bq // 2
                USS = USS2[tsi % 2]
                ur = ("USS", tsi % 2)
                if kk == 0 and bq == 0:
                    for c2 in range(2):
                        for kc in range(8):
                            mm(p0_[:, :], WUS[:, kc, c2 * 128:(c2 + 1) * 128], HT[:, kc, tsi * 512:(tsi + 1) * 512], kc == 0, kc == 7,
                               r=["WUS", ("HT", tsi)], w=[p0r])
                        cp("act", USS[:, c2, :], p0_[:, :], r=[p0r], w=[ur])
                pA, pAr = pAs[n % 2]
                pB, pBr = pBs[n % 2]
                for gi_ in range(4):
                    mm(pA[:, gi_ * TL:(gi_ + 1) * TL], BL[:, 4 * bq + gi_, :], USS[:, ch, kk * TL:(kk + 1) * TL], True, True, r=["BL", ur], w=[pAr])
                for gi_ in range(4):
                    mm(pB[:, gi_ * TL:(gi_ + 1) * TL], IBL[:, 4 * bq + gi_, :], USS[:, ch, kk * TL:(kk + 1) * TL], True, True, r=["BL", ur], w=[pBr])

            def stage_F_dve(n):
                k, bq = n // 4, n % 4
                gs = slice(4 * bq, 4 * bq + 4)
                wb = Wb[n % 2]
                wbr = ("Wb", n % 2)
                pA, pAr = pAs[n % 2]
                pB, pBr = pBs[n % 2]
                pA3 = pA[:, :].rearrange("p (a b) -> p a b", a=4)
                pB3 = pB[:, :].rearrange("p (a b) -> p a b", a=4)
                tt("dve", wb, pA3, TC[:, gs, 0:TL], ALU.mult, r=[pAr, V], w=[wbr])
                tt("dve", T2, pB3, TS[:, gs, 0:TL], ALU.mult, r=[pBr, V], w=["T2"])
                tt("dve", wb, wb, T2, ALU.subtract, r=["T2"], w=[wbr])

            def stage_S(n):
                k, bq = n // 4, n % 4
                wb = Wb[n % 2]
                wbr = ("Wb", n % 2)
                stb = STb[n % 2]
                stbr = ("STb", n % 2)
                sbh = SBh[n % 2]
                sbhr = ("SBh", n % 2)
                for gi_ in range(4):
                    g = 4 * bq + gi_
                    ini = CI[:, g:g + 1] if k > 0 else 0.0
                    S.op("dve", lambda e, gi_=gi_, g=g, ini=ini: e.tensor_tensor_scan(
                        out=stb[:, gi_, :], data0=rho[:, g:g + 1].to_broadcast([128, TL]), data1=wb[:, gi_, :],
                        initial=ini, op0=ALU.mult, op1=ALU.add), r=[wbr, V, ("CI", bq)], w=[stbr])
                cp("act", sbh, stb, r=[stbr], w=[sbhr])
                mm(pC[:, :], jswb[:], sbh.rearrange("p a b -> p (a b)"), True, True, r=["jswb", sbhr], w=[pCr])
                if k < 15:
                    mm(pL[:, 0:4], jsw[:], stb[:, :, TL - 1], True, True, r=["jsw", stbr], w=[pLr])

            def stage_B(n):
                k, bq = n // 4, n % 4
                kk, ch = k % 4, bq // 2
                tsi = k // 4
                gs = slice(4 * bq, 4 * bq + 4)
                stb = STb[n % 2]
                stbr = ("STb", n % 2)
                sbh = SBh[n % 2]
                sbhr = ("SBh", n % 2)
                sb = SB[n % 2]
                sbr = ("SB", n % 2)
                USS = USS2[tsi % 2]
                ur = ("USS", tsi % 2)
                tt("dve", T2b, pC3, TS[:, gs, 0:TL], ALU.mult, r=[pCr, V], w=["T2b"])
                tt("dve", S1, sbh, TCb[:, gs, 0:TL], ALU.mult, r=[sbhr, V], w=["S1"])
                tt("dve", sb, S1, T2b, ALU.add, r=["S1", "T2b"], w=[sbr])
                if k < 15:
                    tt("dve", CA, pL[:, 0:4], TS[:, gs, TL], ALU.mult, r=[pLr, V], w=["CA"])
                    tt("dve", CB, stb[:, :, TL - 1], TC[:, gs, TL], ALU.mult, r=[stbr, V], w=["CB"])
                    tt("dve", CI[:, gs], CA, CB, ALU.add, r=["CA", "CB"], w=[("CI", bq)])
                pY, pYr = ps(5)
                for gi_ in range(4):
                    g = 4 * bq + gi_
                    mm(pY[:, ch * TL:(ch + 1) * TL], CL[:, g, :], sb[:, gi_, :], g % 8 == 0, g % 8 == 7, r=["CL", sbr], w=[pYr])
                if bq == 3:
                    glu_a(k)
                if bq == 0 and k > 0:
                    glu_b(k - 1)

            def glu_a(k):
                kk, tsi = k % 4, k // 4
                USS = USS2[tsi % 2]
                ur = ("USS", tsi % 2)
                cols = slice(kk * TL, (kk + 1) * TL)
                for c2 in range(2):
                    pY2, pY2r = ps(5)
                    stt(YV[:, c2, :], USS[:, c2, cols], sdcol[:, c2:c2 + 1], pY2[:, c2 * TL:(c2 + 1) * TL], ALU.mult, ALU.add, r=[ur, "sdcol", pY2r], w=["YV"])
                act(G1, YV, AF.Square, r=["YV"], w=["G1"])
                ts("pool", G1, G1, 0.044715, 1.0, ALU.mult, ALU.add, r=["G1"], w=["G1"])
                tt("pool", G1, G1, YV, ALU.mult, r=["G1", "YV"], w=["G1"])
                act(G2, G1, AF.Sigmoid, r=["G1"], w=["G2"], scale=2.0 * math.sqrt(2.0 / math.pi))
                tt("pool", GE, YV, G2, ALU.mult, r=["G2", "YV"], w=["GE"])

            def glu_b(k):
                tsi = k // 4
                tok = slice(k * TL, (k + 1) * TL)
                for dch in range(2):
                    pG, pGr = ps(6)
                    for c2 in range(2):
                        mm(pG[:, 128:128 + TL], GW[:, c2, dch * 128:(dch + 1) * 128], GE[:, c2, :], c2 == 0, c2 == 1, r=["GW", "GE"], w=[pGr])
                    act_b(G1[:, dch, :], pG[:, 128:128 + TL], AF.Sigmoid, r=[pGr, "glub", "G1"], w=["G1"], bias=glub[:, dch:dch + 1], scale=1.0)
                    tt("pool", YT[:, 4 + dch, tok], YV[:, dch, :], G1[:, dch, :], ALU.mult, r=["G1", "YV"], w=[("YT", 4 + dch, tsi)])

            stage_F_pe(0)
            for it in range(NB + 2):
                if it + 1 < NB:
                    stage_F_pe(it + 1)
                if it < NB:
                    stage_F_dve(it)
                if 0 <= it - 2 < NB:
                    stage_B(it - 2)
                if 0 <= it - 1 < NB:
                    stage_S(it - 1)
            glu_b(15)

        def bias_setup():
            ar = Arena(0, ARB)
            relb = ar.alloc([32, 8], F32)
            OH = ar.alloc([32, 1152], F32)
            NG = ar.alloc([8, 1152], F32)
            FV = ar.alloc([8, 1152], F32)
            dma("sp", relb, relb_d, w=["relb"])
            dma("sp", OH, oh_d, w=["OH"])
            dma("sp", NG, negm_d, w=["NG"])
            for br in range(3):
                p_, pr_ = ps(br)
                mm(p_[0:8, 0:384], relb, OH[:, br * 384:(br + 1) * 384], True, True, r=["relb", "OH"], w=[pr_])
                tt("dve", FV[:, br * 384:(br + 1) * 384], p_[0:8, 0:384], NG[:, br * 384:(br + 1) * 384], ALU.add, r=[pr_, "NG"], w=["FV"])
            dma("sp", fv_d, FV, r=["FV"], w=["fv_d"])

        def attn_phase(l):
            ar = Arena(0, ARB)
            WQKV = ar.alloc([128, 8, 384], BF16)
            QZ = [ar.alloc([128, SEQ], BF16) for _ in range(2)]
            KT = ar.alloc([128, SEQ], BF16)
            VP = ar.alloc([128, 3, 16, 192], BF16)
            BTp = ar.alloc([128, 2, 768], BF16)
            Hb = ar.alloc([128, 256], F32)
            PT = [ar.alloc([128, 128], BF16) for _ in range(4)]
            RD = [ar.alloc([128, 512], F32) for _ in range(1)]
            VT = ar.alloc([128, SEQ], BF16)
            memset("pool", QZ[0][64:128, :], 0.0, w=[("QZ", 0)])
            memset("pool", QZ[1][0:64, :], 0.0, w=[("QZ", 1)])
            memset("pool", VP[:, :, :, 64:128], 1.0, w=["VP"])
            npt = 0
            nsc = 0
            nrd = 0
            for hp in range(4):
                for j, base in enumerate((0, 512, 1024)):
                    dma("pool", WQKV[:, :, j * 128:(j + 1) * 128],
                        win_d[l, :, base + hp * 128:base + (hp + 1) * 128].rearrange("(kc p) f -> p kc f", p=128), w=["WQKV"])
                for tsi in range(4):
                    pq, pqr = ps(6)
                    for kc in range(8):
                        mm(pq[:, :], WQKV[:, kc, 0:128], HT[:, kc, tsi * 512:(tsi + 1) * 512], kc == 0, kc == 7, r=["WQKV", ("HT", tsi)], w=[pqr])
                    act(QZ[0][0:64, tsi * 512:(tsi + 1) * 512], pq[0:64, :], AF.Copy, r=[pqr], w=[("QZ", 0)], scale=0.125)
                    act(QZ[1][64:128, tsi * 512:(tsi + 1) * 512], pq[64:128, :], AF.Copy, r=[pqr], w=[("QZ", 1)], scale=0.125)
                    pk, pkr = ps(7)
                    for kc in range(8):
                        mm(pk[:, :], WQKV[:, kc, 128:256], HT[:, kc, tsi * 512:(tsi + 1) * 512], kc == 0, kc == 7, r=["WQKV", ("HT", tsi)], w=[pkr])
                    cp("dve", KT[:, tsi * 512:(tsi + 1) * 512], pk[:, :], r=[pkr], w=["KT"])
                for tsi in range(4):
                    pvt, pvtr = ps(6 + tsi % 2)
                    for kc in range(8):
                        mm(pvt[:, :], WQKV[:, kc, 256:384], HT[:, kc, tsi * 512:(tsi + 1) * 512], kc == 0, kc == 7, r=["WQKV", ("HT", tsi)], w=[pvtr])
                    cp("act", VT[:, tsi * 512:(tsi + 1) * 512], pvt[:, :], r=[pvtr], w=["VT"])
                nv = 0
                for br, (win, dil) in enumerate(PATTERNS):
                    nbk = 16 // dil
                    for q4 in range(4):
                        pv, pvr = ps(6 + nv % 2)
                        nv += 1
                        pvb = pv[:, :].bitcast(BF16)
                        for t4 in range(4):
                            tix = q4 * 4 + t4
                            rr, m = tix // nbk, tix % nbk
                            t0 = rr + dil * 128 * m
                            tr(pvb[:, t4 * 128:(t4 + 1) * 128], VT[:, t0:t0 + dil * 127 + 1:dil], ident[:], r=["VT", "ident"], w=[pvr])
                        outv = VP[:, br, q4 * 4:(q4 + 1) * 4, :].rearrange("p t (a c) -> p t a c", a=3)[:, :, 0:3:2, :]
                        inv_ = pvb[:, 0:512].rearrange("p (t a c) -> p t a c", t=4, a=2)
                        cp("dve", outv, inv_, r=[pvr], w=["VP"])
                for hh in range(2):
                    h = 2 * hp + hh
                    for br in range(3):
                        dma("sp", Hb, bass.AP(fv_d.tensor, h * 1152 + br * 384, [[1, 128], [1, 256]]), r=["fv_d"], w=["Hb"])
                        pb_, pbr_ = ps(6 + br % 2)
                        mm(pb_[:, 0:256], jex[:], Hb, True, True, r=["jex", "Hb"], w=[pbr_])
                        act(BTp[:, hh, br * 256:(br + 1) * 256], pb_[:, 0:256], AF.Exp, r=[pbr_], w=["BTp"])
                for hh in range(2):
                    accs = [ps(b) for b in range(4)]
                    started = [False] * 4
                    tasks = []
                    for br, (win, dil) in enumerate(PATTERNS):
                        nbk = 16 // dil
                        for rr in range(dil):
                            for m in range(nbk):
                                for qb in (m, m + 1):
                                    if qb >= nbk:
                                        continue
                                    tasks.append((br, dil, nbk, rr, m, qb))
                    pendq = []

                    def pieces_of(task):
                        br, dil, nbk, rr, m, qb = task
                        if dil == 16:
                            return [(b, rr, 32 * b, 32) for b in range(4)]
                        elif dil == 4:
                            return [(qb, rr, 0, 128)]
                        t0 = 128 * qb
                        return [(t0 // 512, t0 % 512, 0, 128)]

                    remaining = [0] * 4
                    for task in tasks:
                        for (b, c0, i0, n) in pieces_of(task):
                            remaining[b] += 1

                    def do_pv(task, slot):
                        br, dil, nbk, rr, m, qb = task
                        tix = rr * nbk + m
                        lhs = VP[:, br, tix, hh * 64:hh * 64 + 128]
                        for (b, c0, i0, n) in pieces_of(task):
                            acc, accr = accs[b]
                            remaining[b] -= 1
                            mm(acc[:, c0:c0 + dil * (n - 1) + 1:dil], lhs, PT[slot][:, i0:i0 + n], not started[b], remaining[b] == 0,
                               r=["VP", ("PT", slot)], w=[accr])
                            started[b] = True

                    for task in tasks:
                        br, dil, nbk, rr, m, qb = task
                        k0 = rr + dil * 128 * m
                        q0 = rr + dil * 128 * qb
                        psc, pscr = ps(4 + nsc % 2)
                        nsc += 1
                        mm(psc[:, 0:128], KT[:, k0:k0 + dil * 127 + 1:dil], QZ[hh][:, q0:q0 + dil * 127 + 1:dil], True, True,
                           r=["KT", ("QZ", hh)], w=[pscr])
                        off = (qb - m) * 128
                        slot = npt % 4
                        npt += 1
                        act(PT[slot], psc[:, 0:128], AF.Exp, r=[pscr], w=[("PT", slot)])
                        tt("dve", PT[slot], PT[slot], BTp[:, hh, br * 256 + off:br * 256 + off + 128], ALU.mult, r=["BTp", ("PT", slot)], w=[("PT", slot)])
                        pendq.append((task, slot))
                        if len(pendq) > 3:
                            do_pv(*pendq.pop(0))
                    while pendq:
                        do_pv(*pendq.pop(0))
                    for b in range(4):
                        acc, accr = accs[b]
                        rd = RD[0]
                        rdr = ("RD", 0)
                        nrd += 1
                        if hh == 0:
                            act(rd[0:64, :], acc[64:128, :], AF.Ln, r=[accr], w=[rdr])
                            act(rd[0:64, :], rd[0:64, :], AF.Exp, r=[rdr], w=[rdr], scale=-1.0)
                            tt("dve", YT[0:64, hp, b * 512:(b + 1) * 512], acc[0:64, :], rd[0:64, :], ALU.mult, r=[accr, rdr], w=[("YT", hp, b)])
                        else:
                            act(rd[64:128, :], acc[0:64, :], AF.Ln, r=[accr], w=[rdr])
                            act(rd[64:128, :], rd[64:128, :], AF.Exp, r=[rdr], w=[rdr], scale=-1.0)
                            tt("dve", YT[64:128, hp, b * 512:(b + 1) * 512], acc[64:128, :], rd[64:128, :], ALU.mult, r=[accr, rdr], w=[("YT", hp, b)])

        def wout_phase(l, tail):
            ar = Arena(0, ARB)
            WO = ar.alloc([128, 8, D], BF16)
            GBC = ar.alloc([128, D], F32)
            D8 = ar.alloc([128, 8, 128], F32)
            dma("pool", WO, wout_d[l].rearrange("(kc p) d -> p kc d", p=128), w=["WO"])
            gate_bc(l, 1, 1.0 / ALPHA, GBC, D8)
            for kc in range(8):
                tt("pool", WO[:, kc, :], WO[:, kc, :], GBC, ALU.mult, r=["GBC", "WO"], w=["WO"])
            n = 0
            for tt_ in range(NT):
                for hf in range(2):
                    po, por = ps(n % 2)
                    n += 1
                    for kc in range(8):
                        mm(po[:, :], YT[:, kc, tt_ * 128:(tt_ + 1) * 128], WO[:, kc, hf * 512:(hf + 1) * 512], kc == 0, kc == 7,
                           r=["WO"] + [("YT", kc, b) for b in range(4)], w=[por])
                    tt("dve", X[:, tt_, hf * 512:(hf + 1) * 512], X[:, tt_, hf * 512:(hf + 1) * 512], po[:, :], ALU.add,
                       r=[("X", tt_), por], w=[("X", tt_)])
                tail.tile_done(tt_)
                tail.lagged(2)
            tail.lagged(0)

        def mixer(l, tail):
            if "pool" in parts:
                pool_phase(l)
                fence()
            if "ssm" in parts:
                ssm_phase(l)
                fence()
            if "attn" in parts:
                attn_phase(l)
                fence()
            if dbg:
                return
            wout_phase(l, tail)

        with arena_scope():
            ada_phase()
        fence_all()
        with arena_scope():
            bias_setup()
        run_layers()
        if dbg:
            fence_all()
            dbg_d = nc.dram_tensor("dbg", [128, 8, SEQ], F32, kind="ExternalOutput").ap()
            dma("pool", dbg_d, YT[:, :, :], r=["ALL"], final=True)
        if dbg or stop is not None:
            fence_all()
            for q in range(4):
                dma("sp", ov[:, q * 4:(q + 1) * 4, :], X[:, q * 4:(q + 1) * 4, :], r=[("X", t) for t in range(q * 4, q * 4 + 4)], final=True)
        S.emit()
    return nc


def host_inputs(inputs):
    f = lambda a: np.ascontiguousarray(np.asarray(a, dtype=np.float32))
    consts = static_consts()
    shared = {}
    for k in ("ada_w", "ada_b", "ln_g", "ln_b", "ffn_w_gate", "ffn_w_up", "ffn_w_down", "w_in", "w_out", "rel_bias"):
        shared[k] = f(inputs[k])
    a_re, a_im = f(inputs["ssm_a_re"]), f(inputs["ssm_a_im"])
    dup = lambda a: np.concatenate([a, a], axis=1)
    shared["sa_re"] = f(dup(a_re.transpose(0, 2, 1)))
    shared["sa_im"] = f(dup(a_im.transpose(0, 2, 1)))
    shared["sldt"] = f(np.broadcast_to(f(inputs["ssm_log_dt"])[:, None, :], (DEPTH, 128, 16)))
    b_re = f(inputs["ssm_b_re"]).transpose(0, 2, 1, 3).reshape(DEPTH, 64, 256)
    b_im = f(inputs["ssm_b_im"]).transpose(0, 2, 1, 3).reshape(DEPTH, 64, 256)
    shared["sP1"] = f(np.concatenate([b_re, b_im], axis=1))
    shared["sP2"] = f(np.concatenate([b_im, b_re], axis=1))
    c_re = f(inputs["ssm_c_re"]).transpose(0, 3, 1, 2).reshape(DEPTH, 64, 256)
    c_im = f(inputs["ssm_c_im"]).transpose(0, 3, 1, 2).reshape(DEPTH, 64, 256)
    shared["sCT"] = f(np.concatenate([c_re, c_im], axis=1))
    col2 = lambda a: f(f(a).reshape(DEPTH, 2, 128).transpose(0, 2, 1))
    shared["sd"] = col2(inputs["ssm_d"])
    shared["glu_b"] = col2(inputs["glu_b"])
    shared["pool_s"] = col2(inputs["pool_scale"])
    shared["glu_w"] = f(f(inputs["glu_w"]).reshape(DEPTH, 2, 128, 256).transpose(0, 2, 1, 3))
    pw = f(inputs["pool_w"])
    pbd = np.zeros((DEPTH, 128, 2, 128), np.float32)
    for ch in range(2):
        for h in range(2):
            pbd[:, h * 64:(h + 1) * 64, ch, h * 64:(h + 1) * 64] = pw[:, ch * 2 + h]
    shared["pool_w"] = pbd
    shared.update(consts)
    x = f(inputs["x"])
    c = f(inputs["c"])
    maps = []
    for b in range(8):
        m = dict(shared)
        m["x"] = x[b]
        m["cT"] = f(c[b].reshape(8, 128).T)
        maps.append(m)
    return maps


_NC_CACHE = {}


def kernel(**inputs):
    if "nc" not in _NC_CACHE:
        _NC_CACHE["nc"] = build()
    nc = _NC_CACHE["nc"]
    maps = host_inputs(inputs)
    res = run_bass_kernel_spmd(nc, maps, core_ids=list(range(8)))
    return np.stack([np.asarray(r["out"], dtype=np.float32) for r in res.results], axis=0)
```

```python
import contextlib
import math
import numpy as np
import concourse.bass as bass
import concourse.mybir as mybir
from concourse.bass_utils import run_bass_kernel_spmd

F32 = mybir.dt.float32
BF16 = mybir.dt.bfloat16
I32 = mybir.dt.int32
AF = mybir.ActivationFunctionType
ALU = mybir.AluOpType

SEQ = 2048
D = 1024
DFF = 2816
DEPTH = 2
NT = SEQ // 128
ALPHA = (2 * DEPTH) ** 0.25
LN_EPS = 1e-5
NEG = -1e30
PATTERNS = ((128, 1), (512, 4), (2048, 16))

ENGS = ("pe", "act", "dve", "pool", "sp")
NDMASEM = 12


class Op:
    __slots__ = ("eng", "fn", "deps", "marked", "val", "is_dma", "dsem", "dval", "idx")

    def __init__(self, eng, fn, is_dma):
        self.eng = eng
        self.fn = fn
        self.deps = []
        self.marked = False
        self.val = None
        self.is_dma = is_dma
        self.dsem = None
        self.dval = None
        self.idx = None


class Sched:
    def __init__(self, nc):
        self.nc = nc
        self.ops = {e: [] for e in ENGS}
        self.writers = {}
        self.readers = {}
        self.ndma = {e: 0 for e in ENGS}
        self.final_waits = []
        self.scope = None

    def op(self, eng, fn, r=(), w=(), dma=False, final=False):
        o = Op(eng, fn, dma)
        deps = []
        r = list(r)
        w = list(w)
        if "ALL" not in w:
            r.append("ALL")
        if self.scope is not None and self.scope not in w:
            r.append(self.scope)
        for x in r:
            deps.extend(self.writers.get(x, ()))
        for x in w:
            deps.extend(self.writers.get(x, ()))
            deps.extend(self.readers.get(x, ()))
        seen = set()
        for d in deps:
            if d is o or id(d) in seen:
                continue
            seen.add(id(d))
            if d.eng == "pe" and eng == "pe" and not d.is_dma and not dma:
                continue
            o.deps.append(d)
            if not d.is_dma:
                d.marked = True
        for x in w:
            if self.readers.get(x):
                self.writers[x] = [o]
                self.readers[x] = []
            else:
                ws = self.writers.setdefault(x, [])
                ws[:] = [p for p in ws if not (p.eng == eng and p.is_dma == dma and not dma)]
                ws.append(o)
        for x in r:
            if x not in w:
                rs = self.readers.setdefault(x, [])
                rs[:] = [p for p in rs if not (p.eng == eng and not p.is_dma and not dma)]
                rs.append(o)
        if dma:
            j = self.ndma[eng]
            self.ndma[eng] = j + 1
            o.dsem = j % NDMASEM
            o.dval = 16 * (j // NDMASEM + 1)
            o.idx = j
        self.ops[eng].append(o)
        if final:
            self.final_waits.append(o)
        return o

    def emit(self):
        nc = self.nc
        with contextlib.ExitStack() as st:
            csem = {e: st.enter_context(nc.semaphore("c_" + e)) for e in ENGS}
            dsem = {e: [st.enter_context(nc.semaphore("d_%s_%d" % (e, i))) for i in range(NDMASEM)]
                    for e in ENGS if self.ndma[e] > 0}
            for e in ENGS:
                c = 0
                for o in self.ops[e]:
                    if o.marked and not o.is_dma:
                        c += 1
                        o.val = c
            block = st.enter_context(nc.Block())

            def run(e, eng):
                waited = {}
                for o in self.ops[e]:
                    waits = []
                    for d in o.deps:
                        if d.is_dma:
                            waits.append((("d", d.eng, d.dsem), dsem[d.eng][d.dsem], d.dval))
                        else:
                            waits.append((("c", d.eng), csem[d.eng], d.val))
                    if o.is_dma and o.idx >= NDMASEM:
                        waits.append((("d", e, o.dsem), dsem[e][o.dsem], o.dval - 16))
                    for key, s, v in waits:
                        if waited.get(key, 0) >= v:
                            continue
                        eng.wait_ge(s, v)
                        waited[key] = v
                    ins = o.fn(eng)
                    if o.is_dma:
                        ins.then_inc(dsem[e][o.dsem], 16)
                    elif o.marked:
                        ins.then_inc(csem[e], 1)
                for o in self.final_waits:
                    if o.eng == e:
                        eng.wait_ge(dsem[e][o.dsem], o.dval)

            if self.ops["sp"]:
                @block.sync
                def _(eng):
                    run("sp", eng)
            if self.ops["pe"]:
                @block.tensor
                def _(eng):
                    run("pe", eng)
            if self.ops["act"]:
                @block.scalar
                def _(eng):
                    run("act", eng)
            if self.ops["dve"]:
                @block.vector
                def _(eng):
                    run("dve", eng)
            if self.ops["pool"]:
                @block.gpsimd
                def _(eng):
                    run("pool", eng)


def t5_bucket(dist):
    n_buckets, max_distance = 32, 2048
    max_exact = n_buckets // 2
    d = np.maximum(dist, 1).astype(np.float32)
    large = max_exact + (np.log(d / max_exact) / math.log(max_distance / max_exact)
                         * (n_buckets - max_exact)).astype(np.int32)
    large = np.minimum(large, n_buckets - 1)
    return np.where(dist < max_exact, dist, large).astype(np.int32)


def static_consts():
    c = {}
    c["identf"] = np.eye(128, dtype=np.float32)
    c["jex"] = np.eye(128, dtype=np.float32)[::-1].copy()
    jsw = np.zeros((128, 128), np.float32)
    for m in range(64):
        jsw[m + 64, m] = -1.0
        jsw[m, m + 64] = 1.0
    c["jsw"] = jsw
    oh = np.zeros((32, 3 * 384), np.float32)
    negm = np.zeros((8, 3 * 384), np.float32)
    for bi, (win, dil) in enumerate(PATTERNS):
        for u in range(384):
            dist = u - 127
            if 0 <= dist <= win // dil:
                oh[t5_bucket(np.array([dist * dil]))[0], bi * 384 + u] = 1.0
            else:
                negm[:, bi * 384 + u] = NEG
    c["oh"] = oh
    c["negm"] = negm
    gm = np.zeros((128, 8), np.float32)
    for p in range(128):
        gm[p, p // 16] = 1.0
    c["gmask"] = gm
    wins = np.array([2, 4, 8, 16], np.float32)
    wp = np.zeros((128, 2), np.float32)
    for ch in range(2):
        for p in range(128):
            wp[p, ch] = wins[ch * 2 + p // 64]
    c["invw"] = (1.0 / wp).astype(np.float32)
    t = np.arange(16, dtype=np.float32)[None, None, :]
    c["rc16"] = (1.0 / np.minimum(t + 1.0, wp[:, :, None])).astype(np.float32)
    c["sgn"] = np.concatenate([-np.ones((64, 1), np.float32), np.ones((64, 1), np.float32)], 0)
    return c


def build(stop=None, parts=("pool", "ssm", "attn"), dbg=False):
    nc = bass.Bass("TRN2", target_bir_lowering=False)

    def din(name, shape, dt=F32):
        return nc.dram_tensor(name, list(shape), dt, kind="ExternalInput").ap()

    x_d = din("x", [SEQ, D])
    cT_d = din("cT", [128, 8])
    adaw_d = din("ada_w", [DEPTH, D, 9 * D])
    adab_d = din("ada_b", [DEPTH, 9 * D])
    lng_d = din("ln_g", [DEPTH, 3, D])
    lnb_d = din("ln_b", [DEPTH, 3, D])
    wg_d = din("ffn_w_gate", [DEPTH, 2, D, DFF])
    wu_d = din("ffn_w_up", [DEPTH, 2, D, DFF])
    wd_d = din("ffn_w_down", [DEPTH, 2, DFF, D])
    win_d = din("w_in", [DEPTH, D, 2048])
    wout_d = din("w_out", [DEPTH, D, D])
    relb_d = din("rel_bias", [32, 8])
    sare_d = din("sa_re", [DEPTH, 128, 16])
    saim_d = din("sa_im", [DEPTH, 128, 16])
    sldt_d = din("sldt", [DEPTH, 128, 16])
    sp1_d = din("sP1", [DEPTH, 128, 256])
    sp2_d = din("sP2", [DEPTH, 128, 256])
    sct_d = din("sCT", [DEPTH, 128, 256])
    sd_d = din("sd", [DEPTH, 128, 2])
    gluw_d = din("glu_w", [DEPTH, 128, 2, 256])
    glub_d = din("glu_b", [DEPTH, 128, 2])
    poolw_d = din("pool_w", [DEPTH, 128, 2, 128])
    pools_d = din("pool_s", [DEPTH, 128, 2])
    identf_d = din("identf", [128, 128])
    jex_d = din("jex", [128, 128])
    jsw_d = din("jsw", [128, 128])
    oh_d = din("oh", [32, 1152])
    negm_d = din("negm", [8, 1152])
    gmask_d = din("gmask", [128, 8])
    invw_d = din("invw", [128, 2])
    rc16_d = din("rc16", [128, 2, 16])
    sgn_d = din("sgn", [128, 1])
    out_d = nc.dram_tensor("out", [SEQ, D], F32, kind="ExternalOutput").ap()
    fv_d = nc.dram_tensor("fv_scratch", [8, 1152], F32, kind="Internal").ap()

    st = contextlib.ExitStack()
    with st:
        def T(name, shape, dt):
            return st.enter_context(nc.sbuf_tensor(name, list(shape), dt))

        PSB = [st.enter_context(nc.psum_tensor("ps%d" % i, [128, 512], F32)) for i in range(8)]

        def ps(k):
            return PSB[k], ("ps", k)

        X = T("X", [128, NT, D], F32)
        HT = T("HT", [128, 8, SEQ], BF16)
        YT = T("YT", [128, 8, SEQ], BF16)
        ARB = 49152
        AR = T("AR", [128, ARB // 2], BF16)
        ident = T("ident", [128, 128], BF16)
        identf = T("identf_s", [128, 128], F32)
        jex = T("jex_s", [128, 128], F32)
        jsw = T("jsw_s", [128, 128], F32)
        onesf = T("onesf", [128, 128], F32)
        jswb = T("jswb", [128, 128], BF16)
        modcol = T("modcol", [128, DEPTH, 72], F32)
        XH = [T("XH%d" % i, [128, D], BF16) for i in range(4)]
        mv = T("mv", [128, 4, 2], F32)
        sc = T("sc", [128, 4, 4], F32)
        condT = T("condT", [128, 8], F32)
        gmask = T("gmask_s", [128, 8], F32)
        invw = T("invw_s", [128, 2], F32)
        rc16 = T("rc16_s", [128, 2, 16], F32)
        sgn = T("sgn_s", [128, 1], F32)

        S = Sched(nc)

        class Arena:
            def __init__(self, base, size):
                self.off = base
                self.end = base + size

            def alloc(self, shape, dt):
                n = int(np.prod(shape[1:]))
                nb = n * (4 if dt in (F32, I32) else 2)
                nb = (nb + 31) // 32 * 32
                assert self.off + nb <= self.end, ("arena overflow", shape, self.off, nb, self.end)
                v = AR[0:shape[0], self.off // 2:(self.off + nb) // 2]
                self.off += nb
                if dt != BF16:
                    v = v.bitcast(dt)
                v = v[:, 0:n]
                if len(shape) == 3:
                    v = v.rearrange("p (a b) -> p a b", a=shape[1])
                elif len(shape) == 4:
                    v = v.rearrange("p (a b c) -> p a b c", a=shape[1], b=shape[2])
                return v

        class ArenaYT(Arena):
            def alloc(self, shape, dt):
                n = int(np.prod(shape[1:]))
                nb = n * (4 if dt in (F32, I32) else 2)
                nb = (nb + 31) // 32 * 32
                assert self.off + nb <= self.end
                flat = YT[:, :, :].rearrange("p a b -> p (a b)")
                v = flat[0:shape[0], self.off // 2:(self.off + nb) // 2]
                self.off += nb
                if dt != BF16:
                    v = v.bitcast(dt)
                v = v[:, 0:n]
                if len(shape) == 3:
                    v = v.rearrange("p (a b) -> p a b", a=shape[1])
                return v

        def dma(eng, out, in_, r=(), w=(), final=False, slow=False):
            if slow:
                return S.op(eng, lambda e: e.dma_start(out=out, in_=in_, allow_slow_non_contiguous=True), r=r, w=w, dma=True, final=final)
            return S.op(eng, lambda e: e.dma_start(out=out, in_=in_), r=r, w=w, dma=True, final=final)

        def mm(out, lhsT, rhs, start, stop, r, w):
            return S.op("pe", lambda e: e.matmul(out, lhsT=lhsT, rhs=rhs, start=start, stop=stop), r=r, w=w)

        def tr(out, in_, idn, r, w):
            return S.op("pe", lambda e: e.transpose(out=out, in_=in_, identity=idn), r=r, w=w)

        def act(out, in_, func, r, w, bias=0.0, scale=1.0):
            return S.op("act", lambda e: e.activation(out=out, in_=in_, func=func, bias=bias, scale=scale), r=r, w=w)

        def ts(eng, out, in0, s1, s2, op0, op1, r, w):
            if op1 is None:
                return S.op(eng, lambda e: e.tensor_scalar(out=out, in0=in0, scalar1=s1, scalar2=None, op0=op0), r=r, w=w)
            return S.op(eng, lambda e: e.tensor_scalar(out=out, in0=in0, scalar1=s1, scalar2=s2, op0=op0, op1=op1), r=r, w=w)

        def tt(eng, out, in0, in1, op, r, w):
            return S.op(eng, lambda e: e.tensor_tensor(out=out, in0=in0, in1=in1, op=op), r=r, w=w)

        def stt(out, in0, scalar, in1, op0, op1, r, w):
            return S.op("dve", lambda e: e.scalar_tensor_tensor(out=out, in0=in0, scalar=scalar, in1=in1, op0=op0, op1=op1), r=r, w=w)

        def cp(eng, out, in_, r, w):
            if eng == "act":
                return S.op("act", lambda e: e.copy(out=out, in_=in_), r=r, w=w)
            return S.op(eng, lambda e: e.tensor_copy(out=out, in_=in_), r=r, w=w)

        def memset(eng, ap, val, w):
            return S.op(eng, lambda e: e.memset(ap, val), w=w)

        fsrc_d = identf_d[0:1, 0:16]
        fdst_d = nc.dram_tensor("fence_dst", [1, 16], F32, kind="Internal").ap()

        def fence():
            sv = S.scope
            S.scope = None
            S.op("sp", lambda e: e.dma_start(out=fdst_d, in_=fsrc_d), w=["ARENA"], dma=True)
            S.scope = sv

        def fence_all():
            S.op("dve", lambda e: e.memset(sc[:, 0, 3:4], 0.0), w=["ALL", "ARENA", ("sc3", 0)])

        class arena_scope:
            def __enter__(self):
                self.sv = S.scope
                S.scope = "ARENA"

            def __exit__(self, *a):
                S.scope = self.sv

        dma("sp", identf[:], identf_d, w=["identf"])
        dma("sp", jex[:], jex_d, w=["jex"])
        dma("sp", jsw[:], jsw_d, w=["jsw"])
        dma("sp", gmask[:], gmask_d, w=["gmask"])
        dma("sp", invw[:], invw_d, w=["invw"])
        dma("sp", rc16[:], rc16_d, w=["rc16"])
        dma("sp", sgn[:], sgn_d, w=["sgn"])
        cp("dve", ident[:], identf[:], r=["identf"], w=["ident"])
        cp("dve", jswb[:], jsw[:], r=["jsw"], w=["jswb"])
        memset("dve", onesf[:], 1.0, w=["onesf"])
        xv = x_d.rearrange("(t p) d -> p t d", p=128)
        for q in range(4):
            dma("sp", X[:, q * 4:(q + 1) * 4, :], xv[:, q * 4:(q + 1) * 4, :], w=[("X", t) for t in range(q * 4, q * 4 + 4)])

        def ada_phase():
            ar = Arena(0, ARB)
            AW = [ar.alloc([128, 8, 512], F32) for _ in range(2)]
            AB = [ar.alloc([1, 512], F32) for _ in range(2)]
            ROW = [ar.alloc([1, 512], F32) for _ in range(2)]
            cTs = ar.alloc([128, 8], F32)
            dma("sp", cTs, cT_d, w=["cTs"])
            act(condT[:], cTs, AF.Silu, r=["cTs"], w=["condT"])
            it = 0
            import itertools
            gen = itertools.chain(ssm_setup_gen(0), ssm_setup_gen(1))
            prep_q = list(range(NT))
            gen_done = [False]
            for l in range(DEPTH):
                for nb in range(18):
                    for _ in range(6):
                        if next(gen, "done") == "done":
                            gen_done[0] = True
                    b = it % 2
                    it += 1
                    src = adaw_d[l, :, nb * 512:(nb + 1) * 512].rearrange("(kc p) n -> p kc n", p=128)
                    dma("sp", AW[b], src, w=[("AW", b)])
                    dma("sp", AB[b], adab_d[l:l + 1, nb * 512:(nb + 1) * 512], w=[("AB", b)])
                    pr, prr = ps(b)
                    for kc in range(8):
                        mm(pr[0:1, :], condT[:, kc:kc + 1], AW[b][:, kc, :], kc == 0, False,
                           r=["condT", ("AW", b)], w=[prr])
                    mm(pr[0:1, :], onesf[0:1, 0:1], AB[b], False, True, r=["onesf", ("AB", b)], w=[prr])
                    ev = "dve" if gen_done[0] else "act"
                    cp(ev, ROW[b], pr[0:1, :], r=[prr], w=[("ROW", b)])
                    pc, pcr = ps(2 + b)
                    for j in range(4):
                        mm(pc[:, j:j + 1], ROW[b][0:1, j * 128:(j + 1) * 128], onesf[0:1, 0:1], True, True,
                           r=[("ROW", b), "onesf"], w=[pcr])
                    v = nb // 2
                    addc = 1.0 if v % 3 == 1 else 0.0
                    if ev == "act":
                        act(modcol[:, l, nb * 4:(nb + 1) * 4], pc[:, 0:4], AF.Identity, r=[pcr], w=[("modcol", l, v)], bias=float(addc))
                    else:
                        ts("dve", modcol[:, l, nb * 4:(nb + 1) * 4], pc[:, 0:4], float(addc), None, ALU.add, None, r=[pcr], w=[("modcol", l, v)])
                    if gen_done[0] and prep_q:
                        tq_ = prep_q.pop(0)
                        sv_ = S.scope
                        S.scope = None
                        prep_tile_a(0, 0, tq_)
                        prep_tile_b(0, 0, tq_, extra=[("ssc", 0), ("ssc", 1)])
                        S.scope = sv_
            for _ in gen:
                pass
            sv_ = S.scope
            S.scope = None
            while prep_q:
                tq_ = prep_q.pop(0)
                prep_tile_a(0, 0, tq_)
                prep_tile_b(0, 0, tq_, extra=[("ssc", 0), ("ssc", 1)])
            S.scope = sv_


        NSLOT = 4
        sums = T("sums", [128, NSLOT, 4], F32)

        def finish_stats(slot, eps, c0, c1):
            ts("dve", mv[:, slot, 0:1], sums[:, slot, c0:c0 + 1], 1.0 / D, None, ALU.mult, None, r=[("sums", slot, c0)], w=[("mv", slot)])
            tt("dve", mv[:, slot, 1:2], mv[:, slot, 0:1], mv[:, slot, 0:1], ALU.mult, r=[("mv", slot)], w=[("mv1", slot)])
            stt(sc[:, slot, 3:4], sums[:, slot, c1:c1 + 1], 1.0 / D, mv[:, slot, 1:2], ALU.mult, ALU.subtract,
                r=[("sums", slot, c1), ("mv1", slot)], w=[("sc3", slot)])
            act(sc[:, slot, 0:1], sc[:, slot, 3:4], AF.Sqrt, r=[("sc3", slot)], w=[("sc0", slot)], bias=float(eps))
            S.op("dve", lambda e: e.reciprocal(out=sc[:, slot, 1:2], in_=sc[:, slot, 0:1]), r=[("sc0", slot)], w=[("sc1", slot)])

        def act_accum(tt_, slot, func, col):
            S.op("act", lambda e: e.activation(out=XH[slot][:], in_=X[:, tt_, :], func=func, accum_out=sums[:, slot, col:col + 1]),
                 r=[("X", tt_)], w=[("XH", slot), ("sums", slot, col)])

        def prep_tile_a(l, i, tt_, have_sum=False):
            slot = tt_ % NSLOT
            if not have_sum:
                act_accum(tt_, slot, AF.Identity, 2)
            act_accum(tt_, slot, AF.Square, 3)
            finish_stats(slot, LN_EPS, 2, 3)
            ts("dve", sc[:, slot, 2:3], mv[:, slot, 0:1], -1.0, sc[:, slot, 1:2], ALU.mult, ALU.mult,
               r=[("mv", slot), ("sc1", slot)], w=[("sc2", slot)])
            xh = XH[slot]
            act(xh[:], X[:, tt_, :], AF.Identity, r=[("X", tt_), ("sc1", slot), ("sc2", slot)], w=[("XH", slot)],
                bias=sc[:, slot, 2:3], scale=sc[:, slot, 1:2])

        def prep_tile_b(l, i, tt_, extra=()):
            slot = tt_ % NSLOT
            xh = XH[slot]
            pt, ptr = ps(6 + tt_ % 2)
            ptb = pt[:, :].bitcast(BF16)
            for kc in range(8):
                tr(ptb[:, kc * 128:(kc + 1) * 128], xh[:, kc * 128:(kc + 1) * 128], ident[:], r=[("XH", slot), "ident"], w=[ptr])
            for kc in range(8):
                scl = modcol[:, l, (3 * i + 1) * 8 + kc:(3 * i + 1) * 8 + kc + 1]
                shf = modcol[:, l, (3 * i) * 8 + kc:(3 * i) * 8 + kc + 1]
                if kc % 2 == 0:
                    ts("dve", HT[:, kc, tt_ * 128:(tt_ + 1) * 128], ptb[:, kc * 128:(kc + 1) * 128], scl, shf,
                       ALU.mult, ALU.add, r=[ptr, ("modcol", l, 3 * i), ("modcol", l, 3 * i + 1)] + list(extra), w=[("HT", tt_ // 4)])
                else:
                    act(HT[:, kc, tt_ * 128:(tt_ + 1) * 128], ptb[:, kc * 128:(kc + 1) * 128], AF.Identity,
                        r=[ptr, ("modcol", l, 3 * i), ("modcol", l, 3 * i + 1)] + list(extra), w=[("HT", tt_ // 4)], bias=shf, scale=scl)

        def prep(l, i):
            for tt_ in range(NT):
                prep_tile_a(l, i, tt_)
                prep_tile_b(l, i, tt_)

        LNGB = [T("LNG", [128, D], F32), T("LNB", [128, D], F32)]

        def post_setup(l, i):
            LNG, LNB = LNGB[0][:], LNGB[1][:]
            sv = S.scope
            S.scope = None
            dma("sp", LNG, bass.AP(lng_d.tensor, (l * 3 + i) * D, [[0, 128], [1, D]]), w=["LNG"])
            dma("sp", LNB, bass.AP(lnb_d.tensor, (l * 3 + i) * D, [[0, 128], [1, D]]), w=["LNB"])
            S.scope = sv
            return LNG, LNB

        def post_tile(l, i, tt_, LNG, LNB, want_sum):
            slot = tt_ % NSLOT
            act_accum(tt_, slot, AF.Identity, 0)
            act_accum(tt_, slot, AF.Square, 1)
            finish_stats(slot, LN_EPS / (ALPHA * ALPHA), 0, 1)
            stt(X[:, tt_, :], X[:, tt_, :], mv[:, slot, 0:1], LNG, ALU.subtract, ALU.mult,
                r=[("X", tt_), ("mv", slot), "LNG"], w=[("X", tt_)])
            if want_sum:
                S.op("dve", lambda e: e.scalar_tensor_tensor(out=X[:, tt_, :], in0=X[:, tt_, :], scalar=sc[:, slot, 1:2], in1=LNB,
                                                             op0=ALU.mult, op1=ALU.add, accum_out=sums[:, slot, 2:3]),
                     r=[("X", tt_), ("sc1", slot), "LNB"], w=[("X", tt_), ("sums", slot, 2)])
            else:
                stt(X[:, tt_, :], X[:, tt_, :], sc[:, slot, 1:2], LNB, ALU.mult, ALU.add,
                    r=[("X", tt_), ("sc1", slot), "LNB"], w=[("X", tt_)])

        ov = out_d.rearrange("(t p) d -> p t d", p=128)

        class Tail:
            def __init__(self, postli, prepli, final):
                self.postli, self.prepli, self.final = postli, prepli, final
                self.pend = []
                self.LNG, self.LNB = post_setup(*postli)

            def tile_done(self, tt_):
                sv = S.scope
                S.scope = None
                post_tile(self.postli[0], self.postli[1], tt_, self.LNG, self.LNB, self.prepli is not None)
                if self.prepli is not None:
                    prep_tile_a(self.prepli[0], self.prepli[1], tt_, have_sum=True)
                    self.pend.append(tt_)
                if self.final:
                    dma("sp", ov[:, tt_, :], X[:, tt_, :], r=[("X", tt_)], final=True)
                S.scope = sv

            def lagged(self, keep):
                sv = S.scope
                S.scope = None
                while len(self.pend) > keep:
                    prep_tile_b(self.prepli[0], self.prepli[1], self.pend.pop(0))
                S.scope = sv

        def gate_bc(l, i, scale, GBC, D8):
            for kc in range(8):
                col = (3 * i + 2) * 8 + kc
                ts("dve", D8[:, kc, :], identf[:], modcol[:, l, col:col + 1], float(scale), ALU.mult, ALU.mult,
                   r=["identf", ("modcol", l, 3 * i + 2)], w=["D8"])
            for hf in range(2):
                pg, pgr = ps(hf)
                mm(pg[:, :], onesf[:], D8[:, hf * 4:(hf + 1) * 4, :].rearrange("p a b -> p (a b)"), True, True, r=["onesf", "D8"], w=[pgr])
                cp("act", GBC[:, hf * 512:(hf + 1) * 512], pg[:, :], r=[pgr], w=["GBC"])

        def ffn(l, i, si, tail):
            ar = Arena(0, ARB)
            ay = ArenaYT(0, 32768)
            WG = [ay.alloc([128, 8, 512], BF16) for _ in range(2)]
            WU = [ay.alloc([128, 8, 512], BF16) for _ in range(2)]
            WD = [ar.alloc([128, 4, D], BF16) for _ in range(2)]
            ACTB = [ar.alloc([128, 4, 512], BF16) for _ in range(2)]
            SG = [ar.alloc([128, 512], F32) for _ in range(2)]
            GBC = ar.alloc([128, D], F32)
            D8 = ar.alloc([128, 8, 128], F32)
            gate_bc(l, si, 0.5 / ALPHA, GBC, D8)
            groups = [(0, 2)] + [(2 + g * 4, 4) for g in range(5)]
            pend = None
            nsg = 0
            for gi, (c0, nf) in enumerate(groups):
                b = gi % 2
                f0 = c0 * 128
                fw = nf * 128
                dma("pool", WG[b][:, :, 0:fw], wg_d[l, i, :, f0:f0 + fw].rearrange("(kc p) f -> p kc f", p=128), w=[("WG", b)])
                dma("pool", WU[b][:, :, 0:fw], wu_d[l, i, :, f0:f0 + fw].rearrange("(kc p) f -> p kc f", p=128), w=[("WU", b)])
                dma("pool", WD[b][:, 0:nf, :], wd_d[l, i, f0:f0 + fw, :].rearrange("(c p) d -> p c d", p=128), w=[("WD", b)])
                for c in range(nf):
                    tt("pool", WD[b][:, c, :], WD[b][:, c, :], GBC, ALU.mult, r=["GBC", ("WD", b)], w=[("WD", b)])
                for tsi in range(4):
                    ab = (gi * 4 + tsi) % 2
                    for c in range(nf):
                        pgk = (gi * 16 + tsi * 4 + c) % 2
                        pg, pgr = ps(pgk)
                        pu, pur = ps(2 + pgk)
                        for kc in range(8):
                            mm(pg[:, :], WG[b][:, kc, c * 128:(c + 1) * 128], HT[:, kc, tsi * 512:(tsi + 1) * 512], kc == 0, kc == 7,
                               r=[("WG", b), ("HT", tsi)], w=[pgr])
                        for kc in range(8):
                            mm(pu[:, :], WU[b][:, kc, c * 128:(c + 1) * 128], HT[:, kc, tsi * 512:(tsi + 1) * 512], kc == 0, kc == 7,
                               r=[("WU", b), ("HT", tsi)], w=[pur])
                        sgb = nsg % 2
                        nsg += 1
                        act(SG[sgb], pg[:, :], AF.Silu, r=[pgr], w=[("SG", sgb)])
                        tt("dve", ACTB[ab][:, c, :], SG[sgb], pu[:, :], ALU.mult, r=[("SG", sgb), pur], w=[("ACTB", ab, c)])
                    cur = (b, ab, nf, tsi, gi == len(groups) - 1)
                    if pend is not None:
                        down(pend, WD, ACTB, tail)
                    pend = cur
            down(pend, WD, ACTB, tail)
            tail.lagged(0)

        dcount = [0]

        def down(p, WD, ACTB, tail):
            b, ab, nf, tsi, last = p
            for t4 in range(4):
                tt_ = tsi * 4 + t4
                for hf in range(2):
                    k = 4 + dcount[0] % 2
                    dcount[0] += 1
                    pd, pdr = ps(k)
                    for c in range(nf):
                        mm(pd[:, :], ACTB[ab][:, c, t4 * 128:(t4 + 1) * 128], WD[b][:, c, hf * 512:(hf + 1) * 512], c == 0, c == nf - 1,
                           r=[("ACTB", ab, c), ("WD", b)], w=[pdr])
                    tt("dve", X[:, tt_, hf * 512:(hf + 1) * 512], X[:, tt_, hf * 512:(hf + 1) * 512], pd[:, :], ALU.add,
                       r=[("X", tt_), pdr], w=[("X", tt_)])
                if last:
                    tail.tile_done(tt_)
                    tail.lagged(2)

        stage = [0]

        def done_stage():
            stage[0] += 1
            return stop is not None and stage[0] >= stop

        def run_layers():
            for l in range(DEPTH):
                fence()
                with arena_scope():
                    ffn(l, 0, 0, Tail((l, 0), (l, 1), False))
                if done_stage():
                    return
                fence()
                with arena_scope():
                    mixer(l, Tail((l, 1), (l, 2), False) if not dbg else None)
                if dbg:
                    return
                if done_stage():
                    return
                fence()
                lastl = l == DEPTH - 1
                with arena_scope():
                    ffn(l, 1, 2, Tail((l, 2), None if lastl else (l + 1, 0), lastl))
                if done_stage():
                    return

        TWO_PI = 2.0 * math.pi

        def act_b(out, in_, func, r, w, bias, scale):
            return act(out, in_, func, r, w, bias=bias, scale=scale)

        def pool_phase(l):
            ar = Arena(0, ARB)
            WUP = ar.alloc([128, 8, 256], BF16)
            PW = ar.alloc([128, 2, 128], BF16)
            pscol = ar.alloc([128, 2], F32)
            UP = [ar.alloc([128, 2, 528], F32) for _ in range(2)]
            Pb = ar.alloc([128, 2, 528], F32)
            Qb = ar.alloc([128, 2, 528], F32)
            PL = ar.alloc([128, 2, 512], BF16)
            TM = ar.alloc([128, 2, 16], F32)
            dma("pool", WUP, win_d[l, :, 1792:2048].rearrange("(kc p) f -> p kc f", p=128), w=["WUP"])
            dma("pool", PW, poolw_d[l], w=["PW"])
            dma("sp", pscol, pools_d[l], w=["pscol"])
            memset("pool", UP[0][:, :, 0:16], 0.0, w=[("UP", 0)])
            for tsi in range(4):
                ub = UP[tsi % 2]
                ur = ("UP", tsi % 2)
                for ch in range(2):
                    pu, pur = ps(ch)
                    for kc in range(8):
                        mm(pu[:, :], WUP[:, kc, ch * 128:(ch + 1) * 128], HT[:, kc, tsi * 512:(tsi + 1) * 512], kc == 0, kc == 7,
                           r=["WUP", ("HT", tsi)], w=[pur])
                    cp("act", ub[:, ch, 16:528], pu[:, :], r=[pur], w=[ur])
                tt("pool", Pb[:, :, 1:528], ub[:, :, 1:528], ub[:, :, 0:527], ALU.add, r=[ur], w=["Pb"])
                tt("pool", Qb[64:128, 0, 3:528], Pb[64:128, 0, 3:528], Pb[64:128, 0, 1:526], ALU.add, r=["Pb"], w=["Qb"])
                tt("pool", Qb[:, 1, 3:528], Pb[:, 1, 3:528], Pb[:, 1, 1:526], ALU.add, r=["Pb"], w=["Qb"])
                tt("pool", Pb[:, 1, 7:528], Qb[:, 1, 7:528], Qb[:, 1, 3:524], ALU.add, r=["Qb"], w=["Pb"])
                tt("pool", Qb[64:128, 1, 15:528], Pb[64:128, 1, 15:528], Pb[64:128, 1, 7:520], ALU.add, r=["Pb"], w=["Qb"])
                srcs = [(Pb, 0, 64, 0), (Qb, 64, 128, 0), (Pb, 0, 64, 1), (Qb, 64, 128, 1)]
                for (sb, p0, p1, ch) in srcs:
                    stt(PL[p0:p1, ch, :], sb[p0:p1, ch, 16:528], invw[p0:p1, ch:ch + 1], ub[p0:p1, ch, 16:528], ALU.mult, ALU.subtract,
                        r=["Pb", "Qb", ur, "invw"], w=["PL"])
                    if tsi == 0:
                        tt("dve", TM[p0:p1, ch, :], sb[p0:p1, ch, 16:32], rc16[p0:p1, ch, :], ALU.mult, r=["Pb", "Qb", "rc16"], w=["TM"])
                        tt("dve", PL[p0:p1, ch, 0:16], TM[p0:p1, ch, :], ub[p0:p1, ch, 16:32], ALU.subtract, r=["TM", ur], w=["PL"])
                if tsi < 3:
                    cp("pool", UP[(tsi + 1) % 2][:, :, 0:16], ub[:, :, 512:528], r=[ur], w=[("UP", (tsi + 1) % 2)])
                for ch in range(2):
                    py, pyr = ps(2 + ch)
                    mm(py[:, :], PW[:, ch, :], PL[:, ch, :], True, True, r=["PW", "PL"], w=[pyr])
                    act_b(YT[:, 6 + ch, tsi * 512:(tsi + 1) * 512], py[:, :], AF.Identity, r=[pyr, "pscol"], w=[("YT", 6 + ch, tsi)],
                          bias=0.0, scale=pscol[:, ch:ch + 1])

        TL = 128
        NTB = 16 * (TL + 1)
        ssc_f = nc.dram_tensor("ssm_scr_f", [DEPTH, 128, 2 * NTB + 224], F32, kind="Internal").ap()
        ssc_b = nc.dram_tensor("ssm_scr_b", [DEPTH, 128, NTB + 3 * 2048], BF16, kind="Internal").ap()

        class ArenaOn(Arena):
            def __init__(self, flat, size):
                self.flat = flat
                self.off = 0
                self.end = size

            def alloc(self, shape, dt):
                n = int(np.prod(shape[1:]))
                nb = n * (4 if dt in (F32, I32) else 2)
                nb = (nb + 31) // 32 * 32
                assert self.off + nb <= self.end, ("arenaOn overflow", shape, self.off, nb, self.end)
                v = self.flat[0:shape[0], self.off // 2:(self.off + nb) // 2]
                self.off += nb
                if dt != BF16:
                    v = v.bitcast(dt)
                v = v[:, 0:n]
                if len(shape) == 3:
                    v = v.rearrange("p (a b) -> p a b", a=shape[1])
                return v

        def ssm_setup_gen(l):
            ah = ArenaOn(HT[:, :, :].rearrange("p a b -> p (a b)"), 32768)
            ay = ArenaOn(YT[:, :, :].rearrange("p a b -> p (a b)"), 32768)
            ar = Arena(40992, ARB - 40992)
            TC = ah.alloc([128, 16, TL + 1], F32)
            TS = ah.alloc([128, 16, TL + 1], F32)
            ANG = ah.alloc([128, 16, TL + 1], F32)
            BL = ay.alloc([128, 16, 128], BF16)
            IBL = ay.alloc([128, 16, 128], BF16)
            CL = ay.alloc([128, 16, 128], BF16)
            SV = ay.alloc([128, 14, 16], F32)
            P1 = ay.alloc([128, 16, 16], F32)
            P2 = ay.alloc([128, 16, 16], F32)
            CT = ay.alloc([128, 256], F32)
            TCb = ay.alloc([128, 16, TL + 1], BF16)
            AI = ay.alloc([128, 16, TL + 1], I32)
            JF = ar.alloc([128, TL + 1], F32)
            JI = ar.alloc([128, TL + 1], I32)
            Bc = ar.alloc([128, 16, 16], F32)
            IBc = ar.alloc([128, 16, 16], F32)
            Tm = ar.alloc([128, 16, 16], F32)
            are, aim, ldt, dtv, lr, thn, rho, sn, cs, er, ei, gr, gi, tq = [SV[:, k, :] for k in range(14)]
            V = "ssmv"
            dma("pool", are, sare_d[l], w=["are", V])
            dma("pool", aim, saim_d[l], w=["aim", V])
            dma("pool", ldt, sldt_d[l], w=["ldt", V])
            dma("pool", P1, sp1_d[l].rearrange("p (g c) -> p g c", g=16), w=["P1"])
            dma("pool", P2, sp2_d[l].rearrange("p (g c) -> p g c", g=16), w=["P2"])
            dma("pool", CT, sct_d[l], w=["CT"])
            yield
            act(dtv, ldt, AF.Exp, r=["ldt"], w=[V])
            tt("dve", lr, are, dtv, ALU.mult, r=["are", V], w=[V])
            tt("dve", thn, aim, dtv, ALU.mult, r=["aim", V], w=[V])
            ts("dve", thn, thn, 1.0 / TWO_PI, None, ALU.mult, None, r=[V], w=[V])
            yield
            ts("dve", rho, lr, 1.0 / 720.0, 1.0 / 120.0, ALU.mult, ALU.add, r=[V], w=[V])
            for cst in (1.0 / 24.0, 1.0 / 6.0, 0.5, 1.0, 1.0):
                tt("dve", rho, rho, lr, ALU.mult, r=[V], w=[V])
                ts("dve", rho, rho, float(cst), None, ALU.add, None, r=[V], w=[V])
                yield

            def sincos(dst, src, shift, ai):
                ts("dve", dst, src, float(shift), None, ALU.add, None, r=[V], w=[V])
                cp("dve", ai, dst, r=[V], w=[V])
                tt("dve", dst, dst, ai, ALU.subtract, r=[V], w=[V])
                act(dst, dst, AF.Sin, r=[V], w=[V], scale=TWO_PI)

            sincos(sn, thn, 0.0, AI[:, 0, 0:16])
            yield
            sincos(cs, thn, 0.25, AI[:, 0, 0:16])
            yield
            tt("dve", er, rho, cs, ALU.mult, r=[V], w=[V])
            ts("dve", er, er, -1.0, None, ALU.add, None, r=[V], w=[V])
            tt("dve", ei, rho, sn, ALU.mult, r=[V], w=[V])
            tt("dve", tq, are, are, ALU.mult, r=[V, "are"], w=[V])
            yield
            tt("dve", gr, aim, aim, ALU.mult, r=[V, "aim"], w=[V])
            tt("dve", tq, tq, gr, ALU.add, r=[V], w=[V])
            S.op("dve", lambda e: e.reciprocal(out=tq, in_=tq), r=[V], w=[V])
            tt("dve", gr, er, are, ALU.mult, r=[V], w=[V])
            yield
            tt("dve", gi, ei, aim, ALU.mult, r=[V], w=[V])
            tt("dve", gr, gr, gi, ALU.add, r=[V], w=[V])
            tt("dve", gr, gr, tq, ALU.mult, r=[V], w=[V])
            tt("dve", gi, ei, are, ALU.mult, r=[V], w=[V])
            yield
            tt("dve", er, er, aim, ALU.mult, r=[V], w=[V])
            tt("dve", gi, gi, er, ALU.subtract, r=[V], w=[V])
            tt("dve", gi, gi, tq, ALU.mult, r=[V], w=[V])
            S2, S3, S4 = ei, er, tq
            ts("dve", S2, gi, sgn[:, 0:1], None, ALU.mult, None, r=[V, "sgn"], w=[V])
            yield
            ts("dve", S3, gr, sgn[:, 0:1], None, ALU.mult, None, r=[V, "sgn"], w=[V])
            ts("dve", S4, gi, -1.0, None, ALU.mult, None, r=[V], w=[V])
            bc = lambda v: v.unsqueeze(2).to_broadcast([128, 16, 16])
            tt("dve", Bc, P1, bc(gr), ALU.mult, r=[V, "P1"], w=["Bc"])
            tt("dve", Tm, P2, bc(S2), ALU.mult, r=[V, "P2"], w=["Tm"])
            yield
            tt("dve", Bc, Bc, Tm, ALU.add, r=["Tm"], w=["Bc"])
            tt("dve", IBc, P2, bc(S3), ALU.mult, r=[V, "P2"], w=["IBc"])
            tt("dve", Tm, P1, bc(S4), ALU.mult, r=[V, "P1", "Bc"], w=["Tm"])
            tt("dve", IBc, IBc, Tm, ALU.add, r=["Tm"], w=["IBc"])
            yield
            for (src, dst, nm) in ((Bc, BL, "Bc"), (IBc, IBL, "IBc")):
                flat = src.rearrange("p g c -> p (g c)")
                for ch in range(2):
                    pt_, ptr_ = ps(7)
                    tr(pt_[:, 0:128], flat[:, ch * 128:(ch + 1) * 128], identf[:], r=[nm, "identf"], w=[ptr_])
                    for g8 in range(8):
                        ts("dve", dst[:, ch * 8 + g8, :], pt_[:, 0:128], gmask[:, g8:g8 + 1], None, ALU.mult, None,
                           r=[ptr_, "gmask"], w=["BL"])
                        if g8 % 4 == 3:
                            yield
            ts("dve", CT, CT, sgn[:, 0:1], -1.0, ALU.mult, ALU.mult, r=["CT", "sgn"], w=["CT"])
            memset("pool", CL, 0.0, w=["CL"])
            for g in range(16):
                g8 = g % 8
                cp("dve", CL[:, g, 16 * g8:16 * g8 + 16], CT[:, g * 16:(g + 1) * 16], r=["CT"], w=["CL"])
                if g % 4 == 3:
                    yield
            S.op("pool", lambda e: e.iota(out=JI, pattern=[[1, TL + 1]], base=0, channel_multiplier=0), w=["JI"])
            cp("dve", JF, JI, r=["JI"], w=["JF"])
            tt("dve", ANG, JF.unsqueeze(1).to_broadcast([128, 16, TL + 1]), thn.unsqueeze(2).to_broadcast([128, 16, TL + 1]),
               ALU.mult, r=["JF", V], w=[V])
            yield
            sincos(TS, ANG, 0.0, AI)
            yield
            sincos(TC, ANG, 0.25, AI)
            yield
            cp("dve", TCb, TC, r=[V], w=[V])
            dma("pool", ssc_f[l, :, 0:NTB], TC.rearrange("p a b -> p (a b)"), r=[V], w=[("ssc", l)])
            dma("pool", ssc_f[l, :, NTB:2 * NTB], TS.rearrange("p a b -> p (a b)"), r=[V], w=[("ssc", l)])
            dma("pool", ssc_f[l, :, 2 * NTB:2 * NTB + 224], SV.rearrange("p a b -> p (a b)"), r=[V], w=[("ssc", l)])
            dma("pool", ssc_b[l, :, 0:NTB], TCb.rearrange("p a b -> p (a b)"), r=[V], w=[("ssc", l)])
            dma("pool", ssc_b[l, :, NTB:NTB + 2048], BL.rearrange("p a b -> p (a b)"), r=["BL"], w=[("ssc", l)])
            dma("pool", ssc_b[l, :, NTB + 2048:NTB + 4096], IBL.rearrange("p a b -> p (a b)"), r=["BL"], w=[("ssc", l)])
            dma("pool", ssc_b[l, :, NTB + 4096:NTB + 6144], CL.rearrange("p a b -> p (a b)"), r=["CL"], w=[("ssc", l)])
            yield

        def ssm_phase(l):
            ay = ArenaOn(YT[:, 0:4, :].rearrange("p a b -> p (a b)"), 16384)
            ar = Arena(0, ARB)
            BL = ay.alloc([128, 16, 128], BF16)
            IBL = ay.alloc([128, 16, 128], BF16)
            CL = ay.alloc([128, 16, 128], BF16)
            SV = ay.alloc([128, 14, 16], F32)
            WUS = ar.alloc([128, 8, 256], BF16)
            TC = ar.alloc([128, 16, TL + 1], F32)
            TS = ar.alloc([128, 16, TL + 1], F32)
            GW = ar.alloc([128, 2, 256], BF16)
            sdcol = ar.alloc([128, 2], F32)
            glub = ar.alloc([128, 2], F32)
            TCb = ar.alloc([128, 16, TL + 1], BF16)
            rho = SV[:, 6, :]
            V = "ssmv2"
            dma("sp", TC.rearrange("p a b -> p (a b)"), ssc_f[l, :, 0:NTB], r=[("ssc", l)], w=[V])
            dma("sp", TS.rearrange("p a b -> p (a b)"), ssc_f[l, :, NTB:2 * NTB], r=[("ssc", l)], w=[V])
            dma("sp", SV.rearrange("p a b -> p (a b)"), ssc_f[l, :, 2 * NTB:2 * NTB + 224], r=[("ssc", l)], w=[V])
            dma("sp", TCb.rearrange("p a b -> p (a b)"), ssc_b[l, :, 0:NTB], r=[("ssc", l)], w=[V])
            dma("sp", BL.rearrange("p a b -> p (a b)"), ssc_b[l, :, NTB:NTB + 2048], r=[("ssc", l)], w=["BL"])
            dma("sp", IBL.rearrange("p a b -> p (a b)"), ssc_b[l, :, NTB + 2048:NTB + 4096], r=[("ssc", l)], w=["BL"])
            dma("sp", CL.rearrange("p a b -> p (a b)"), ssc_b[l, :, NTB + 4096:NTB + 6144], r=[("ssc", l)], w=["CL"])
            dma("sp", sdcol, sd_d[l], w=["sdcol"])
            dma("sp", glub, glub_d[l], w=["glub"])
            dma("pool", GW, gluw_d[l], w=["GW"])
            dma("pool", WUS, win_d[l, :, 1536:1792].rearrange("(kc p) f -> p kc f", p=128), w=["WUS"])
            USS2 = [ar.alloc([128, 2, 512], BF16) for _ in range(2)]
            Wb = [ar.alloc([128, 4, TL], BF16) for _ in range(2)]
            T2 = ar.alloc([128, 4, TL], BF16)
            STb = [ar.alloc([128, 4, TL], F32) for _ in range(2)]
            SBh = [ar.alloc([128, 4, TL], BF16) for _ in range(2)]
            T2b = ar.alloc([128, 4, TL], BF16)
            S1 = ar.alloc([128, 4, TL], BF16)
            SB = [ar.alloc([128, 4, TL], BF16) for _ in range(2)]
            GE = ar.alloc([128, 2, TL], BF16)
            CI = ar.alloc([128, 16], F32)
            CA = ar.alloc([128, 4], F32)
            CB = ar.alloc([128, 4], F32)
            YV = ay.alloc([128, 2, TL], F32)
            G1 = ay.alloc([128, 2, TL], F32)
            G2 = ay.alloc([128, 2, TL], F32)
            NB = 64
            p0_, p0r = ps(0)
            pL, pLr = ps(6)
            pAs = [ps(1), ps(2)]
            pBs = [ps(3), ps(7)]
            pC, pCr = ps(4)
            pC3 = pC[:, :].rearrange("p (a b) -> p a b", a=4)

            def stage_F_pe(n):
                k, bq = n // 4, n % 4
                tsi, kk, ch = k // 4, k % 4, bq // 2
                USS = USS2[tsi % 2]
                ur = ("USS", tsi % 2)
                if kk == 0 and bq == 0:
                    for c2 in range(2):
                        for kc in range(8):
                            mm(p0_[:, :], WUS[:, kc, c2 * 128:(c2 + 1) * 128], HT[:, kc, tsi * 512:(tsi + 1) * 512], kc == 0, kc == 7,
                               r=["WUS", ("HT", tsi)], w=[p0r])
                        cp("act", USS[:, c2, :], p0_[:, :], r=[p0r], w=[ur])
                pA, pAr = pAs[n % 2]
                pB, pBr = pBs[n % 2]
                for gi_ in range(4):
                    mm(pA[:, gi_ * TL:(gi_ + 1) * TL], BL[:, 4 * bq + gi_, :], USS[:, ch, kk * TL:(kk + 1) * TL], True, True, r=["BL", ur], w=[pAr])
                for gi_ in range(4):
                    mm(pB[:, gi_ * TL:(gi_ + 1) * TL], IBL[:, 4 * bq + gi_, :], USS[:, ch, kk * TL:(kk + 1) * TL], True, True, r=["BL", ur], w=[pBr])

            def stage_F_dve(n):
                k, bq = n // 4, n % 4
                gs = slice(4 * bq, 4 * bq + 4)
                wb = Wb[n % 2]
                wbr = ("Wb", n % 2)
                pA, pAr = pAs[n % 2]
                pB, pBr = pBs[n % 2]
                pA3 = pA[:, :].rearrange("p (a b) -> p a b", a=4)
                pB3 = pB[:, :].rearrange("p (a b) -> p a b", a=4)
                tt("dve", wb, pA3, TC[:, gs, 0:TL], ALU.mult, r=[pAr, V], w=[wbr])
                tt("dve", T2, pB3, TS[:, gs, 0:TL], ALU.mult, r=[pBr, V], w=["T2"])
                tt("dve", wb, wb, T2, ALU.subtract, r=["T2"], w=[wbr])

            def stage_S(n):
                k, bq = n // 4, n % 4
                wb = Wb[n % 2]
                wbr = ("Wb", n % 2)
                stb = STb[n % 2]
                stbr = ("STb", n % 2)
                sbh = SBh[n % 2]
                sbhr = ("SBh", n % 2)
                for gi_ in range(4):
                    g = 4 * bq + gi_
                    ini = CI[:, g:g + 1] if k > 0 else 0.0
                    S.op("dve", lambda e, gi_=gi_, g=g, ini=ini: e.tensor_tensor_scan(
                        out=stb[:, gi_, :], data0=rho[:, g:g + 1].to_broadcast([128, TL]), data1=wb[:, gi_, :],
                        initial=ini, op0=ALU.mult, op1=ALU.add), r=[wbr, V, ("CI", bq)], w=[stbr])
                cp("act", sbh, stb, r=[stbr], w=[sbhr])
                mm(pC[:, :], jswb[:], sbh.rearrange("p a b -> p (a b)"), True, True, r=["jswb", sbhr], w=[pCr])
                if k < 15:
                    mm(pL[:, 0:4], jsw[:], stb[:, :, TL - 1], True, True, r=["jsw", stbr], w=[pLr])

            def stage_B(n):
                k, bq = n // 4, n % 4
                kk, ch = k % 4, bq // 2
                tsi = k // 4
                gs = slice(4 * bq, 4 * bq + 4)
                stb = STb[n % 2]
                stbr = ("STb", n % 2)
                sbh = SBh[n % 2]
                sbhr = ("SBh", n % 2)
                sb = SB[n % 2]
                sbr = ("SB", n % 2)
                USS = USS2[tsi % 2]
                ur = ("USS", tsi % 2)
                tt("dve", T2b, pC3, TS[:, gs, 0:TL], ALU.mult, r=[pCr, V], w=["T2b"])
                tt("dve", S1, sbh, TCb[:, gs, 0:TL], ALU.mult, r=[sbhr, V], w=["S1"])
                tt("dve", sb, S1, T2b, ALU.add, r=["S1", "T2b"], w=[sbr])
                if k < 15:
                    tt("dve", CA, pL[:, 0:4], TS[:, gs, TL], ALU.mult, r=[pLr, V], w=["CA"])
                    tt("dve", CB, stb[:, :, TL - 1], TC[:, gs, TL], ALU.mult, r=[stbr, V], w=["CB"])
                    tt("dve", CI[:, gs], CA, CB, ALU.add, r=["CA", "CB"], w=[("CI", bq)])
                pY, pYr = ps(5)
                for gi_ in range(4):
                    g = 4 * bq + gi_
                    mm(pY[:, ch * TL:(ch + 1) * TL], CL[:, g, :], sb[:, gi_, :], g % 8 == 0, g % 8 == 7, r=["CL", sbr], w=[pYr])
                if bq == 3:
                    glu_a(k)
                if bq == 0 and k > 0:
                    glu_b(k - 1)

            def glu_a(k):
                kk, tsi = k % 4, k // 4
                USS = USS2[tsi % 2]
                ur = ("USS", tsi % 2)
                cols = slice(kk * TL, (kk + 1) * TL)
                for c2 in range(2):
                    pY2, pY2r = ps(5)
                    stt(YV[:, c2, :], USS[:, c2, cols], sdcol[:, c2:c2 + 1], pY2[:, c2 * TL:(c2 + 1) * TL], ALU.mult, ALU.add, r=[ur, "sdcol", pY2r], w=["YV"])
                act(G1, YV, AF.Square, r=["YV"], w=["G1"])
                ts("pool", G1, G1, 0.044715, 1.0, ALU.mult, ALU.add, r=["G1"], w=["G1"])
                tt("pool", G1, G1, YV, ALU.mult, r=["G1", "YV"], w=["G1"])
                act(G2, G1, AF.Sigmoid, r=["G1"], w=["G2"], scale=2.0 * math.sqrt(2.0 / math.pi))
                tt("pool", GE, YV, G2, ALU.mult, r=["G2", "YV"], w=["GE"])

            def glu_b(k):
                tsi = k // 4
                tok = slice(k * TL, (k + 1) * TL)
                for dch in range(2):
                    pG, pGr = ps(6)
                    for c2 in range(2):
                        mm(pG[:, 128:128 + TL], GW[:, c2, dch * 128:(dch + 1) * 128], GE[:, c2, :], c2 == 0, c2 == 1, r=["GW", "GE"], w=[pGr])
                    act_b(G1[:, dch, :], pG[:, 128:128 + TL], AF.Sigmoid, r=[pGr, "glub", "G1"], w=["G1"], bias=glub[:, dch:dch + 1], scale=1.0)
                    tt("pool", YT[:, 4 + dch, tok], YV[:, dch, :], G1[:, dch, :], ALU.mult, r=["G1", "YV"], w=[("YT", 4 + dch, tsi)])

            stage_F_pe(0)
            for it in range(NB + 2):
                if it + 1 < NB:
                    stage_F_pe(it + 1)
                if it < NB:
                    stage_F_dve(it)
                if 0 <= it - 2 < NB:
                    stage_B(it - 2)
                if 0 <= it - 1 < NB:
                    stage_S(it - 1)
            glu_b(15)

        def bias_setup():
            ar = Arena(0, ARB)
            relb = ar.alloc([32, 8], F32)
            OH = ar.alloc([32, 1152], F32)
            NG = ar.alloc([8, 1152], F32)
            FV = ar.alloc([8, 1152], F32)
            dma("sp", relb, relb_d, w=["relb"])
            dma("sp", OH, oh_d, w=["OH"])
            dma("sp", NG, negm_d, w=["NG"])
            for br in range(3):
                p_, pr_ = ps(br)
                mm(p_[0:8, 0:384], relb, OH[:, br * 384:(br + 1) * 384], True, True, r=["relb", "OH"], w=[pr_])
                tt("dve", FV[:, br * 384:(br + 1) * 384], p_[0:8, 0:384], NG[:, br * 384:(br + 1) * 384], ALU.add, r=[pr_, "NG"], w=["FV"])
            dma("sp", fv_d, FV, r=["FV"], w=["fv_d"])

        def attn_phase(l):
            ar = Arena(0, ARB)
            WQKV = ar.alloc([128, 8, 384], BF16)
            QZ = [ar.alloc([128, SEQ], BF16) for _ in range(2)]
            KT = ar.alloc([128, SEQ], BF16)
            VP = ar.alloc([128, 3, 16, 192], BF16)
            BTp = ar.alloc([128, 2, 768], BF16)
            Hb = ar.alloc([128, 256], F32)
            PT = [ar.alloc([128, 128], BF16) for _ in range(8)]
            RD = [ar.alloc([128, 512], F32) for _ in range(1)]
            VT = ar.alloc([128, SEQ], BF16)
            memset("pool", QZ[0][64:128, :], 0.0, w=[("QZ", 0)])
            memset("pool", QZ[1][0:64, :], 0.0, w=[("QZ", 1)])
            memset("pool", VP[:, :, :, 64:128], 1.0, w=["VP"])
            npt = 0
            nsc = 0
            nrd = 0
            for hp in range(4):
                for j, base in enumerate((0, 512, 1024)):
                    dma("pool", WQKV[:, :, j * 128:(j + 1) * 128],
                        win_d[l, :, base + hp * 128:base + (hp + 1) * 128].rearrange("(kc p) f -> p kc f", p=128), w=["WQKV"])
                for tsi in range(4):
                    pq, pqr = ps(6)
                    for kc in range(8):
                        mm(pq[:, :], WQKV[:, kc, 0:128], HT[:, kc, tsi * 512:(tsi + 1) * 512], kc == 0, kc == 7, r=["WQKV", ("HT", tsi)], w=[pqr])
                    act(QZ[0][0:64, tsi * 512:(tsi + 1) * 512], pq[0:64, :], AF.Copy, r=[pqr], w=[("QZ", 0)], scale=0.125)
                    act(QZ[1][64:128, tsi * 512:(tsi + 1) * 512], pq[64:128, :], AF.Copy, r=[pqr], w=[("QZ", 1)], scale=0.125)
                    pk, pkr = ps(7)
                    for kc in range(8):
                        mm(pk[:, :], WQKV[:, kc, 128:256], HT[:, kc, tsi * 512:(tsi + 1) * 512], kc == 0, kc == 7, r=["WQKV", ("HT", tsi)], w=[pkr])
                    cp("dve", KT[:, tsi * 512:(tsi + 1) * 512], pk[:, :], r=[pkr], w=["KT"])
                for tsi in range(4):
                    pvt, pvtr = ps(6 + tsi % 2)
                    for kc in range(8):
                        mm(pvt[:, :], WQKV[:, kc, 256:384], HT[:, kc, tsi * 512:(tsi + 1) * 512], kc == 0, kc == 7, r=["WQKV", ("HT", tsi)], w=[pvtr])
                    cp("act", VT[:, tsi * 512:(tsi + 1) * 512], pvt[:, :], r=[pvtr], w=["VT"])
                nv = 0
                for br, (win, dil) in enumerate(PATTERNS):
                    nbk = 16 // dil
                    for q4 in range(4):
                        pv, pvr = ps(6 + nv % 2)
                        nv += 1
                        pvb = pv[:, :].bitcast(BF16)
                        for t4 in range(4):
                            tix = q4 * 4 + t4
                            rr, m = tix // nbk, tix % nbk
                            t0 = rr + dil * 128 * m
                            tr(pvb[:, t4 * 128:(t4 + 1) * 128], VT[:, t0:t0 + dil * 127 + 1:dil], ident[:], r=["VT", "ident"], w=[pvr])
                        outv = VP[:, br, q4 * 4:(q4 + 1) * 4, :].rearrange("p t (a c) -> p t a c", a=3)[:, :, 0:3:2, :]
                        inv_ = pvb[:, 0:512].rearrange("p (t a c) -> p t a c", t=4, a=2)
                        cp("dve", outv, inv_, r=[pvr], w=["VP"])
                for hh in range(2):
                    h = 2 * hp + hh
                    for br in range(3):
                        dma("sp", Hb, bass.AP(fv_d.tensor, h * 1152 + br * 384, [[1, 128], [1, 256]]), r=["fv_d"], w=["Hb"])
                        pb_, pbr_ = ps(6 + br % 2)
                        mm(pb_[:, 0:256], jex[:], Hb, True, True, r=["jex", "Hb"], w=[pbr_])
                        act(BTp[:, hh, br * 256:(br + 1) * 256], pb_[:, 0:256], AF.Exp, r=[pbr_], w=["BTp"])
                for hh in range(2):
                    accs = [ps(b) for b in range(4)]
                    started = [False] * 4
                    tasks = []
                    for br, (win, dil) in enumerate(PATTERNS):
                        nbk = 16 // dil
                        for rr in range(dil):
                            for m in range(nbk):
                                for qb in (m, m + 1):
                                    if qb >= nbk:
                                        continue
                                    tasks.append((br, dil, nbk, rr, m, qb))
                    pendq = []

                    def pieces_of(task):
                        br, dil, nbk, rr, m, qb = task
                        if dil == 16:
                            return [(b, rr, 32 * b, 32) for b in range(4)]
                        elif dil == 4:
                            return [(qb, rr, 0, 128)]
                        t0 = 128 * qb
                        return [(t0 // 512, t0 % 512, 0, 128)]

                    remaining = [0] * 4
                    for task in tasks:
                        for (b, c0, i0, n) in pieces_of(task):
                            remaining[b] += 1

                    def do_pv(task, slot):
                        br, dil, nbk, rr, m, qb = task
                        tix = rr * nbk + m
                        lhs = VP[:, br, tix, hh * 64:hh * 64 + 128]
                        for (b, c0, i0, n) in pieces_of(task):
                            acc, accr = accs[b]
                            remaining[b] -= 1
                            mm(acc[:, c0:c0 + dil * (n - 1) + 1:dil], lhs, PT[slot][:, i0:i0 + n], not started[b], remaining[b] == 0,
                               r=["VP", ("PT", slot)], w=[accr])
                            started[b] = True

                    for task in tasks:
                        br, dil, nbk, rr, m, qb = task
                        k0 = rr + dil * 128 * m
                        q0 = rr + dil * 128 * qb
                        psc, pscr = ps(4 + nsc % 4)
                        nsc += 1
                        mm(psc[:, 0:128], KT[:, k0:k0 + dil * 127 + 1:dil], QZ[hh][:, q0:q0 + dil * 127 + 1:dil], True, True,
                           r=["KT", ("QZ", hh)], w=[pscr])
                        off = (qb - m) * 128
                        slot = npt % 8
                        npt += 1
                        act(PT[slot], psc[:, 0:128], AF.Exp, r=[pscr], w=[("PT", slot)])
                        tt("dve", PT[slot], PT[slot], BTp[:, hh, br * 256 + off:br * 256 + off + 128], ALU.mult, r=["BTp", ("PT", slot)], w=[("PT", slot)])
                        pendq.append((task, slot))
                        if len(pendq) > 6:
                            do_pv(*pendq.pop(0))
                    while pendq:
                        do_pv(*pendq.pop(0))
                    for b in range(4):
                        acc, accr = accs[b]
                        rd = RD[0]
                        rdr = ("RD", 0)
                        nrd += 1
                        if hh == 0:
                            act(rd[0:64, :], acc[64:128, :], AF.Ln, r=[accr], w=[rdr])
                            act(rd[0:64, :], rd[0:64, :], AF.Exp, r=[rdr], w=[rdr], scale=-1.0)
                            tt("dve", YT[0:64, hp, b * 512:(b + 1) * 512], acc[0:64, :], rd[0:64, :], ALU.mult, r=[accr, rdr], w=[("YT", hp, b)])
                        else:
                            act(rd[64:128, :], acc[0:64, :], AF.Ln, r=[accr], w=[rdr])
                            act(rd[64:128, :], rd[64:128, :], AF.Exp, r=[rdr], w=[rdr], scale=-1.0)
                            tt("dve", YT[64:128, hp, b * 512:(b + 1) * 512], acc[64:128, :], rd[64:128, :], ALU.mult, r=[accr, rdr], w=[("YT", hp, b)])

        def wout_phase(l, tail):
            ar = Arena(0, ARB)
            WO = ar.alloc([128, 8, D], BF16)
            GBC = ar.alloc([128, D], F32)
            D8 = ar.alloc([128, 8, 128], F32)
            dma("pool", WO, wout_d[l].rearrange("(kc p) d -> p kc d", p=128), w=["WO"])
            gate_bc(l, 1, 1.0 / ALPHA, GBC, D8)
            for kc in range(8):
                tt("pool", WO[:, kc, :], WO[:, kc, :], GBC, ALU.mult, r=["GBC", "WO"], w=["WO"])
            n = 0
            for tt_ in range(NT):
                for hf in range(2):
                    po, por = ps(n % 2)
                    n += 1
                    for kc in range(8):
                        mm(po[:, :], YT[:, kc, tt_ * 128:(tt_ + 1) * 128], WO[:, kc, hf * 512:(hf + 1) * 512], kc == 0, kc == 7,
                           r=["WO"] + [("YT", kc, b) for b in range(4)], w=[por])
                    tt("dve", X[:, tt_, hf * 512:(hf + 1) * 512], X[:, tt_, hf * 512:(hf + 1) * 512], po[:, :], ALU.add,
                       r=[("X", tt_), por], w=[("X", tt_)])
                tail.tile_done(tt_)
                tail.lagged(2)
            tail.lagged(0)

        def mixer(l, tail):
            if "pool" in parts:
                pool_phase(l)
                fence()
            if "ssm" in parts:
                ssm_phase(l)
                fence()
            if "attn" in parts:
                attn_phase(l)
                fence()
            if dbg:
                return
            wout_phase(l, tail)

        with arena_scope():
            ada_phase()
        fence_all()
        with arena_scope():
            bias_setup()
        run_layers()
        if dbg:
            fence_all()
            dbg_d = nc.dram_tensor("dbg", [128, 8, SEQ], F32, kind="ExternalOutput").ap()
            dma("pool", dbg_d, YT[:, :, :], r=["ALL"], final=True)
        if dbg or stop is not None:
            fence_all()
            for q in range(4):
                dma("sp", ov[:, q * 4:(q + 1) * 4, :], X[:, q * 4:(q + 1) * 4, :], r=[("X", t) for t in range(q * 4, q * 4 + 4)], final=True)
        S.emit()
    return nc


def host_inputs(inputs):
    f = lambda a: np.ascontiguousarray(np.asarray(a, dtype=np.float32))
    consts = static_consts()
    shared = {}
    for k in ("ada_w", "ada_b", "ln_g", "ln_b", "ffn_w_gate", "ffn_w_up", "ffn_w_down", "w_in", "w_out", "rel_bias"):
        shared[k] = f(inputs[k])
    a_re, a_im = f(inputs["ssm_a_re"]), f(inputs["ssm_a_im"])
    dup = lambda a: np.concatenate([a, a], axis=1)
    shared["sa_re"] = f(dup(a_re.transpose(0, 2, 1)))
    shared["sa_im"] = f(dup(a_im.transpose(0, 2, 1)))
    shared["sldt"] = f(np.broadcast_to(f(inputs["ssm_log_dt"])[:, None, :], (DEPTH, 128, 16)))
    b_re = f(inputs["ssm_b_re"]).transpose(0, 2, 1, 3).reshape(DEPTH, 64, 256)
    b_im = f(inputs["ssm_b_im"]).transpose(0, 2, 1, 3).reshape(DEPTH, 64, 256)
    shared["sP1"] = f(np.concatenate([b_re, b_im], axis=1))
    shared["sP2"] = f(np.concatenate([b_im, b_re], axis=1))
    c_re = f(inputs["ssm_c_re"]).transpose(0, 3, 1, 2).reshape(DEPTH, 64, 256)
    c_im = f(inputs["ssm_c_im"]).transpose(0, 3, 1, 2).reshape(DEPTH, 64, 256)
    shared["sCT"] = f(np.concatenate([c_re, c_im], axis=1))
    col2 = lambda a: f(f(a).reshape(DEPTH, 2, 128).transpose(0, 2, 1))
    shared["sd"] = col2(inputs["ssm_d"])
    shared["glu_b"] = col2(inputs["glu_b"])
    shared["pool_s"] = col2(inputs["pool_scale"])
    shared["glu_w"] = f(f(inputs["glu_w"]).reshape(DEPTH, 2, 128, 256).transpose(0, 2, 1, 3))
    pw = f(inputs["pool_w"])
    pbd = np.zeros((DEPTH, 128, 2, 128), np.float32)
    for ch in range(2):
        for h in range(2):
            pbd[:, h * 64:(h + 1) * 64, ch, h * 64:(h + 1) * 64] = pw[:, ch * 2 + h]
    shared["pool_w"] = pbd
    shared.update(consts)
    x = f(inputs["x"])
    c = f(inputs["c"])
    maps = []
    for b in range(8):
        m = dict(shared)
        m["x"] = x[b]
        m["cT"] = f(c[b].reshape(8, 128).T)
        maps.append(m)
    return maps


_NC_CACHE = {}


def kernel(**inputs):
    if "nc" not in _NC_CACHE:
        _NC_CACHE["nc"] = build()
    nc = _NC_CACHE["nc"]
    maps = host_inputs(inputs)
    res = run_bass_kernel_spmd(nc, maps, core_ids=list(range(8)))
    return np.stack([np.asarray(r["out"], dtype=np.float32) for r in res.results], axis=0)
```

```python
import contextlib
import math
import numpy as np
import concourse.bass as bass
import concourse.mybir as mybir
from concourse.bass_utils import run_bass_kernel_spmd

F32 = mybir.dt.float32
BF16 = mybir.dt.bfloat16
I32 = mybir.dt.int32
AF = mybir.ActivationFunctionType
ALU = mybir.AluOpType

SEQ = 2048
D = 1024
DFF = 2816
DEPTH = 2
NT = SEQ // 128
ALPHA = (2 * DEPTH) ** 0.25
LN_EPS = 1e-5
NEG = -1e30
PATTERNS = ((128, 1), (512, 4), (2048, 16))

ENGS = ("pe", "act", "dve", "pool", "sp")
NDMASEM = 12


class Op:
    __slots__ = ("eng", "fn", "deps", "marked", "val", "is_dma", "dsem", "dval", "idx")

    def __init__(self, eng, fn, is_dma):
        self.eng = eng
        self.fn = fn
        self.deps = []
        self.marked = False
        self.val = None
        self.is_dma = is_dma
        self.dsem = None
        self.dval = None
        self.idx = None


class Sched:
    def __init__(self, nc):
        self.nc = nc
        self.ops = {e: [] for e in ENGS}
        self.writers = {}
        self.readers = {}
        self.ndma = {e: 0 for e in ENGS}
        self.final_waits = []
        self.scope = None

    def op(self, eng, fn, r=(), w=(), dma=False, final=False):
        o = Op(eng, fn, dma)
        deps = []
        r = list(r)
        w = list(w)
        if "ALL" not in w:
            r.append("ALL")
        if self.scope is not None and self.scope not in w:
            r.append(self.scope)
        for x in r:
            deps.extend(self.writers.get(x, ()))
        for x in w:
            deps.extend(self.writers.get(x, ()))
            deps.extend(self.readers.get(x, ()))
        seen = set()
        for d in deps:
            if d is o or id(d) in seen:
                continue
            seen.add(id(d))
            if d.eng == "pe" and eng == "pe" and not d.is_dma and not dma:
                continue
            o.deps.append(d)
            if not d.is_dma:
                d.marked = True
        for x in w:
            if self.readers.get(x):
                self.writers[x] = [o]
                self.readers[x] = []
            else:
                ws = self.writers.setdefault(x, [])
                ws[:] = [p for p in ws if not (p.eng == eng and p.is_dma == dma and not dma)]
                ws.append(o)
        for x in r:
            if x not in w:
                rs = self.readers.setdefault(x, [])
                rs[:] = [p for p in rs if not (p.eng == eng and not p.is_dma and not dma)]
                rs.append(o)
        if dma:
            j = self.ndma[eng]
            self.ndma[eng] = j + 1
            o.dsem = j % NDMASEM
            o.dval = 16 * (j // NDMASEM + 1)
            o.idx = j
        self.ops[eng].append(o)
        if final:
            self.final_waits.append(o)
        return o

    def emit(self):
        nc = self.nc
        with contextlib.ExitStack() as st:
            csem = {e: st.enter_context(nc.semaphore("c_" + e)) for e in ENGS}
            dsem = {e: [st.enter_context(nc.semaphore("d_%s_%d" % (e, i))) for i in range(NDMASEM)]
                    for e in ENGS if self.ndma[e] > 0}
            for e in ENGS:
                c = 0
                for o in self.ops[e]:
                    if o.marked and not o.is_dma:
                        c += 1
                        o.val = c
            block = st.enter_context(nc.Block())

            def run(e, eng):
                waited = {}
                for o in self.ops[e]:
                    waits = []
                    for d in o.deps:
                        if d.is_dma:
                            waits.append((("d", d.eng, d.dsem), dsem[d.eng][d.dsem], d.dval))
                        else:
                            waits.append((("c", d.eng), csem[d.eng], d.val))
                    if o.is_dma and o.idx >= NDMASEM:
                        waits.append((("d", e, o.dsem), dsem[e][o.dsem], o.dval - 16))
                    for key, s, v in waits:
                        if waited.get(key, 0) >= v:
                            continue
                        eng.wait_ge(s, v)
                        waited[key] = v
                    ins = o.fn(eng)
                    if o.is_dma:
                        ins.then_inc(dsem[e][o.dsem], 16)
                    elif o.marked:
                        ins.then_inc(csem[e], 1)
                for o in self.final_waits:
                    if o.eng == e:
                        eng.wait_ge(dsem[e][o.dsem], o.dval)

            if self.ops["sp"]:
                @block.sync
                def _(eng):
                    run("sp", eng)
            if self.ops["pe"]:
                @block.tensor
                def _(eng):
                    run("pe", eng)
            if self.ops["act"]:
                @block.scalar
                def _(eng):
                    run("act", eng)
            if self.ops["dve"]:
                @block.vector
                def _(eng):
                    run("dve", eng)
            if self.ops["pool"]:
                @block.gpsimd
                def _(eng):
                    run("pool", eng)


def t5_bucket(dist):
    n_buckets, max_distance = 32, 2048
    max_exact = n_buckets // 2
    d = np.maximum(dist, 1).astype(np.float32)
    large = max_exact + (np.log(d / max_exact) / math.log(max_distance / max_exact)
                         * (n_buckets - max_exact)).astype(np.int32)
    large = np.minimum(large, n_buckets - 1)
    return np.where(dist < max_exact, dist, large).astype(np.int32)


def static_consts():
    c = {}
    c["identf"] = np.eye(128, dtype=np.float32)
    c["jex"] = np.eye(128, dtype=np.float32)[::-1].copy()
    jsw = np.zeros((128, 128), np.float32)
    for m in range(64):
        jsw[m + 64, m] = -1.0
        jsw[m, m + 64] = 1.0
    c["jsw"] = jsw
    oh = np.zeros((32, 3 * 384), np.float32)
    negm = np.zeros((8, 3 * 384), np.float32)
    for bi, (win, dil) in enumerate(PATTERNS):
        for u in range(384):
            dist = u - 127
            if 0 <= dist <= win // dil:
                oh[t5_bucket(np.array([dist * dil]))[0], bi * 384 + u] = 1.0
            else:
                negm[:, bi * 384 + u] = NEG
    c["oh"] = oh
    c["negm"] = negm
    gm = np.zeros((128, 8), np.float32)
    for p in range(128):
        gm[p, p // 16] = 1.0
    c["gmask"] = gm
    wins = np.array([2, 4, 8, 16], np.float32)
    wp = np.zeros((128, 2), np.float32)
    for ch in range(2):
        for p in range(128):
            wp[p, ch] = wins[ch * 2 + p // 64]
    c["invw"] = (1.0 / wp).astype(np.float32)
    t = np.arange(16, dtype=np.float32)[None, None, :]
    c["rc16"] = (1.0 / np.minimum(t + 1.0, wp[:, :, None])).astype(np.float32)
    c["sgn"] = np.concatenate([-np.ones((64, 1), np.float32), np.ones((64, 1), np.float32)], 0)
    return c


def build(stop=None, parts=("pool", "ssm", "attn"), dbg=False):
    nc = bass.Bass("TRN2", target_bir_lowering=False)

    def din(name, shape, dt=F32):
        return nc.dram_tensor(name, list(shape), dt, kind="ExternalInput").ap()

    x_d = din("x", [SEQ, D])
    cT_d = din("cT", [128, 8])
    adaw_d = din("ada_w", [DEPTH, D, 9 * D])
    adab_d = din("ada_b", [DEPTH, 9 * D])
    lng_d = din("ln_g", [DEPTH, 3, D])
    lnb_d = din("ln_b", [DEPTH, 3, D])
    wg_d = din("ffn_w_gate", [DEPTH, 2, D, DFF])
    wu_d = din("ffn_w_up", [DEPTH, 2, D, DFF])
    wd_d = din("ffn_w_down", [DEPTH, 2, DFF, D])
    win_d = din("w_in", [DEPTH, D, 2048])
    wout_d = din("w_out", [DEPTH, D, D])
    relb_d = din("rel_bias", [32, 8])
    sare_d = din("sa_re", [DEPTH, 128, 16])
    saim_d = din("sa_im", [DEPTH, 128, 16])
    sldt_d = din("sldt", [DEPTH, 128, 16])
    sp1_d = din("sP1", [DEPTH, 128, 256])
    sp2_d = din("sP2", [DEPTH, 128, 256])
    sct_d = din("sCT", [DEPTH, 128, 256])
    sd_d = din("sd", [DEPTH, 128, 2])
    gluw_d = din("glu_w", [DEPTH, 128, 2, 256])
    glub_d = din("glu_b", [DEPTH, 128, 2])
    poolw_d = din("pool_w", [DEPTH, 128, 2, 128])
    pools_d = din("pool_s", [DEPTH, 128, 2])
    identf_d = din("identf", [128, 128])
    jex_d = din("jex", [128, 128])
    jsw_d = din("jsw", [128, 128])
    oh_d = din("oh", [32, 1152])
    negm_d = din("negm", [8, 1152])
    gmask_d = din("gmask", [128, 8])
    invw_d = din("invw", [128, 2])
    rc16_d = din("rc16", [128, 2, 16])
    sgn_d = din("sgn", [128, 1])
    out_d = nc.dram_tensor("out", [SEQ, D], F32, kind="ExternalOutput").ap()
    fv_d = nc.dram_tensor("fv_scratch", [8, 1152], F32, kind="Internal").ap()

    st = contextlib.ExitStack()
    with st:
        def T(name, shape, dt):
            return st.enter_context(nc.sbuf_tensor(name, list(shape), dt))

        PSB = [st.enter_context(nc.psum_tensor("ps%d" % i, [128, 512], F32)) for i in range(8)]

        def ps(k):
            return PSB[k], ("ps", k)

        X = T("X", [128, NT, D], F32)
        HT = T("HT", [128, 8, SEQ], BF16)
        YT = T("YT", [128, 8, SEQ], BF16)
        ARB = 49152
        AR = T("AR", [128, ARB // 2], BF16)
        ident = T("ident", [128, 128], BF16)
        identf = T("identf_s", [128, 128], F32)
        jex = T("jex_s", [128, 128], F32)
        jsw = T("jsw_s", [128, 128], F32)
        onesf = T("onesf", [128, 128], F32)
        jswb = T("jswb", [128, 128], BF16)
        modcol = T("modcol", [128, DEPTH, 72], F32)
        XH = [T("XH%d" % i, [128, D], BF16) for i in range(4)]
        mv = T("mv", [128, 4, 2], F32)
        sc = T("sc", [128, 4, 4], F32)
        condT = T("condT", [128, 8], F32)
        gmask = T("gmask_s", [128, 8], F32)
        invw = T("invw_s", [128, 2], F32)
        rc16 = T("rc16_s", [128, 2, 16], F32)
        sgn = T("sgn_s", [128, 1], F32)

        S = Sched(nc)

        class Arena:
            def __init__(self, base, size):
                self.off = base
                self.end = base + size

            def alloc(self, shape, dt):
                n = int(np.prod(shape[1:]))
                nb = n * (4 if dt in (F32, I32) else 2)
                nb = (nb + 31) // 32 * 32
                assert self.off + nb <= self.end, ("arena overflow", shape, self.off, nb, self.end)
                v = AR[0:shape[0], self.off // 2:(self.off + nb) // 2]
                self.off += nb
                if dt != BF16:
                    v = v.bitcast(dt)
                v = v[:, 0:n]
                if len(shape) == 3:
                    v = v.rearrange("p (a b) -> p a b", a=shape[1])
                elif len(shape) == 4:
                    v = v.rearrange("p (a b c) -> p a b c", a=shape[1], b=shape[2])
                return v

        class ArenaYT(Arena):
            def alloc(self, shape, dt):
                n = int(np.prod(shape[1:]))
                nb = n * (4 if dt in (F32, I32) else 2)
                nb = (nb + 31) // 32 * 32
                assert self.off + nb <= self.end
                flat = YT[:, :, :].rearrange("p a b -> p (a b)")
                v = flat[0:shape[0], self.off // 2:(self.off + nb) // 2]
                self.off += nb
                if dt != BF16:
                    v = v.bitcast(dt)
                v = v[:, 0:n]
                if len(shape) == 3:
                    v = v.rearrange("p (a b) -> p a b", a=shape[1])
                return v

        def dma(eng, out, in_, r=(), w=(), final=False, slow=False):
            if slow:
                return S.op(eng, lambda e: e.dma_start(out=out, in_=in_, allow_slow_non_contiguous=True), r=r, w=w, dma=True, final=final)
            return S.op(eng, lambda e: e.dma_start(out=out, in_=in_), r=r, w=w, dma=True, final=final)

        def mm(out, lhsT, rhs, start, stop, r, w):
            return S.op("pe", lambda e: e.matmul(out, lhsT=lhsT, rhs=rhs, start=start, stop=stop), r=r, w=w)

        def tr(out, in_, idn, r, w):
            return S.op("pe", lambda e: e.transpose(out=out, in_=in_, identity=idn), r=r, w=w)

        def act(out, in_, func, r, w, bias=0.0, scale=1.0):
            return S.op("act", lambda e: e.activation(out=out, in_=in_, func=func, bias=bias, scale=scale), r=r, w=w)

        def ts(eng, out, in0, s1, s2, op0, op1, r, w):
            if op1 is None:
                return S.op(eng, lambda e: e.tensor_scalar(out=out, in0=in0, scalar1=s1, scalar2=None, op0=op0), r=r, w=w)
            return S.op(eng, lambda e: e.tensor_scalar(out=out, in0=in0, scalar1=s1, scalar2=s2, op0=op0, op1=op1), r=r, w=w)

        def tt(eng, out, in0, in1, op, r, w):
            return S.op(eng, lambda e: e.tensor_tensor(out=out, in0=in0, in1=in1, op=op), r=r, w=w)

        def stt(out, in0, scalar, in1, op0, op1, r, w):
            return S.op("dve", lambda e: e.scalar_tensor_tensor(out=out, in0=in0, scalar=scalar, in1=in1, op0=op0, op1=op1), r=r, w=w)

        def cp(eng, out, in_, r, w):
            if eng == "act":
                return S.op("act", lambda e: e.copy(out=out, in_=in_), r=r, w=w)
            return S.op(eng, lambda e: e.tensor_copy(out=out, in_=in_), r=r, w=w)

        def memset(eng, ap, val, w):
            return S.op(eng, lambda e: e.memset(ap, val), w=w)

        fsrc_d = identf_d[0:1, 0:16]
        fdst_d = nc.dram_tensor("fence_dst", [1, 16], F32, kind="Internal").ap()

        def fence():
            sv = S.scope
            S.scope = None
            S.op("sp", lambda e: e.dma_start(out=fdst_d, in_=fsrc_d), w=["ARENA"], dma=True)
            S.scope = sv

        def fence_all():
            S.op("dve", lambda e: e.memset(sc[:, 0, 3:4], 0.0), w=["ALL", "ARENA", ("sc3", 0)])

        class arena_scope:
            def __enter__(self):
                self.sv = S.scope
                S.scope = "ARENA"

            def __exit__(self, *a):
                S.scope = self.sv

        dma("sp", identf[:], identf_d, w=["identf"])
        dma("sp", jex[:], jex_d, w=["jex"])
        dma("sp", jsw[:], jsw_d, w=["jsw"])
        dma("sp", gmask[:], gmask_d, w=["gmask"])
        dma("sp", invw[:], invw_d, w=["invw"])
        dma("sp", rc16[:], rc16_d, w=["rc16"])
        dma("sp", sgn[:], sgn_d, w=["sgn"])
        cp("dve", ident[:], identf[:], r=["identf"], w=["ident"])
        cp("dve", jswb[:], jsw[:], r=["jsw"], w=["jswb"])
        memset("dve", onesf[:], 1.0, w=["onesf"])
        xv = x_d.rearrange("(t p) d -> p t d", p=128)
        for q in range(4):
            dma("sp", X[:, q * 4:(q + 1) * 4, :], xv[:, q * 4:(q + 1) * 4, :], w=[("X", t) for t in range(q * 4, q * 4 + 4)])

        def ada_phase():
            ar = Arena(0, ARB)
            AW = [ar.alloc([128, 8, 512], F32) for _ in range(2)]
            AB = [ar.alloc([1, 512], F32) for _ in range(2)]
            ROW = [ar.alloc([1, 512], F32) for _ in range(2)]
            cTs = ar.alloc([128, 8], F32)
            dma("sp", cTs, cT_d, w=["cTs"])
            act(condT[:], cTs, AF.Silu, r=["cTs"], w=["condT"])
            it = 0
            import itertools
            gen = itertools.chain(ssm_setup_gen(0), ssm_setup_gen(1))
            prep_q = list(range(NT))
            gen_done = [False]
            for l in range(DEPTH):
                for nb in range(18):
                    for _ in range(6):
                        if next(gen, "done") == "done":
                            gen_done[0] = True
                    b = it % 2
                    it += 1
                    src = adaw_d[l, :, nb * 512:(nb + 1) * 512].rearrange("(kc p) n -> p kc n", p=128)
                    dma("sp", AW[b], src, w=[("AW", b)])
                    dma("sp", AB[b], adab_d[l:l + 1, nb * 512:(nb + 1) * 512], w=[("AB", b)])
                    pr, prr = ps(b)
                    for kc in range(8):
                        mm(pr[0:1, :], condT[:, kc:kc + 1], AW[b][:, kc, :], kc == 0, False,
                           r=["condT", ("AW", b)], w=[prr])
                    mm(pr[0:1, :], onesf[0:1, 0:1], AB[b], False, True, r=["onesf", ("AB", b)], w=[prr])
                    ev = "dve" if gen_done[0] else "act"
                    cp(ev, ROW[b], pr[0:1, :], r=[prr], w=[("ROW", b)])
                    pc, pcr = ps(2 + b)
                    for j in range(4):
                        mm(pc[:, j:j + 1], ROW[b][0:1, j * 128:(j + 1) * 128], onesf[0:1, 0:1], True, True,
                           r=[("ROW", b), "onesf"], w=[pcr])
                    v = nb // 2
                    addc = 1.0 if v % 3 == 1 else 0.0
                    if ev == "act":
                        act(modcol[:, l, nb * 4:(nb + 1) * 4], pc[:, 0:4], AF.Identity, r=[pcr], w=[("modcol", l, v)], bias=float(addc))
                    else:
                        ts("dve", modcol[:, l, nb * 4:(nb + 1) * 4], pc[:, 0:4], float(addc), None, ALU.add, None, r=[pcr], w=[("modcol", l, v)])
                    if gen_done[0] and prep_q:
                        tq_ = prep_q.pop(0)
                        sv_ = S.scope
                        S.scope = None
                        prep_tile_a(0, 0, tq_)
                        prep_tile_b(0, 0, tq_, extra=[("ssc", 0), ("ssc", 1)])
                        S.scope = sv_
            for _ in gen:
                pass
            sv_ = S.scope
            S.scope = None
            while prep_q:
                tq_ = prep_q.pop(0)
                prep_tile_a(0, 0, tq_)
                prep_tile_b(0, 0, tq_, extra=[("ssc", 0), ("ssc", 1)])
            S.scope = sv_


        NSLOT = 4
        sums = T("sums", [128, NSLOT, 4], F32)

        def finish_stats(slot, eps, c0, c1):
            ts("dve", mv[:, slot, 0:1], sums[:, slot, c0:c0 + 1], 1.0 / D, None, ALU.mult, None, r=[("sums", slot, c0)], w=[("mv", slot)])
            tt("dve", mv[:, slot, 1:2], mv[:, slot, 0:1], mv[:, slot, 0:1], ALU.mult, r=[("mv", slot)], w=[("mv1", slot)])
            stt(sc[:, slot, 3:4], sums[:, slot, c1:c1 + 1], 1.0 / D, mv[:, slot, 1:2], ALU.mult, ALU.subtract,
                r=[("sums", slot, c1), ("mv1", slot)], w=[("sc3", slot)])
            act(sc[:, slot, 0:1], sc[:, slot, 3:4], AF.Sqrt, r=[("sc3", slot)], w=[("sc0", slot)], bias=float(eps))
            S.op("dve", lambda e: e.reciprocal(out=sc[:, slot, 1:2], in_=sc[:, slot, 0:1]), r=[("sc0", slot)], w=[("sc1", slot)])

        def act_accum(tt_, slot, func, col):
            S.op("act", lambda e: e.activation(out=XH[slot][:], in_=X[:, tt_, :], func=func, accum_out=sums[:, slot, col:col + 1]),
                 r=[("X", tt_)], w=[("XH", slot), ("sums", slot, col)])

        def prep_tile_a(l, i, tt_, have_sum=False):
            slot = tt_ % NSLOT
            if not have_sum:
                act_accum(tt_, slot, AF.Identity, 2)
            act_accum(tt_, slot, AF.Square, 3)
            finish_stats(slot, LN_EPS, 2, 3)
            ts("dve", sc[:, slot, 2:3], mv[:, slot, 0:1], -1.0, sc[:, slot, 1:2], ALU.mult, ALU.mult,
               r=[("mv", slot), ("sc1", slot)], w=[("sc2", slot)])
            xh = XH[slot]
            act(xh[:], X[:, tt_, :], AF.Identity, r=[("X", tt_), ("sc1", slot), ("sc2", slot)], w=[("XH", slot)],
                bias=sc[:, slot, 2:3], scale=sc[:, slot, 1:2])

        def prep_tile_b(l, i, tt_, extra=()):
            slot = tt_ % NSLOT
            xh = XH[slot]
            pt, ptr = ps(6 + tt_ % 2)
            ptb = pt[:, :].bitcast(BF16)
            for kc in range(8):
                tr(ptb[:, kc * 128:(kc + 1) * 128], xh[:, kc * 128:(kc + 1) * 128], ident[:], r=[("XH", slot), "ident"], w=[ptr])
            for kc in range(8):
                scl = modcol[:, l, (3 * i + 1) * 8 + kc:(3 * i + 1) * 8 + kc + 1]
                shf = modcol[:, l, (3 * i) * 8 + kc:(3 * i) * 8 + kc + 1]
                if kc % 2 == 0:
                    ts("dve", HT[:, kc, tt_ * 128:(tt_ + 1) * 128], ptb[:, kc * 128:(kc + 1) * 128], scl, shf,
                       ALU.mult, ALU.add, r=[ptr, ("modcol", l, 3 * i), ("modcol", l, 3 * i + 1)] + list(extra), w=[("HT", tt_ // 4)])
                else:
                    act(HT[:, kc, tt_ * 128:(tt_ + 1) * 128], ptb[:, kc * 128:(kc + 1) * 128], AF.Identity,
                        r=[ptr, ("modcol", l, 3 * i), ("modcol", l, 3 * i + 1)] + list(extra), w=[("HT", tt_ // 4)], bias=shf, scale=scl)

        def prep(l, i):
            for tt_ in range(NT):
                prep_tile_a(l, i, tt_)
                prep_tile_b(l, i, tt_)

        LNGB = [T("LNG", [128, D], F32), T("LNB", [128, D], F32)]

        def post_setup(l, i):
            LNG, LNB = LNGB[0][:], LNGB[1][:]
            sv = S.scope
            S.scope = None
            dma("sp", LNG, bass.AP(lng_d.tensor, (l * 3 + i) * D, [[0, 128], [1, D]]), w=["LNG"])
            dma("sp", LNB, bass.AP(lnb_d.tensor, (l * 3 + i) * D, [[0, 128], [1, D]]), w=["LNB"])
            S.scope = sv
            return LNG, LNB

        def post_tile(l, i, tt_, LNG, LNB, want_sum):
            slot = tt_ % NSLOT
            act_accum(tt_, slot, AF.Identity, 0)
            act_accum(tt_, slot, AF.Square, 1)
            finish_stats(slot, LN_EPS / (ALPHA * ALPHA), 0, 1)
            stt(X[:, tt_, :], X[:, tt_, :], mv[:, slot, 0:1], LNG, ALU.subtract, ALU.mult,
                r=[("X", tt_), ("mv", slot), "LNG"], w=[("X", tt_)])
            if want_sum:
                S.op("dve", lambda e: e.scalar_tensor_tensor(out=X[:, tt_, :], in0=X[:, tt_, :], scalar=sc[:, slot, 1:2], in1=LNB,
                                                             op0=ALU.mult, op1=ALU.add, accum_out=sums[:, slot, 2:3]),
                     r=[("X", tt_), ("sc1", slot), "LNB"], w=[("X", tt_), ("sums", slot, 2)])
            else:
                stt(X[:, tt_, :], X[:, tt_, :], sc[:, slot, 1:2], LNB, ALU.mult, ALU.add,
                    r=[("X", tt_), ("sc1", slot), "LNB"], w=[("X", tt_)])

        ov = out_d.rearrange("(t p) d -> p t d", p=128)

        class Tail:
            def __init__(self, postli, prepli, final):
                self.postli, self.prepli, self.final = postli, prepli, final
                self.pend = []
                self.LNG, self.LNB = post_setup(*postli)

            def tile_done(self, tt_):
                sv = S.scope
                S.scope = None
                post_tile(self.postli[0], self.postli[1], tt_, self.LNG, self.LNB, self.prepli is not None)
                if self.prepli is not None:
                    prep_tile_a(self.prepli[0], self.prepli[1], tt_, have_sum=True)
                    self.pend.append(tt_)
                if self.final:
                    dma("sp", ov[:, tt_, :], X[:, tt_, :], r=[("X", tt_)], final=True)
                S.scope = sv

            def lagged(self, keep):
                sv = S.scope
                S.scope = None
                while len(self.pend) > keep:
                    prep_tile_b(self.prepli[0], self.prepli[1], self.pend.pop(0))
                S.scope = sv

        def gate_bc(l, i, scale, GBC, D8):
            for kc in range(8):
                col = (3 * i + 2) * 8 + kc
                ts("dve", D8[:, kc, :], identf[:], modcol[:, l, col:col + 1], float(scale), ALU.mult, ALU.mult,
                   r=["identf", ("modcol", l, 3 * i + 2)], w=["D8"])
            for hf in range(2):
                pg, pgr = ps(hf)
                mm(pg[:, :], onesf[:], D8[:, hf * 4:(hf + 1) * 4, :].rearrange("p a b -> p (a b)"), True, True, r=["onesf", "D8"], w=[pgr])
                cp("act", GBC[:, hf * 512:(hf + 1) * 512], pg[:, :], r=[pgr], w=["GBC"])

        def ffn(l, i, si, tail):
            ar = Arena(0, ARB)
            ay = ArenaYT(0, 32768)
            WG = [ay.alloc([128, 8, 512], BF16) for _ in range(2)]
            WU = [ay.alloc([128, 8, 512], BF16) for _ in range(2)]
            WD = [ar.alloc([128, 4, D], BF16) for _ in range(2)]
            ACTB = [ar.alloc([128, 4, 512], BF16) for _ in range(2)]
            SG = [ar.alloc([128, 512], F32) for _ in range(2)]
            GBC = ar.alloc([128, D], F32)
            D8 = ar.alloc([128, 8, 128], F32)
            gate_bc(l, si, 0.5 / ALPHA, GBC, D8)
            groups = [(0, 2)] + [(2 + g * 4, 4) for g in range(5)]
            pend = None
            nsg = 0
            for gi, (c0, nf) in enumerate(groups):
                b = gi % 2
                f0 = c0 * 128
                fw = nf * 128
                dma("pool", WG[b][:, :, 0:fw], wg_d[l, i, :, f0:f0 + fw].rearrange("(kc p) f -> p kc f", p=128), w=[("WG", b)])
                dma("pool", WU[b][:, :, 0:fw], wu_d[l, i, :, f0:f0 + fw].rearrange("(kc p) f -> p kc f", p=128), w=[("WU", b)])
                dma("pool", WD[b][:, 0:nf, :], wd_d[l, i, f0:f0 + fw, :].rearrange("(c p) d -> p c d", p=128), w=[("WD", b)])
                for c in range(nf):
                    tt("pool", WD[b][:, c, :], WD[b][:, c, :], GBC, ALU.mult, r=["GBC", ("WD", b)], w=[("WD", b)])
                for tsi in range(4):
                    ab = (gi * 4 + tsi) % 2
                    for c in range(nf):
                        pgk = (gi * 16 + tsi * 4 + c) % 2
                        pg, pgr = ps(pgk)
                        pu, pur = ps(2 + pgk)
                        for kc in range(8):
                            mm(pg[:, :], WG[b][:, kc, c * 128:(c + 1) * 128], HT[:, kc, tsi * 512:(tsi + 1) * 512], kc == 0, kc == 7,
                               r=[("WG", b), ("HT", tsi)], w=[pgr])
                        for kc in range(8):
                            mm(pu[:, :], WU[b][:, kc, c * 128:(c + 1) * 128], HT[:, kc, tsi * 512:(tsi + 1) * 512], kc == 0, kc == 7,
                               r=[("WU", b), ("HT", tsi)], w=[pur])
                        sgb = nsg % 2
                        nsg += 1
                        act(SG[sgb], pg[:, :], AF.Silu, r=[pgr], w=[("SG", sgb)])
                        tt("dve", ACTB[ab][:, c, :], SG[sgb], pu[:, :], ALU.mult, r=[("SG", sgb), pur], w=[("ACTB", ab, c)])
                    cur = (b, ab, nf, tsi, gi == len(groups) - 1)
                    if pend is not None:
                        down(pend, WD, ACTB, tail)
                    pend = cur
            down(pend, WD, ACTB, tail)
            tail.lagged(0)

        dcount = [0]

        def down(p, WD, ACTB, tail):
            b, ab, nf, tsi, last = p
            for t4 in range(4):
                tt_ = tsi * 4 + t4
                for hf in range(2):
                    k = 4 + dcount[0] % 2
                    dcount[0] += 1
                    pd, pdr = ps(k)
                    for c in range(nf):
                        mm(pd[:, :], ACTB[ab][:, c, t4 * 128:(t4 + 1) * 128], WD[b][:, c, hf * 512:(hf + 1) * 512], c == 0, c == nf - 1,
                           r=[("ACTB", ab, c), ("WD", b)], w=[pdr])
                    tt("dve", X[:, tt_, hf * 512:(hf + 1) * 512], X[:, tt_, hf * 512:(hf + 1) * 512], pd[:, :], ALU.add,
                       r=[("X", tt_), pdr], w=[("X", tt_)])
                if last:
                    tail.tile_done(tt_)
                    tail.lagged(2)

        stage = [0]

        def done_stage():
            stage[0] += 1
            return stop is not None and stage[0] >= stop

        def run_layers():
            for l in range(DEPTH):
                fence()
                with arena_scope():
                    ffn(l, 0, 0, Tail((l, 0), (l, 1), False))
                if done_stage():
                    return
                fence()
                with arena_scope():
                    mixer(l, Tail((l, 1), (l, 2), False) if not dbg else None)
                if dbg:
                    return
                if done_stage():
                    return
                fence()
                lastl = l == DEPTH - 1
                with arena_scope():
                    ffn(l, 1, 2, Tail((l, 2), None if lastl else (l + 1, 0), lastl))
                if done_stage():
                    return

        TWO_PI = 2.0 * math.pi

        def act_b(out, in_, func, r, w, bias, scale):
            return act(out, in_, func, r, w, bias=bias, scale=scale)

        def pool_phase(l):
            ar = Arena(0, ARB)
            WUP = ar.alloc([128, 8, 256], BF16)
            PW = ar.alloc([128, 2, 128], BF16)
            pscol = ar.alloc([128, 2], F32)
            UP = [ar.alloc([128, 2, 528], F32) for _ in range(2)]
            Pb = ar.alloc([128, 2, 528], F32)
            Qb = ar.alloc([128, 2, 528], F32)
            PL = ar.alloc([128, 2, 512], BF16)
            TM = ar.alloc([128, 2, 16], F32)
            dma("pool", WUP, win_d[l, :, 1792:2048].rearrange("(kc p) f -> p kc f", p=128), w=["WUP"])
            dma("pool", PW, poolw_d[l], w=["PW"])
            dma("sp", pscol, pools_d[l], w=["pscol"])
            memset("pool", UP[0][:, :, 0:16], 0.0, w=[("UP", 0)])
            for tsi in range(4):
                ub = UP[tsi % 2]
                ur = ("UP", tsi % 2)
                for ch in range(2):
                    pu, pur = ps(ch)
                    for kc in range(8):
                        mm(pu[:, :], WUP[:, kc, ch * 128:(ch + 1) * 128], HT[:, kc, tsi * 512:(tsi + 1) * 512], kc == 0, kc == 7,
                           r=["WUP", ("HT", tsi)], w=[pur])
                    cp("act", ub[:, ch, 16:528], pu[:, :], r=[pur], w=[ur])
                tt("dve", Pb[:, :, 1:528], ub[:, :, 1:528], ub[:, :, 0:527], ALU.add, r=[ur], w=["Pb"])
                tt("dve", Qb[64:128, 0, 3:528], Pb[64:128, 0, 3:528], Pb[64:128, 0, 1:526], ALU.add, r=["Pb"], w=["Qb"])
                tt("dve", Qb[:, 1, 3:528], Pb[:, 1, 3:528], Pb[:, 1, 1:526], ALU.add, r=["Pb"], w=["Qb"])
                tt("dve", Pb[:, 1, 7:528], Qb[:, 1, 7:528], Qb[:, 1, 3:524], ALU.add, r=["Qb"], w=["Pb"])
                tt("dve", Qb[64:128, 1, 15:528], Pb[64:128, 1, 15:528], Pb[64:128, 1, 7:520], ALU.add, r=["Pb"], w=["Qb"])
                srcs = [(Pb, 0, 64, 0), (Qb, 64, 128, 0), (Pb, 0, 64, 1), (Qb, 64, 128, 1)]
                for (sb, p0, p1, ch) in srcs:
                    stt(PL[p0:p1, ch, :], sb[p0:p1, ch, 16:528], invw[p0:p1, ch:ch + 1], ub[p0:p1, ch, 16:528], ALU.mult, ALU.subtract,
                        r=["Pb", "Qb", ur, "invw"], w=["PL"])
                    if tsi == 0:
                        tt("dve", TM[p0:p1, ch, :], sb[p0:p1, ch, 16:32], rc16[p0:p1, ch, :], ALU.mult, r=["Pb", "Qb", "rc16"], w=["TM"])
                        tt("dve", PL[p0:p1, ch, 0:16], TM[p0:p1, ch, :], ub[p0:p1, ch, 16:32], ALU.subtract, r=["TM", ur], w=["PL"])
                if tsi < 3:
                    cp("pool", UP[(tsi + 1) % 2][:, :, 0:16], ub[:, :, 512:528], r=[ur], w=[("UP", (tsi + 1) % 2)])
                for ch in range(2):
                    py, pyr = ps(2 + ch)
                    mm(py[:, :], PW[:, ch, :], PL[:, ch, :], True, True, r=["PW", "PL"], w=[pyr])
                    act_b(YT[:, 6 + ch, tsi * 512:(tsi + 1) * 512], py[:, :], AF.Identity, r=[pyr, "pscol"], w=[("YT", 6 + ch, tsi)],
                          bias=0.0, scale=pscol[:, ch:ch + 1])

        TL = 128
        NTB = 16 * (TL + 1)
        ssc_f = nc.dram_tensor("ssm_scr_f", [DEPTH, 128, 2 * NTB + 224], F32, kind="Internal").ap()
        ssc_b = nc.dram_tensor("ssm_scr_b", [DEPTH, 128, NTB + 3 * 2048], BF16, kind="Internal").ap()

        class ArenaOn(Arena):
            def __init__(self, flat, size):
                self.flat = flat
                self.off = 0
                self.end = size

            def alloc(self, shape, dt):
                n = int(np.prod(shape[1:]))
                nb = n * (4 if dt in (F32, I32) else 2)
                nb = (nb + 31) // 32 * 32
                assert self.off + nb <= self.end, ("arenaOn overflow", shape, self.off, nb, self.end)
                v = self.flat[0:shape[0], self.off // 2:(self.off + nb) // 2]
                self.off += nb
                if dt != BF16:
                    v = v.bitcast(dt)
                v = v[:, 0:n]
                if len(shape) == 3:
                    v = v.rearrange("p (a b) -> p a b", a=shape[1])
                return v

        def ssm_setup_gen(l):
            ah = ArenaOn(HT[:, :, :].rearrange("p a b -> p (a b)"), 32768)
            ay = ArenaOn(YT[:, :, :].rearrange("p a b -> p (a b)"), 32768)
            ar = Arena(40992, ARB - 40992)
            TC = ah.alloc([128, 16, TL + 1], F32)
            TS = ah.alloc([128, 16, TL + 1], F32)
            ANG = ah.alloc([128, 16, TL + 1], F32)
            BL = ay.alloc([128, 16, 128], BF16)
            IBL = ay.alloc([128, 16, 128], BF16)
            CL = ay.alloc([128, 16, 128], BF16)
            SV = ay.alloc([128, 14, 16], F32)
            P1 = ay.alloc([128, 16, 16], F32)
            P2 = ay.alloc([128, 16, 16], F32)
            CT = ay.alloc([128, 256], F32)
            TCb = ay.alloc([128, 16, TL + 1], BF16)
            AI = ay.alloc([128, 16, TL + 1], I32)
            JF = ar.alloc([128, TL + 1], F32)
            JI = ar.alloc([128, TL + 1], I32)
            Bc = ar.alloc([128, 16, 16], F32)
            IBc = ar.alloc([128, 16, 16], F32)
            Tm = ar.alloc([128, 16, 16], F32)
            are, aim, ldt, dtv, lr, thn, rho, sn, cs, er, ei, gr, gi, tq = [SV[:, k, :] for k in range(14)]
            V = "ssmv"
            dma("pool", are, sare_d[l], w=["are", V])
            dma("pool", aim, saim_d[l], w=["aim", V])
            dma("pool", ldt, sldt_d[l], w=["ldt", V])
            dma("pool", P1, sp1_d[l].rearrange("p (g c) -> p g c", g=16), w=["P1"])
            dma("pool", P2, sp2_d[l].rearrange("p (g c) -> p g c", g=16), w=["P2"])
            dma("pool", CT, sct_d[l], w=["CT"])
            yield
            act(dtv, ldt, AF.Exp, r=["ldt"], w=[V])
            tt("dve", lr, are, dtv, ALU.mult, r=["are", V], w=[V])
            tt("dve", thn, aim, dtv, ALU.mult, r=["aim", V], w=[V])
            ts("dve", thn, thn, 1.0 / TWO_PI, None, ALU.mult, None, r=[V], w=[V])
            yield
            ts("dve", rho, lr, 1.0 / 720.0, 1.0 / 120.0, ALU.mult, ALU.add, r=[V], w=[V])
            for cst in (1.0 / 24.0, 1.0 / 6.0, 0.5, 1.0, 1.0):
                tt("dve", rho, rho, lr, ALU.mult, r=[V], w=[V])
                ts("dve", rho, rho, float(cst), None, ALU.add, None, r=[V], w=[V])
                yield

            def sincos(dst, src, shift, ai):
                ts("dve", dst, src, float(shift), None, ALU.add, None, r=[V], w=[V])
                cp("dve", ai, dst, r=[V], w=[V])
                tt("dve", dst, dst, ai, ALU.subtract, r=[V], w=[V])
                act(dst, dst, AF.Sin, r=[V], w=[V], scale=TWO_PI)

            sincos(sn, thn, 0.0, AI[:, 0, 0:16])
            yield
            sincos(cs, thn, 0.25, AI[:, 0, 0:16])
            yield
            tt("dve", er, rho, cs, ALU.mult, r=[V], w=[V])
            ts("dve", er, er, -1.0, None, ALU.add, None, r=[V], w=[V])
            tt("dve", ei, rho, sn, ALU.mult, r=[V], w=[V])
            tt("dve", tq, are, are, ALU.mult, r=[V, "are"], w=[V])
            yield
            tt("dve", gr, aim, aim, ALU.mult, r=[V, "aim"], w=[V])
            tt("dve", tq, tq, gr, ALU.add, r=[V], w=[V])
            S.op("dve", lambda e: e.reciprocal(out=tq, in_=tq), r=[V], w=[V])
            tt("dve", gr, er, are, ALU.mult, r=[V], w=[V])
            yield
            tt("dve", gi, ei, aim, ALU.mult, r=[V], w=[V])
            tt("dve", gr, gr, gi, ALU.add, r=[V], w=[V])
            tt("dve", gr, gr, tq, ALU.mult, r=[V], w=[V])
            tt("dve", gi, ei, are, ALU.mult, r=[V], w=[V])
            yield
            tt("dve", er, er, aim, ALU.mult, r=[V], w=[V])
            tt("dve", gi, gi, er, ALU.subtract, r=[V], w=[V])
            tt("dve", gi, gi, tq, ALU.mult, r=[V], w=[V])
            S2, S3, S4 = ei, er, tq
            ts("dve", S2, gi, sgn[:, 0:1], None, ALU.mult, None, r=[V, "sgn"], w=[V])
            yield
            ts("dve", S3, gr, sgn[:, 0:1], None, ALU.mult, None, r=[V, "sgn"], w=[V])
            ts("dve", S4, gi, -1.0, None, ALU.mult, None, r=[V], w=[V])
            bc = lambda v: v.unsqueeze(2).to_broadcast([128, 16, 16])
            tt("dve", Bc, P1, bc(gr), ALU.mult, r=[V, "P1"], w=["Bc"])
            tt("dve", Tm, P2, bc(S2), ALU.mult, r=[V, "P2"], w=["Tm"])
            yield
            tt("dve", Bc, Bc, Tm, ALU.add, r=["Tm"], w=["Bc"])
            tt("dve", IBc, P2, bc(S3), ALU.mult, r=[V, "P2"], w=["IBc"])
            tt("dve", Tm, P1, bc(S4), ALU.mult, r=[V, "P1", "Bc"], w=["Tm"])
            tt("dve", IBc, IBc, Tm, ALU.add, r=["Tm"], w=["IBc"])
            yield
            for (src, dst, nm) in ((Bc, BL, "Bc"), (IBc, IBL, "IBc")):
                flat = src.rearrange("p g c -> p (g c)")
                for ch in range(2):
                    pt_, ptr_ = ps(7)
                    tr(pt_[:, 0:128], flat[:, ch * 128:(ch + 1) * 128], identf[:], r=[nm, "identf"], w=[ptr_])
                    for g8 in range(8):
                        ts("dve", dst[:, ch * 8 + g8, :], pt_[:, 0:128], gmask[:, g8:g8 + 1], None, ALU.mult, None,
                           r=[ptr_, "gmask"], w=["BL"])
                        if g8 % 4 == 3:
                            yield
            ts("dve", CT, CT, sgn[:, 0:1], -1.0, ALU.mult, ALU.mult, r=["CT", "sgn"], w=["CT"])
            memset("pool", CL, 0.0, w=["CL"])
            for g in range(16):
                g8 = g % 8
                cp("dve", CL[:, g, 16 * g8:16 * g8 + 16], CT[:, g * 16:(g + 1) * 16], r=["CT"], w=["CL"])
                if g % 4 == 3:
                    yield
            S.op("pool", lambda e: e.iota(out=JI, pattern=[[1, TL + 1]], base=0, channel_multiplier=0), w=["JI"])
            cp("dve", JF, JI, r=["JI"], w=["JF"])
            tt("dve", ANG, JF.unsqueeze(1).to_broadcast([128, 16, TL + 1]), thn.unsqueeze(2).to_broadcast([128, 16, TL + 1]),
               ALU.mult, r=["JF", V], w=[V])
            yield
            sincos(TS, ANG, 0.0, AI)
            yield
            sincos(TC, ANG, 0.25, AI)
            yield
            cp("dve", TCb, TC, r=[V], w=[V])
            dma("pool", ssc_f[l, :, 0:NTB], TC.rearrange("p a b -> p (a b)"), r=[V], w=[("ssc", l)])
            dma("pool", ssc_f[l, :, NTB:2 * NTB], TS.rearrange("p a b -> p (a b)"), r=[V], w=[("ssc", l)])
            dma("pool", ssc_f[l, :, 2 * NTB:2 * NTB + 224], SV.rearrange("p a b -> p (a b)"), r=[V], w=[("ssc", l)])
            dma("pool", ssc_b[l, :, 0:NTB], TCb.rearrange("p a b -> p (a b)"), r=[V], w=[("ssc", l)])
            dma("pool", ssc_b[l, :, NTB:NTB + 2048], BL.rearrange("p a b -> p (a b)"), r=["BL"], w=[("ssc", l)])
            dma("pool", ssc_b[l, :, NTB + 2048:NTB + 4096], IBL.rearrange("p a b -> p (a b)"), r=["BL"], w=[("ssc", l)])
            dma("pool", ssc_b[l, :, NTB + 4096:NTB + 6144], CL.rearrange("p a b -> p (a b)"), r=["CL"], w=[("ssc", l)])
            yield

        def ssm_phase(l):
            ay = ArenaOn(YT[:, 0:4, :].rearrange("p a b -> p (a b)"), 16384)
            ar = Arena(0, ARB)
            BL = ay.alloc([128, 16, 128], BF16)
            IBL = ay.alloc([128, 16, 128], BF16)
            CL = ay.alloc([128, 16, 128], BF16)
            SV = ay.alloc([128, 14, 16], F32)
            WUS = ar.alloc([128, 8, 256], BF16)
            TC = ar.alloc([128, 16, TL + 1], F32)
            TS = ar.alloc([128, 16, TL + 1], F32)
            GW = ar.alloc([128, 2, 256], BF16)
            sdcol = ar.alloc([128, 2], F32)
            glub = ar.alloc([128, 2], F32)
            TCb = ar.alloc([128, 16, TL + 1], BF16)
            rho = SV[:, 6, :]
            V = "ssmv2"
            dma("sp", TC.rearrange("p a b -> p (a b)"), ssc_f[l, :, 0:NTB], r=[("ssc", l)], w=[V])
            dma("sp", TS.rearrange("p a b -> p (a b)"), ssc_f[l, :, NTB:2 * NTB], r=[("ssc", l)], w=[V])
            dma("sp", SV.rearrange("p a b -> p (a b)"), ssc_f[l, :, 2 * NTB:2 * NTB + 224], r=[("ssc", l)], w=[V])
            dma("sp", TCb.rearrange("p a b -> p (a b)"), ssc_b[l, :, 0:NTB], r=[("ssc", l)], w=[V])
            dma("sp", BL.rearrange("p a b -> p (a b)"), ssc_b[l, :, NTB:NTB + 2048], r=[("ssc", l)], w=["BL"])
            dma("sp", IBL.rearrange("p a b -> p (a b)"), ssc_b[l, :, NTB + 2048:NTB + 4096], r=[("ssc", l)], w=["BL"])
            dma("sp", CL.rearrange("p a b -> p (a b)"), ssc_b[l, :, NTB + 4096:NTB + 6144], r=[("ssc", l)], w=["CL"])
            dma("sp", sdcol, sd_d[l], w=["sdcol"])
            dma("sp", glub, glub_d[l], w=["glub"])
            dma("pool", GW, gluw_d[l], w=["GW"])
            dma("pool", WUS, win_d[l, :, 1536:1792].rearrange("(kc p) f -> p kc f", p=128), w=["WUS"])
            USS2 = [ar.alloc([128, 2, 512], BF16) for _ in range(2)]
            Wb = [ar.alloc([128, 4, TL], BF16) for _ in range(2)]
            T2 = ar.alloc([128, 4, TL], BF16)
            STb = [ar.alloc([128, 4, TL], F32) for _ in range(2)]
            SBh = [ar.alloc([128, 4, TL], BF16) for _ in range(2)]
            T2b = ar.alloc([128, 4, TL], BF16)
            S1 = ar.alloc([128, 4, TL], BF16)
            SB = [ar.alloc([128, 4, TL], BF16) for _ in range(2)]
            GE = ar.alloc([128, 2, TL], BF16)
            CI = ar.alloc([128, 16], F32)
            CA = ar.alloc([128, 4], F32)
            CB = ar.alloc([128, 4], F32)
            YV = ay.alloc([128, 2, TL], F32)
            G1 = ay.alloc([128, 2, TL], F32)
            G2 = ay.alloc([128, 2, TL], F32)
            NB = 64
            p0_, p0r = ps(0)
            pL, pLr = ps(6)
            pAs = [ps(1), ps(2)]
            pBs = [ps(3), ps(7)]
            pC, pCr = ps(4)
            pC3 = pC[:, :].rearrange("p (a b) -> p a b", a=4)

            def stage_F_pe(n):
                k, bq = n // 4, n % 4
                tsi, kk, ch = k // 4, k % 4, bq // 2
                USS = USS2[tsi % 2]
                ur = ("USS", tsi % 2)
                if kk == 0 and bq == 0:
                    for c2 in range(2):
                        for kc in range(8):
                            mm(p0_[:, :], WUS[:, kc, c2 * 128:(c2 + 1) * 128], HT[:, kc, tsi * 512:(tsi + 1) * 512], kc == 0, kc == 7,
                               r=["WUS", ("HT", tsi)], w=[p0r])
                        cp("act", USS[:, c2, :], p0_[:, :], r=[p0r], w=[ur])
                pA, pAr = pAs[n % 2]
                pB, pBr = pBs[n % 2]
                for gi_ in range(4):
                    mm(pA[:, gi_ * TL:(gi_ + 1) * TL], BL[:, 4 * bq + gi_, :], USS[:, ch, kk * TL:(kk + 1) * TL], True, True, r=["BL", ur], w=[pAr])
                for gi_ in range(4):
                    mm(pB[:, gi_ * TL:(gi_ + 1) * TL], IBL[:, 4 * bq + gi_, :], USS[:, ch, kk * TL:(kk + 1) * TL], True, True, r=["BL", ur], w=[pBr])

            def stage_F_dve(n):
                k, bq = n // 4, n % 4
                gs = slice(4 * bq, 4 * bq + 4)
                wb = Wb[n % 2]
                wbr = ("Wb", n % 2)
                pA, pAr = pAs[n % 2]
                pB, pBr = pBs[n % 2]
                pA3 = pA[:, :].rearrange("p (a b) -> p a b", a=4)
                pB3 = pB[:, :].rearrange("p (a b) -> p a b", a=4)
                tt("dve", wb, pA3, TC[:, gs, 0:TL], ALU.mult, r=[pAr, V], w=[wbr])
                tt("dve", T2, pB3, TS[:, gs, 0:TL], ALU.mult, r=[pBr, V], w=["T2"])
                tt("dve", wb, wb, T2, ALU.subtract, r=["T2"], w=[wbr])

            def stage_S(n):
                k, bq = n // 4, n % 4
                wb = Wb[n % 2]
                wbr = ("Wb", n % 2)
                stb = STb[n % 2]
                stbr = ("STb", n % 2)
                sbh = SBh[n % 2]
                sbhr = ("SBh", n % 2)
                for gi_ in range(4):
                    g = 4 * bq + gi_
                    ini = CI[:, g:g + 1] if k > 0 else 0.0
                    S.op("dve", lambda e, gi_=gi_, g=g, ini=ini: e.tensor_tensor_scan(
                        out=stb[:, gi_, :], data0=rho[:, g:g + 1].to_broadcast([128, TL]), data1=wb[:, gi_, :],
                        initial=ini, op0=ALU.mult, op1=ALU.add), r=[wbr, V, ("CI", bq)], w=[stbr])
                cp("act", sbh, stb, r=[stbr], w=[sbhr])
                mm(pC[:, :], jswb[:], sbh.rearrange("p a b -> p (a b)"), True, True, r=["jswb", sbhr], w=[pCr])
                if k < 15:
                    mm(pL[:, 0:4], jsw[:], stb[:, :, TL - 1], True, True, r=["jsw", stbr], w=[pLr])

            def stage_B(n):
                k, bq = n // 4, n % 4
                kk, ch = k % 4, bq // 2
                tsi = k // 4
                gs = slice(4 * bq, 4 * bq + 4)
                stb = STb[n % 2]
                stbr = ("STb", n % 2)
                sbh = SBh[n % 2]
                sbhr = ("SBh", n % 2)
                sb = SB[n % 2]
                sbr = ("SB", n % 2)
                USS = USS2[tsi % 2]
                ur = ("USS", tsi % 2)
                tt("dve", T2b, pC3, TS[:, gs, 0:TL], ALU.mult, r=[pCr, V], w=["T2b"])
                tt("dve", S1, sbh, TCb[:, gs, 0:TL], ALU.mult, r=[sbhr, V], w=["S1"])
                tt("dve", sb, S1, T2b, ALU.add, r=["S1", "T2b"], w=[sbr])
                if k < 15:
                    tt("dve", CA, pL[:, 0:4], TS[:, gs, TL], ALU.mult, r=[pLr, V], w=["CA"])
                    tt("dve", CB, stb[:, :, TL - 1], TC[:, gs, TL], ALU.mult, r=[stbr, V], w=["CB"])
                    tt("dve", CI[:, gs], CA, CB, ALU.add, r=["CA", "CB"], w=[("CI", bq)])
                pY, pYr = ps(5)
                for gi_ in range(4):
                    g = 4 * bq + gi_
                    mm(pY[:, ch * TL:(ch + 1) * TL], CL[:, g, :], sb[:, gi_, :], g % 8 == 0, g % 8 == 7, r=["CL", sbr], w=[pYr])
                if bq == 3:
                    glu_a(k)
                if bq == 0 and k > 0:
                    glu_b(k - 1)

            def glu_a(k):
                kk, tsi = k % 4, k // 4
                USS = USS2[tsi % 2]
                ur = ("USS", tsi % 2)
                cols = slice(kk * TL, (kk + 1) * TL)
                for c2 in range(2):
                    pY2, pY2r = ps(5)
                    stt(YV[:, c2, :], USS[:, c2, cols], sdcol[:, c2:c2 + 1], pY2[:, c2 * TL:(c2 + 1) * TL], ALU.mult, ALU.add, r=[ur, "sdcol", pY2r], w=["YV"])
                act(G1, YV, AF.Square, r=["YV"], w=["G1"])
                ts("pool", G1, G1, 0.044715, 1.0, ALU.mult, ALU.add, r=["G1"], w=["G1"])
                tt("pool", G1, G1, YV, ALU.mult, r=["G1", "YV"], w=["G1"])
                act(G2, G1, AF.Sigmoid, r=["G1"], w=["G2"], scale=2.0 * math.sqrt(2.0 / math.pi))
                tt("pool", GE, YV, G2, ALU.mult, r=["G2", "YV"], w=["GE"])

            def glu_b(k):
                tsi = k // 4
                tok = slice(k * TL, (k + 1) * TL)
                for dch in range(2):
                    pG, pGr = ps(6)
                    for c2 in range(2):
                        mm(pG[:, 128:128 + TL], GW[:, c2, dch * 128:(dch + 1) * 128], GE[:, c2, :], c2 == 0, c2 == 1, r=["GW", "GE"], w=[pGr])
                    act_b(G1[:, dch, :], pG[:, 128:128 + TL], AF.Sigmoid, r=[pGr, "glub", "G1"], w=["G1"], bias=glub[:, dch:dch + 1], scale=1.0)
                    tt("pool", YT[:, 4 + dch, tok], YV[:, dch, :], G1[:, dch, :], ALU.mult, r=["G1", "YV"], w=[("YT", 4 + dch, tsi)])

            stage_F_pe(0)
            for it in range(NB + 2):
                if it + 1 < NB:
                    stage_F_pe(it + 1)
                if it < NB:
                    stage_F_dve(it)
                if 0 <= it - 2 < NB:
                    stage_B(it - 2)
                if 0 <= it - 1 < NB:
                    stage_S(it - 1)
            glu_b(15)

        def bias_setup():
            ar = Arena(0, ARB)
            relb = ar.alloc([32, 8], F32)
            OH = ar.alloc([32, 1152], F32)
            NG = ar.alloc([8, 1152], F32)
            FV = ar.alloc([8, 1152], F32)
            dma("sp", relb, relb_d, w=["relb"])
            dma("sp", OH, oh_d, w=["OH"])
            dma("sp", NG, negm_d, w=["NG"])
            for br in range(3):
                p_, pr_ = ps(br)
                mm(p_[0:8, 0:384], relb, OH[:, br * 384:(br + 1) * 384], True, True, r=["relb", "OH"], w=[pr_])
                tt("dve", FV[:, br * 384:(br + 1) * 384], p_[0:8, 0:384], NG[:, br * 384:(br + 1) * 384], ALU.add, r=[pr_, "NG"], w=["FV"])
            dma("sp", fv_d, FV, r=["FV"], w=["fv_d"])

        def attn_phase(l):
            ar = Arena(0, ARB)
            WQKV = ar.alloc([128, 8, 384], BF16)
            QZ = [ar.alloc([128, SEQ], BF16) for _ in range(2)]
            KT = ar.alloc([128, SEQ], BF16)
            VP = ar.alloc([128, 3, 16, 192], BF16)
            BTp = ar.alloc([128, 2, 768], BF16)
            Hb = ar.alloc([128, 256], F32)
            PT = [ar.alloc([128, 128], BF16) for _ in range(8)]
            RD = [ar.alloc([128, 512], F32) for _ in range(1)]
            VT = ar.alloc([128, SEQ], BF16)
            memset("pool", QZ[0][64:128, :], 0.0, w=[("QZ", 0)])
            memset("pool", QZ[1][0:64, :], 0.0, w=[("QZ", 1)])
            memset("pool", VP[:, :, :, 64:128], 1.0, w=["VP"])
            npt = 0
            nsc = 0
            nrd = 0
            for hp in range(4):
                for j, base in enumerate((0, 512, 1024)):
                    dma("pool", WQKV[:, :, j * 128:(j + 1) * 128],
                        win_d[l, :, base + hp * 128:base + (hp + 1) * 128].rearrange("(kc p) f -> p kc f", p=128), w=["WQKV"])
                for tsi in range(4):
                    pq, pqr = ps(6)
                    for kc in range(8):
                        mm(pq[:, :], WQKV[:, kc, 0:128], HT[:, kc, tsi * 512:(tsi + 1) * 512], kc == 0, kc == 7, r=["WQKV", ("HT", tsi)], w=[pqr])
                    act(QZ[0][0:64, tsi * 512:(tsi + 1) * 512], pq[0:64, :], AF.Copy, r=[pqr], w=[("QZ", 0)], scale=0.125)
                    act(QZ[1][64:128, tsi * 512:(tsi + 1) * 512], pq[64:128, :], AF.Copy, r=[pqr], w=[("QZ", 1)], scale=0.125)
                    pk, pkr = ps(7)
                    for kc in range(8):
                        mm(pk[:, :], WQKV[:, kc, 128:256], HT[:, kc, tsi * 512:(tsi + 1) * 512], kc == 0, kc == 7, r=["WQKV", ("HT", tsi)], w=[pkr])
                    cp("dve", KT[:, tsi * 512:(tsi + 1) * 512], pk[:, :], r=[pkr], w=["KT"])
                for tsi in range(4):
                    pvt, pvtr = ps(6 + tsi % 2)
                    for kc in range(8):
                        mm(pvt[:, :], WQKV[:, kc, 256:384], HT[:, kc, tsi * 512:(tsi + 1) * 512], kc == 0, kc == 7, r=["WQKV", ("HT", tsi)], w=[pvtr])
                    cp("act", VT[:, tsi * 512:(tsi + 1) * 512], pvt[:, :], r=[pvtr], w=["VT"])
                nv = 0
                for br, (win, dil) in enumerate(PATTERNS):
                    nbk = 16 // dil
                    for q4 in range(4):
                        pv, pvr = ps(6 + nv % 2)
                        nv += 1
                        pvb = pv[:, :].bitcast(BF16)
                        for t4 in range(4):
                            tix = q4 * 4 + t4
                            rr, m = tix // nbk, tix % nbk
                            t0 = rr + dil * 128 * m
                            tr(pvb[:, t4 * 128:(t4 + 1) * 128], VT[:, t0:t0 + dil * 127 + 1:dil], ident[:], r=["VT", "ident"], w=[pvr])
                        outv = VP[:, br, q4 * 4:(q4 + 1) * 4, :].rearrange("p t (a c) -> p t a c", a=3)[:, :, 0:3:2, :]
                        inv_ = pvb[:, 0:512].rearrange("p (t a c) -> p t a c", t=4, a=2)
                        cp("dve", outv, inv_, r=[pvr], w=["VP"])
                for hh in range(2):
                    h = 2 * hp + hh
                    for br in range(3):
                        dma("sp", Hb, bass.AP(fv_d.tensor, h * 1152 + br * 384, [[1, 128], [1, 256]]), r=["fv_d"], w=["Hb"])
                        pb_, pbr_ = ps(6 + br % 2)
                        mm(pb_[:, 0:256], jex[:], Hb, True, True, r=["jex", "Hb"], w=[pbr_])
                        act(BTp[:, hh, br * 256:(br + 1) * 256], pb_[:, 0:256], AF.Exp, r=[pbr_], w=["BTp"])
                for hh in range(2):
                    accs = [ps(b) for b in range(4)]
                    started = [False] * 4
                    tasks = []
                    for br, (win, dil) in enumerate(PATTERNS):
                        nbk = 16 // dil
                        for rr in range(dil):
                            for m in range(nbk):
                                for qb in (m, m + 1):
                                    if qb >= nbk:
                                        continue
                                    tasks.append((br, dil, nbk, rr, m, qb))
                    pendq = []

                    def pieces_of(task):
                        br, dil, nbk, rr, m, qb = task
                        if dil == 16:
                            return [(b, rr, 32 * b, 32) for b in range(4)]
                        elif dil == 4:
                            return [(qb, rr, 0, 128)]
                        t0 = 128 * qb
                        return [(t0 // 512, t0 % 512, 0, 128)]

                    remaining = [0] * 4
                    for task in tasks:
                        for (b, c0, i0, n) in pieces_of(task):
                            remaining[b] += 1

                    def do_pv(task, slot):
                        br, dil, nbk, rr, m, qb = task
                        tix = rr * nbk + m
                        lhs = VP[:, br, tix, hh * 64:hh * 64 + 128]
                        for (b, c0, i0, n) in pieces_of(task):
                            acc, accr = accs[b]
                            remaining[b] -= 1
                            mm(acc[:, c0:c0 + dil * (n - 1) + 1:dil], lhs, PT[slot][:, i0:i0 + n], not started[b], remaining[b] == 0,
                               r=["VP", ("PT", slot)], w=[accr])
                            started[b] = True

                    for task in tasks:
                        br, dil, nbk, rr, m, qb = task
                        k0 = rr + dil * 128 * m
                        q0 = rr + dil * 128 * qb
                        psc, pscr = ps(4 + nsc % 4)
                        nsc += 1
                        mm(psc[:, 0:128], KT[:, k0:k0 + dil * 127 + 1:dil], QZ[hh][:, q0:q0 + dil * 127 + 1:dil], True, True,
                           r=["KT", ("QZ", hh)], w=[pscr])
                        off = (qb - m) * 128
                        slot = npt % 8
                        npt += 1
                        act(PT[slot], psc[:, 0:128], AF.Exp, r=[pscr], w=[("PT", slot)])
                        tt("dve", PT[slot], PT[slot], BTp[:, hh, br * 256 + off:br * 256 + off + 128], ALU.mult, r=["BTp", ("PT", slot)], w=[("PT", slot)])
                        pendq.append((task, slot))
                        if len(pendq) > 6:
                            do_pv(*pendq.pop(0))
                    while pendq:
                        do_pv(*pendq.pop(0))
                    for b in range(4):
                        acc, accr = accs[b]
                        rd = RD[0]
                        rdr = ("RD", 0)
                        nrd += 1
                        if hh == 0:
                            act(rd[0:64, :], acc[64:128, :], AF.Ln, r=[accr], w=[rdr])
                            act(rd[0:64, :], rd[0:64, :], AF.Exp, r=[rdr], w=[rdr], scale=-1.0)
                            tt("dve", YT[0:64, hp, b * 512:(b + 1) * 512], acc[0:64, :], rd[0:64, :], ALU.mult, r=[accr, rdr], w=[("YT", hp, b)])
                        else:
                            act(rd[64:128, :], acc[0:64, :], AF.Ln, r=[accr], w=[rdr])
                            act(rd[64:128, :], rd[64:128, :], AF.Exp, r=[rdr], w=[rdr], scale=-1.0)
                            tt("dve", YT[64:128, hp, b * 512:(b + 1) * 512], acc[64:128, :], rd[64:128, :], ALU.mult, r=[accr, rdr], w=[("YT", hp, b)])

        def wout_phase(l, tail):
            ar = Arena(0, ARB)
            WO = ar.alloc([128, 8, D], BF16)
            GBC = ar.alloc([128, D], F32)
            D8 = ar.alloc([128, 8, 128], F32)
            dma("pool", WO, wout_d[l].rearrange("(kc p) d -> p kc d", p=128), w=["WO"])
            gate_bc(l, 1, 1.0 / ALPHA, GBC, D8)
            for kc in range(8):
                tt("pool", WO[:, kc, :], WO[:, kc, :], GBC, ALU.mult, r=["GBC", "WO"], w=["WO"])
            n = 0
            for tt_ in range(NT):
                for hf in range(2):
                    po, por = ps(n % 2)
                    n += 1
                    for kc in range(8):
                        mm(po[:, :], YT[:, kc, tt_ * 128:(tt_ + 1) * 128], WO[:, kc, hf * 512:(hf + 1) * 512], kc == 0, kc == 7,
                           r=["WO"] + [("YT", kc, b) for b in range(4)], w=[por])
                    tt("dve", X[:, tt_, hf * 512:(hf + 1) * 512], X[:, tt_, hf * 512:(hf + 1) * 512], po[:, :], ALU.add,
                       r=[("X", tt_), por], w=[("X", tt_)])
                tail.tile_done(tt_)
                tail.lagged(2)
            tail.lagged(0)

        def mixer(l, tail):
            if "pool" in parts:
                pool_phase(l)
                fence()
            if "ssm" in parts:
                ssm_phase(l)
                fence()
            if "attn" in parts:
                attn_phase(l)
                fence()
            if dbg:
                return
            wout_phase(l, tail)

        with arena_scope():
            ada_phase()
        fence_all()
        with arena_scope():
            bias_setup()
        run_layers()
        if dbg:
            fence_all()
            dbg_d = nc.dram_tensor("dbg", [128, 8, SEQ], F32, kind="ExternalOutput").ap()
            dma("pool", dbg_d, YT[:, :, :], r=["ALL"], final=True)
        if dbg or stop is not None:
            fence_all()
            for q in range(4):
                dma("sp", ov[:, q * 4:(q + 1) * 4, :], X[:, q * 4:(q + 1) * 4, :], r=[("X", t) for t in range(q * 4, q * 4 + 4)], final=True)
        S.emit()
    return nc


def host_inputs(inputs):
    f = lambda a: np.ascontiguousarray(np.asarray(a, dtype=np.float32))
    consts = static_consts()
    shared = {}
    for k in ("ada_w", "ada_b", "ln_g", "ln_b", "ffn_w_gate", "ffn_w_up", "ffn_w_down", "w_in", "w_out", "rel_bias"):
        shared[k] = f(inputs[k])
    a_re, a_im = f(inputs["ssm_a_re"]), f(inputs["ssm_a_im"])
    dup = lambda a: np.concatenate([a, a], axis=1)
    shared["sa_re"] = f(dup(a_re.transpose(0, 2, 1)))
    shared["sa_im"] = f(dup(a_im.transpose(0, 2, 1)))
    shared["sldt"] = f(np.broadcast_to(f(inputs["ssm_log_dt"])[:, None, :], (DEPTH, 128, 16)))
    b_re = f(inputs["ssm_b_re"]).transpose(0, 2, 1, 3).reshape(DEPTH, 64, 256)
    b_im = f(inputs["ssm_b_im"]).transpose(0, 2, 1, 3).reshape(DEPTH, 64, 256)
    shared["sP1"] = f(np.concatenate([b_re, b_im], axis=1))
    shared["sP2"] = f(np.concatenate([b_im, b_re], axis=1))
    c_re = f(inputs["ssm_c_re"]).transpose(0, 3, 1, 2).reshape(DEPTH, 64, 256)
    c_im = f(inputs["ssm_c_im"]).transpose(0, 3, 1, 2).reshape(DEPTH, 64, 256)
    shared["sCT"] = f(np.concatenate([c_re, c_im], axis=1))
    col2 = lambda a: f(f(a).reshape(DEPTH, 2, 128).transpose(0, 2, 1))
    shared["sd"] = col2(inputs["ssm_d"])
    shared["glu_b"] = col2(inputs["glu_b"])
    shared["pool_s"] = col2(inputs["pool_scale"])
    shared["glu_w"] = f(f(inputs["glu_w"]).reshape(DEPTH, 2, 128, 256).transpose(0, 2, 1, 3))
    pw = f(inputs["pool_w"])
    pbd = np.zeros((DEPTH, 128, 2, 128), np.float32)
    for ch in range(2):
        for h in range(2):
            pbd[:, h * 64:(h + 1) * 64, ch, h * 64:(h + 1) * 64] = pw[:, ch * 2 + h]
    shared["pool_w"] = pbd
    shared.update(consts)
    x = f(inputs["x"])
    c = f(inputs["c"])
    maps = []
    for b in range(8):
        m = dict(shared)
        m["x"] = x[b]
        m["cT"] = f(c[b].reshape(8, 128).T)
        maps.append(m)
    return maps


_NC_CACHE = {}


def kernel(**inputs):
    if "nc" not in _NC_CACHE:
        _NC_CACHE["nc"] = build()
    nc = _NC_CACHE["nc"]
    maps = host_inputs(inputs)
    res = run_bass_kernel_spmd(nc, maps, core_ids=list(range(8)))
    return np.stack([np.asarray(r["out"], dtype=np.float32) for r in res.results], axis=0)
```
